# Optimizing a Trainium2 kernel written in Bass

```python
import math
import jax, jax.numpy as jnp
from jax import lax
import numpy as np

D_MODEL = 1024
BATCH = 4
SEQ = 4096
DEPTH = 4

N_MEM = 256
MEM_HEADS = 4
MEM_HEAD_DIM = 64
MEM_WIDTH = MEM_HEADS * MEM_HEAD_DIM
SSM_WIDTH = D_MODEL // 2
MLA_WIDTH = D_MODEL // 2
MIX_WIDTH = SSM_WIDTH + MLA_WIDTH
SSM_GROUP = 16
SSM_GROUPS = SSM_WIDTH // SSM_GROUP
SSM_STATE = 64
MLA_HEADS = 8
QK_NOPE = 64
QK_ROPE = 32
QK_DIM = QK_NOPE + QK_ROPE
V_DIM = MLA_WIDTH // MLA_HEADS
Q_LORA = 256
KV_LORA = 128
ROPE_THETA = 10000.0
Q_BLOCK = 128
D_FF = 4 * D_MODEL
IN_COLS = SSM_WIDTH + Q_LORA + KV_LORA + QK_ROPE
EPS = 1e-6

kernel_name = 'hymba_s5_mla_memory_trunk'


def rms_norm(x, gain):
    xf = x.astype(jnp.float32)
    y = xf * lax.rsqrt(jnp.mean(xf * xf, axis=-1, keepdims=True) + EPS)
    return y.astype(x.dtype) * gain


def rope(x, positions):
    half = QK_ROPE // 2
    inv_freq = ROPE_THETA ** (-jnp.arange(half, dtype=jnp.float32) / half)
    ang = positions.astype(jnp.float32)[..., None] * inv_freq
    ang = ang.reshape(ang.shape[:2] + (1,) * (x.ndim - 3) + (half,))
    cos = jnp.cos(ang).astype(x.dtype)
    sin = jnp.sin(ang).astype(x.dtype)
    x1, x2 = x[..., :half], x[..., half:]
    return jnp.concatenate([x1 * cos - x2 * sin, x2 * cos + x1 * sin], axis=-1)


def s5_mixer(u, lam_re, lam_im, log_step, b_re, b_im, c_re, c_im, d, w_glu, b_glu):
    bsz, seq, _ = u.shape
    f32 = jnp.float32
    uf = u.astype(f32).reshape(bsz, seq, SSM_GROUPS, SSM_GROUP)
    lam = lax.complex(lam_re.astype(f32), lam_im.astype(f32))
    step = jnp.exp(log_step.astype(f32))[:, None]
    a_bar = jnp.exp(lam * step)
    b = lax.complex(b_re.astype(f32), b_im.astype(f32))
    b_bar = ((a_bar - 1.0) / lam)[..., None] * b
    c = lax.complex(c_re.astype(f32), c_im.astype(f32))
    bu = jnp.einsum('gph,bsgh->bsgp', b_bar, uf.astype(jnp.complex64))
    a_seq = jnp.broadcast_to(a_bar, bu.shape)

    def combine(e1, e2):
        a1, s1 = e1
        a2, s2 = e2
        return a2 * a1, a2 * s1 + s2

    _, states = lax.associative_scan(combine, (a_seq, bu), axis=1)
    y = jnp.einsum('ghp,bsgp->bsgh', c, states).real + d.astype(f32).reshape(SSM_GROUPS, SSM_GROUP) * uf
    y = jax.nn.gelu(y.reshape(bsz, seq, SSM_WIDTH)).astype(u.dtype)
    return y * jax.nn.sigmoid(y @ w_glu + b_glu)


def causal_block_attention(q, k, v):
    bsz, seq, heads, dq = q.shape
    n_blocks = seq // Q_BLOCK
    scale = 1.0 / math.sqrt(dq)
    qb = q.reshape(bsz, n_blocks, Q_BLOCK, heads, dq).transpose(1, 0, 3, 2, 4)
    kt = k.transpose(0, 2, 1, 3)
    vt = v.transpose(0, 2, 1, 3)
    k_pos = jnp.arange(seq)

    def block(args):
        q_blk, blk = args
        s = jnp.einsum('bhqd,bhkd->bhqk', q_blk, kt).astype(jnp.float32) * scale
        q_pos = blk * Q_BLOCK + jnp.arange(Q_BLOCK)
        s = jnp.where(k_pos[None, :] <= q_pos[:, None], s, -jnp.inf)
        p = jax.nn.softmax(s, axis=-1).astype(vt.dtype)
        return jnp.einsum('bhqk,bhkd->bhqd', p, vt)

    o = lax.map(block, (qb, jnp.arange(n_blocks)))
    return o.transpose(1, 0, 3, 2, 4).reshape(bsz, seq, heads, v.shape[-1])


def mla_mixer(c_q, c_kv, k_rope, positions, q_norm, w_uq, kv_norm, w_ukv, q_gain, k_gain):
    bsz, seq, _ = c_q.shape
    q = (rms_norm(c_q, q_norm) @ w_uq).reshape(bsz, seq, MLA_HEADS, QK_DIM)
    kv = (rms_norm(c_kv, kv_norm) @ w_ukv).reshape(bsz, seq, MLA_HEADS, QK_NOPE + V_DIM)
    k_nope, v = kv[..., :QK_NOPE], kv[..., QK_NOPE:]
    k_pe = jnp.broadcast_to(k_rope[:, :, None, :], (bsz, seq, MLA_HEADS, QK_ROPE))
    k = jnp.concatenate([k_nope, k_pe], axis=-1)
    q = rms_norm(q, q_gain)
    k = rms_norm(k, k_gain)
    q = jnp.concatenate([q[..., :QK_NOPE], rope(q[..., QK_NOPE:], positions)], axis=-1)
    k = jnp.concatenate([k[..., :QK_NOPE], rope(k[..., QK_NOPE:], positions)], axis=-1)
    out = causal_block_attention(q, k, v)
    return out.reshape(bsz, seq, MLA_WIDTH)


def memory_cross_attention(h, mem_h, w_q, w_kv, q_gain, k_gain, w_o):
    bsz, seq, _ = h.shape
    n_mem = mem_h.shape[1]
    q = (h @ w_q).reshape(bsz, seq, MEM_HEADS, MEM_HEAD_DIM)
    kv = (mem_h @ w_kv).reshape(bsz, n_mem, MEM_HEADS, 2 * MEM_HEAD_DIM)
    k, v = kv[..., :MEM_HEAD_DIM], kv[..., MEM_HEAD_DIM:]
    q = rms_norm(q, q_gain)
    k = rms_norm(k, k_gain)
    s = jnp.einsum('bqhd,bkhd->bhqk', q, k).astype(jnp.float32) / math.sqrt(MEM_HEAD_DIM)
    p = jax.nn.softmax(s, axis=-1).astype(v.dtype)
    o = jnp.einsum('bhqk,bkhd->bqhd', p, v).reshape(bsz, seq, MEM_WIDTH)
    return o @ w_o


def setup_inputs(seed: int = 0) -> dict:
    key = jax.random.key(seed)
    ks = jax.random.split(key, 40)
    f32 = jnp.float32

    def nrm(k, shape, scale):
        return jax.random.normal(k, shape, f32) * scale

    def gain(k, shape):
        return 1.0 + 0.01 * jax.random.normal(k, shape, f32)

    L = DEPTH
    G, P, H = SSM_GROUPS, SSM_STATE, SSM_GROUP
    lam_im = jnp.broadcast_to(jnp.pi * jnp.arange(P, dtype=f32), (L, G, P)) + 0.01 * jax.random.normal(ks[4], (L, G, P), f32)
    lam_re = -0.5 + 0.01 * jax.random.normal(ks[3], (L, G, P), f32)
    log_step = jax.random.uniform(ks[5], (L, G), f32, math.log(1e-3), math.log(1e-1))
    return {
        'x': jax.random.normal(ks[0], (BATCH, SEQ, D_MODEL), f32),
        'mem': jax.random.normal(ks[1], (BATCH, N_MEM, D_MODEL), f32),
        'positions': jnp.broadcast_to(jnp.arange(SEQ, dtype=jnp.int32), (BATCH, SEQ)),
        'norm_mix': gain(ks[2], (L, D_MODEL)),
        'w_in': nrm(ks[6], (L, D_MODEL, IN_COLS), D_MODEL ** -0.5),
        'ssm_lambda_re': lam_re,
        'ssm_lambda_im': lam_im,
        'ssm_log_step': log_step,
        'ssm_b_re': nrm(ks[7], (L, G, P, H), (2 * H) ** -0.5),
        'ssm_b_im': nrm(ks[8], (L, G, P, H), (2 * H) ** -0.5),
        'ssm_c_re': nrm(ks[9], (L, G, H, P), (2 * P) ** -0.5),
        'ssm_c_im': nrm(ks[10], (L, G, H, P), (2 * P) ** -0.5),
        'ssm_d': nrm(ks[11], (L, SSM_WIDTH), 1.0),
        'ssm_w_glu': nrm(ks[12], (L, SSM_WIDTH, SSM_WIDTH), SSM_WIDTH ** -0.5),
        'ssm_b_glu': nrm(ks[13], (L, SSM_WIDTH), 0.02),
        'mla_q_norm': gain(ks[14], (L, Q_LORA)),
        'mla_w_uq': nrm(ks[15], (L, Q_LORA, MLA_HEADS * QK_DIM), Q_LORA ** -0.5),
        'mla_kv_norm': gain(ks[16], (L, KV_LORA)),
        'mla_w_ukv': nrm(ks[17], (L, KV_LORA, MLA_HEADS * (QK_NOPE + V_DIM)), KV_LORA ** -0.5),
        'mla_q_gain': gain(ks[18], (L, QK_DIM)),
        'mla_k_gain': gain(ks[19], (L, QK_DIM)),
        'out_norm_ssm': gain(ks[20], (L, SSM_WIDTH)),
        'out_norm_mla': gain(ks[21], (L, MLA_WIDTH)),
        'w_out': nrm(ks[22], (L, MIX_WIDTH, D_MODEL), MIX_WIDTH ** -0.5),
        'norm_mem_q': gain(ks[23], (L, D_MODEL)),
        'norm_mem_kv': gain(ks[24], (L, D_MODEL)),
        'mem_w_q': nrm(ks[25], (L, D_MODEL, MEM_WIDTH), D_MODEL ** -0.5),
        'mem_w_kv': nrm(ks[26], (L, D_MODEL, 2 * MEM_WIDTH), D_MODEL ** -0.5),
        'mem_q_gain': gain(ks[27], (L, MEM_HEAD_DIM)),
        'mem_k_gain': gain(ks[28], (L, MEM_HEAD_DIM)),
        'mem_w_o': nrm(ks[29], (L, MEM_WIDTH, D_MODEL), MEM_WIDTH ** -0.5),
        'norm_mlp': gain(ks[30], (L, D_MODEL)),
        'mlp_w1': nrm(ks[31], (L, D_MODEL, D_FF), D_MODEL ** -0.5),
        'mlp_w2': nrm(ks[32], (L, D_FF, D_MODEL), D_FF ** -0.5),
    }


def reference(x, mem, positions, norm_mix, w_in,
              ssm_lambda_re, ssm_lambda_im, ssm_log_step, ssm_b_re, ssm_b_im,
              ssm_c_re, ssm_c_im, ssm_d, ssm_w_glu, ssm_b_glu,
              mla_q_norm, mla_w_uq, mla_kv_norm, mla_w_ukv, mla_q_gain, mla_k_gain,
              out_norm_ssm, out_norm_mla, w_out,
              norm_mem_q, norm_mem_kv, mem_w_q, mem_w_kv, mem_q_gain, mem_k_gain, mem_w_o,
              norm_mlp, mlp_w1, mlp_w2):
    s1 = SSM_WIDTH
    s2 = s1 + Q_LORA
    s3 = s2 + KV_LORA
    for l in range(DEPTH):
        h = rms_norm(x, norm_mix[l])
        proj = h @ w_in[l]
        u, c_q, c_kv, k_rope = proj[..., :s1], proj[..., s1:s2], proj[..., s2:s3], proj[..., s3:]
        y_ssm = s5_mixer(u, ssm_lambda_re[l], ssm_lambda_im[l], ssm_log_step[l],
                         ssm_b_re[l], ssm_b_im[l], ssm_c_re[l], ssm_c_im[l],
                         ssm_d[l], ssm_w_glu[l], ssm_b_glu[l])
        y_mla = mla_mixer(c_q, c_kv, k_rope, positions, mla_q_norm[l], mla_w_uq[l],
                          mla_kv_norm[l], mla_w_ukv[l], mla_q_gain[l], mla_k_gain[l])
        y = jnp.concatenate([rms_norm(y_ssm, out_norm_ssm[l]), rms_norm(y_mla, out_norm_mla[l])], axis=-1)
        x = x + y @ w_out[l]
        x = x + memory_cross_attention(rms_norm(x, norm_mem_q[l]), rms_norm(mem, norm_mem_kv[l]),
                                       mem_w_q[l], mem_w_kv[l], mem_q_gain[l], mem_k_gain[l], mem_w_o[l])
        h = rms_norm(x, norm_mlp[l])
        x = x + jnp.square(jax.nn.relu(h @ mlp_w1[l])) @ mlp_w2[l]
    return x
```

```python
import numpy as np
from contextlib import ExitStack
import concourse.bass as bass
import concourse.mybir as mybir
from concourse.bass_utils import run_bass_kernel_spmd

F32 = mybir.dt.float32
BF16 = mybir.dt.bfloat16
I32 = mybir.dt.int32
AF = mybir.ActivationFunctionType
ALU = mybir.AluOpType
AX = mybir.AxisListType

ENGS = ("pe", "act", "dve", "pool", "sp")


class Op:
    __slots__ = ("eng", "fn", "reads", "writes", "dma_key", "idx", "deps", "signal",
                 "tick", "waits")

    def __init__(self, eng, fn, reads, writes, dma_key):
        self.eng = eng
        self.fn = fn
        self.reads = reads
        self.writes = writes
        self.dma_key = dma_key
        self.deps = []
        self.signal = False
        self.tick = None
        self.waits = []


class Prog:
    def __init__(self):
        self.ops = []

    def add(self, eng, fn, reads=(), writes=(), dma_key=None):
        op = Op(eng, fn, tuple(reads), tuple(writes), dma_key)
        op.idx = len(self.ops)
        self.ops.append(op)
        return op

    def analyse(self):
        last_w = {}
        readers = {}
        for op in self.ops:
            deps = set()
            for k in op.reads:
                w = last_w.get(k)
                if w is not None:
                    deps.add(w.idx)
            for k in op.writes:
                w = last_w.get(k)
                if w is not None:
                    deps.add(w.idx)
                for r in readers.get(k, {}).values():
                    deps.add(r.idx)
            deps.discard(op.idx)
            keep = []
            for d in deps:
                a = self.ops[d]
                if a.dma_key is None and op.dma_key is None and a.eng == op.eng:
                    if a.eng == "pe":
                        continue
                    raw = any(k in a.writes for k in op.reads)
                    if not raw:
                        continue
                keep.append(d)
            op.deps = keep
            for d in keep:
                self.ops[d].signal = True
            for k in op.writes:
                last_w[k] = op
                readers[k] = {}
            for k in op.reads:
                rk = op.dma_key if op.dma_key is not None else op.eng
                readers.setdefault(k, {})[rk] = op
        cnt = {e: 0 for e in ENGS}
        dcnt = {}
        for op in self.ops:
            if op.dma_key is not None:
                dcnt[op.dma_key] = dcnt.get(op.dma_key, 0) + 1
                op.tick = ("dma", op.dma_key, dcnt[op.dma_key] * 16)
            elif op.signal:
                cnt[op.eng] += 1
                op.tick = ("eng", op.eng, cnt[op.eng])
        waited = {e: {} for e in ENGS}
        dma_issued = {}
        for op in self.ops:
            need = {}
            for d in op.deps:
                a = self.ops[d]
                t = a.tick
                if t[0] == "dma":
                    v = dma_issued.get(t[1], 0) * 16
                    v = max(v, t[2])
                else:
                    v = t[2]
                sk = (t[0], t[1])
                if v > need.get(sk, 0):
                    need[sk] = v
            for sk, v in need.items():
                if waited[op.eng].get(sk, 0) >= v:
                    continue
                waited[op.eng][sk] = v
                op.waits.append((sk, v))
            if op.dma_key is not None:
                dma_issued[op.dma_key] = dma_issued.get(op.dma_key, 0) + 1
        self.n_eng_sems = cnt
        self.dma_keys = list(dcnt.keys())

    def emit(self, nc, es):
        self.analyse()
        sems = {}
        for e in ENGS:
            sems[("eng", e)] = es.enter_context(nc.semaphore("s_" + e))
        for i, k in enumerate(self.dma_keys):
            sems[("dma", k)] = es.enter_context(nc.semaphore("d%d" % i))
        self.sems = sems
        per = {e: [op for op in self.ops if op.eng == e] for e in ENGS}
        block = es.enter_context(nc.Block())

        def run(engine, lst):
            for op in lst:
                for sk, v in op.waits:
                    engine.wait_ge(sems[sk], v)
                ins = op.fn(engine)
                if op.dma_key is not None:
                    ins.then_inc(sems[("dma", op.dma_key)], 16)
                elif op.signal:
                    ins.then_inc(sems[("eng", op.eng)], 1)

        @block.tensor
        def _(e):
            run(e, per["pe"])

        @block.scalar
        def _(e):
            run(e, per["act"])

        @block.vector
        def _(e):
            run(e, per["dve"])

        @block.gpsimd
        def _(e):
            run(e, per["pool"])

        @block.sync
        def _(e):
            run(e, per["sp"])


D = 1024
T = 2048
NT4 = T // 512
NT1 = T // 128
KC = D // 128
IN_COLS = 928
EPS = 1e-6
PI = float(np.pi)
SHIFT = 10.0


class Builder:
    def __init__(self, nc, es, n_layers, debug=()):
        self.nc = nc
        self.es = es
        self.p = Prog()
        self.L = n_layers
        self.debug = set(debug)
        self.dram_in = {}
        self.dram_out = {}
        self._uid = 0
        self.out_keys = []

    def din(self, name, shape, dt=F32):
        t = self.nc.dram_tensor(name, list(shape), dt, kind="ExternalInput")
        self.dram_in[name] = t
        return t

    def dout(self, name, shape, dt=F32):
        t = self.nc.dram_tensor(name, list(shape), dt, kind="ExternalOutput")
        self.dram_out[name] = t
        return t

    def sb(self, name, shape, dt=F32):
        return self.es.enter_context(self.nc.sbuf_tensor(name, list(shape), dt))

    def ps(self, name, shape, dt=F32):
        return self.es.enter_context(self.nc.psum_tensor(name, list(shape), dt))

    def dma(self, out, in_, reads=(), writes=(), key=None, eng="sp", **kw):
        writes = list(writes)
        if key is None:
            self._uid += 1
            key = "a%d_%s" % (self._uid % 12, eng)
            writes.append(("dmakey", key))
        return self.p.add(eng, lambda e, o=out, i=in_, kw=kw: e.dma_start(out=o, in_=i, **kw),
                          reads=reads, writes=writes, dma_key=key)

    def mm(self, out, lhsT, rhs, start, stop, reads=(), writes=()):
        return self.p.add("pe", lambda e: e.matmul(out, lhsT, rhs, start=start, stop=stop),
                          reads=reads, writes=writes)

    def tr(self, out, in_, ident, reads=(), writes=()):
        return self.p.add("pe", lambda e: e.transpose(out, in_, ident), reads=reads, writes=writes)

    def act(self, out, in_, func, reads=(), writes=(), eng="act", **kw):
        return self.p.add("act", lambda e: e.activation(out, in_, func, **kw), reads=reads, writes=writes)

    def v(self, eng, name, *args, reads=(), writes=(), **kw):
        return self.p.add(eng, lambda e: getattr(e, name)(*args, **kw), reads=reads, writes=writes)


def host_constants():
    c = np.zeros((128, 1664), np.float32)
    c[:, 0:128] = np.eye(128)
    c[:, 128:256] = 1.0
    r = np.arange(128)
    c[:, 256:384] = (r[:, None] <= r[None, :])
    c[127, 384:512] = 1.0
    c[127, 512:640] = -1.0
    m = np.arange(896)
    c[:, 640:1536] = (m[None, :] - r[:, None] >= 384)
    c[:, 1536:1552] = (10000.0 ** (-np.arange(16, dtype=np.float32) / 16))[None, :]
    c[:, 1552] = r + 1
    c[:, 1553] = -(r + 1)
    return c


def chunkT(v, k):
    return np.ascontiguousarray(v.reshape((k, 128) + v.shape[1:]).swapaxes(0, 1))


def prep_weights(inp, layers):
    f = lambda a: np.asarray(a, np.float32)
    W = {}
    Ls = list(layers)
    n = len(Ls)
    vec = np.zeros((n, 128, 64), np.float32)
    row = np.zeros((n, 320), np.float32)
    bm = np.zeros((n, 4, 128, 1024), np.float32)
    cm = np.zeros((n, 4, 128, 8, 128), np.float32)
    lam = np.zeros((n, 4, 3, 512), np.float32)
    for i, l in enumerate(Ls):
        vec[i, :, 0:8] = f(inp["norm_mix"][l]).reshape(8, 128).T
        vec[i, :, 8:16] = f(inp["norm_mem_q"][l]).reshape(8, 128).T
        vec[i, :, 16:24] = f(inp["norm_mlp"][l]).reshape(8, 128).T
        vec[i, :, 24:32] = f(inp["norm_mem_kv"][l]).reshape(8, 128).T
        vec[i, :, 32:34] = f(inp["mla_q_norm"][l]).reshape(2, 128).T
        vec[i, :, 34] = f(inp["mla_kv_norm"][l])
        vec[i, :, 35:39] = f(inp["ssm_d"][l]).reshape(4, 128).T
        vec[i, :, 39:43] = f(inp["ssm_b_glu"][l]).reshape(4, 128).T
        vec[i, :, 43:47] = f(inp["out_norm_ssm"][l]).reshape(4, 128).T
        vec[i, :, 47:51] = f(inp["out_norm_mla"][l]).reshape(4, 128).T
        row[i, 0:96] = f(inp["mla_q_gain"][l])
        row[i, 96:192] = f(inp["mla_k_gain"][l])
        row[i, 192:256] = f(inp["mem_q_gain"][l])
        row[i, 256:320] = f(inp["mem_k_gain"][l])
        bre = f(inp["ssm_b_re"][l])
        bim = f(inp["ssm_b_im"][l])
        cre = f(inp["ssm_c_re"][l])
        cim = f(inp["ssm_c_im"][l])
        lre = f(inp["ssm_lambda_re"][l])
        lim = f(inp["ssm_lambda_im"][l])
        lst = f(inp["ssm_log_step"][l])
        for c in range(4):
            for j in range(8):
                g = 8 * c + j
                bm[i, c, 16 * j:16 * j + 16, j * 64:(j + 1) * 64] = bre[g].T
                bm[i, c, 16 * j:16 * j + 16, 512 + j * 64:512 + (j + 1) * 64] = bim[g].T
                pr, e = j // 2, j % 2
                cm[i, c, e * 64:(e + 1) * 64, pr, 16 * j:16 * j + 16] = cre[g].T
                cm[i, c, e * 64:(e + 1) * 64, 4 + pr, 16 * j:16 * j + 16] = cim[g].T
                lam[i, c, 0, j * 64:(j + 1) * 64] = lre[g]
                lam[i, c, 1, j * 64:(j + 1) * 64] = lim[g]
                lam[i, c, 2, j * 64:(j + 1) * 64] = lst[g]
    W["vec"] = vec
    W["row"] = row
    W["bm"] = bm
    W["cm"] = cm
    W["lam"] = lam
    st = lambda name, k: np.stack([chunkT(f(inp[name][l]), k) for l in Ls])
    W["w_in"] = st("w_in", 8)
    W["w_uq"] = st("mla_w_uq", 2)
    W["w_ukv"] = np.stack([f(inp["mla_w_ukv"][l]) for l in Ls])
    W["w_glu"] = st("ssm_w_glu", 4)
    W["w_out"] = st("w_out", 8)
    W["mwq"] = st("mem_w_q", 8)
    W["mwkv"] = st("mem_w_kv", 8)
    W["mwo"] = np.stack([np.ascontiguousarray(f(inp["mem_w_o"][l]).reshape(4, 64, 1024).swapaxes(0, 1)) for l in Ls])
    W["w1"] = st("mlp_w1", 8)
    W["w2"] = st("mlp_w2", 32)
    return W


TWO_PI = 2.0 * PI
GA = 1.5957691216057308
GB = 0.044715
QSCALE = 1.0 / float(np.sqrt(96.0))
MSCALE = 1.0 / 8.0
MSHIFT = 8.0
NEG_BIG = -30000.0


def build_program(n_layers=1, debug=()):
    nc = bass.Bass("TRN2", target_bir_lowering=False)
    es = ExitStack()
    b = Builder(nc, es, n_layers, debug)
    P = b.p
    L = n_layers

    xT_d = b.din("xT", [2, 128, 8, T]).ap()
    memT_d = b.din("memT", [128, 8, 256]).ap()
    pos_d = b.din("pos", [2, 128, 16]).ap()
    hinit_d = nc.dram_tensor("sc_h", [L, 4, 1024], BF16, kind="Internal").ap()
    pkt_d = nc.dram_tensor("sc_kt", [L, 4, 96, 2, T], BF16, kind="Internal").ap()
    pve_d = nc.dram_tensor("sc_ve", [L, 4, 128, 16, 65], BF16, kind="Internal").ap()
    pvo_d = nc.dram_tensor("sc_vo", [L, 4, 128, 16, 128], BF16, kind="Internal").ap()
    cst_d = b.din("cst", [128, 1664]).ap()
    vec_d = b.din("vec", [L, 128, 64]).ap()
    row_d = b.din("row", [L, 192 + 64 + 64]).ap()
    w_in_d = b.din("w_in", [L, 128, 8, IN_COLS]).ap()
    w_uq_d = b.din("w_uq", [L, 128, 2, 768]).ap()
    w_ukv_d = b.din("w_ukv", [L, 128, 1024]).ap()
    w_glu_d = b.din("w_glu", [L, 128, 4, 512]).ap()
    w_out_d = b.din("w_out", [L, 128, 8, 1024]).ap()
    mwq_d = b.din("mwq", [L, 128, 8, 256]).ap()
    mwkv_d = b.din("mwkv", [L, 128, 8, 512]).ap()
    mwo_d = b.din("mwo", [L, 64, 4, 1024]).ap()
    w1_d = b.din("w1", [L, 128, 8, 4096]).ap()
    w2_d = b.din("w2", [L, 128, 32, 1024]).ap()
    bm_d = b.din("bm", [L, 4, 128, 1024]).ap()
    cm_d = b.din("cm", [L, 4, 128, 8, 128]).ap()
    lam_d = b.din("lam", [L, 4, 3, 512]).ap()
    outT_d = b.dout("outT", [2, 128, 8, T]).ap()
    okt_d, ove_d, ovo_d, hfin_d = pkt_d, pve_d, pvo_d, hinit_d
    dbg_d = {}

    xT = b.sb("xT_s", [128, 8, T])
    hT = b.sb("hT_s", [128, 8, 1024], BF16)
    wb = [b.sb("wb%d" % i, [128, 8192], BF16) for i in range(2)]
    cst = b.sb("cst_s", [128, 160])
    cbf = b.sb("cbf_s", [128, 1536], BF16)
    vec = b.sb("vec_s", [128, L, 64])
    rowb = b.sb("rowb_s", [128, 320])
    sq = b.sb("sq_s", [128, 8, 512], BF16)
    rstd = b.sb("rstd_s", [128, 512])
    posf = b.sb("pos_s", [128, 16])
    cvec = b.sb("cvec_s", [128, 8])
    AR = 37888
    arena = b.sb("arena", [128, AR], BF16)
    arena_f = arena.bitcast(F32)

    psb = [b.ps("psb%d" % i, [128, 512]) for i in range(8)]
    psb_bf = [t.bitcast(BF16) for t in psb]

    ident = cbf[:, 0:128]
    ones = cbf[:, 128:256]
    tri = cbf[:, 256:384]
    e127 = cbf[:, 384:512]
    e127n = cbf[:, 512:640]
    cmask = cbf[:, 640:1536]
    ones_f = cst[:, 0:128]
    invf = cst[:, 128:144]
    sp1 = cst[:, 144:145]
    nsp1 = cst[:, 145:146]

    def A16(off_kib, shape):
        n = int(np.prod(shape[1:]))
        o = int(round(off_kib * 512))
        assert o + n <= AR, (off_kib, shape)
        v = arena[:, o:o + n]
        if len(shape) == 3:
            v = v.rearrange("p (a b) -> p a b", a=shape[1])
        elif len(shape) == 4:
            v = v.rearrange("p (a b c) -> p a b c", a=shape[1], b=shape[2])
        return v

    def A32(off_kib, shape):
        n = int(np.prod(shape[1:]))
        o = int(round(off_kib * 256))
        assert (o + n) * 2 <= AR, (off_kib, shape)
        v = arena_f[:, o:o + n]
        if len(shape) == 3:
            v = v.rearrange("p (a b) -> p a b", a=shape[1])
        elif len(shape) == 4:
            v = v.rearrange("p (a b c) -> p a b c", a=shape[1], b=shape[2])
        return v

    uT = A16(0, [128, 4, T])
    cqn = A16(16, [128, 2, T])
    ckvn = A16(24, [128, T])
    krope = A32(28, [128, 16, 32])
    cossin = A32(30, [128, 2, 16, 16])
    ymla = A16(32, [128, 4, T])
    WK = 48

    b.dma(cst[:, 0:128], cst_d[:, 128:256], writes=["cst"])
    b.dma(cst[:, 128:160], cst_d[:, 1536:1568], writes=["cst"])
    b.dma(cbf[:], cst_d[:, 0:1536], writes=["cbf"], eng="pool")
    b.dma(vec[:], vec_d.rearrange("l p k -> p l k"), writes=["vec"])
    def load_x(half):
        b.dma(posf[:], pos_d[half], writes=["pos"])
        for c in range(8):
            b.dma(xT[:, c, :], xT_d[half, :, c, :], writes=[("xT", c, t4) for t4 in range(4)])
    b.v("pool", "memset", cvec[:, 0:1], -SHIFT, writes=["cvec"])
    b.v("pool", "memset", cvec[:, 1:2], -MSHIFT, writes=["cvec"])
    b.v("pool", "memset", cvec[:, 2:3], -PI, writes=["cvec"])
    b.v("pool", "memset", cvec[:, 3:4], EPS, writes=["cvec"])
    nshift = cvec[:, 0:1]
    nmshift = cvec[:, 1:2]
    npi = cvec[:, 2:3]
    epsv = cvec[:, 3:4]

    cos_t = cossin[:, 0]
    sin_t = cossin[:, 1]

    def wrap_pi(dst, src, ti, tf, tm, rk, wk):
        b.v("dve", "tensor_scalar_mul", ti, src, 1.0 / TWO_PI, reads=rk, writes=[wk + "_ti"])
        b.v("dve", "tensor_copy", tf, ti, reads=[wk + "_ti"], writes=[wk + "_tf"])
        b.v("dve", "scalar_tensor_tensor", dst, tf, -TWO_PI, src, ALU.mult, ALU.add, reads=[wk + "_tf"] + rk, writes=[wk])
        b.v("dve", "tensor_scalar", tm, dst, PI, -TWO_PI, ALU.is_gt, ALU.mult, reads=[wk], writes=[wk + "_tm"])
        b.v("dve", "tensor_tensor", dst, dst, tm, ALU.add, reads=[wk, wk + "_tm"], writes=[wk])
        b.v("dve", "tensor_scalar", tm, dst, -PI, TWO_PI, ALU.is_lt, ALU.mult, reads=[wk], writes=[wk + "_tm"])
        b.v("dve", "tensor_tensor", dst, dst, tm, ALU.add, reads=[wk, wk + "_tm"], writes=[wk])

    def rope_tables():
        ang = A32(WK, [128, 16, 16])
        ang2 = A32(WK + 1, [128, 16, 16])
        ti = A32(WK + 2, [128, 16, 16]).bitcast(I32)
        tf = A32(WK + 3, [128, 16, 16])
        tm = A32(WK + 4, [128, 16, 16])
        wr = A32(WK + 5, [128, 16, 16])
        for t in range(16):
            b.v("dve", "tensor_scalar_mul", ang[:, t, :], invf, posf[:, t:t + 1],
                reads=["cst", "pos"], writes=["ropeang"])
        wrap_pi(wr, ang, ti, tf, tm, ["ropeang"], "ropew")
        b.act(sin_t, wr, AF.Sin, reads=["ropew"], writes=["sin"])
        b.v("dve", "tensor_scalar_add", ang2, ang, PI / 2, reads=["ropeang"], writes=["ropeang2"])
        wrap_pi(wr, ang2, ti, tf, tm, ["ropeang2", "sin"], "ropew")
        b.act(cos_t, wr, AF.Sin, reads=["ropew"], writes=["cos"])

    state = {"ps": 0, "wb": 0}
    SQK = [("sq", i) for i in range(8)]

    def next_ps():
        i = state["ps"]
        state["ps"] = (i + 1) % 4
        return i

    def load_w(view, src, key, eng="pool"):
        return b.dma(view, src, writes=[key], key=("w", key), eng=eng)

    def rms_scale(src_keys, nchunk, n, src_fn, ps_i, inv_dim):
        for c in range(nchunk):
            b.mm(psb[ps_i][:, 0:n], ones, src_fn(c), c == 0, c == nchunk - 1,
                 reads=["cbf"] + src_keys, writes=[("ps", ps_i)])
        b.act(rstd[:, 0:n], psb[ps_i][:, 0:n], AF.Sqrt, reads=[("ps", ps_i), "cvec"], writes=["rstd"],
              bias=epsv, scale=inv_dim)
        b.v("dve", "reciprocal", rstd[:, 0:n], rstd[:, 0:n], reads=["rstd"], writes=["rstd"])

    def norm_x(l, gain_col, tok0, nt, hkeys):
        for t in range(nt):
            g4 = (tok0 // 512) + t
            xs = xT[:, :, g4 * 512:(g4 + 1) * 512]
            xk = [("xT", c, g4) for c in range(8)]
            b.act(sq[:], xs, AF.Square, reads=xk, writes=SQK)
            pi = next_ps()
            rms_scale(SQK, 8, 512, lambda c: sq[:, c, :], pi, 1.0 / D)
            for c in range(8):
                b.v("dve", "scalar_tensor_tensor", hT[:, c, t * 512:(t + 1) * 512], xT[:, c, g4 * 512:(g4 + 1) * 512],
                    vec[:, l, gain_col + c:gain_col + c + 1], rstd[:], ALU.mult, ALU.mult,
                    reads=[("xT", c, g4), "vec", "rstd"], writes=[("hT", t)])

    def dbg(name, view, shape, reads, dt=F32):
        if name in b.debug:
            d = b.dout("dbg_" + name, shape, dt).ap()
            b.dma(d, view, reads=reads, writes=[("dbgdone", "dbg_" + name)], key="dbg_" + name)
            b.out_keys.append("dbg_" + name)

    def layer(l):
        V0 = lambda c: vec[:, l, c:c + 1]
        b.dma(rowb[:], row_d[l].partition_broadcast(128), writes=["rowb"])
        gq = rowb[:, 0:96]
        gk = rowb[:, 96:192]
        mgq = rowb[:, 192:256]
        mgk = rowb[:, 256:320]
        b.v("dve", "tensor_scalar_mul", gq, gq, QSCALE, reads=["rowb"], writes=["rowb"])
        b.v("dve", "tensor_scalar_mul", mgq, mgq, MSCALE, reads=["rowb"], writes=["rowb"])

        w_in = wb[0][:, 0:8 * IN_COLS].rearrange("p (k n) -> p k n", k=8)
        load_w(w_in, w_in_d[l], ("wb", 0))
        cqraw = A32(WK, [128, 3, 512])
        for hf in range(2):
            norm_x(l, 0, hf * 1024, 2, None)
            for t in range(2):
                g4 = hf * 2 + t
                tok = slice(g4 * 512, (g4 + 1) * 512)
                hk = [("hT", t)]
                for oc in range(7):
                    pi = next_ps()
                    for k in range(8):
                        b.mm(psb[pi][:], w_in[:, k, oc * 128:(oc + 1) * 128], hT[:, k, t * 512:(t + 1) * 512],
                             k == 0, k == 7, reads=[("wb", 0)] + hk, writes=[("ps", pi)])
                    if oc < 4:
                        b.act(uT[:, oc, tok], psb[pi][:], AF.Copy, reads=[("ps", pi)], writes=[("uT", oc, g4)])
                    else:
                        b.act(cqraw[:, oc - 4, :], psb[pi][:], AF.Copy, reads=[("ps", pi)], writes=[("cqraw", oc - 4)])
                b.act(sq[:, 0:2, :], cqraw[:, 0:2, :], AF.Square, reads=[("cqraw", 0), ("cqraw", 1)], writes=SQK)
                pi = next_ps()
                rms_scale(SQK, 2, 512, lambda c: sq[:, c, :], pi, 1.0 / 256)
                for c in range(2):
                    b.v("dve", "scalar_tensor_tensor", cqn[:, c, tok], cqraw[:, c, :], V0(32 + c), rstd[:],
                        ALU.mult, ALU.mult, reads=[("cqraw", c), "vec", "rstd"], writes=[("cqn", g4)])
                b.act(sq[:, 2, :], cqraw[:, 2, :], AF.Square, reads=[("cqraw", 2)], writes=SQK)
                pi = next_ps()
                rms_scale(SQK, 1, 512, lambda c: sq[:, 2, :], pi, 1.0 / 128)
                b.v("dve", "scalar_tensor_tensor", ckvn[:, tok], cqraw[:, 2, :], V0(34), rstd[:],
                    ALU.mult, ALU.mult, reads=[("cqraw", 2), "vec", "rstd"], writes=[("ckvn", g4)])
                for s4 in range(4):
                    t16 = g4 * 4 + s4
                    pi = next_ps()
                    for k in range(8):
                        b.mm(psb[pi][:, 0:32], hT[:, k, t * 512 + s4 * 128:t * 512 + (s4 + 1) * 128], w_in[:, k, 896:928],
                             k == 0, k == 7, reads=[("wb", 0)] + hk, writes=[("ps", pi)])
                    b.act(krope[:, t16, :], psb[pi][:, 0:32], AF.Copy, reads=[("ps", pi)], writes=[("krope", t16)])
        dbg("uT", uT, [128, 4, T], [("uT", c, g) for c in range(4) for g in range(4)], BF16)
        dbg("cqn", cqn, [128, 2, T], [("cqn", g) for g in range(4)], BF16)
        dbg("ckvn", ckvn, [128, T], [("ckvn", g) for g in range(4)], BF16)
        dbg("krope", krope, [128, 16, 32], [("krope", g) for g in range(16)])
        dbg("cos", cos_t, [128, 16, 16], ["cos"])
        dbg("sin", sin_t, [128, 16, 16], ["sin"])


    hT_flat = hT[:].rearrange("p a b -> p (a b)")
    pkt_s = hT_flat[:, 0:4096].rearrange("p (e t) -> p e t", e=2)
    pve_s = hT_flat[:, 4096:4096 + 1040].rearrange("p (t c) -> p t c", t=16)
    pvo_s = hT_flat[:, 5248:5248 + 2048].rearrange("p (t c) -> p t c", t=16)
    HK = [("hT", 0), ("hT", 1)]
    QT = A16(48, [128, 2, T])
    KT = A16(56, [128, 2, T])
    Ve = A16(64, [128, 16, 65])
    Vo = A16(66.5, [128, 16, 128])
    sqq = A32(70.5, [128, 2, 96])
    qn = A32(71.25, [128, 2, 96])
    qf = A16(72, [128, 2, 96])
    rta = A32(72.5, [128, 2, 16])
    rtb = A32(72.625, [128, 2, 16])
    ssq = A32(72.75, [128, 2])
    kraw = A32(73, [128, 2, 96])
    wb1f = wb[1].bitcast(F32)
    rc = wb1f[:, 3584:4096]
    PT = [sq[:, i, :] for i in range(8)]

    def qk_post(src, gain, dstT, ps_bank, t16, srck, tag):
        b.act(sqq, src, AF.Square, reads=srck, writes=["sqq"])
        b.v("dve", "tensor_reduce", ssq, sqq, AX.X, ALU.add, reads=["sqq"], writes=["ssq"])
        b.act(ssq, ssq, AF.Sqrt, reads=["ssq", "cvec"], writes=["ssq"], bias=epsv, scale=1.0 / 96)
        b.v("dve", "reciprocal", ssq, ssq, reads=["ssq"], writes=["ssq"])
        b.v("dve", "tensor_tensor", qn, src, ssq.unsqueeze(2).to_broadcast([128, 2, 96]), ALU.mult,
            reads=srck + ["ssq"], writes=["qn"])
        b.v("dve", "tensor_tensor", qn, qn, gain.unsqueeze(1).to_broadcast([128, 2, 96]), ALU.mult,
            reads=["qn", "rowb"], writes=["qn"])
        cs = cos_t[:, t16, :].unsqueeze(1).to_broadcast([128, 2, 16])
        sn = sin_t[:, t16, :].unsqueeze(1).to_broadcast([128, 2, 16])
        b.v("pool", "tensor_copy", qf[:, :, 0:64], qn[:, :, 0:64], reads=["qn"], writes=["qf"])
        b.v("pool", "tensor_tensor", rta, qn[:, :, 64:80], cs, ALU.mult, reads=["qn", "cos"], writes=["rta"])
        b.v("pool", "tensor_tensor", rtb, qn[:, :, 80:96], sn, ALU.mult, reads=["qn", "sin"], writes=["rtb"])
        b.v("pool", "tensor_tensor", qf[:, :, 64:80], rta, rtb, ALU.subtract, reads=["rta", "rtb"], writes=["qf"])
        b.v("pool", "tensor_tensor", rta, qn[:, :, 80:96], cs, ALU.mult, reads=["qn", "cos", "qf"], writes=["rta"])
        b.v("pool", "tensor_tensor", rtb, qn[:, :, 64:80], sn, ALU.mult, reads=["qn", "sin", "qf"], writes=["rtb"])
        b.v("pool", "tensor_tensor", qf[:, :, 80:96], rta, rtb, ALU.add, reads=["rta", "rtb"], writes=["qf"])
        for e in range(2):
            slot = e * 4 + (t16 % 4)
            b.tr(psb_bf[ps_bank][0:96, slot * 128:(slot + 1) * 128], qf[:, e, :], ident,
                 reads=["qf", "cbf"], writes=[("ps", ps_bank)])
        if t16 % 4 == 3:
            g4 = t16 // 4
            b.act(dstT[0:96, :, g4 * 512:(g4 + 1) * 512],
                  psb_bf[ps_bank][0:96, :].rearrange("p (e n) -> p e n", e=2), AF.Copy,
                  reads=[("ps", ps_bank)], writes=[(tag, g4)])

    def mla(l, half=1):
        wuq = wb[1][:, 0:1536].rearrange("p (k n) -> p k n", k=2)
        wukv = wb[1][:, 1536:2560]
        load_w(wuq, w_uq_d[l], ("wb", 1))
        load_w(wukv, w_ukv_d[l], ("wb", 1))
        gq = rowb[:, 0:96]
        gk = rowb[:, 96:192]
        b.v("pool", "memset", Ve[:, :, 64:65], 1.0, writes=["Ve"])
        b.v("pool", "memset", Vo[:, :, 0:64], 0.0, writes=["Vo"])
        b.v("pool", "memset", Vo[:, :, 0:1], 1.0, writes=["Vo"])
        for hp in range(4):
            if half == 1:
                b.dma(pkt_s[0:96], pkt_d[l, hp], reads=[("skv", l, hp)], writes=HK, key="pk")
                b.dma(pve_s, pve_d[l, hp], reads=[("skv", l, hp)], writes=HK, key="pk")
                b.dma(pvo_s, pvo_d[l, hp], reads=[("skv", l, hp)], writes=HK, key="pk")
            for t16 in range(16):
                tok = slice(t16 * 128, (t16 + 1) * 128)
                g4 = t16 // 4
                for k in range(2):
                    b.mm(psb[4][:, 0:192], cqn[:, k, tok], wuq[:, k, hp * 192:(hp + 1) * 192], k == 0, k == 1,
                         reads=[("cqn", g4), ("wb", 1)], writes=[("ps", 4)])
                b.mm(psb[5][:, 0:256], ckvn[:, tok], wukv[:, hp * 256:(hp + 1) * 256], True, True,
                     reads=[("ckvn", g4), ("wb", 1)], writes=[("ps", 5)])
                qk_post(psb[4][:, 0:192].rearrange("p (e d) -> p e d", e=2), gq, QT, 6, t16, [("ps", 4)], "QT")
                kvv = psb[5][:, 0:256].rearrange("p (e d) -> p e d", e=2)
                b.act(kraw[:, :, 0:64], kvv[:, :, 0:64], AF.Copy, reads=[("ps", 5)], writes=["kraw"])
                b.v("pool", "tensor_copy", kraw[:, :, 64:96], krope[:, t16, :].unsqueeze(1).to_broadcast([128, 2, 32]),
                    reads=[("krope", t16)], writes=["kraw"])
                b.act(Ve[:, t16, 0:64], psb[5][:, 64:128], AF.Copy, reads=[("ps", 5)], writes=["Ve"])
                b.act(Vo[:, t16, 64:128], psb[5][:, 192:256], AF.Copy, reads=[("ps", 5)], writes=["Vo"])
                qk_post(kraw, gk, KT, 7, t16, ["kraw"], "KT")
            QK = [("QT", g) for g in range(4)]
            KK = [("KT", g) for g in range(4)]
            if half == 0:
                b.dma(okt_d[l, hp], KT[0:96], reads=KK, writes=[("skv", l, hp)], key="okv")
                b.dma(ove_d[l, hp], Ve, reads=["Ve"], writes=[("skv", l, hp)], key="okv")
                b.dma(ovo_d[l, hp], Vo, reads=["Vo"], writes=[("skv", l, hp)], key="okv")
            for e in range(2):
                Vt = Ve if e == 0 else Vo
                vk = "Ve" if e == 0 else "Vo"
                pV = pve_s if e == 0 else pvo_s
                M = 65 if e == 0 else 128
                for qt in range(4):
                    blocks = ([("p", kb) for kb in range(16)] if half == 1 else []) + [("o", kb) for kb in range(4 * qt + 4)]
                    po = 2 + (qt % 2)
                    for bi, (kind, kb) in enumerate(blocks):
                        pi = bi % 2
                        pt = PT[bi % 4]
                        ptk = ("sq", bi % 4)
                        kts = (pkt_s if kind == "p" else KT)[0:96, e, kb * 128:(kb + 1) * 128]
                        kk = HK if kind == "p" else [("KT", kb // 4)]
                        b.mm(psb[pi][:], kts, QT[0:96, e, qt * 512:(qt + 1) * 512], True, True,
                             reads=kk + [("QT", qt)], writes=[("ps", pi)])
                        b.act(pt, psb[pi][:], AF.Exp, reads=[("ps", pi), "cvec"], writes=[ptk],
                              bias=nshift, scale=1.0)
                        if kind == "o" and kb >= 4 * qt:
                            o = kb - 4 * qt
                            b.v("dve", "tensor_tensor", pt, pt, cmask[:, (3 - o) * 128:(3 - o) * 128 + 512], ALU.mult,
                                reads=[ptk, "cbf"], writes=[ptk])
                        vs = (pV if kind == "p" else Vt)[:, kb, 0:M]
                        b.mm(psb[po][0:M, :], vs, pt, bi == 0, bi == len(blocks) - 1,
                             reads=(HK if kind == "p" else [vk]) + [ptk], writes=[("ps", po)])
                    dr = 64 if e == 0 else 0
                    b.v("dve", "reciprocal", rc[dr:dr + 1, :], psb[po][dr:dr + 1, :], reads=[("ps", po)], writes=["rc"])
                    if e == 0:
                        b.mm(psb[4][0:64, :], ones_f[64:65, 0:64], rc[64:65, :], True, True, reads=["cst", "rc"], writes=[("ps", 4)])
                        rows = slice(0, 64)
                    else:
                        b.mm(psb[4][:, :], ones_f[0:1, :], rc[0:1, :], True, True, reads=["cst", "rc"], writes=[("ps", 4)])
                        rows = slice(64, 128)
                    b.act(rstd[rows, :], psb[4][rows, :], AF.Copy, reads=[("ps", 4)], writes=["rstd"])
                    b.v("dve", "tensor_tensor", ymla[rows, hp, qt * 512:(qt + 1) * 512], psb[po][rows, :], rstd[rows, :], ALU.mult,
                        reads=[("ps", po), "rstd"], writes=[("ymla", hp, qt)])
        dbg("ymla", ymla, [128, 4, T], [("ymla", h, q) for h in range(4) for q in range(4)], BF16)
        dbg("QT", QT, [128, 2, T], [("QT", g) for g in range(4)], BF16)
        dbg("KT", KT, [128, 2, T], [("KT", g) for g in range(4)], BF16)
        dbg("Vo", Vo, [128, 16, 128], ["Vo"], BF16)


    dummy = b.sb("dummy_s", [128, 2])

    def barrier(keys):
        b.v("pool", "memset", dummy[:, 0:1], 0.0, writes=list(keys))

    MIXER_KEYS = ([("uT", c, g) for c in range(4) for g in range(4)] + [("cqn", g) for g in range(4)]
                  + [("ckvn", g) for g in range(4)] + [("krope", g) for g in range(16)] + ["cos", "sin"]
                  + [("ymla", h, q) for h in range(4) for q in range(4)])
    ATT_KEYS = ([("QT", g) for g in range(4)] + [("KT", g) for g in range(4)]
                + ["Ve", "Vo", "sqq", "qn", "qf", "rta", "rtb", "ssq", "kraw"])
    SSM_KEYS = ["sA", "sB", "sC", "sD", "sE", "sF", "sG", "sH", "sI", "sJ", "sK", "sM", "sN"]
    PROJ_KEYS = [("cqraw", i) for i in range(3)] + ["ropeang", "ropeang2", "ropew", "ropew_ti", "ropew_tf", "ropew_tm"]

    def ssm(l, half=1):
        slot = {n: A32(48 + 2 * i, [128, 512]) for i, n in enumerate("ABCDEFGHIJKMN")}
        sk = lambda n: "s" + n
        def tt(eng, out, a, bb_, op):
            b.v(eng, "tensor_tensor", slot[out], slot[a], slot[bb_], op, reads=[sk(a), sk(bb_)], writes=[sk(out)])
        def wrap(dst, src, ti, tf, tm):
            wrap_pi_s(slot[dst], slot[src], slot[ti].bitcast(I32), slot[tf], slot[tm], sk(dst), sk(src), sk(ti), sk(tf), sk(tm))
        def wrap_pi_s(dst, src, ti, tf, tm, kd, ks, kti, ktf, ktm):
            b.v("dve", "tensor_scalar_mul", ti, src, 1.0 / TWO_PI, reads=[ks], writes=[kti])
            b.v("dve", "tensor_copy", tf, ti, reads=[kti], writes=[ktf])
            b.v("dve", "scalar_tensor_tensor", dst, tf, -TWO_PI, src, ALU.mult, ALU.add, reads=[ktf, ks], writes=[kd])
            b.v("dve", "tensor_scalar", tm, dst, PI, -TWO_PI, ALU.is_gt, ALU.mult, reads=[kd], writes=[ktm])
            b.v("dve", "tensor_tensor", dst, dst, tm, ALU.add, reads=[kd, ktm], writes=[kd])
            b.v("dve", "tensor_scalar", tm, dst, -PI, TWO_PI, ALU.is_lt, ALU.mult, reads=[kd], writes=[ktm])
            b.v("dve", "tensor_tensor", dst, dst, tm, ALU.add, reads=[kd, ktm], writes=[kd])
        def actf(out, a, func, **kw):
            b.act(slot[out], slot[a], func, reads=[sk(a), "cst"], writes=[sk(out)], **kw)

        for c in range(4):
            for i, n in enumerate("KMN"):
                b.dma(slot[n], lam_d[l, c, i].partition_broadcast(128), writes=[sk(n)])
            actf("N", "N", AF.Exp)
            tt("dve", "G", "K", "N", ALU.mult)
            tt("dve", "H", "M", "N", ALU.mult)
            actf("A", "G", AF.Exp, scale=sp1)
            actf("C", "G", AF.Exp, scale=nsp1)
            b.v("dve", "tensor_scalar_mul", slot["I"], slot["H"], sp1, reads=[sk("H"), "cst"], writes=[sk("I")])
            wrap("J", "I", "N", "E", "F")
            actf("B", "J", AF.Sin)
            b.v("dve", "tensor_scalar_add", slot["I"], slot["I"], PI / 2, reads=[sk("I")], writes=[sk("I")])
            wrap("J", "I", "N", "E", "F")
            actf("D", "J", AF.Sin)
            tt("dve", "E", "A", "D", ALU.mult)
            tt("dve", "F", "A", "B", ALU.mult)
            tt("dve", "A", "C", "D", ALU.mult)
            b.v("dve", "scalar_tensor_tensor", slot["B"], slot["C"], -1.0, slot["B"], ALU.mult, ALU.mult,
                reads=[sk("C"), sk("B")], writes=[sk("B")])
            wrap("J", "H", "N", "C", "D")
            actf("I", "J", AF.Sin)
            b.v("dve", "tensor_scalar_add", slot["H"], slot["H"], PI / 2, reads=[sk("H")], writes=[sk("H")])
            wrap("J", "H", "N", "C", "D")
            actf("N", "J", AF.Sin)
            actf("C", "G", AF.Exp)
            tt("dve", "D", "C", "N", ALU.mult)
            b.v("dve", "tensor_scalar_add", slot["D"], slot["D"], -1.0, reads=[sk("D")], writes=[sk("D")])
            tt("dve", "I", "C", "I", ALU.mult)
            tt("dve", "J", "K", "K", ALU.mult)
            tt("dve", "N", "M", "M", ALU.mult)
            tt("dve", "J", "J", "N", ALU.add)
            b.v("dve", "reciprocal", slot["J"], slot["J"], reads=[sk("J")], writes=[sk("J")])
            tt("dve", "C", "D", "K", ALU.mult)
            tt("dve", "N", "I", "M", ALU.mult)
            tt("dve", "C", "C", "N", ALU.add)
            tt("dve", "C", "C", "J", ALU.mult)
            tt("dve", "N", "I", "K", ALU.mult)
            tt("dve", "G", "D", "M", ALU.mult)
            tt("dve", "N", "N", "G", ALU.subtract)
            tt("dve", "N", "N", "J", ALU.mult)
            tt("dve", "D", "A", "C", ALU.mult)
            tt("dve", "G", "B", "N", ALU.mult)
            tt("dve", "D", "D", "G", ALU.subtract)
            tt("dve", "I", "A", "N", ALU.mult)
            tt("dve", "G", "B", "C", ALU.mult)
            tt("dve", "I", "I", "G", ALU.add)
            Tre, Tim, Tpr, Tpi = slot["E"], slot["F"], slot["D"], slot["I"]
            TK = [sk("E"), sk("F"), sk("D"), sk("I")]
            Bm = slot["A"].bitcast(BF16)
            Cm = slot["B"].bitcast(BF16).rearrange("p (k n) -> p k n", k=8)
            X = slot["C"].bitcast(BF16)
            Sb = [slot["G"].bitcast(BF16), slot["H"].bitcast(BF16)]
            SbK = [sk("G"), sk("H")]
            ST = slot["J"].bitcast(BF16).rearrange("p (k n) -> p k n", k=8)
            load_w(Bm, bm_d[l, c], sk("A"))
            load_w(Cm, cm_d[l, c], sk("B"))
            b.v("pool", "memset", Sb[1], 0.0, writes=[SbK[1]])
            if half == 1:
                b.dma(Sb[1][127:128, :], hinit_d[l, c:c + 1, :], reads=[("sh", l, c)], writes=[SbK[1]])
            for k in range(16):
                tok = slice(k * 128, (k + 1) * 128)
                g4 = k // 4
                cur, prv = k % 2, 1 - (k % 2)
                b.mm(psb[0][:], uT[:, c, tok], Bm[:, 0:512], True, True, reads=[("uT", c, g4), sk("A")], writes=[("ps", 0)])
                b.mm(psb[1][:], uT[:, c, tok], Bm[:, 512:1024], True, True, reads=[("uT", c, g4), sk("A")], writes=[("ps", 1)])
                K_, M_, N_ = slot["K"], slot["M"], slot["N"]
                b.v("dve", "tensor_tensor", K_, psb[0][:], Tpr, ALU.mult, reads=[("ps", 0)] + TK, writes=[sk("K")])
                b.v("dve", "tensor_tensor", M_, psb[1][:], Tpi, ALU.mult, reads=[("ps", 1)] + TK, writes=[sk("M")])
                b.v("pool", "tensor_tensor", X[:, 0:512], K_, M_, ALU.subtract, reads=[sk("K"), sk("M")], writes=["Xr"])
                b.v("dve", "tensor_tensor", N_, psb[1][:], Tpr, ALU.mult, reads=[("ps", 1)] + TK, writes=[sk("N")])
                b.v("dve", "tensor_tensor", K_, psb[0][:], Tpi, ALU.mult, reads=[("ps", 0)] + TK + ["Xr"], writes=[sk("K")])
                b.v("pool", "tensor_tensor", X[:, 512:1024], N_, K_, ALU.add, reads=[sk("K"), sk("N")], writes=["Xi"])
                b.mm(psb[2][:], tri, X[:, 0:512], True, False, reads=["cbf", "Xr"], writes=[("ps", 2)])
                b.mm(psb[2][:], e127, Sb[prv][:, 0:512], False, True, reads=["cbf", SbK[prv]], writes=[("ps", 2)])
                b.mm(psb[3][:], tri, X[:, 512:1024], True, False, reads=["cbf", "Xi"], writes=[("ps", 3)])
                b.mm(psb[3][:], e127n, Sb[prv][:, 512:1024], False, True, reads=["cbf", SbK[prv]], writes=[("ps", 3)])
                b.v("dve", "tensor_tensor", K_, psb[2][:], Tre, ALU.mult, reads=[("ps", 2)] + TK + ["Xi"], writes=[sk("K")])
                b.v("dve", "tensor_tensor", M_, psb[3][:], Tim, ALU.mult, reads=[("ps", 3)] + TK + ["Xr"], writes=[sk("M")])
                b.v("pool", "tensor_tensor", Sb[cur][:, 0:512], K_, M_, ALU.subtract, reads=[sk("K"), sk("M")], writes=[SbK[cur]])
                b.v("dve", "scalar_tensor_tensor", N_, psb[3][:], -1.0, Tre, ALU.mult, ALU.mult,
                    reads=[("ps", 3)] + TK + ["Xi"], writes=[sk("N")])
                b.v("dve", "scalar_tensor_tensor", K_, psb[2][:], -1.0, Tim, ALU.mult, ALU.mult,
                    reads=[("ps", 2)] + TK + [SbK[cur]], writes=[sk("K")])
                b.v("pool", "tensor_tensor", Sb[cur][:, 512:1024], N_, K_, ALU.add,
                    reads=[sk("K"), sk("N")], writes=[SbK[cur]])
                tb = 4 + (k % 2)
                for blk in range(8):
                    b.tr(psb_bf[tb][:, blk * 128:(blk + 1) * 128], Sb[cur][:, blk * 128:(blk + 1) * 128], ident,
                         reads=[SbK[cur], "cbf"], writes=[("ps", tb)])
                b.act(ST, psb_bf[tb][:, :].rearrange("p (k n) -> p k n", k=8), AF.Copy, reads=[("ps", tb)], writes=[sk("J")])
                yb = 6 + (g4 % 2)
                for blk in range(8):
                    b.mm(psb[yb][:, (k % 4) * 128:(k % 4 + 1) * 128], Cm[:, blk, :], ST[:, blk, :], blk == 0, blk == 7,
                         reads=[sk("B"), sk("J")], writes=[("ps", yb)])
                if k % 4 == 3:
                    t4 = slice(g4 * 512, (g4 + 1) * 512)
                    b.v("dve", "scalar_tensor_tensor", uT[:, c, t4], uT[:, c, t4], vec[:, l, 35 + c:36 + c], psb[yb][:],
                        ALU.mult, ALU.add, reads=[("uT", c, g4), "vec", ("ps", yb)], writes=[("uT", c, g4)])
            if half == 0:
                b.dma(hfin_d[l, c:c + 1, :], Sb[1][127:128, :], reads=[SbK[1]], writes=[("sh", l, c)], key="hfin")
        dbg("yssm", uT, [128, 4, T], [("uT", c, g) for c in range(4) for g in range(4)], BF16)


    def group_norm(buf, keyf, gcol, l, g4):
        tok = slice(g4 * 512, (g4 + 1) * 512)
        ks = [keyf(c, g4) for c in range(4)]
        b.act(sq[:, 0:4, :], buf[:, :, tok], AF.Square, reads=ks, writes=SQK)
        pi = next_ps()
        rms_scale(SQK, 4, 512, lambda c: sq[:, c, :], pi, 1.0 / 512)
        for c in range(4):
            b.v("dve", "scalar_tensor_tensor", buf[:, c, tok], buf[:, c, tok], vec[:, l, gcol + c:gcol + c + 1], rstd[:],
                ALU.mult, ALU.mult, reads=[keyf(c, g4), "vec", "rstd"], writes=[keyf(c, g4)])

    def ssm_post(l):
        wglu = wb[1][:, 4096:6144].rearrange("p (k n) -> p k n", k=4)
        load_w(wglu, w_glu_d[l], ("wb", 1, "glu"))
        tK = A32(48, [128, 512])
        tM = A32(50, [128, 512])
        gates = A32(52, [128, 4, 512])
        for g4 in range(4):
            tok = slice(g4 * 512, (g4 + 1) * 512)
            for c in range(4):
                y = uT[:, c, tok]
                yk = ("uT", c, g4)
                b.v("dve", "tensor_tensor", tK, y, y, ALU.mult, reads=[yk], writes=["gK"])
                b.v("dve", "tensor_scalar", tK, tK, GB, 1.0, ALU.mult, ALU.add, reads=["gK"], writes=["gK"])
                b.v("dve", "tensor_tensor", tK, tK, y, ALU.mult, reads=["gK", yk], writes=["gK"])
                b.act(tM, tK, AF.Sigmoid, reads=["gK"], writes=["gM"], scale=GA)
                b.v("dve", "tensor_tensor", y, y, tM, ALU.mult, reads=["gM", yk], writes=[yk])
            for oc in range(4):
                pi = 4 + oc
                for k in range(4):
                    b.mm(psb[pi][:], wglu[:, k, oc * 128:(oc + 1) * 128], uT[:, k, tok], k == 0, k == 3,
                         reads=[("wb", 1, "glu"), ("uT", k, g4)], writes=[("ps", pi)])
                b.act(gates[:, oc, :], psb[pi][:], AF.Sigmoid, reads=[("ps", pi), "vec"], writes=[("gate", oc)],
                      bias=vec[:, l, 39 + oc:40 + oc], scale=1.0)
            for oc in range(4):
                b.v("dve", "tensor_tensor", uT[:, oc, tok], uT[:, oc, tok], gates[:, oc, :], ALU.mult,
                    reads=[("uT", oc, g4), ("gate", oc)], writes=[("uT", oc, g4)])
            group_norm(uT, lambda c, g: ("uT", c, g), 43, l, g4)
        dbg("yssmn", uT, [128, 4, T], [("uT", c, g) for c in range(4) for g in range(4)], BF16)

    def mix_out(l):
        wout = wb[0][:, :].rearrange("p (k n) -> p k n", k=8)
        load_w(wout, w_out_d[l], ("wb", 0))
        for g4 in range(4):
            tok = slice(g4 * 512, (g4 + 1) * 512)
            group_norm(ymla, lambda c, g: ("ymla", c, g), 47, l, g4)
            for oc in range(8):
                pi = next_ps()
                for k in range(8):
                    src = uT[:, k, tok] if k < 4 else ymla[:, k - 4, tok]
                    sk_ = ("uT", k, g4) if k < 4 else ("ymla", k - 4, g4)
                    b.mm(psb[pi][:], wout[:, k, oc * 128:(oc + 1) * 128], src, k == 0, k == 7,
                         reads=[("wb", 0), sk_], writes=[("ps", pi)])
                b.v("dve", "tensor_tensor", xT[:, oc, tok], xT[:, oc, tok], psb[pi][:], ALU.add,
                    reads=[("xT", oc, g4), ("ps", pi)], writes=[("xT", oc, g4)])
        dbg("x1", xT[:], [128, 8, T], [("xT", c, g) for c in range(8) for g in range(4)])

    CR_KEYS = ["mhT", "KmT", "Vm", "memx", "mtmp", "mss", "mkn", ("QmT", 0), ("QmT", 1), ("QmT", 2), ("QmT", 3)] + \
              [("omT", h, q) for h in range(4) for q in range(4)]

    def cross(l):
        mhT = A16(0, [128, 8, 256])
        KmT = A16(4, [128, 4, 256])
        Vm = A16(6, [128, 2, 4, 65])
        QmT = A16(8, [128, 4, T])
        omT = A16(24, [128, 4, T])
        memx = A32(40, [128, 8, 256])
        mtmp = A32(48, [128, 4, 64])
        mss = A32(49, [128, 4])
        mkn = A16(50, [128, 4, 64])
        mgq = rowb[:, 192:256]
        mgk = rowb[:, 256:320]
        mwkv = wb[1][:, 0:4096].rearrange("p (k n) -> p k n", k=8)
        mwq = wb[1][:, 4096:6144].rearrange("p (k n) -> p k n", k=8)
        load_w(mwkv, mwkv_d[l], ("wb", 1))
        load_w(mwq, mwq_d[l], ("wb", 1, "glu"))
        b.dma(memx, memT_d, writes=["memx"])
        sqm = sq[:, :, 0:256]
        b.act(sqm, memx, AF.Square, reads=["memx"], writes=SQK)
        pi = next_ps()
        rms_scale(SQK, 8, 256, lambda c: sq[:, c, 0:256], pi, 1.0 / D)
        for c in range(8):
            b.v("dve", "scalar_tensor_tensor", mhT[:, c, :], memx[:, c, :], vec[:, l, 24 + c:25 + c], rstd[:, 0:256],
                ALU.mult, ALU.mult, reads=["memx", "vec", "rstd"], writes=["mhT"])
        b.v("pool", "memset", Vm[:, :, :, 64:65], 1.0, writes=["Vm"])

        def head_norm(src, srck, gain, n_slots_bank, slot_fn):
            b.act(mtmp, src, AF.Square, reads=srck, writes=["mtmp"])
            b.v("dve", "tensor_reduce", mss, mtmp, AX.X, ALU.add, reads=["mtmp"], writes=["mss"])
            b.act(mss, mss, AF.Sqrt, reads=["mss", "cvec"], writes=["mss"], bias=epsv, scale=1.0 / 64)
            b.v("dve", "reciprocal", mss, mss, reads=["mss"], writes=["mss"])
            b.v("dve", "tensor_tensor", mtmp, src, mss.unsqueeze(2).to_broadcast([128, 4, 64]), ALU.mult,
                reads=srck + ["mss"], writes=["mtmp"])
            b.v("dve", "tensor_tensor", mkn, mtmp, gain.unsqueeze(1).to_broadcast([128, 4, 64]), ALU.mult,
                reads=["mtmp", "rowb"], writes=["mkn"])
            for h in range(4):
                sl = slot_fn(h)
                b.tr(psb_bf[n_slots_bank][0:64, sl * 128:(sl + 1) * 128], mkn[:, h, :], ident,
                     reads=["mkn", "cbf"], writes=[("ps", n_slots_bank)])

        for mt in range(2):
            pi = next_ps()
            for k in range(8):
                b.mm(psb[pi][:], mhT[:, k, mt * 128:(mt + 1) * 128], mwkv[:, k, :], k == 0, k == 7,
                     reads=["mhT", ("wb", 1)], writes=[("ps", pi)])
            kvv = psb[pi][:].rearrange("p (h d) -> p h d", h=4)
            b.act(Vm[:, mt, :, 0:64], kvv[:, :, 64:128], AF.Copy, reads=[("ps", pi)], writes=["Vm"])
            head_norm(kvv[:, :, 0:64], [("ps", pi)], mgk, 6, lambda h: h * 2 + mt)
        b.act(KmT[0:64, :, :], psb_bf[6][0:64, :].rearrange("p (h n) -> p h n", h=4), AF.Copy,
              reads=[("ps", 6)], writes=["KmT"])

        for hf in range(2):
            norm_x(l, 8, hf * 1024, 2, None)
            for t8 in range(8):
                t16 = hf * 8 + t8
                pi = next_ps()
                for k in range(8):
                    b.mm(psb[pi][:, 0:256], hT[:, k, t8 * 128:(t8 + 1) * 128], mwq[:, k, :], k == 0, k == 7,
                         reads=[("hT", t8 // 4), ("wb", 1, "glu")], writes=[("ps", pi)])
                head_norm(psb[pi][:, 0:256].rearrange("p (h d) -> p h d", h=4), [("ps", pi)], mgq, 7,
                          lambda h: h * 2 + (t16 % 2))
                if t16 % 2 == 1:
                    b.act(QmT[0:64, :, (t16 // 2) * 256:(t16 // 2 + 1) * 256],
                          psb_bf[7][0:64, :].rearrange("p (h n) -> p h n", h=4), AF.Copy,
                          reads=[("ps", 7)], writes=[("QmT", t16 // 4)])
        for h in range(4):
            for qt in range(4):
                po = 2 + (qt % 2)
                for kb in range(2):
                    pi = kb
                    pt = PT[kb]
                    b.mm(psb[pi][:], KmT[0:64, h, kb * 128:(kb + 1) * 128], QmT[0:64, h, qt * 512:(qt + 1) * 512], True, True,
                         reads=["KmT", ("QmT", qt)], writes=[("ps", pi)])
                    b.act(pt, psb[pi][:], AF.Exp, reads=[("ps", pi), "cvec"], writes=[("sq", kb)], bias=nmshift, scale=1.0)
                    b.mm(psb[po][0:65, :], Vm[:, kb, h, :], pt, kb == 0, kb == 1, reads=["Vm", ("sq", kb)], writes=[("ps", po)])
                b.v("dve", "reciprocal", rc[64:65, :], psb[po][64:65, :], reads=[("ps", po)], writes=["rc"])
                b.mm(psb[4][0:64, :], ones_f[64:65, 0:64], rc[64:65, :], True, True, reads=["cst", "rc"], writes=[("ps", 4)])
                b.act(rstd[0:64, :], psb[4][0:64, :], AF.Copy, reads=[("ps", 4)], writes=["rstd"])
                b.v("dve", "tensor_tensor", omT[0:64, h, qt * 512:(qt + 1) * 512], psb[po][0:64, :], rstd[0:64, :], ALU.mult,
                    reads=[("ps", po), "rstd"], writes=[("omT", h, qt)])
        mwo = wb[0][0:64, 0:4096].rearrange("p (h n) -> p h n", h=4)
        load_w(mwo, mwo_d[l], ("wb", 0))
        for g4 in range(4):
            tok = slice(g4 * 512, (g4 + 1) * 512)
            for oc in range(8):
                pi = next_ps()
                for h in range(4):
                    b.mm(psb[pi][:], mwo[:, h, oc * 128:(oc + 1) * 128], omT[0:64, h, tok], h == 0, h == 3,
                         reads=[("wb", 0), ("omT", h, g4)], writes=[("ps", pi)])
                b.v("dve", "tensor_tensor", xT[:, oc, tok], xT[:, oc, tok], psb[pi][:], ALU.add,
                    reads=[("xT", oc, g4), ("ps", pi)], writes=[("xT", oc, g4)])
        dbg("x2", xT[:], [128, 8, T], [("xT", c, g) for c in range(8) for g in range(4)])

    MLP_KEYS = [("hid", j, t) for j in range(32) for t in range(2)] + [("rt", 0), ("rt", 1)]

    def mlp(l):
        hid = A16(0, [128, 32, 1024])
        rt = [A16(64, [128, 512]), A16(65, [128, 512])]
        n = 0
        for hf in range(2):
            norm_x(l, 16, hf * 1024, 2, None)
            for jg in range(8):
                w1p = wb[jg % 2][:, 0:4096].rearrange("p (k n) -> p k n", k=8)
                load_w(w1p, w1_d[l][:, :, jg * 512:(jg + 1) * 512], ("wb", jg % 2))
                for jj in range(4):
                    j = jg * 4 + jj
                    for t in range(2):
                        pi = next_ps()
                        for k in range(8):
                            b.mm(psb[pi][:], w1p[:, k, jj * 128:(jj + 1) * 128], hT[:, k, t * 512:(t + 1) * 512], k == 0, k == 7,
                                 reads=[("wb", jg % 2), ("hT", t)], writes=[("ps", pi)])
                        r = rt[n % 2]
                        b.act(r, psb[pi][:], AF.Relu, reads=[("ps", pi)], writes=[("rt", n % 2)])
                        b.v("pool", "tensor_tensor", hid[:, j, t * 512:(t + 1) * 512], r, r, ALU.mult,
                            reads=[("rt", n % 2)], writes=[("hid", j, t)])
                        n += 1
            for oc in range(8):
                w2p = wb[oc % 2][:, 0:4096].rearrange("p (j n) -> p j n", j=32)
                load_w(w2p, w2_d[l][:, :, oc * 128:(oc + 1) * 128], ("wb", oc % 2))
                for t in range(2):
                    g4 = hf * 2 + t
                    tok = slice(g4 * 512, (g4 + 1) * 512)
                    pi = next_ps()
                    for j in range(32):
                        b.mm(psb[pi][:], w2p[:, j, :], hid[:, j, t * 512:(t + 1) * 512], j == 0, j == 31,
                             reads=[("wb", oc % 2), ("hid", j, t)], writes=[("ps", pi)])
                    b.v("dve", "tensor_tensor", xT[:, oc, tok], xT[:, oc, tok], psb[pi][:], ALU.add,
                        reads=[("xT", oc, g4), ("ps", pi)], writes=[("xT", oc, g4)])

    def run_layer(l, half=1, stop_after=None):
        rope_tables()
        layer(l)
        barrier(PROJ_KEYS + SSM_KEYS)
        ssm(l, half)
        if stop_after == "ssm":
            return
        barrier(SSM_KEYS + ["gK", "gM"] + [("gate", i) for i in range(4)])
        ssm_post(l)
        if stop_after == "ssm_post":
            return
        barrier(SSM_KEYS + ["gK", "gM"] + [("gate", i) for i in range(4)] + ATT_KEYS)
        mla(l, half)
        mix_out(l)
        if stop_after == "mix":
            return
        barrier(MIXER_KEYS + ATT_KEYS + CR_KEYS)
        cross(l)
        if stop_after == "cross":
            return
        barrier(CR_KEYS + MLP_KEYS)
        mlp(l)
        barrier(MLP_KEYS + MIXER_KEYS + PROJ_KEYS)

    def store_x(half):
        for c in range(8):
            b.dma(outT_d[half, :, c, :], xT[:, c, :], reads=[("xT", c, g) for g in range(4)], writes=[("dramout", "out")], key="out")

    def finish():
        b.p.add("sp", lambda e: e.nop(), reads=[("dramout", "out")] + [("dbgdone", k) for k in b.out_keys])

    b.ctx = dict(locals())
    return b


_CACHE = {}
N_LAYERS = 4


def _program():
    if "b" not in _CACHE:
        b = build_program(N_LAYERS)
        c = b.ctx
        for half in range(2):
            c["load_x"](half)
            for l in range(N_LAYERS):
                c["run_layer"](l, half)
            c["store_x"](half)
        c["finish"]()
        b.p.emit(b.nc, b.es)
        _CACHE["b"] = b
    return _CACHE["b"]


def kernel(**inputs):
    inp = {k: np.asarray(v) for k, v in inputs.items()}
    B = inp["x"].shape[0]
    b = _program()
    cst = host_constants()
    W = prep_weights(inp, range(N_LAYERS))
    in_maps = []
    for core in range(8):
        bb = core % B
        xs = np.asarray(inp["x"][bb], np.float32)
        xT = np.stack([chunkT(np.ascontiguousarray(xs[hf * T:(hf + 1) * T].T), 8) for hf in range(2)])
        pos = np.asarray(inp["positions"][bb]).astype(np.float32)
        m = {"xT": xT,
             "memT": chunkT(np.ascontiguousarray(np.asarray(inp["mem"][bb], np.float32).T), 8),
             "pos": np.stack([np.ascontiguousarray(pos[hf * T:(hf + 1) * T].reshape(16, 128).T) for hf in range(2)]),
             "cst": cst}
        m.update(W)
        in_maps.append(m)
    res = run_bass_kernel_spmd(b.nc, in_maps, core_ids=list(range(8)))
    out = np.zeros((B, 2 * T, D), np.float32)
    for bb in range(B):
        o = np.asarray(res.results[bb]["outT"], np.float32)
        for hf in range(2):
            out[bb, hf * T:(hf + 1) * T] = o[hf].transpose(2, 1, 0).reshape(T, D)
    return out
```

```python
import numpy as np
from contextlib import ExitStack
import concourse.bass as bass
import concourse.mybir as mybir
from concourse.bass_utils import run_bass_kernel_spmd

F32 = mybir.dt.float32
BF16 = mybir.dt.bfloat16
I32 = mybir.dt.int32
AF = mybir.ActivationFunctionType
ALU = mybir.AluOpType
AX = mybir.AxisListType

ENGS = ("pe", "act", "dve", "pool", "sp")


class Op:
    __slots__ = ("eng", "fn", "reads", "writes", "dma_key", "idx", "deps", "signal",
                 "tick", "waits")

    def __init__(self, eng, fn, reads, writes, dma_key):
        self.eng = eng
        self.fn = fn
        self.reads = reads
        self.writes = writes
        self.dma_key = dma_key
        self.deps = []
        self.signal = False
        self.tick = None
        self.waits = []


class Prog:
    def __init__(self):
        self.ops = []

    def add(self, eng, fn, reads=(), writes=(), dma_key=None):
        op = Op(eng, fn, tuple(reads), tuple(writes), dma_key)
        op.idx = len(self.ops)
        self.ops.append(op)
        return op

    def analyse(self):
        last_w = {}
        readers = {}
        for op in self.ops:
            deps = set()
            for k in op.reads:
                w = last_w.get(k)
                if w is not None:
                    deps.add(w.idx)
            for k in op.writes:
                w = last_w.get(k)
                if w is not None:
                    deps.add(w.idx)
                for r in readers.get(k, {}).values():
                    deps.add(r.idx)
            deps.discard(op.idx)
            keep = []
            for d in deps:
                a = self.ops[d]
                if a.dma_key is None and op.dma_key is None and a.eng == op.eng:
                    if a.eng == "pe":
                        continue
                    raw = any(k in a.writes for k in op.reads)
                    if not raw:
                        continue
                keep.append(d)
            op.deps = keep
            for d in keep:
                self.ops[d].signal = True
            for k in op.writes:
                last_w[k] = op
                readers[k] = {}
            for k in op.reads:
                rk = op.dma_key if op.dma_key is not None else op.eng
                readers.setdefault(k, {})[rk] = op
        cnt = {e: 0 for e in ENGS}
        dcnt = {}
        for op in self.ops:
            if op.dma_key is not None:
                dcnt[op.dma_key] = dcnt.get(op.dma_key, 0) + 1
                op.tick = ("dma", op.dma_key, dcnt[op.dma_key] * 16)
            elif op.signal:
                cnt[op.eng] += 1
                op.tick = ("eng", op.eng, cnt[op.eng])
        waited = {e: {} for e in ENGS}
        dma_issued = {}
        for op in self.ops:
            need = {}
            for d in op.deps:
                a = self.ops[d]
                t = a.tick
                if t[0] == "dma":
                    v = dma_issued.get(t[1], 0) * 16
                    v = max(v, t[2])
                else:
                    v = t[2]
                sk = (t[0], t[1])
                if v > need.get(sk, 0):
                    need[sk] = v
            for sk, v in need.items():
                if waited[op.eng].get(sk, 0) >= v:
                    continue
                waited[op.eng][sk] = v
                op.waits.append((sk, v))
            if op.dma_key is not None:
                dma_issued[op.dma_key] = dma_issued.get(op.dma_key, 0) + 1
        self.n_eng_sems = cnt
        self.dma_keys = list(dcnt.keys())

    def emit(self, nc, es):
        self.analyse()
        sems = {}
        for e in ENGS:
            sems[("eng", e)] = es.enter_context(nc.semaphore("s_" + e))
        for i, k in enumerate(self.dma_keys):
            sems[("dma", k)] = es.enter_context(nc.semaphore("d%d" % i))
        self.sems = sems
        per = {e: [op for op in self.ops if op.eng == e] for e in ENGS}
        block = es.enter_context(nc.Block())

        def run(engine, lst):
            for op in lst:
                for sk, v in op.waits:
                    engine.wait_ge(sems[sk], v)
                ins = op.fn(engine)
                if op.dma_key is not None:
                    ins.then_inc(sems[("dma", op.dma_key)], 16)
                elif op.signal:
                    ins.then_inc(sems[("eng", op.eng)], 1)

        @block.tensor
        def _(e):
            run(e, per["pe"])

        @block.scalar
        def _(e):
            run(e, per["act"])

        @block.vector
        def _(e):
            run(e, per["dve"])

        @block.gpsimd
        def _(e):
            run(e, per["pool"])

        @block.sync
        def _(e):
            run(e, per["sp"])


D = 1024
T = 2048
NT4 = T // 512
NT1 = T // 128
KC = D // 128
IN_COLS = 928
EPS = 1e-6
PI = float(np.pi)
SHIFT = 10.0


class Builder:
    def __init__(self, nc, es, n_layers, debug=()):
        self.nc = nc
        self.es = es
        self.p = Prog()
        self.L = n_layers
        self.debug = set(debug)
        self.dram_in = {}
        self.dram_out = {}
        self._uid = 0
        self.out_keys = []

    def din(self, name, shape, dt=F32):
        t = self.nc.dram_tensor(name, list(shape), dt, kind="ExternalInput")
        self.dram_in[name] = t
        return t

    def dout(self, name, shape, dt=F32):
        t = self.nc.dram_tensor(name, list(shape), dt, kind="ExternalOutput")
        self.dram_out[name] = t
        return t

    def sb(self, name, shape, dt=F32):
        return self.es.enter_context(self.nc.sbuf_tensor(name, list(shape), dt))

    def ps(self, name, shape, dt=F32):
        return self.es.enter_context(self.nc.psum_tensor(name, list(shape), dt))

    def dma(self, out, in_, reads=(), writes=(), key=None, eng="sp", **kw):
        writes = list(writes)
        if key is None:
            self._uid += 1
            key = "a%d_%s" % (self._uid % 12, eng)
            writes.append(("dmakey", key))
        return self.p.add(eng, lambda e, o=out, i=in_, kw=kw: e.dma_start(out=o, in_=i, **kw),
                          reads=reads, writes=writes, dma_key=key)

    def mm(self, out, lhsT, rhs, start, stop, reads=(), writes=()):
        return self.p.add("pe", lambda e: e.matmul(out, lhsT, rhs, start=start, stop=stop),
                          reads=reads, writes=writes)

    def tr(self, out, in_, ident, reads=(), writes=()):
        return self.p.add("pe", lambda e: e.transpose(out, in_, ident), reads=reads, writes=writes)

    def act(self, out, in_, func, reads=(), writes=(), eng="act", **kw):
        return self.p.add("act", lambda e: e.activation(out, in_, func, **kw), reads=reads, writes=writes)

    def v(self, eng, name, *args, reads=(), writes=(), **kw):
        return self.p.add(eng, lambda e: getattr(e, name)(*args, **kw), reads=reads, writes=writes)


def host_constants():
    c = np.zeros((128, 1664), np.float32)
    c[:, 0:128] = np.eye(128)
    c[:, 128:256] = 1.0
    r = np.arange(128)
    c[:, 256:384] = (r[:, None] <= r[None, :])
    c[127, 384:512] = 1.0
    c[127, 512:640] = -1.0
    m = np.arange(896)
    c[:, 640:1536] = (m[None, :] - r[:, None] >= 384)
    c[:, 1536:1552] = (10000.0 ** (-np.arange(16, dtype=np.float32) / 16))[None, :]
    c[:, 1552] = r + 1
    c[:, 1553] = -(r + 1)
    return c


def chunkT(v, k):
    return np.ascontiguousarray(v.reshape((k, 128) + v.shape[1:]).swapaxes(0, 1))


def prep_weights(inp, layers):
    f = lambda a: np.asarray(a, np.float32)
    W = {}
    Ls = list(layers)
    n = len(Ls)
    vec = np.zeros((n, 128, 64), np.float32)
    row = np.zeros((n, 320), np.float32)
    bm = np.zeros((n, 4, 128, 1024), np.float32)
    cm = np.zeros((n, 4, 128, 8, 128), np.float32)
    lam = np.zeros((n, 4, 3, 512), np.float32)
    for i, l in enumerate(Ls):
        vec[i, :, 0:8] = f(inp["norm_mix"][l]).reshape(8, 128).T
        vec[i, :, 8:16] = f(inp["norm_mem_q"][l]).reshape(8, 128).T
        vec[i, :, 16:24] = f(inp["norm_mlp"][l]).reshape(8, 128).T
        vec[i, :, 24:32] = f(inp["norm_mem_kv"][l]).reshape(8, 128).T
        vec[i, :, 32:34] = f(inp["mla_q_norm"][l]).reshape(2, 128).T
        vec[i, :, 34] = f(inp["mla_kv_norm"][l])
        vec[i, :, 35:39] = f(inp["ssm_d"][l]).reshape(4, 128).T
        vec[i, :, 39:43] = f(inp["ssm_b_glu"][l]).reshape(4, 128).T
        vec[i, :, 43:47] = f(inp["out_norm_ssm"][l]).reshape(4, 128).T
        vec[i, :, 47:51] = f(inp["out_norm_mla"][l]).reshape(4, 128).T
        row[i, 0:96] = f(inp["mla_q_gain"][l])
        row[i, 96:192] = f(inp["mla_k_gain"][l])
        row[i, 192:256] = f(inp["mem_q_gain"][l])
        row[i, 256:320] = f(inp["mem_k_gain"][l])
        bre = f(inp["ssm_b_re"][l])
        bim = f(inp["ssm_b_im"][l])
        cre = f(inp["ssm_c_re"][l])
        cim = f(inp["ssm_c_im"][l])
        lre = f(inp["ssm_lambda_re"][l])
        lim = f(inp["ssm_lambda_im"][l])
        lst = f(inp["ssm_log_step"][l])
        for c in range(4):
            for j in range(8):
                g = 8 * c + j
                bm[i, c, 16 * j:16 * j + 16, j * 64:(j + 1) * 64] = bre[g].T
                bm[i, c, 16 * j:16 * j + 16, 512 + j * 64:512 + (j + 1) * 64] = bim[g].T
                pr, e = j // 2, j % 2
                cm[i, c, e * 64:(e + 1) * 64, pr, 16 * j:16 * j + 16] = cre[g].T
                cm[i, c, e * 64:(e + 1) * 64, 4 + pr, 16 * j:16 * j + 16] = cim[g].T
                lam[i, c, 0, j * 64:(j + 1) * 64] = lre[g]
                lam[i, c, 1, j * 64:(j + 1) * 64] = lim[g]
                lam[i, c, 2, j * 64:(j + 1) * 64] = lst[g]
    W["vec"] = vec
    W["row"] = row
    W["bm"] = bm
    W["cm"] = cm
    W["lam"] = lam
    st = lambda name, k: np.stack([chunkT(f(inp[name][l]), k) for l in Ls])
    W["w_in"] = st("w_in", 8)
    W["w_uq"] = st("mla_w_uq", 2)
    W["w_ukv"] = np.stack([f(inp["mla_w_ukv"][l]) for l in Ls])
    W["w_glu"] = st("ssm_w_glu", 4)
    W["w_out"] = st("w_out", 8)
    W["mwq"] = st("mem_w_q", 8)
    W["mwkv"] = st("mem_w_kv", 8)
    W["mwo"] = np.stack([np.ascontiguousarray(f(inp["mem_w_o"][l]).reshape(4, 64, 1024).swapaxes(0, 1)) for l in Ls])
    W["w1"] = st("mlp_w1", 8)
    W["w2"] = st("mlp_w2", 32)
    return W


TWO_PI = 2.0 * PI
GA = 1.5957691216057308
GB = 0.044715
QSCALE = 1.0 / float(np.sqrt(96.0))
MSCALE = 1.0 / 8.0
MSHIFT = 8.0
NEG_BIG = -30000.0


def build_program(n_layers=1, debug=()):
    nc = bass.Bass("TRN2", target_bir_lowering=False)
    es = ExitStack()
    b = Builder(nc, es, n_layers, debug)
    P = b.p
    L = n_layers

    xT_d = b.din("xT", [2, 128, 8, T]).ap()
    memT_d = b.din("memT", [128, 8, 256]).ap()
    pos_d = b.din("pos", [2, 128, 16]).ap()
    hinit_d = nc.dram_tensor("sc_h", [L, 4, 1024], BF16, kind="Internal").ap()
    pkt_d = nc.dram_tensor("sc_kt", [L, 4, 96, 2, T], BF16, kind="Internal").ap()
    pve_d = nc.dram_tensor("sc_ve", [L, 4, 128, 16, 65], BF16, kind="Internal").ap()
    pvo_d = nc.dram_tensor("sc_vo", [L, 4, 128, 16, 128], BF16, kind="Internal").ap()
    cst_d = b.din("cst", [128, 1664]).ap()
    vec_d = b.din("vec", [L, 128, 64]).ap()
    row_d = b.din("row", [L, 192 + 64 + 64]).ap()
    w_in_d = b.din("w_in", [L, 128, 8, IN_COLS]).ap()
    w_uq_d = b.din("w_uq", [L, 128, 2, 768]).ap()
    w_ukv_d = b.din("w_ukv", [L, 128, 1024]).ap()
    w_glu_d = b.din("w_glu", [L, 128, 4, 512]).ap()
    w_out_d = b.din("w_out", [L, 128, 8, 1024]).ap()
    mwq_d = b.din("mwq", [L, 128, 8, 256]).ap()
    mwkv_d = b.din("mwkv", [L, 128, 8, 512]).ap()
    mwo_d = b.din("mwo", [L, 64, 4, 1024]).ap()
    w1_d = b.din("w1", [L, 128, 8, 4096]).ap()
    w2_d = b.din("w2", [L, 128, 32, 1024]).ap()
    bm_d = b.din("bm", [L, 4, 128, 1024]).ap()
    cm_d = b.din("cm", [L, 4, 128, 8, 128]).ap()
    lam_d = b.din("lam", [L, 4, 3, 512]).ap()
    outT_d = b.dout("outT", [2, 128, 8, T]).ap()
    okt_d, ove_d, ovo_d, hfin_d = pkt_d, pve_d, pvo_d, hinit_d
    dbg_d = {}

    xT = b.sb("xT_s", [128, 8, T])
    hT = b.sb("hT_s", [128, 8, 1024], BF16)
    wb = [b.sb("wb%d" % i, [128, 8192], BF16) for i in range(2)]
    cst = b.sb("cst_s", [128, 160])
    cbf = b.sb("cbf_s", [128, 1536], BF16)
    vec = b.sb("vec_s", [128, L, 64])
    rowb = b.sb("rowb_s", [128, 320])
    sq = b.sb("sq_s", [128, 8, 512], BF16)
    rstd = b.sb("rstd_s", [128, 512])
    posf = b.sb("pos_s", [128, 16])
    cvec = b.sb("cvec_s", [128, 8])
    AR = 37888
    arena = b.sb("arena", [128, AR], BF16)
    arena_f = arena.bitcast(F32)

    psp = [b.ps("psp%d" % i, [128, 1024]) for i in range(4)]
    psb = [psp[i // 2][:, (i % 2) * 512:(i % 2 + 1) * 512] for i in range(8)]
    psb_bf = [a_.bitcast(BF16) for a_ in psb]

    ident = cbf[:, 0:128]
    ones = cbf[:, 128:256]
    tri = cbf[:, 256:384]
    e127 = cbf[:, 384:512]
    e127n = cbf[:, 512:640]
    cmask = cbf[:, 640:1536]
    ones_f = cst[:, 0:128]
    invf = cst[:, 128:144]
    sp1 = cst[:, 144:145]
    nsp1 = cst[:, 145:146]

    def A16(off_kib, shape):
        n = int(np.prod(shape[1:]))
        o = int(round(off_kib * 512))
        assert o + n <= AR, (off_kib, shape)
        v = arena[:, o:o + n]
        if len(shape) == 3:
            v = v.rearrange("p (a b) -> p a b", a=shape[1])
        elif len(shape) == 4:
            v = v.rearrange("p (a b c) -> p a b c", a=shape[1], b=shape[2])
        return v

    def A32(off_kib, shape):
        n = int(np.prod(shape[1:]))
        o = int(round(off_kib * 256))
        assert (o + n) * 2 <= AR, (off_kib, shape)
        v = arena_f[:, o:o + n]
        if len(shape) == 3:
            v = v.rearrange("p (a b) -> p a b", a=shape[1])
        elif len(shape) == 4:
            v = v.rearrange("p (a b c) -> p a b c", a=shape[1], b=shape[2])
        return v

    uT = A16(0, [128, 4, T])
    cqn = A16(16, [128, 2, T])
    ckvn = A16(24, [128, T])
    krope = A32(28, [128, 16, 32])
    cossin = A32(30, [128, 2, 16, 16])
    ymla = A16(32, [128, 4, T])
    WK = 48

    b.dma(cst[:, 0:128], cst_d[:, 128:256], writes=["cst"])
    b.dma(cst[:, 128:160], cst_d[:, 1536:1568], writes=["cst"])
    b.dma(cbf[:], cst_d[:, 0:1536], writes=["cbf"], eng="pool")
    b.dma(vec[:], vec_d.rearrange("l p k -> p l k"), writes=["vec"])
    def load_x(half):
        b.dma(posf[:], pos_d[half], writes=["pos"])
        for c in range(8):
            b.dma(xT[:, c, :], xT_d[half, :, c, :], writes=[("xT", c, t4) for t4 in range(4)])
    b.v("pool", "memset", cvec[:, 0:1], -SHIFT, writes=["cvec"])
    b.v("pool", "memset", cvec[:, 1:2], -MSHIFT, writes=["cvec"])
    b.v("pool", "memset", cvec[:, 2:3], -PI, writes=["cvec"])
    b.v("pool", "memset", cvec[:, 3:4], EPS, writes=["cvec"])
    nshift = cvec[:, 0:1]
    nmshift = cvec[:, 1:2]
    npi = cvec[:, 2:3]
    epsv = cvec[:, 3:4]

    cos_t = cossin[:, 0]
    sin_t = cossin[:, 1]

    def wrap_pi(dst, src, ti, tf, tm, rk, wk):
        b.v("dve", "tensor_scalar_mul", ti, src, 1.0 / TWO_PI, reads=rk, writes=[wk + "_ti"])
        b.v("dve", "tensor_copy", tf, ti, reads=[wk + "_ti"], writes=[wk + "_tf"])
        b.v("dve", "scalar_tensor_tensor", dst, tf, -TWO_PI, src, ALU.mult, ALU.add, reads=[wk + "_tf"] + rk, writes=[wk])
        b.v("dve", "tensor_scalar", tm, dst, PI, -TWO_PI, ALU.is_gt, ALU.mult, reads=[wk], writes=[wk + "_tm"])
        b.v("dve", "tensor_tensor", dst, dst, tm, ALU.add, reads=[wk, wk + "_tm"], writes=[wk])
        b.v("dve", "tensor_scalar", tm, dst, -PI, TWO_PI, ALU.is_lt, ALU.mult, reads=[wk], writes=[wk + "_tm"])
        b.v("dve", "tensor_tensor", dst, dst, tm, ALU.add, reads=[wk, wk + "_tm"], writes=[wk])

    def rope_tables():
        ang = A32(WK, [128, 16, 16])
        ang2 = A32(WK + 1, [128, 16, 16])
        ti = A32(WK + 2, [128, 16, 16]).bitcast(I32)
        tf = A32(WK + 3, [128, 16, 16])
        tm = A32(WK + 4, [128, 16, 16])
        wr = A32(WK + 5, [128, 16, 16])
        for t in range(16):
            b.v("dve", "tensor_scalar_mul", ang[:, t, :], invf, posf[:, t:t + 1],
                reads=["cst", "pos"], writes=["ropeang"])
        wrap_pi(wr, ang, ti, tf, tm, ["ropeang"], "ropew")
        b.act(sin_t, wr, AF.Sin, reads=["ropew"], writes=["sin"])
        b.v("dve", "tensor_scalar_add", ang2, ang, PI / 2, reads=["ropeang"], writes=["ropeang2"])
        wrap_pi(wr, ang2, ti, tf, tm, ["ropeang2", "sin"], "ropew")
        b.act(cos_t, wr, AF.Sin, reads=["ropew"], writes=["cos"])

    state = {"ps": 0, "wb": 0}
    SQK = [("sq", i) for i in range(8)]

    def next_ps():
        i = state["ps"]
        state["ps"] = (i + 1) % 4
        return i

    def load_w(view, src, key, eng="pool"):
        return b.dma(view, src, writes=[key], key=("w", key), eng=eng)

    def rms_scale(src_keys, nchunk, n, src_fn, ps_i, inv_dim):
        for c in range(nchunk):
            b.mm(psb[ps_i][:, 0:n], ones, src_fn(c), c == 0, c == nchunk - 1,
                 reads=["cbf"] + src_keys, writes=[("ps", ps_i)])
        b.act(rstd[:, 0:n], psb[ps_i][:, 0:n], AF.Sqrt, reads=[("ps", ps_i), "cvec"], writes=["rstd"],
              bias=epsv, scale=inv_dim)
        b.v("dve", "reciprocal", rstd[:, 0:n], rstd[:, 0:n], reads=["rstd"], writes=["rstd"])

    def norm_x(l, gain_col, tok0, nt, hkeys):
        for t in range(nt):
            g4 = (tok0 // 512) + t
            xs = xT[:, :, g4 * 512:(g4 + 1) * 512]
            xk = [("xT", c, g4) for c in range(8)]
            b.act(sq[:], xs, AF.Square, reads=xk, writes=SQK)
            pi = next_ps()
            rms_scale(SQK, 8, 512, lambda c: sq[:, c, :], pi, 1.0 / D)
            for c in range(8):
                b.v("dve", "scalar_tensor_tensor", hT[:, c, t * 512:(t + 1) * 512], xT[:, c, g4 * 512:(g4 + 1) * 512],
                    vec[:, l, gain_col + c:gain_col + c + 1], rstd[:], ALU.mult, ALU.mult,
                    reads=[("xT", c, g4), "vec", "rstd"], writes=[("hT", t)])

    def dbg(name, view, shape, reads, dt=F32):
        if name in b.debug:
            d = b.dout("dbg_" + name, shape, dt).ap()
            b.dma(d, view, reads=reads, writes=[("dbgdone", "dbg_" + name)], key="dbg_" + name)
            b.out_keys.append("dbg_" + name)

    def layer(l):
        V0 = lambda c: vec[:, l, c:c + 1]
        b.dma(rowb[:], row_d[l].partition_broadcast(128), writes=["rowb"])
        gq = rowb[:, 0:96]
        gk = rowb[:, 96:192]
        mgq = rowb[:, 192:256]
        mgk = rowb[:, 256:320]
        b.v("dve", "tensor_scalar_mul", gq, gq, QSCALE, reads=["rowb"], writes=["rowb"])
        b.v("dve", "tensor_scalar_mul", mgq, mgq, MSCALE, reads=["rowb"], writes=["rowb"])

        w_in = wb[0][:, 0:8 * IN_COLS].rearrange("p (k n) -> p k n", k=8)
        load_w(w_in, w_in_d[l], ("wb", 0))
        cqraw = A32(WK, [128, 3, 512])
        for hf in range(2):
            norm_x(l, 0, hf * 1024, 2, None)
            for t in range(2):
                g4 = hf * 2 + t
                tok = slice(g4 * 512, (g4 + 1) * 512)
                hk = [("hT", t)]
                for oc in range(7):
                    pi = next_ps()
                    for k in range(8):
                        b.mm(psb[pi][:], w_in[:, k, oc * 128:(oc + 1) * 128], hT[:, k, t * 512:(t + 1) * 512],
                             k == 0, k == 7, reads=[("wb", 0)] + hk, writes=[("ps", pi)])
                    if oc < 4:
                        b.act(uT[:, oc, tok], psb[pi][:], AF.Copy, reads=[("ps", pi)], writes=[("uT", oc, g4)])
                    else:
                        b.act(cqraw[:, oc - 4, :], psb[pi][:], AF.Copy, reads=[("ps", pi)], writes=[("cqraw", oc - 4)])
                b.act(sq[:, 0:2, :], cqraw[:, 0:2, :], AF.Square, reads=[("cqraw", 0), ("cqraw", 1)], writes=SQK)
                pi = next_ps()
                rms_scale(SQK, 2, 512, lambda c: sq[:, c, :], pi, 1.0 / 256)
                for c in range(2):
                    b.v("dve", "scalar_tensor_tensor", cqn[:, c, tok], cqraw[:, c, :], V0(32 + c), rstd[:],
                        ALU.mult, ALU.mult, reads=[("cqraw", c), "vec", "rstd"], writes=[("cqn", g4)])
                b.act(sq[:, 2, :], cqraw[:, 2, :], AF.Square, reads=[("cqraw", 2)], writes=SQK)
                pi = next_ps()
                rms_scale(SQK, 1, 512, lambda c: sq[:, 2, :], pi, 1.0 / 128)
                b.v("dve", "scalar_tensor_tensor", ckvn[:, tok], cqraw[:, 2, :], V0(34), rstd[:],
                    ALU.mult, ALU.mult, reads=[("cqraw", 2), "vec", "rstd"], writes=[("ckvn", g4)])
                for s4 in range(4):
                    t16 = g4 * 4 + s4
                    pi = next_ps()
                    for k in range(8):
                        b.mm(psb[pi][:, 0:32], hT[:, k, t * 512 + s4 * 128:t * 512 + (s4 + 1) * 128], w_in[:, k, 896:928],
                             k == 0, k == 7, reads=[("wb", 0)] + hk, writes=[("ps", pi)])
                    b.act(krope[:, t16, :], psb[pi][:, 0:32], AF.Copy, reads=[("ps", pi)], writes=[("krope", t16)])
        dbg("uT", uT, [128, 4, T], [("uT", c, g) for c in range(4) for g in range(4)], BF16)
        dbg("cqn", cqn, [128, 2, T], [("cqn", g) for g in range(4)], BF16)
        dbg("ckvn", ckvn, [128, T], [("ckvn", g) for g in range(4)], BF16)
        dbg("krope", krope, [128, 16, 32], [("krope", g) for g in range(16)])
        dbg("cos", cos_t, [128, 16, 16], ["cos"])
        dbg("sin", sin_t, [128, 16, 16], ["sin"])


    hT_flat = hT[:].rearrange("p a b -> p (a b)")
    pkt_s = hT_flat[:, 0:4096].rearrange("p (e t) -> p e t", e=2)
    pve_s = hT_flat[:, 4096:4096 + 1040].rearrange("p (t c) -> p t c", t=16)
    pvo_s = hT_flat[:, 5248:5248 + 2048].rearrange("p (t c) -> p t c", t=16)
    HK = [("hT", 0), ("hT", 1)]
    QT = A16(48, [128, 2, T])
    KT = A16(56, [128, 2, T])
    Ve = A16(64, [128, 16, 65])
    Vo = A16(66.5, [128, 16, 128])
    sq_f = sq[:].rearrange("p a b -> p (a b)").bitcast(F32)
    sqq = sq_f[:, 0:768].rearrange("p (g e d) -> p g e d", g=4, e=2)
    qn = sq_f[:, 768:1536].rearrange("p (g e d) -> p g e d", g=4, e=2)
    SQ6 = [("sq", i) for i in range(6)]
    kraw = wb[1].bitcast(F32)[:, 2048:2816].rearrange("p (g e d) -> p g e d", g=4, e=2)
    KRK = ("wb", 1, "glu")
    qf = A16(70.5, [128, 4, 192]).rearrange("p g (e d) -> p g e d", e=2)
    rta = A32(72, [128, 4, 32]).rearrange("p g (e d) -> p g e d", e=2)
    rtb = A32(72.5, [128, 4, 32]).rearrange("p g (e d) -> p g e d", e=2)
    ssq = A32(73, [128, 4, 2])
    wb1f = wb[1].bitcast(F32)
    rc = wb1f[:, 3584:4096]
    PT = [sq[:, i, :] for i in range(8)]

    def qk_post(src, gain, dstT, ps_bank, g4, srck, tag):
        b.act(sqq, src, AF.Square, reads=srck, writes=SQ6)
        b.v("dve", "tensor_reduce", ssq, sqq, AX.X, ALU.add, reads=SQ6, writes=["ssq"])
        b.act(ssq, ssq, AF.Sqrt, reads=["ssq", "cvec"], writes=["ssq"], bias=epsv, scale=1.0 / 96)
        b.v("dve", "reciprocal", ssq, ssq, reads=["ssq"], writes=["ssq"])
        b.v("dve", "tensor_tensor", qn, src, ssq.unsqueeze(3).to_broadcast([128, 4, 2, 96]), ALU.mult,
            reads=srck + ["ssq"], writes=SQ6)
        b.v("dve", "tensor_tensor", qn, qn, gain.unsqueeze(1).unsqueeze(1).to_broadcast([128, 4, 2, 96]), ALU.mult,
            reads=SQ6 + ["rowb"], writes=SQ6)
        cs = cos_t[:, g4 * 4:(g4 + 1) * 4, :].unsqueeze(2).to_broadcast([128, 4, 2, 16])
        sn = sin_t[:, g4 * 4:(g4 + 1) * 4, :].unsqueeze(2).to_broadcast([128, 4, 2, 16])
        b.v("pool", "tensor_copy", qf[:, :, :, 0:64], qn[:, :, :, 0:64], reads=SQ6, writes=["qf"])
        b.v("pool", "tensor_tensor", rta, qn[:, :, :, 64:80], cs, ALU.mult, reads=SQ6 + ["cos"], writes=["rta"])
        b.v("pool", "tensor_tensor", rtb, qn[:, :, :, 80:96], sn, ALU.mult, reads=SQ6 + ["sin"], writes=["rtb"])
        b.v("pool", "tensor_tensor", qf[:, :, :, 64:80], rta, rtb, ALU.subtract, reads=["rta", "rtb"], writes=["qf"])
        b.v("pool", "tensor_tensor", rta, qn[:, :, :, 80:96], cs, ALU.mult, reads=SQ6 + ["cos", "qf"], writes=["rta"])
        b.v("pool", "tensor_tensor", rtb, qn[:, :, :, 64:80], sn, ALU.mult, reads=SQ6 + ["sin", "qf"], writes=["rtb"])
        b.v("pool", "tensor_tensor", qf[:, :, :, 80:96], rta, rtb, ALU.add, reads=["rta", "rtb"], writes=["qf"])
        for i in range(4):
            for e in range(2):
                slot = e * 4 + i
                b.tr(psb_bf[ps_bank][0:96, slot * 128:(slot + 1) * 128], qf[:, i, e, :], ident,
                     reads=["qf", "cbf"], writes=[("ps", ps_bank)])
        b.act(dstT[0:96, :, g4 * 512:(g4 + 1) * 512],
              psb_bf[ps_bank][0:96, :].rearrange("p (e n) -> p e n", e=2), AF.Copy,
              reads=[("ps", ps_bank)], writes=[(tag, g4)])

    def mla(l, half=1):
        wuq = wb[1][:, 0:1536].rearrange("p (k n) -> p k n", k=2)
        wukv = wb[1][:, 1536:2560]
        load_w(wuq, w_uq_d[l], ("wb", 1))
        load_w(wukv, w_ukv_d[l], ("wb", 1))
        gq = rowb[:, 0:96]
        gk = rowb[:, 96:192]
        b.v("pool", "memset", Ve[:, :, 64:65], 1.0, writes=["Ve"])
        b.v("pool", "memset", Vo[:, :, 0:64], 0.0, writes=["Vo"])
        b.v("pool", "memset", Vo[:, :, 0:1], 1.0, writes=["Vo"])
        for hp in range(4):
            if half == 1:
                b.dma(pkt_s[0:96], pkt_d[l, hp], reads=[("skv", l, hp)], writes=HK, key="pk")
                b.dma(pve_s, pve_d[l, hp], reads=[("skv", l, hp)], writes=HK, key="pk")
                b.dma(pvo_s, pvo_d[l, hp], reads=[("skv", l, hp)], writes=HK, key="pk")
            qps4 = psp[0][:, :].rearrange("p (g n) -> p g n", g=4)
            kvps4 = psp[1][:, :].rearrange("p (g n) -> p g n", g=4)
            PQ = [("ps", 0), ("ps", 1)]
            PKV = [("ps", 2), ("ps", 3)]
            for g4 in range(4):
                for i in range(4):
                    t16 = g4 * 4 + i
                    tok = slice(t16 * 128, (t16 + 1) * 128)
                    for k in range(2):
                        b.mm(qps4[:, i, 0:192], cqn[:, k, tok], wuq[:, k, hp * 192:(hp + 1) * 192], k == 0, k == 1,
                             reads=[("cqn", g4), ("wb", 1)], writes=[("ps", i // 2)])
                    b.mm(kvps4[:, i, :], ckvn[:, tok], wukv[:, hp * 256:(hp + 1) * 256], True, True,
                         reads=[("ckvn", g4), ("wb", 1)], writes=[("ps", 2 + i // 2)])
                qk_post(qps4[:, :, 0:192].rearrange("p g (e d) -> p g e d", e=2), gq, QT, 6, g4, PQ, "QT")
                kv4 = kvps4.rearrange("p g (e d) -> p g e d", e=2)
                b.act(kraw[:, :, :, 0:64], kv4[:, :, :, 0:64], AF.Copy, reads=PKV, writes=[KRK])
                b.v("pool", "tensor_copy", kraw[:, :, :, 64:96],
                    krope[:, g4 * 4:(g4 + 1) * 4, :].unsqueeze(2).to_broadcast([128, 4, 2, 32]),
                    reads=[("krope", g4 * 4 + i) for i in range(4)], writes=[KRK])
                b.act(Ve[:, g4 * 4:(g4 + 1) * 4, 0:64], kvps4[:, :, 64:128], AF.Copy, reads=PKV, writes=["Ve"])
                b.act(Vo[:, g4 * 4:(g4 + 1) * 4, 64:128], kvps4[:, :, 192:256], AF.Copy, reads=PKV, writes=["Vo"])
                qk_post(kraw, gk, KT, 7, g4, [KRK], "KT")
            QK = [("QT", g) for g in range(4)]
            KK = [("KT", g) for g in range(4)]
            if half == 0:
                b.dma(okt_d[l, hp], KT[0:96], reads=KK, writes=[("skv", l, hp)], key="okv")
                b.dma(ove_d[l, hp], Ve, reads=["Ve"], writes=[("skv", l, hp)], key="okv")
                b.dma(ovo_d[l, hp], Vo, reads=["Vo"], writes=[("skv", l, hp)], key="okv")
            for e in range(2):
                Vt = Ve if e == 0 else Vo
                vk = "Ve" if e == 0 else "Vo"
                pV = pve_s if e == 0 else pvo_s
                M = 65 if e == 0 else 128
                for qt in range(4):
                    blocks = ([("p", kb) for kb in range(16)] if half == 1 else []) + [("o", kb) for kb in range(4 * qt + 4)]
                    po = 2 + (qt % 2)
                    for bi, (kind, kb) in enumerate(blocks):
                        pi = bi % 2
                        pt = PT[bi % 4]
                        ptk = ("sq", bi % 4)
                        kts = (pkt_s if kind == "p" else KT)[0:96, e, kb * 128:(kb + 1) * 128]
                        kk = HK if kind == "p" else [("KT", kb // 4)]
                        b.mm(psb[pi][:], kts, QT[0:96, e, qt * 512:(qt + 1) * 512], True, True,
                             reads=kk + [("QT", qt)], writes=[("ps", pi)])
                        b.act(pt, psb[pi][:], AF.Exp, reads=[("ps", pi), "cvec"], writes=[ptk],
                              bias=nshift, scale=1.0)
                        if kind == "o" and kb >= 4 * qt:
                            o = kb - 4 * qt
                            b.v("dve", "tensor_tensor", pt, pt, cmask[:, (3 - o) * 128:(3 - o) * 128 + 512], ALU.mult,
                                reads=[ptk, "cbf"], writes=[ptk])
                        vs = (pV if kind == "p" else Vt)[:, kb, 0:M]
                        b.mm(psb[po][0:M, :], vs, pt, bi == 0, bi == len(blocks) - 1,
                             reads=(HK if kind == "p" else [vk]) + [ptk], writes=[("ps", po)])
                    dr = 64 if e == 0 else 0
                    b.v("dve", "reciprocal", rc[dr:dr + 1, :], psb[po][dr:dr + 1, :], reads=[("ps", po)], writes=["rc"])
                    if e == 0:
                        b.mm(psb[4][0:64, :], ones_f[64:65, 0:64], rc[64:65, :], True, True, reads=["cst", "rc"], writes=[("ps", 4)])
                        rows = slice(0, 64)
                    else:
                        b.mm(psb[4][:, :], ones_f[0:1, :], rc[0:1, :], True, True, reads=["cst", "rc"], writes=[("ps", 4)])
                        rows = slice(64, 128)
                    b.act(rstd[rows, :], psb[4][rows, :], AF.Copy, reads=[("ps", 4)], writes=["rstd"])
                    b.v("dve", "tensor_tensor", ymla[rows, hp, qt * 512:(qt + 1) * 512], psb[po][rows, :], rstd[rows, :], ALU.mult,
                        reads=[("ps", po), "rstd"], writes=[("ymla", hp, qt)])
        dbg("ymla", ymla, [128, 4, T], [("ymla", h, q) for h in range(4) for q in range(4)], BF16)
        dbg("QT", QT, [128, 2, T], [("QT", g) for g in range(4)], BF16)
        dbg("KT", KT, [128, 2, T], [("KT", g) for g in range(4)], BF16)
        dbg("Vo", Vo, [128, 16, 128], ["Vo"], BF16)


    dummy = b.sb("dummy_s", [128, 2])

    def barrier(keys):
        b.v("pool", "memset", dummy[:, 0:1], 0.0, writes=list(keys))

    MIXER_KEYS = ([("uT", c, g) for c in range(4) for g in range(4)] + [("cqn", g) for g in range(4)]
                  + [("ckvn", g) for g in range(4)] + [("krope", g) for g in range(16)] + ["cos", "sin"]
                  + [("ymla", h, q) for h in range(4) for q in range(4)])
    ATT_KEYS = ([("QT", g) for g in range(4)] + [("KT", g) for g in range(4)]
                + ["Ve", "Vo", "sqq", "qn", "qf", "rta", "rtb", "ssq", "kraw"])
    PROJ_KEYS = [("cqraw", i) for i in range(3)] + ["ropeang", "ropeang2", "ropew", "ropew_ti", "ropew_tf", "ropew_tm"]

    hT_f = hT[:].rearrange("p a b -> p (a b)").bitcast(F32)
    SLOTS = [{n: A32(48 + 2 * i, [128, 512]) for i, n in enumerate("ABCDEFGHIJKMN")},
             dict([(n, A32(32 + 2 * i, [128, 512])) for i, n in enumerate("ABCDEFGH")]
                  + [(n, hT_f[:, i * 512:(i + 1) * 512]) for i, n in enumerate("IJKMN")])]
    SSM_KEYS = ["s%d%s" % (ch, n) for ch in range(2) for n in "ABCDEFGHIJKMN"] + \
               ["Xr0", "Xi0", "Xr1", "Xi1"]
    SSM_BANKS = [(0, 1, 2, 3), (4, 5, 6, 7)]

    def ssm_prep(l, c, ch, half):
        slot = SLOTS[ch]
        sk = lambda n: "s%d%s" % (ch, n)
        def tt(eng, out, a, bb_, op):
            b.v(eng, "tensor_tensor", slot[out], slot[a], slot[bb_], op, reads=[sk(a), sk(bb_)], writes=[sk(out)])
        def wrap(dst, src, ti, tf, tm):
            d_, s_, ti_, tf_, tm_ = slot[dst], slot[src], slot[ti].bitcast(I32), slot[tf], slot[tm]
            kd, ks, kti, ktf, ktm = sk(dst), sk(src), sk(ti), sk(tf), sk(tm)
            b.v("dve", "tensor_scalar_mul", ti_, s_, 1.0 / TWO_PI, reads=[ks], writes=[kti])
            b.v("dve", "tensor_copy", tf_, ti_, reads=[kti], writes=[ktf])
            b.v("dve", "scalar_tensor_tensor", d_, tf_, -TWO_PI, s_, ALU.mult, ALU.add, reads=[ktf, ks], writes=[kd])
            b.v("dve", "tensor_scalar", tm_, d_, PI, -TWO_PI, ALU.is_gt, ALU.mult, reads=[kd], writes=[ktm])
            b.v("dve", "tensor_tensor", d_, d_, tm_, ALU.add, reads=[kd, ktm], writes=[kd])
            b.v("dve", "tensor_scalar", tm_, d_, -PI, TWO_PI, ALU.is_lt, ALU.mult, reads=[kd], writes=[ktm])
            b.v("dve", "tensor_tensor", d_, d_, tm_, ALU.add, reads=[kd, ktm], writes=[kd])
        def actf(out, a, func, **kw):
            b.act(slot[out], slot[a], func, reads=[sk(a), "cst"], writes=[sk(out)], **kw)
        for i, n in enumerate("KMN"):
            b.dma(slot[n], lam_d[l, c, i].partition_broadcast(128), writes=[sk(n)])
        actf("N", "N", AF.Exp)
        tt("dve", "G", "K", "N", ALU.mult)
        tt("dve", "H", "M", "N", ALU.mult)
        actf("A", "G", AF.Exp, scale=sp1)
        actf("C", "G", AF.Exp, scale=nsp1)
        b.v("dve", "tensor_scalar_mul", slot["I"], slot["H"], sp1, reads=[sk("H"), "cst"], writes=[sk("I")])
        wrap("J", "I", "N", "E", "F")
        actf("B", "J", AF.Sin)
        b.v("dve", "tensor_scalar_add", slot["I"], slot["I"], PI / 2, reads=[sk("I")], writes=[sk("I")])
        wrap("J", "I", "N", "E", "F")
        actf("D", "J", AF.Sin)
        tt("dve", "E", "A", "D", ALU.mult)
        tt("dve", "F", "A", "B", ALU.mult)
        tt("dve", "A", "C", "D", ALU.mult)
        b.v("dve", "scalar_tensor_tensor", slot["B"], slot["C"], -1.0, slot["B"], ALU.mult, ALU.mult,
            reads=[sk("C"), sk("B")], writes=[sk("B")])
        wrap("J", "H", "N", "C", "D")
        actf("I", "J", AF.Sin)
        b.v("dve", "tensor_scalar_add", slot["H"], slot["H"], PI / 2, reads=[sk("H")], writes=[sk("H")])
        wrap("J", "H", "N", "C", "D")
        actf("N", "J", AF.Sin)
        actf("C", "G", AF.Exp)
        tt("dve", "D", "C", "N", ALU.mult)
        b.v("dve", "tensor_scalar_add", slot["D"], slot["D"], -1.0, reads=[sk("D")], writes=[sk("D")])
        tt("dve", "I", "C", "I", ALU.mult)
        tt("dve", "J", "K", "K", ALU.mult)
        tt("dve", "N", "M", "M", ALU.mult)
        tt("dve", "J", "J", "N", ALU.add)
        b.v("dve", "reciprocal", slot["J"], slot["J"], reads=[sk("J")], writes=[sk("J")])
        tt("dve", "C", "D", "K", ALU.mult)
        tt("dve", "N", "I", "M", ALU.mult)
        tt("dve", "C", "C", "N", ALU.add)
        tt("dve", "C", "C", "J", ALU.mult)
        tt("dve", "N", "I", "K", ALU.mult)
        tt("dve", "G", "D", "M", ALU.mult)
        tt("dve", "N", "N", "G", ALU.subtract)
        tt("dve", "N", "N", "J", ALU.mult)
        tt("dve", "D", "A", "C", ALU.mult)
        tt("dve", "G", "B", "N", ALU.mult)
        tt("dve", "D", "D", "G", ALU.subtract)
        tt("dve", "I", "A", "N", ALU.mult)
        tt("dve", "G", "B", "C", ALU.mult)
        tt("dve", "I", "I", "G", ALU.add)
        cx = dict(slot=slot, sk=sk, ch=ch, c=c,
                  Tre=slot["E"], Tim=slot["F"], Tpr=slot["D"], Tpi=slot["I"],
                  TK=[sk("E"), sk("F"), sk("D"), sk("I")],
                  Bm=slot["A"].bitcast(BF16),
                  Cm=slot["B"].bitcast(BF16).rearrange("p (k n) -> p k n", k=8),
                  X=slot["C"].bitcast(BF16),
                  Sb=[slot["G"].bitcast(BF16), slot["H"].bitcast(BF16)], SbK=[sk("G"), sk("H")],
                  ST=slot["J"].bitcast(BF16).rearrange("p (k n) -> p k n", k=8))
        load_w(cx["Bm"], bm_d[l, c], sk("A"))
        load_w(cx["Cm"], cm_d[l, c], sk("B"))
        b.v("pool", "memset", cx["Sb"][1], 0.0, writes=[cx["SbK"][1]])
        if half == 1:
            b.dma(cx["Sb"][1][127:128, :], hinit_d[l, c:c + 1, :], reads=[("sh", l, c)], writes=[cx["SbK"][1]])
        return cx

    def ssm_tile(l, cx, k):
        slot, sk, ch, c = cx["slot"], cx["sk"], cx["ch"], cx["c"]
        Tre, Tim, Tpr, Tpi, TK = cx["Tre"], cx["Tim"], cx["Tpr"], cx["Tpi"], cx["TK"]
        Bm, Cm, X, Sb, SbK, ST = cx["Bm"], cx["Cm"], cx["X"], cx["Sb"], cx["SbK"], cx["ST"]
        b0, b1, tb, yb = SSM_BANKS[ch]
        Xr, Xi = "Xr%d" % ch, "Xi%d" % ch
        tok = slice(k * 128, (k + 1) * 128)
        g4 = k // 4
        cur, prv = k % 2, 1 - (k % 2)
        K_, M_, N_ = slot["K"], slot["M"], slot["N"]
        b.mm(psb[b0][:], uT[:, c, tok], Bm[:, 0:512], True, True, reads=[("uT", c, g4), sk("A")], writes=[("ps", b0)])
        b.mm(psb[b1][:], uT[:, c, tok], Bm[:, 512:1024], True, True, reads=[("uT", c, g4), sk("A")], writes=[("ps", b1)])
        b.v("dve", "tensor_tensor", K_, psb[b0][:], Tpr, ALU.mult, reads=[("ps", b0)] + TK, writes=[sk("K")])
        b.v("dve", "tensor_tensor", M_, psb[b1][:], Tpi, ALU.mult, reads=[("ps", b1)] + TK, writes=[sk("M")])
        b.v("pool", "tensor_tensor", X[:, 0:512], K_, M_, ALU.subtract, reads=[sk("K"), sk("M")], writes=[Xr])
        b.v("dve", "tensor_tensor", N_, psb[b1][:], Tpr, ALU.mult, reads=[("ps", b1)] + TK, writes=[sk("N")])
        b.v("dve", "tensor_tensor", K_, psb[b0][:], Tpi, ALU.mult, reads=[("ps", b0)] + TK + [Xr], writes=[sk("K")])
        b.v("pool", "tensor_tensor", X[:, 512:1024], N_, K_, ALU.add, reads=[sk("K"), sk("N")], writes=[Xi])
        b.mm(psb[b0][:], tri, X[:, 0:512], True, False, reads=["cbf", Xr], writes=[("ps", b0)])
        b.mm(psb[b0][:], e127, Sb[prv][:, 0:512], False, True, reads=["cbf", SbK[prv]], writes=[("ps", b0)])
        b.mm(psb[b1][:], tri, X[:, 512:1024], True, False, reads=["cbf", Xi], writes=[("ps", b1)])
        b.mm(psb[b1][:], e127n, Sb[prv][:, 512:1024], False, True, reads=["cbf", SbK[prv]], writes=[("ps", b1)])
        b.v("dve", "tensor_tensor", K_, psb[b0][:], Tre, ALU.mult, reads=[("ps", b0)] + TK + [Xi], writes=[sk("K")])
        b.v("dve", "tensor_tensor", M_, psb[b1][:], Tim, ALU.mult, reads=[("ps", b1)] + TK + [Xr], writes=[sk("M")])
        b.v("pool", "tensor_tensor", Sb[cur][:, 0:512], K_, M_, ALU.subtract, reads=[sk("K"), sk("M")], writes=[SbK[cur]])
        b.v("dve", "scalar_tensor_tensor", N_, psb[b1][:], -1.0, Tre, ALU.mult, ALU.mult,
            reads=[("ps", b1)] + TK + [Xi], writes=[sk("N")])
        b.v("dve", "scalar_tensor_tensor", K_, psb[b0][:], -1.0, Tim, ALU.mult, ALU.mult,
            reads=[("ps", b0)] + TK + [SbK[cur]], writes=[sk("K")])
        b.v("pool", "tensor_tensor", Sb[cur][:, 512:1024], N_, K_, ALU.add,
            reads=[sk("K"), sk("N")], writes=[SbK[cur]])
        for blk in range(8):
            b.tr(psb_bf[tb][:, blk * 128:(blk + 1) * 128], Sb[cur][:, blk * 128:(blk + 1) * 128], ident,
                 reads=[SbK[cur], "cbf"], writes=[("ps", tb)])
        b.act(ST, psb_bf[tb][:, :].rearrange("p (k n) -> p k n", k=8), AF.Copy, reads=[("ps", tb)], writes=[sk("J")])
        for blk in range(8):
            b.mm(psb[yb][:, (k % 4) * 128:(k % 4 + 1) * 128], Cm[:, blk, :], ST[:, blk, :], blk == 0, blk == 7,
                 reads=[sk("B"), sk("J")], writes=[("ps", yb)])
        if k % 4 == 3:
            t4 = slice(g4 * 512, (g4 + 1) * 512)
            b.v("dve", "scalar_tensor_tensor", uT[:, c, t4], uT[:, c, t4], vec[:, l, 35 + c:36 + c], psb[yb][:],
                ALU.mult, ALU.add, reads=[("uT", c, g4), "vec", ("ps", yb)], writes=[("uT", c, g4)])

    def ssm(l, half=1):
        for pair in range(2):
            cxs = [ssm_prep(l, 2 * pair + ch, ch, half) for ch in range(2)]
            for k in range(16):
                for cx in cxs:
                    ssm_tile(l, cx, k)
            if half == 0:
                for cx in cxs:
                    c = cx["c"]
                    b.dma(hfin_d[l, c:c + 1, :], cx["Sb"][1][127:128, :], reads=[cx["SbK"][1]], writes=[("sh", l, c)], key="hfin")
        dbg("yssm", uT, [128, 4, T], [("uT", c, g) for c in range(4) for g in range(4)], BF16)


    def group_norm(buf, keyf, gcol, l, g4):
        tok = slice(g4 * 512, (g4 + 1) * 512)
        ks = [keyf(c, g4) for c in range(4)]
        b.act(sq[:, 0:4, :], buf[:, :, tok], AF.Square, reads=ks, writes=SQK)
        pi = next_ps()
        rms_scale(SQK, 4, 512, lambda c: sq[:, c, :], pi, 1.0 / 512)
        for c in range(4):
            b.v("dve", "scalar_tensor_tensor", buf[:, c, tok], buf[:, c, tok], vec[:, l, gcol + c:gcol + c + 1], rstd[:],
                ALU.mult, ALU.mult, reads=[keyf(c, g4), "vec", "rstd"], writes=[keyf(c, g4)])

    def ssm_post(l):
        wglu = wb[1][:, 4096:6144].rearrange("p (k n) -> p k n", k=4)
        load_w(wglu, w_glu_d[l], ("wb", 1, "glu"))
        tK = A32(48, [128, 512])
        tM = A32(50, [128, 512])
        gates = A32(52, [128, 4, 512])
        for g4 in range(4):
            tok = slice(g4 * 512, (g4 + 1) * 512)
            for c in range(4):
                y = uT[:, c, tok]
                yk = ("uT", c, g4)
                b.v("dve", "tensor_tensor", tK, y, y, ALU.mult, reads=[yk], writes=["gK"])
                b.v("dve", "tensor_scalar", tK, tK, GB, 1.0, ALU.mult, ALU.add, reads=["gK"], writes=["gK"])
                b.v("dve", "tensor_tensor", tK, tK, y, ALU.mult, reads=["gK", yk], writes=["gK"])
                b.act(tM, tK, AF.Sigmoid, reads=["gK"], writes=["gM"], scale=GA)
                b.v("dve", "tensor_tensor", y, y, tM, ALU.mult, reads=["gM", yk], writes=[yk])
            for oc in range(4):
                pi = 4 + oc
                for k in range(4):
                    b.mm(psb[pi][:], wglu[:, k, oc * 128:(oc + 1) * 128], uT[:, k, tok], k == 0, k == 3,
                         reads=[("wb", 1, "glu"), ("uT", k, g4)], writes=[("ps", pi)])
                b.act(gates[:, oc, :], psb[pi][:], AF.Sigmoid, reads=[("ps", pi), "vec"], writes=[("gate", oc)],
                      bias=vec[:, l, 39 + oc:40 + oc], scale=1.0)
            for oc in range(4):
                b.v("dve", "tensor_tensor", uT[:, oc, tok], uT[:, oc, tok], gates[:, oc, :], ALU.mult,
                    reads=[("uT", oc, g4), ("gate", oc)], writes=[("uT", oc, g4)])
            group_norm(uT, lambda c, g: ("uT", c, g), 43, l, g4)
        dbg("yssmn", uT, [128, 4, T], [("uT", c, g) for c in range(4) for g in range(4)], BF16)

    def mix_out(l):
        wout = wb[0][:, :].rearrange("p (k n) -> p k n", k=8)
        load_w(wout, w_out_d[l], ("wb", 0))
        for g4 in range(4):
            tok = slice(g4 * 512, (g4 + 1) * 512)
            group_norm(ymla, lambda c, g: ("ymla", c, g), 47, l, g4)
            for oc in range(8):
                pi = next_ps()
                for k in range(8):
                    src = uT[:, k, tok] if k < 4 else ymla[:, k - 4, tok]
                    sk_ = ("uT", k, g4) if k < 4 else ("ymla", k - 4, g4)
                    b.mm(psb[pi][:], wout[:, k, oc * 128:(oc + 1) * 128], src, k == 0, k == 7,
                         reads=[("wb", 0), sk_], writes=[("ps", pi)])
                b.v("dve", "tensor_tensor", xT[:, oc, tok], xT[:, oc, tok], psb[pi][:], ALU.add,
                    reads=[("xT", oc, g4), ("ps", pi)], writes=[("xT", oc, g4)])
        dbg("x1", xT[:], [128, 8, T], [("xT", c, g) for c in range(8) for g in range(4)])

    CR_KEYS = ["mhT", "KmT", "Vm", "memx", "mtmp", "mss", "mkn", ("QmT", 0), ("QmT", 1), ("QmT", 2), ("QmT", 3)] + \
              [("omT", h, q) for h in range(4) for q in range(4)]

    def cross(l):
        mhT = A16(0, [128, 8, 256])
        KmT = A16(4, [128, 4, 256])
        Vm = A16(6, [128, 2, 4, 65])
        QmT = A16(8, [128, 4, T])
        omT = A16(24, [128, 4, T])
        memx = A32(40, [128, 8, 256])
        mtmp = A32(48, [128, 4, 64])
        mss = A32(49, [128, 4])
        mkn = A16(50, [128, 4, 64])
        mgq = rowb[:, 192:256]
        mgk = rowb[:, 256:320]
        mwkv = wb[1][:, 0:4096].rearrange("p (k n) -> p k n", k=8)
        mwq = wb[1][:, 4096:6144].rearrange("p (k n) -> p k n", k=8)
        load_w(mwkv, mwkv_d[l], ("wb", 1))
        load_w(mwq, mwq_d[l], ("wb", 1, "glu"))
        b.dma(memx, memT_d, writes=["memx"])
        sqm = sq[:, :, 0:256]
        b.act(sqm, memx, AF.Square, reads=["memx"], writes=SQK)
        pi = next_ps()
        rms_scale(SQK, 8, 256, lambda c: sq[:, c, 0:256], pi, 1.0 / D)
        for c in range(8):
            b.v("dve", "scalar_tensor_tensor", mhT[:, c, :], memx[:, c, :], vec[:, l, 24 + c:25 + c], rstd[:, 0:256],
                ALU.mult, ALU.mult, reads=["memx", "vec", "rstd"], writes=["mhT"])
        b.v("pool", "memset", Vm[:, :, :, 64:65], 1.0, writes=["Vm"])

        def head_norm(src, srck, gain, n_slots_bank, slot_fn):
            b.act(mtmp, src, AF.Square, reads=srck, writes=["mtmp"])
            b.v("dve", "tensor_reduce", mss, mtmp, AX.X, ALU.add, reads=["mtmp"], writes=["mss"])
            b.act(mss, mss, AF.Sqrt, reads=["mss", "cvec"], writes=["mss"], bias=epsv, scale=1.0 / 64)
            b.v("dve", "reciprocal", mss, mss, reads=["mss"], writes=["mss"])
            b.v("dve", "tensor_tensor", mtmp, src, mss.unsqueeze(2).to_broadcast([128, 4, 64]), ALU.mult,
                reads=srck + ["mss"], writes=["mtmp"])
            b.v("dve", "tensor_tensor", mkn, mtmp, gain.unsqueeze(1).to_broadcast([128, 4, 64]), ALU.mult,
                reads=["mtmp", "rowb"], writes=["mkn"])
            for h in range(4):
                sl = slot_fn(h)
                b.tr(psb_bf[n_slots_bank][0:64, sl * 128:(sl + 1) * 128], mkn[:, h, :], ident,
                     reads=["mkn", "cbf"], writes=[("ps", n_slots_bank)])

        for mt in range(2):
            pi = next_ps()
            for k in range(8):
                b.mm(psb[pi][:], mhT[:, k, mt * 128:(mt + 1) * 128], mwkv[:, k, :], k == 0, k == 7,
                     reads=["mhT", ("wb", 1)], writes=[("ps", pi)])
            kvv = psb[pi][:].rearrange("p (h d) -> p h d", h=4)
            b.act(Vm[:, mt, :, 0:64], kvv[:, :, 64:128], AF.Copy, reads=[("ps", pi)], writes=["Vm"])
            head_norm(kvv[:, :, 0:64], [("ps", pi)], mgk, 6, lambda h: h * 2 + mt)
        b.act(KmT[0:64, :, :], psb_bf[6][0:64, :].rearrange("p (h n) -> p h n", h=4), AF.Copy,
              reads=[("ps", 6)], writes=["KmT"])

        for hf in range(2):
            norm_x(l, 8, hf * 1024, 2, None)
            for t8 in range(8):
                t16 = hf * 8 + t8
                pi = next_ps()
                for k in range(8):
                    b.mm(psb[pi][:, 0:256], hT[:, k, t8 * 128:(t8 + 1) * 128], mwq[:, k, :], k == 0, k == 7,
                         reads=[("hT", t8 // 4), ("wb", 1, "glu")], writes=[("ps", pi)])
                head_norm(psb[pi][:, 0:256].rearrange("p (h d) -> p h d", h=4), [("ps", pi)], mgq, 7,
                          lambda h: h * 2 + (t16 % 2))
                if t16 % 2 == 1:
                    b.act(QmT[0:64, :, (t16 // 2) * 256:(t16 // 2 + 1) * 256],
                          psb_bf[7][0:64, :].rearrange("p (h n) -> p h n", h=4), AF.Copy,
                          reads=[("ps", 7)], writes=[("QmT", t16 // 4)])
        for h in range(4):
            for qt in range(4):
                po = 2 + (qt % 2)
                for kb in range(2):
                    pi = kb
                    pt = PT[kb]
                    b.mm(psb[pi][:], KmT[0:64, h, kb * 128:(kb + 1) * 128], QmT[0:64, h, qt * 512:(qt + 1) * 512], True, True,
                         reads=["KmT", ("QmT", qt)], writes=[("ps", pi)])
                    b.act(pt, psb[pi][:], AF.Exp, reads=[("ps", pi), "cvec"], writes=[("sq", kb)], bias=nmshift, scale=1.0)
                    b.mm(psb[po][0:65, :], Vm[:, kb, h, :], pt, kb == 0, kb == 1, reads=["Vm", ("sq", kb)], writes=[("ps", po)])
                b.v("dve", "reciprocal", rc[64:65, :], psb[po][64:65, :], reads=[("ps", po)], writes=["rc"])
                b.mm(psb[4][0:64, :], ones_f[64:65, 0:64], rc[64:65, :], True, True, reads=["cst", "rc"], writes=[("ps", 4)])
                b.act(rstd[0:64, :], psb[4][0:64, :], AF.Copy, reads=[("ps", 4)], writes=["rstd"])
                b.v("dve", "tensor_tensor", omT[0:64, h, qt * 512:(qt + 1) * 512], psb[po][0:64, :], rstd[0:64, :], ALU.mult,
                    reads=[("ps", po), "rstd"], writes=[("omT", h, qt)])
        mwo = wb[0][0:64, 0:4096].rearrange("p (h n) -> p h n", h=4)
        load_w(mwo, mwo_d[l], ("wb", 0))
        for g4 in range(4):
            tok = slice(g4 * 512, (g4 + 1) * 512)
            for oc in range(8):
                pi = next_ps()
                for h in range(4):
                    b.mm(psb[pi][:], mwo[:, h, oc * 128:(oc + 1) * 128], omT[0:64, h, tok], h == 0, h == 3,
                         reads=[("wb", 0), ("omT", h, g4)], writes=[("ps", pi)])
                b.v("dve", "tensor_tensor", xT[:, oc, tok], xT[:, oc, tok], psb[pi][:], ALU.add,
                    reads=[("xT", oc, g4), ("ps", pi)], writes=[("xT", oc, g4)])
        dbg("x2", xT[:], [128, 8, T], [("xT", c, g) for c in range(8) for g in range(4)])

    MLP_KEYS = [("hid", j, t) for j in range(32) for t in range(2)] + [("rt", 0), ("rt", 1)]

    def mlp(l):
        hid = A16(0, [128, 32, 1024])
        rt = [A16(64, [128, 512]), A16(65, [128, 512])]
        n = 0
        for hf in range(2):
            norm_x(l, 16, hf * 1024, 2, None)
            for jg in range(8):
                w1p = wb[jg % 2][:, 0:4096].rearrange("p (k n) -> p k n", k=8)
                load_w(w1p, w1_d[l][:, :, jg * 512:(jg + 1) * 512], ("wb", jg % 2))
                for jj in range(4):
                    j = jg * 4 + jj
                    for t in range(2):
                        pi = next_ps()
                        for k in range(8):
                            b.mm(psb[pi][:], w1p[:, k, jj * 128:(jj + 1) * 128], hT[:, k, t * 512:(t + 1) * 512], k == 0, k == 7,
                                 reads=[("wb", jg % 2), ("hT", t)], writes=[("ps", pi)])
                        r = rt[n % 2]
                        b.act(r, psb[pi][:], AF.Relu, reads=[("ps", pi)], writes=[("rt", n % 2)])
                        b.v("pool", "tensor_tensor", hid[:, j, t * 512:(t + 1) * 512], r, r, ALU.mult,
                            reads=[("rt", n % 2)], writes=[("hid", j, t)])
                        n += 1
            for oc in range(8):
                w2p = wb[oc % 2][:, 0:4096].rearrange("p (j n) -> p j n", j=32)
                load_w(w2p, w2_d[l][:, :, oc * 128:(oc + 1) * 128], ("wb", oc % 2))
                for t in range(2):
                    g4 = hf * 2 + t
                    tok = slice(g4 * 512, (g4 + 1) * 512)
                    pi = next_ps()
                    for j in range(32):
                        b.mm(psb[pi][:], w2p[:, j, :], hid[:, j, t * 512:(t + 1) * 512], j == 0, j == 31,
                             reads=[("wb", oc % 2), ("hid", j, t)], writes=[("ps", pi)])
                    b.v("dve", "tensor_tensor", xT[:, oc, tok], xT[:, oc, tok], psb[pi][:], ALU.add,
                        reads=[("xT", oc, g4), ("ps", pi)], writes=[("xT", oc, g4)])

    def run_layer(l, half=1, stop_after=None):
        rope_tables()
        layer(l)
        YMK = [("ymla", h, q) for h in range(4) for q in range(4)]
        barrier(PROJ_KEYS + SSM_KEYS + HK + YMK)
        ssm(l, half)
        if stop_after == "ssm":
            return
        barrier(SSM_KEYS + HK + YMK + ["gK", "gM"] + [("gate", i) for i in range(4)])
        ssm_post(l)
        if stop_after == "ssm_post":
            return
        barrier(SSM_KEYS + HK + YMK + ["gK", "gM"] + [("gate", i) for i in range(4)] + ATT_KEYS + SQK)
        mla(l, half)
        mix_out(l)
        if stop_after == "mix":
            return
        barrier(MIXER_KEYS + ATT_KEYS + CR_KEYS)
        cross(l)
        if stop_after == "cross":
            return
        barrier(CR_KEYS + MLP_KEYS)
        mlp(l)
        barrier(MLP_KEYS + MIXER_KEYS + PROJ_KEYS)

    def store_x(half):
        for c in range(8):
            b.dma(outT_d[half, :, c, :], xT[:, c, :], reads=[("xT", c, g) for g in range(4)], writes=[("dramout", "out")], key="out")

    def finish():
        b.p.add("sp", lambda e: e.nop(), reads=[("dramout", "out")] + [("dbgdone", k) for k in b.out_keys])

    b.ctx = dict(locals())
    return b


_CACHE = {}
N_LAYERS = 4


def _program():
    if "b" not in _CACHE:
        b = build_program(N_LAYERS)
        c = b.ctx
        for half in range(2):
            c["load_x"](half)
            for l in range(N_LAYERS):
                c["run_layer"](l, half)
            c["store_x"](half)
        c["finish"]()
        b.p.emit(b.nc, b.es)
        _CACHE["b"] = b
    return _CACHE["b"]


def kernel(**inputs):
    inp = {k: np.asarray(v) for k, v in inputs.items()}
    B = inp["x"].shape[0]
    b = _program()
    cst = host_constants()
    W = prep_weights(inp, range(N_LAYERS))
    in_maps = []
    for core in range(8):
        bb = core % B
        xs = np.asarray(inp["x"][bb], np.float32)
        xT = np.stack([chunkT(np.ascontiguousarray(xs[hf * T:(hf + 1) * T].T), 8) for hf in range(2)])
        pos = np.asarray(inp["positions"][bb]).astype(np.float32)
        m = {"xT": xT,
             "memT": chunkT(np.ascontiguousarray(np.asarray(inp["mem"][bb], np.float32).T), 8),
             "pos": np.stack([np.ascontiguousarray(pos[hf * T:(hf + 1) * T].reshape(16, 128).T) for hf in range(2)]),
             "cst": cst}
        m.update(W)
        in_maps.append(m)
    res = run_bass_kernel_spmd(b.nc, in_maps, core_ids=list(range(8)))
    out = np.zeros((B, 2 * T, D), np.float32)
    for bb in range(B):
        o = np.asarray(res.results[bb]["outT"], np.float32)
        for hf in range(2):
            out[bb, hf * T:(hf + 1) * T] = o[hf].transpose(2, 1, 0).reshape(T, D)
    return out
```

```python
import numpy as np
from contextlib import ExitStack
import concourse.bass as bass
import concourse.mybir as mybir
from concourse.bass_utils import run_bass_kernel_spmd

F32 = mybir.dt.float32
BF16 = mybir.dt.bfloat16
I32 = mybir.dt.int32
AF = mybir.ActivationFunctionType
ALU = mybir.AluOpType
AX = mybir.AxisListType

ENGS = ("pe", "act", "dve", "pool", "sp")


class Op:
    __slots__ = ("eng", "fn", "reads", "writes", "dma_key", "idx", "deps", "signal",
                 "tick", "waits")

    def __init__(self, eng, fn, reads, writes, dma_key):
        self.eng = eng
        self.fn = fn
        self.reads = reads
        self.writes = writes
        self.dma_key = dma_key
        self.deps = []
        self.signal = False
        self.tick = None
        self.waits = []


class Prog:
    def __init__(self):
        self.ops = []

    def add(self, eng, fn, reads=(), writes=(), dma_key=None):
        op = Op(eng, fn, tuple(reads), tuple(writes), dma_key)
        op.idx = len(self.ops)
        self.ops.append(op)
        return op

    def analyse(self):
        last_w = {}
        readers = {}
        for op in self.ops:
            deps = set()
            for k in op.reads:
                w = last_w.get(k)
                if w is not None:
                    deps.add(w.idx)
            for k in op.writes:
                w = last_w.get(k)
                if w is not None:
                    deps.add(w.idx)
                for r in readers.get(k, {}).values():
                    deps.add(r.idx)
            deps.discard(op.idx)
            keep = []
            for d in deps:
                a = self.ops[d]
                if a.dma_key is None and op.dma_key is None and a.eng == op.eng:
                    if a.eng == "pe":
                        continue
                    raw = any(k in a.writes for k in op.reads)
                    if not raw:
                        continue
                keep.append(d)
            op.deps = keep
            for d in keep:
                self.ops[d].signal = True
            for k in op.writes:
                last_w[k] = op
                readers[k] = {}
            for k in op.reads:
                rk = op.dma_key if op.dma_key is not None else op.eng
                readers.setdefault(k, {})[rk] = op
        cnt = {e: 0 for e in ENGS}
        dcnt = {}
        for op in self.ops:
            if op.dma_key is not None:
                dcnt[op.dma_key] = dcnt.get(op.dma_key, 0) + 1
                op.tick = ("dma", op.dma_key, dcnt[op.dma_key] * 16)
            elif op.signal:
                cnt[op.eng] += 1
                op.tick = ("eng", op.eng, cnt[op.eng])
        waited = {e: {} for e in ENGS}
        dma_issued = {}
        for op in self.ops:
            need = {}
            for d in op.deps:
                a = self.ops[d]
                t = a.tick
                if t[0] == "dma":
                    v = dma_issued.get(t[1], 0) * 16
                    v = max(v, t[2])
                else:
                    v = t[2]
                sk = (t[0], t[1])
                if v > need.get(sk, 0):
                    need[sk] = v
            for sk, v in need.items():
                if waited[op.eng].get(sk, 0) >= v:
                    continue
                waited[op.eng][sk] = v
                op.waits.append((sk, v))
            if op.dma_key is not None:
                dma_issued[op.dma_key] = dma_issued.get(op.dma_key, 0) + 1
        self.n_eng_sems = cnt
        self.dma_keys = list(dcnt.keys())

    def emit(self, nc, es):
        self.analyse()
        sems = {}
        for e in ENGS:
            sems[("eng", e)] = es.enter_context(nc.semaphore("s_" + e))
        for i, k in enumerate(self.dma_keys):
            sems[("dma", k)] = es.enter_context(nc.semaphore("d%d" % i))
        self.sems = sems
        per = {e: [op for op in self.ops if op.eng == e] for e in ENGS}
        block = es.enter_context(nc.Block())

        def run(engine, lst):
            for op in lst:
                for sk, v in op.waits:
                    engine.wait_ge(sems[sk], v)
                ins = op.fn(engine)
                if op.dma_key is not None:
                    ins.then_inc(sems[("dma", op.dma_key)], 16)
                elif op.signal:
                    ins.then_inc(sems[("eng", op.eng)], 1)

        @block.tensor
        def _(e):
            run(e, per["pe"])

        @block.scalar
        def _(e):
            run(e, per["act"])

        @block.vector
        def _(e):
            run(e, per["dve"])

        @block.gpsimd
        def _(e):
            run(e, per["pool"])

        @block.sync
        def _(e):
            run(e, per["sp"])


D = 1024
T = 2048
NT4 = T // 512
NT1 = T // 128
KC = D // 128
IN_COLS = 928
EPS = 1e-6
PI = float(np.pi)
SHIFT = 10.0


class Builder:
    def __init__(self, nc, es, n_layers, debug=()):
        self.nc = nc
        self.es = es
        self.p = Prog()
        self.L = n_layers
        self.debug = set(debug)
        self.dram_in = {}
        self.dram_out = {}
        self._uid = 0
        self.out_keys = []

    def din(self, name, shape, dt=F32):
        t = self.nc.dram_tensor(name, list(shape), dt, kind="ExternalInput")
        self.dram_in[name] = t
        return t

    def dout(self, name, shape, dt=F32):
        t = self.nc.dram_tensor(name, list(shape), dt, kind="ExternalOutput")
        self.dram_out[name] = t
        return t

    def sb(self, name, shape, dt=F32):
        return self.es.enter_context(self.nc.sbuf_tensor(name, list(shape), dt))

    def ps(self, name, shape, dt=F32):
        return self.es.enter_context(self.nc.psum_tensor(name, list(shape), dt))

    def dma(self, out, in_, reads=(), writes=(), key=None, eng="sp", **kw):
        writes = list(writes)
        if key is None:
            self._uid += 1
            key = "a%d_%s" % (self._uid % 12, eng)
            writes.append(("dmakey", key))
        return self.p.add(eng, lambda e, o=out, i=in_, kw=kw: e.dma_start(out=o, in_=i, **kw),
                          reads=reads, writes=writes, dma_key=key)

    def mm(self, out, lhsT, rhs, start, stop, reads=(), writes=()):
        return self.p.add("pe", lambda e: e.matmul(out, lhsT, rhs, start=start, stop=stop),
                          reads=reads, writes=writes)

    def tr(self, out, in_, ident, reads=(), writes=()):
        return self.p.add("pe", lambda e: e.transpose(out, in_, ident), reads=reads, writes=writes)

    def act(self, out, in_, func, reads=(), writes=(), eng="act", **kw):
        return self.p.add("act", lambda e: e.activation(out, in_, func, **kw), reads=reads, writes=writes)

    def v(self, eng, name, *args, reads=(), writes=(), **kw):
        return self.p.add(eng, lambda e: getattr(e, name)(*args, **kw), reads=reads, writes=writes)


def host_constants():
    c = np.zeros((128, 1664), np.float32)
    c[:, 0:128] = np.eye(128)
    c[:, 128:256] = 1.0
    r = np.arange(128)
    c[:, 256:384] = (r[:, None] <= r[None, :])
    c[127, 384:512] = 1.0
    c[127, 512:640] = -1.0
    m = np.arange(896)
    c[:, 640:1536] = (m[None, :] - r[:, None] >= 384)
    c[:, 1536:1552] = (10000.0 ** (-np.arange(16, dtype=np.float32) / 16))[None, :]
    c[:, 1552] = r + 1
    c[:, 1553] = -(r + 1)
    return c


def chunkT(v, k):
    return np.ascontiguousarray(v.reshape((k, 128) + v.shape[1:]).swapaxes(0, 1))


def prep_weights(inp, layers):
    f = lambda a: np.asarray(a, np.float32)
    W = {}
    Ls = list(layers)
    n = len(Ls)
    vec = np.zeros((n, 128, 64), np.float32)
    row = np.zeros((n, 320), np.float32)
    bm = np.zeros((n, 4, 128, 1024), np.float32)
    cm = np.zeros((n, 4, 128, 8, 128), np.float32)
    lam = np.zeros((n, 4, 3, 512), np.float32)
    for i, l in enumerate(Ls):
        vec[i, :, 0:8] = f(inp["norm_mix"][l]).reshape(8, 128).T
        vec[i, :, 8:16] = f(inp["norm_mem_q"][l]).reshape(8, 128).T
        vec[i, :, 16:24] = f(inp["norm_mlp"][l]).reshape(8, 128).T
        vec[i, :, 24:32] = f(inp["norm_mem_kv"][l]).reshape(8, 128).T
        vec[i, :, 32:34] = f(inp["mla_q_norm"][l]).reshape(2, 128).T
        vec[i, :, 34] = f(inp["mla_kv_norm"][l])
        vec[i, :, 35:39] = f(inp["ssm_d"][l]).reshape(4, 128).T
        vec[i, :, 39:43] = f(inp["ssm_b_glu"][l]).reshape(4, 128).T
        vec[i, :, 43:47] = f(inp["out_norm_ssm"][l]).reshape(4, 128).T
        vec[i, :, 47:51] = f(inp["out_norm_mla"][l]).reshape(4, 128).T
        row[i, 0:96] = f(inp["mla_q_gain"][l])
        row[i, 96:192] = f(inp["mla_k_gain"][l])
        row[i, 192:256] = f(inp["mem_q_gain"][l])
        row[i, 256:320] = f(inp["mem_k_gain"][l])
        bre = f(inp["ssm_b_re"][l])
        bim = f(inp["ssm_b_im"][l])
        cre = f(inp["ssm_c_re"][l])
        cim = f(inp["ssm_c_im"][l])
        lre = f(inp["ssm_lambda_re"][l])
        lim = f(inp["ssm_lambda_im"][l])
        lst = f(inp["ssm_log_step"][l])
        for c in range(4):
            for j in range(8):
                g = 8 * c + j
                bm[i, c, 16 * j:16 * j + 16, j * 64:(j + 1) * 64] = bre[g].T
                bm[i, c, 16 * j:16 * j + 16, 512 + j * 64:512 + (j + 1) * 64] = bim[g].T
                pr, e = j // 2, j % 2
                cm[i, c, e * 64:(e + 1) * 64, pr, 16 * j:16 * j + 16] = cre[g].T
                cm[i, c, e * 64:(e + 1) * 64, 4 + pr, 16 * j:16 * j + 16] = cim[g].T
                lam[i, c, 0, j * 64:(j + 1) * 64] = lre[g]
                lam[i, c, 1, j * 64:(j + 1) * 64] = lim[g]
                lam[i, c, 2, j * 64:(j + 1) * 64] = lst[g]
    W["vec"] = vec
    W["row"] = row
    W["bm"] = bm
    W["cm"] = cm
    W["lam"] = lam
    st = lambda name, k: np.stack([chunkT(f(inp[name][l]), k) for l in Ls])
    W["w_in"] = st("w_in", 8)
    W["w_uq"] = st("mla_w_uq", 2)
    W["w_ukv"] = np.stack([f(inp["mla_w_ukv"][l]) for l in Ls])
    W["w_glu"] = st("ssm_w_glu", 4)
    W["w_out"] = st("w_out", 8)
    W["mwq"] = st("mem_w_q", 8)
    W["mwkv"] = st("mem_w_kv", 8)
    W["mwo"] = np.stack([np.ascontiguousarray(f(inp["mem_w_o"][l]).reshape(4, 64, 1024).swapaxes(0, 1)) for l in Ls])
    W["w1"] = st("mlp_w1", 8)
    W["w2"] = st("mlp_w2", 32)
    return W


TWO_PI = 2.0 * PI
GA = 1.5957691216057308
GB = 0.044715
QSCALE = 1.0 / float(np.sqrt(96.0))
MSCALE = 1.0 / 8.0
MSHIFT = 8.0
NEG_BIG = -30000.0


def build_program(n_layers=1, debug=()):
    nc = bass.Bass("TRN2", target_bir_lowering=False)
    es = ExitStack()
    b = Builder(nc, es, n_layers, debug)
    P = b.p
    L = n_layers

    xT_d = b.din("xT", [2, 128, 8, T]).ap()
    memT_d = b.din("memT", [128, 8, 256]).ap()
    pos_d = b.din("pos", [2, 128, 16]).ap()
    hinit_d = nc.dram_tensor("sc_h", [L, 4, 1024], BF16, kind="Internal").ap()
    pkt_d = nc.dram_tensor("sc_kt", [L, 4, 96, 2, T], BF16, kind="Internal").ap()
    pve_d = nc.dram_tensor("sc_ve", [L, 4, 128, 16, 65], BF16, kind="Internal").ap()
    pvo_d = nc.dram_tensor("sc_vo", [L, 4, 128, 16, 128], BF16, kind="Internal").ap()
    cst_d = b.din("cst", [128, 1664]).ap()
    vec_d = b.din("vec", [L, 128, 64]).ap()
    row_d = b.din("row", [L, 192 + 64 + 64]).ap()
    w_in_d = b.din("w_in", [L, 128, 8, IN_COLS]).ap()
    w_uq_d = b.din("w_uq", [L, 128, 2, 768]).ap()
    w_ukv_d = b.din("w_ukv", [L, 128, 1024]).ap()
    w_glu_d = b.din("w_glu", [L, 128, 4, 512]).ap()
    w_out_d = b.din("w_out", [L, 128, 8, 1024]).ap()
    mwq_d = b.din("mwq", [L, 128, 8, 256]).ap()
    mwkv_d = b.din("mwkv", [L, 128, 8, 512]).ap()
    mwo_d = b.din("mwo", [L, 64, 4, 1024]).ap()
    w1_d = b.din("w1", [L, 128, 8, 4096]).ap()
    w2_d = b.din("w2", [L, 128, 32, 1024]).ap()
    bm_d = b.din("bm", [L, 4, 128, 1024]).ap()
    cm_d = b.din("cm", [L, 4, 128, 8, 128]).ap()
    lam_d = b.din("lam", [L, 4, 3, 512]).ap()
    outT_d = b.dout("outT", [2, 128, 8, T]).ap()
    okt_d, ove_d, ovo_d, hfin_d = pkt_d, pve_d, pvo_d, hinit_d
    dbg_d = {}

    xT = b.sb("xT_s", [128, 8, T])
    hT = b.sb("hT_s", [128, 8, 1024], BF16)
    wb = [b.sb("wb%d" % i, [128, 8192], BF16) for i in range(2)]
    cst = b.sb("cst_s", [128, 160])
    cbf = b.sb("cbf_s", [128, 1536], BF16)
    vec = b.sb("vec_s", [128, L, 64])
    rowb = b.sb("rowb_s", [128, 320])
    sq = b.sb("sq_s", [128, 8, 512], BF16)
    rstd = b.sb("rstd_s", [128, 512])
    posf = b.sb("pos_s", [128, 16])
    cvec = b.sb("cvec_s", [128, 8])
    AR = 37888
    arena = b.sb("arena", [128, AR], BF16)
    arena_f = arena.bitcast(F32)

    psp = [b.ps("psp%d" % i, [128, 1024]) for i in range(4)]
    psb = [psp[i // 2][:, (i % 2) * 512:(i % 2 + 1) * 512] for i in range(8)]
    psb_bf = [a_.bitcast(BF16) for a_ in psb]

    ident = cbf[:, 0:128]
    ones = cbf[:, 128:256]
    tri = cbf[:, 256:384]
    e127 = cbf[:, 384:512]
    e127n = cbf[:, 512:640]
    cmask = cbf[:, 640:1536]
    ones_f = cst[:, 0:128]
    invf = cst[:, 128:144]
    sp1 = cst[:, 144:145]
    nsp1 = cst[:, 145:146]

    def A16(off_kib, shape):
        n = int(np.prod(shape[1:]))
        o = int(round(off_kib * 512))
        assert o + n <= AR, (off_kib, shape)
        v = arena[:, o:o + n]
        if len(shape) == 3:
            v = v.rearrange("p (a b) -> p a b", a=shape[1])
        elif len(shape) == 4:
            v = v.rearrange("p (a b c) -> p a b c", a=shape[1], b=shape[2])
        return v

    def A32(off_kib, shape):
        n = int(np.prod(shape[1:]))
        o = int(round(off_kib * 256))
        assert (o + n) * 2 <= AR, (off_kib, shape)
        v = arena_f[:, o:o + n]
        if len(shape) == 3:
            v = v.rearrange("p (a b) -> p a b", a=shape[1])
        elif len(shape) == 4:
            v = v.rearrange("p (a b c) -> p a b c", a=shape[1], b=shape[2])
        return v

    uT = A16(0, [128, 4, T])
    cqn = A16(16, [128, 2, T])
    ckvn = A16(24, [128, T])
    krope = A32(28, [128, 16, 32])
    cossin = A32(30, [128, 2, 16, 16])
    ymla = A16(32, [128, 4, T])
    WK = 48

    b.dma(cst[:, 0:128], cst_d[:, 128:256], writes=["cst"])
    b.dma(cst[:, 128:160], cst_d[:, 1536:1568], writes=["cst"])
    b.dma(cbf[:], cst_d[:, 0:1536], writes=["cbf"], eng="pool")
    b.dma(vec[:], vec_d.rearrange("l p k -> p l k"), writes=["vec"])
    def load_x(half):
        b.dma(posf[:], pos_d[half], writes=["pos"])
        for c in range(8):
            b.dma(xT[:, c, :], xT_d[half, :, c, :], writes=[("xT", c, t4) for t4 in range(4)])
    b.v("pool", "memset", cvec[:, 0:1], -SHIFT, writes=["cvec"])
    b.v("pool", "memset", cvec[:, 1:2], -MSHIFT, writes=["cvec"])
    b.v("pool", "memset", cvec[:, 2:3], -PI, writes=["cvec"])
    b.v("pool", "memset", cvec[:, 3:4], EPS, writes=["cvec"])
    nshift = cvec[:, 0:1]
    nmshift = cvec[:, 1:2]
    npi = cvec[:, 2:3]
    epsv = cvec[:, 3:4]

    cos_t = cossin[:, 0]
    sin_t = cossin[:, 1]

    def wrap_pi(dst, src, ti, tf, tm, rk, wk):
        b.v("dve", "tensor_scalar_mul", ti, src, 1.0 / TWO_PI, reads=rk, writes=[wk + "_ti"])
        b.v("dve", "tensor_copy", tf, ti, reads=[wk + "_ti"], writes=[wk + "_tf"])
        b.v("dve", "scalar_tensor_tensor", dst, tf, -TWO_PI, src, ALU.mult, ALU.add, reads=[wk + "_tf"] + rk, writes=[wk])
        b.v("dve", "tensor_scalar", tm, dst, PI, -TWO_PI, ALU.is_gt, ALU.mult, reads=[wk], writes=[wk + "_tm"])
        b.v("dve", "tensor_tensor", dst, dst, tm, ALU.add, reads=[wk, wk + "_tm"], writes=[wk])
        b.v("dve", "tensor_scalar", tm, dst, -PI, TWO_PI, ALU.is_lt, ALU.mult, reads=[wk], writes=[wk + "_tm"])
        b.v("dve", "tensor_tensor", dst, dst, tm, ALU.add, reads=[wk, wk + "_tm"], writes=[wk])

    def rope_tables():
        ang = A32(WK, [128, 16, 16])
        ang2 = A32(WK + 1, [128, 16, 16])
        ti = A32(WK + 2, [128, 16, 16]).bitcast(I32)
        tf = A32(WK + 3, [128, 16, 16])
        tm = A32(WK + 4, [128, 16, 16])
        wr = A32(WK + 5, [128, 16, 16])
        for t in range(16):
            b.v("dve", "tensor_scalar_mul", ang[:, t, :], invf, posf[:, t:t + 1],
                reads=["cst", "pos"], writes=["ropeang"])
        wrap_pi(wr, ang, ti, tf, tm, ["ropeang"], "ropew")
        b.act(sin_t, wr, AF.Sin, reads=["ropew"], writes=["sin"])
        b.v("dve", "tensor_scalar_add", ang2, ang, PI / 2, reads=["ropeang"], writes=["ropeang2"])
        wrap_pi(wr, ang2, ti, tf, tm, ["ropeang2", "sin"], "ropew")
        b.act(cos_t, wr, AF.Sin, reads=["ropew"], writes=["cos"])

    state = {"ps": 0, "wb": 0}
    SQK = [("sq", i) for i in range(8)]

    def next_ps():
        i = state["ps"]
        state["ps"] = (i + 1) % 4
        return i

    def load_w(view, src, key, eng="pool"):
        return b.dma(view, src, writes=[key], key=("w", key), eng=eng)

    def rms_scale(src_keys, nchunk, n, src_fn, ps_i, inv_dim):
        for c in range(nchunk):
            b.mm(psb[ps_i][:, 0:n], ones, src_fn(c), c == 0, c == nchunk - 1,
                 reads=["cbf"] + src_keys, writes=[("ps", ps_i)])
        b.act(rstd[:, 0:n], psb[ps_i][:, 0:n], AF.Sqrt, reads=[("ps", ps_i), "cvec"], writes=["rstd"],
              bias=epsv, scale=inv_dim)
        b.v("dve", "reciprocal", rstd[:, 0:n], rstd[:, 0:n], reads=["rstd"], writes=["rstd"])

    def norm_x(l, gain_col, tok0, nt, hkeys):
        for t in range(nt):
            g4 = (tok0 // 512) + t
            xs = xT[:, :, g4 * 512:(g4 + 1) * 512]
            xk = [("xT", c, g4) for c in range(8)]
            b.act(sq[:], xs, AF.Square, reads=xk, writes=SQK)
            pi = next_ps()
            rms_scale(SQK, 8, 512, lambda c: sq[:, c, :], pi, 1.0 / D)
            for c in range(8):
                b.v("dve", "scalar_tensor_tensor", hT[:, c, t * 512:(t + 1) * 512], xT[:, c, g4 * 512:(g4 + 1) * 512],
                    vec[:, l, gain_col + c:gain_col + c + 1], rstd[:], ALU.mult, ALU.mult,
                    reads=[("xT", c, g4), "vec", "rstd"], writes=[("hT", t)])

    def dbg(name, view, shape, reads, dt=F32):
        if name in b.debug:
            d = b.dout("dbg_" + name, shape, dt).ap()
            b.dma(d, view, reads=reads, writes=[("dbgdone", "dbg_" + name)], key="dbg_" + name)
            b.out_keys.append("dbg_" + name)

    def layer(l):
        V0 = lambda c: vec[:, l, c:c + 1]
        b.dma(rowb[:], row_d[l].partition_broadcast(128), writes=["rowb"])
        gq = rowb[:, 0:96]
        gk = rowb[:, 96:192]
        mgq = rowb[:, 192:256]
        mgk = rowb[:, 256:320]
        b.v("dve", "tensor_scalar_mul", gq, gq, QSCALE, reads=["rowb"], writes=["rowb"])
        b.v("dve", "tensor_scalar_mul", mgq, mgq, MSCALE, reads=["rowb"], writes=["rowb"])

        w_in = wb[0][:, 0:8 * IN_COLS].rearrange("p (k n) -> p k n", k=8)
        load_w(w_in, w_in_d[l], ("wb", 0))
        cqraw = A32(WK, [128, 3, 512])
        for hf in range(2):
            norm_x(l, 0, hf * 1024, 2, None)
            for t in range(2):
                g4 = hf * 2 + t
                tok = slice(g4 * 512, (g4 + 1) * 512)
                hk = [("hT", t)]
                for oc in range(7):
                    pi = next_ps()
                    for k in range(8):
                        b.mm(psb[pi][:], w_in[:, k, oc * 128:(oc + 1) * 128], hT[:, k, t * 512:(t + 1) * 512],
                             k == 0, k == 7, reads=[("wb", 0)] + hk, writes=[("ps", pi)])
                    if oc < 4:
                        b.act(uT[:, oc, tok], psb[pi][:], AF.Copy, reads=[("ps", pi)], writes=[("uT", oc, g4)])
                    else:
                        b.act(cqraw[:, oc - 4, :], psb[pi][:], AF.Copy, reads=[("ps", pi)], writes=[("cqraw", oc - 4)])
                b.act(sq[:, 0:2, :], cqraw[:, 0:2, :], AF.Square, reads=[("cqraw", 0), ("cqraw", 1)], writes=SQK)
                pi = next_ps()
                rms_scale(SQK, 2, 512, lambda c: sq[:, c, :], pi, 1.0 / 256)
                for c in range(2):
                    b.v("dve", "scalar_tensor_tensor", cqn[:, c, tok], cqraw[:, c, :], V0(32 + c), rstd[:],
                        ALU.mult, ALU.mult, reads=[("cqraw", c), "vec", "rstd"], writes=[("cqn", g4)])
                b.act(sq[:, 2, :], cqraw[:, 2, :], AF.Square, reads=[("cqraw", 2)], writes=SQK)
                pi = next_ps()
                rms_scale(SQK, 1, 512, lambda c: sq[:, 2, :], pi, 1.0 / 128)
                b.v("dve", "scalar_tensor_tensor", ckvn[:, tok], cqraw[:, 2, :], V0(34), rstd[:],
                    ALU.mult, ALU.mult, reads=[("cqraw", 2), "vec", "rstd"], writes=[("ckvn", g4)])
                for s4 in range(4):
                    t16 = g4 * 4 + s4
                    pi = next_ps()
                    for k in range(8):
                        b.mm(psb[pi][:, 0:32], hT[:, k, t * 512 + s4 * 128:t * 512 + (s4 + 1) * 128], w_in[:, k, 896:928],
                             k == 0, k == 7, reads=[("wb", 0)] + hk, writes=[("ps", pi)])
                    b.act(krope[:, t16, :], psb[pi][:, 0:32], AF.Copy, reads=[("ps", pi)], writes=[("krope", t16)])
        dbg("uT", uT, [128, 4, T], [("uT", c, g) for c in range(4) for g in range(4)], BF16)
        dbg("cqn", cqn, [128, 2, T], [("cqn", g) for g in range(4)], BF16)
        dbg("ckvn", ckvn, [128, T], [("ckvn", g) for g in range(4)], BF16)
        dbg("krope", krope, [128, 16, 32], [("krope", g) for g in range(16)])
        dbg("cos", cos_t, [128, 16, 16], ["cos"])
        dbg("sin", sin_t, [128, 16, 16], ["sin"])


    hT_flat = hT[:].rearrange("p a b -> p (a b)")
    pkt_s = hT_flat[:, 0:4096].rearrange("p (e t) -> p e t", e=2)
    pve_s = hT_flat[:, 4096:4096 + 1040].rearrange("p (t c) -> p t c", t=16)
    pvo_s = hT_flat[:, 5248:5248 + 2048].rearrange("p (t c) -> p t c", t=16)
    HK = [("hT", 0), ("hT", 1)]
    QT = A16(48, [128, 2, T])
    KT = A16(56, [128, 2, T])
    Ve = A16(64, [128, 16, 65])
    Vo = A16(66.5, [128, 16, 128])
    sq_f = sq[:].rearrange("p a b -> p (a b)").bitcast(F32)
    sqq = sq_f[:, 0:768].rearrange("p (g e d) -> p g e d", g=4, e=2)
    qn = sq_f[:, 768:1536].rearrange("p (g e d) -> p g e d", g=4, e=2)
    SQ6 = [("sq", i) for i in range(6)]
    kraw = wb[1].bitcast(F32)[:, 2048:2816].rearrange("p (g e d) -> p g e d", g=4, e=2)
    KRK = ("wb", 1, "glu")
    qf = A16(70.5, [128, 4, 192]).rearrange("p g (e d) -> p g e d", e=2)
    rta = A32(72, [128, 4, 32]).rearrange("p g (e d) -> p g e d", e=2)
    rtb = A32(72.5, [128, 4, 32]).rearrange("p g (e d) -> p g e d", e=2)
    ssq = A32(73, [128, 4, 2])
    wb1f = wb[1].bitcast(F32)
    rc = wb1f[:, 3584:4096]
    PT = [sq[:, i, :] for i in range(8)]

    def qk_post(src, gain, dstT, ps_bank, g4, srck, tag):
        b.act(sqq, src, AF.Square, reads=srck, writes=SQ6)
        b.v("dve", "tensor_reduce", ssq, sqq, AX.X, ALU.add, reads=SQ6, writes=["ssq"])
        b.act(ssq, ssq, AF.Sqrt, reads=["ssq", "cvec"], writes=["ssq"], bias=epsv, scale=1.0 / 96)
        b.v("dve", "reciprocal", ssq, ssq, reads=["ssq"], writes=["ssq"])
        b.v("dve", "tensor_tensor", qn, src, ssq.unsqueeze(3).to_broadcast([128, 4, 2, 96]), ALU.mult,
            reads=srck + ["ssq"], writes=SQ6)
        b.v("dve", "tensor_tensor", qn, qn, gain.unsqueeze(1).unsqueeze(1).to_broadcast([128, 4, 2, 96]), ALU.mult,
            reads=SQ6 + ["rowb"], writes=SQ6)
        cs = cos_t[:, g4 * 4:(g4 + 1) * 4, :].unsqueeze(2).to_broadcast([128, 4, 2, 16])
        sn = sin_t[:, g4 * 4:(g4 + 1) * 4, :].unsqueeze(2).to_broadcast([128, 4, 2, 16])
        b.v("pool", "tensor_copy", qf[:, :, :, 0:64], qn[:, :, :, 0:64], reads=SQ6, writes=["qf"])
        b.v("pool", "tensor_tensor", rta, qn[:, :, :, 64:80], cs, ALU.mult, reads=SQ6 + ["cos"], writes=["rta"])
        b.v("pool", "tensor_tensor", rtb, qn[:, :, :, 80:96], sn, ALU.mult, reads=SQ6 + ["sin"], writes=["rtb"])
        b.v("pool", "tensor_tensor", qf[:, :, :, 64:80], rta, rtb, ALU.subtract, reads=["rta", "rtb"], writes=["qf"])
        b.v("pool", "tensor_tensor", rta, qn[:, :, :, 80:96], cs, ALU.mult, reads=SQ6 + ["cos", "qf"], writes=["rta"])
        b.v("pool", "tensor_tensor", rtb, qn[:, :, :, 64:80], sn, ALU.mult, reads=SQ6 + ["sin", "qf"], writes=["rtb"])
        b.v("pool", "tensor_tensor", qf[:, :, :, 80:96], rta, rtb, ALU.add, reads=["rta", "rtb"], writes=["qf"])
        for i in range(4):
            for e in range(2):
                slot = e * 4 + i
                b.tr(psb_bf[ps_bank][0:96, slot * 128:(slot + 1) * 128], qf[:, i, e, :], ident,
                     reads=["qf", "cbf"], writes=[("ps", ps_bank)])
        b.act(dstT[0:96, :, g4 * 512:(g4 + 1) * 512],
              psb_bf[ps_bank][0:96, :].rearrange("p (e n) -> p e n", e=2), AF.Copy,
              reads=[("ps", ps_bank)], writes=[(tag, g4)])

    def mla(l, half=1):
        wuq = wb[1][:, 0:1536].rearrange("p (k n) -> p k n", k=2)
        wukv = wb[1][:, 1536:2560]
        load_w(wuq, w_uq_d[l], ("wb", 1))
        load_w(wukv, w_ukv_d[l], ("wb", 1))
        gq = rowb[:, 0:96]
        gk = rowb[:, 96:192]
        b.v("pool", "memset", Ve[:, :, 64:65], 1.0, writes=["Ve"])
        b.v("pool", "memset", Vo[:, :, 0:64], 0.0, writes=["Vo"])
        b.v("pool", "memset", Vo[:, :, 0:1], 1.0, writes=["Vo"])
        for hp in range(4):
            if half == 1:
                b.dma(pkt_s[0:96], pkt_d[l, hp], reads=[("skv", l, hp)], writes=HK, key="pk")
                b.dma(pve_s, pve_d[l, hp], reads=[("skv", l, hp)], writes=HK, key="pk")
                b.dma(pvo_s, pvo_d[l, hp], reads=[("skv", l, hp)], writes=HK, key="pk")
            qps4 = psp[0][:, :].rearrange("p (g n) -> p g n", g=4)
            kvps4 = psp[1][:, :].rearrange("p (g n) -> p g n", g=4)
            PQ = [("ps", 0), ("ps", 1)]
            PKV = [("ps", 2), ("ps", 3)]
            for g4 in range(4):
                for i in range(4):
                    t16 = g4 * 4 + i
                    tok = slice(t16 * 128, (t16 + 1) * 128)
                    for k in range(2):
                        b.mm(qps4[:, i, 0:192], cqn[:, k, tok], wuq[:, k, hp * 192:(hp + 1) * 192], k == 0, k == 1,
                             reads=[("cqn", g4), ("wb", 1)], writes=[("ps", i // 2)])
                    b.mm(kvps4[:, i, :], ckvn[:, tok], wukv[:, hp * 256:(hp + 1) * 256], True, True,
                         reads=[("ckvn", g4), ("wb", 1)], writes=[("ps", 2 + i // 2)])
                qk_post(qps4[:, :, 0:192].rearrange("p g (e d) -> p g e d", e=2), gq, QT, 6, g4, PQ, "QT")
                kv4 = kvps4.rearrange("p g (e d) -> p g e d", e=2)
                b.act(kraw[:, :, :, 0:64], kv4[:, :, :, 0:64], AF.Copy, reads=PKV, writes=[KRK])
                b.v("pool", "tensor_copy", kraw[:, :, :, 64:96],
                    krope[:, g4 * 4:(g4 + 1) * 4, :].unsqueeze(2).to_broadcast([128, 4, 2, 32]),
                    reads=[("krope", g4 * 4 + i) for i in range(4)], writes=[KRK])
                b.act(Ve[:, g4 * 4:(g4 + 1) * 4, 0:64], kvps4[:, :, 64:128], AF.Copy, reads=PKV, writes=["Ve"])
                b.act(Vo[:, g4 * 4:(g4 + 1) * 4, 64:128], kvps4[:, :, 192:256], AF.Copy, reads=PKV, writes=["Vo"])
                qk_post(kraw, gk, KT, 7, g4, [KRK], "KT")
            QK = [("QT", g) for g in range(4)]
            KK = [("KT", g) for g in range(4)]
            if half == 0:
                b.dma(okt_d[l, hp], KT[0:96], reads=KK, writes=[("skv", l, hp)], key="okv")
                b.dma(ove_d[l, hp], Ve, reads=["Ve"], writes=[("skv", l, hp)], key="okv")
                b.dma(ovo_d[l, hp], Vo, reads=["Vo"], writes=[("skv", l, hp)], key="okv")
            for e in range(2):
                Vt = Ve if e == 0 else Vo
                vk = "Ve" if e == 0 else "Vo"
                pV = pve_s if e == 0 else pvo_s
                M = 65 if e == 0 else 128
                for qt in range(4):
                    blocks = ([("p", kb) for kb in range(16)] if half == 1 else []) + [("o", kb) for kb in range(4 * qt + 4)]
                    po = 2 + (qt % 2)
                    SB = [0, 1, 5]

                    def emit_qk(bi):
                        kind, kb = blocks[bi]
                        pi = SB[bi % 3]
                        kts = (pkt_s if kind == "p" else KT)[0:96, e, kb * 128:(kb + 1) * 128]
                        kk = HK if kind == "p" else [("KT", kb // 4)]
                        b.mm(psb[pi][:], kts, QT[0:96, e, qt * 512:(qt + 1) * 512], True, True,
                             reads=kk + [("QT", qt)], writes=[("ps", pi)])

                    emit_qk(0)
                    if len(blocks) > 1:
                        emit_qk(1)
                    for bi, (kind, kb) in enumerate(blocks):
                        pi = SB[bi % 3]
                        pt = PT[bi % 4]
                        ptk = ("sq", bi % 4)
                        b.act(pt, psb[pi][:], AF.Exp, reads=[("ps", pi), "cvec"], writes=[ptk],
                              bias=nshift, scale=1.0)
                        if kind == "o" and kb >= 4 * qt:
                            o = kb - 4 * qt
                            b.v("dve", "tensor_tensor", pt, pt, cmask[:, (3 - o) * 128:(3 - o) * 128 + 512], ALU.mult,
                                reads=[ptk, "cbf"], writes=[ptk])
                        if bi + 2 < len(blocks):
                            emit_qk(bi + 2)
                        vs = (pV if kind == "p" else Vt)[:, kb, 0:M]
                        b.mm(psb[po][0:M, :], vs, pt, bi == 0, bi == len(blocks) - 1,
                             reads=(HK if kind == "p" else [vk]) + [ptk], writes=[("ps", po)])
                    dr = 64 if e == 0 else 0
                    b.v("dve", "reciprocal", rc[dr:dr + 1, :], psb[po][dr:dr + 1, :], reads=[("ps", po)], writes=["rc"])
                    if e == 0:
                        b.mm(psb[4][0:64, :], ones_f[64:65, 0:64], rc[64:65, :], True, True, reads=["cst", "rc"], writes=[("ps", 4)])
                        rows = slice(0, 64)
                    else:
                        b.mm(psb[4][:, :], ones_f[0:1, :], rc[0:1, :], True, True, reads=["cst", "rc"], writes=[("ps", 4)])
                        rows = slice(64, 128)
                    b.act(rstd[rows, :], psb[4][rows, :], AF.Copy, reads=[("ps", 4)], writes=["rstd"])
                    b.v("dve", "tensor_tensor", ymla[rows, hp, qt * 512:(qt + 1) * 512], psb[po][rows, :], rstd[rows, :], ALU.mult,
                        reads=[("ps", po), "rstd"], writes=[("ymla", hp, qt)])
        dbg("ymla", ymla, [128, 4, T], [("ymla", h, q) for h in range(4) for q in range(4)], BF16)
        dbg("QT", QT, [128, 2, T], [("QT", g) for g in range(4)], BF16)
        dbg("KT", KT, [128, 2, T], [("KT", g) for g in range(4)], BF16)
        dbg("Vo", Vo, [128, 16, 128], ["Vo"], BF16)


    dummy = b.sb("dummy_s", [128, 2])

    def barrier(keys):
        b.v("pool", "memset", dummy[:, 0:1], 0.0, writes=list(keys))

    MIXER_KEYS = ([("uT", c, g) for c in range(4) for g in range(4)] + [("cqn", g) for g in range(4)]
                  + [("ckvn", g) for g in range(4)] + [("krope", g) for g in range(16)] + ["cos", "sin"]
                  + [("ymla", h, q) for h in range(4) for q in range(4)])
    ATT_KEYS = ([("QT", g) for g in range(4)] + [("KT", g) for g in range(4)]
                + ["Ve", "Vo", "sqq", "qn", "qf", "rta", "rtb", "ssq", "kraw"])
    PROJ_KEYS = [("cqraw", i) for i in range(3)] + ["ropeang", "ropeang2", "ropew", "ropew_ti", "ropew_tf", "ropew_tm"]

    hT_f = hT[:].rearrange("p a b -> p (a b)").bitcast(F32)
    SLOTS = [{n: A32(48 + 2 * i, [128, 512]) for i, n in enumerate("ABCDEFGHIJKMN")},
             dict([(n, A32(32 + 2 * i, [128, 512])) for i, n in enumerate("ABCDEFGH")]
                  + [(n, hT_f[:, i * 512:(i + 1) * 512]) for i, n in enumerate("IJKMN")])]
    SSM_KEYS = ["s%d%s" % (ch, n) for ch in range(2) for n in "ABCDEFGHIJKMN"] + \
               ["Xr0", "Xi0", "Xr1", "Xi1"]
    SSM_BANKS = [(0, 1, 2, 3), (4, 5, 6, 7)]

    def ssm_prep(l, c, ch, half):
        slot = SLOTS[ch]
        sk = lambda n: "s%d%s" % (ch, n)
        def tt(eng, out, a, bb_, op):
            b.v(eng, "tensor_tensor", slot[out], slot[a], slot[bb_], op, reads=[sk(a), sk(bb_)], writes=[sk(out)])
        def wrap(dst, src, ti, tf, tm):
            d_, s_, ti_, tf_, tm_ = slot[dst], slot[src], slot[ti].bitcast(I32), slot[tf], slot[tm]
            kd, ks, kti, ktf, ktm = sk(dst), sk(src), sk(ti), sk(tf), sk(tm)
            b.v("dve", "tensor_scalar_mul", ti_, s_, 1.0 / TWO_PI, reads=[ks], writes=[kti])
            b.v("dve", "tensor_copy", tf_, ti_, reads=[kti], writes=[ktf])
            b.v("dve", "scalar_tensor_tensor", d_, tf_, -TWO_PI, s_, ALU.mult, ALU.add, reads=[ktf, ks], writes=[kd])
            b.v("dve", "tensor_scalar", tm_, d_, PI, -TWO_PI, ALU.is_gt, ALU.mult, reads=[kd], writes=[ktm])
            b.v("dve", "tensor_tensor", d_, d_, tm_, ALU.add, reads=[kd, ktm], writes=[kd])
            b.v("dve", "tensor_scalar", tm_, d_, -PI, TWO_PI, ALU.is_lt, ALU.mult, reads=[kd], writes=[ktm])
            b.v("dve", "tensor_tensor", d_, d_, tm_, ALU.add, reads=[kd, ktm], writes=[kd])
        def actf(out, a, func, **kw):
            b.act(slot[out], slot[a], func, reads=[sk(a), "cst"], writes=[sk(out)], **kw)
        for i, n in enumerate("KMN"):
            b.dma(slot[n], lam_d[l, c, i].partition_broadcast(128), writes=[sk(n)])
        actf("N", "N", AF.Exp)
        tt("dve", "G", "K", "N", ALU.mult)
        tt("dve", "H", "M", "N", ALU.mult)
        actf("A", "G", AF.Exp, scale=sp1)
        actf("C", "G", AF.Exp, scale=nsp1)
        b.v("dve", "tensor_scalar_mul", slot["I"], slot["H"], sp1, reads=[sk("H"), "cst"], writes=[sk("I")])
        wrap("J", "I", "N", "E", "F")
        actf("B", "J", AF.Sin)
        b.v("dve", "tensor_scalar_add", slot["I"], slot["I"], PI / 2, reads=[sk("I")], writes=[sk("I")])
        wrap("J", "I", "N", "E", "F")
        actf("D", "J", AF.Sin)
        tt("dve", "E", "A", "D", ALU.mult)
        tt("dve", "F", "A", "B", ALU.mult)
        tt("dve", "A", "C", "D", ALU.mult)
        b.v("dve", "scalar_tensor_tensor", slot["B"], slot["C"], -1.0, slot["B"], ALU.mult, ALU.mult,
            reads=[sk("C"), sk("B")], writes=[sk("B")])
        wrap("J", "H", "N", "C", "D")
        actf("I", "J", AF.Sin)
        b.v("dve", "tensor_scalar_add", slot["H"], slot["H"], PI / 2, reads=[sk("H")], writes=[sk("H")])
        wrap("J", "H", "N", "C", "D")
        actf("N", "J", AF.Sin)
        actf("C", "G", AF.Exp)
        tt("dve", "D", "C", "N", ALU.mult)
        b.v("dve", "tensor_scalar_add", slot["D"], slot["D"], -1.0, reads=[sk("D")], writes=[sk("D")])
        tt("dve", "I", "C", "I", ALU.mult)
        tt("dve", "J", "K", "K", ALU.mult)
        tt("dve", "N", "M", "M", ALU.mult)
        tt("dve", "J", "J", "N", ALU.add)
        b.v("dve", "reciprocal", slot["J"], slot["J"], reads=[sk("J")], writes=[sk("J")])
        tt("dve", "C", "D", "K", ALU.mult)
        tt("dve", "N", "I", "M", ALU.mult)
        tt("dve", "C", "C", "N", ALU.add)
        tt("dve", "C", "C", "J", ALU.mult)
        tt("dve", "N", "I", "K", ALU.mult)
        tt("dve", "G", "D", "M", ALU.mult)
        tt("dve", "N", "N", "G", ALU.subtract)
        tt("dve", "N", "N", "J", ALU.mult)
        tt("dve", "D", "A", "C", ALU.mult)
        tt("dve", "G", "B", "N", ALU.mult)
        tt("dve", "D", "D", "G", ALU.subtract)
        tt("dve", "I", "A", "N", ALU.mult)
        tt("dve", "G", "B", "C", ALU.mult)
        tt("dve", "I", "I", "G", ALU.add)
        cx = dict(slot=slot, sk=sk, ch=ch, c=c,
                  Tre=slot["E"], Tim=slot["F"], Tpr=slot["D"], Tpi=slot["I"],
                  TK=[sk("E"), sk("F"), sk("D"), sk("I")],
                  Bm=slot["A"].bitcast(BF16),
                  Cm=slot["B"].bitcast(BF16).rearrange("p (k n) -> p k n", k=8),
                  X=slot["C"].bitcast(BF16),
                  Sb=[slot["G"].bitcast(BF16), slot["H"].bitcast(BF16)], SbK=[sk("G"), sk("H")],
                  ST=slot["J"].bitcast(BF16).rearrange("p (k n) -> p k n", k=8))
        load_w(cx["Bm"], bm_d[l, c], sk("A"))
        load_w(cx["Cm"], cm_d[l, c], sk("B"))
        b.v("pool", "memset", cx["Sb"][1], 0.0, writes=[cx["SbK"][1]])
        if half == 1:
            b.dma(cx["Sb"][1][127:128, :], hinit_d[l, c:c + 1, :], reads=[("sh", l, c)], writes=[cx["SbK"][1]])
        return cx

    def ssm_tile(l, cx, k, stage):
        slot, sk, ch, c = cx["slot"], cx["sk"], cx["ch"], cx["c"]
        Tre, Tim, Tpr, Tpi, TK = cx["Tre"], cx["Tim"], cx["Tpr"], cx["Tpi"], cx["TK"]
        Bm, Cm, X, Sb, SbK, ST = cx["Bm"], cx["Cm"], cx["X"], cx["Sb"], cx["SbK"], cx["ST"]
        b0, b1, tb, yb = SSM_BANKS[ch]
        Xr, Xi = "Xr%d" % ch, "Xi%d" % ch
        tok = slice(k * 128, (k + 1) * 128)
        g4 = k // 4
        cur, prv = k % 2, 1 - (k % 2)
        K_, M_, N_ = slot["K"], slot["M"], slot["N"]
        if stage == 2:
            return ssm_tile2(l, cx, k)
        if stage == 3:
            return ssm_tile3(l, cx, k)
        b.mm(psb[b0][:], uT[:, c, tok], Bm[:, 0:512], True, True, reads=[("uT", c, g4), sk("A")], writes=[("ps", b0)])
        b.mm(psb[b1][:], uT[:, c, tok], Bm[:, 512:1024], True, True, reads=[("uT", c, g4), sk("A")], writes=[("ps", b1)])
        b.v("dve", "tensor_tensor", K_, psb[b0][:], Tpr, ALU.mult, reads=[("ps", b0)] + TK, writes=[sk("K")])
        b.v("dve", "tensor_tensor", M_, psb[b1][:], Tpi, ALU.mult, reads=[("ps", b1)] + TK, writes=[sk("M")])
        b.v("pool", "tensor_tensor", X[:, 0:512], K_, M_, ALU.subtract, reads=[sk("K"), sk("M")], writes=[Xr])
        b.v("dve", "tensor_tensor", N_, psb[b1][:], Tpr, ALU.mult, reads=[("ps", b1)] + TK, writes=[sk("N")])
        b.v("dve", "tensor_tensor", K_, psb[b0][:], Tpi, ALU.mult, reads=[("ps", b0)] + TK + [Xr], writes=[sk("K")])
        b.v("pool", "tensor_tensor", X[:, 512:1024], N_, K_, ALU.add, reads=[sk("K"), sk("N")], writes=[Xi])

    def ssm_tile2(l, cx, k):
        slot, sk, ch, c = cx["slot"], cx["sk"], cx["ch"], cx["c"]
        Tre, Tim, Tpr, Tpi, TK = cx["Tre"], cx["Tim"], cx["Tpr"], cx["Tpi"], cx["TK"]
        Bm, Cm, X, Sb, SbK, ST = cx["Bm"], cx["Cm"], cx["X"], cx["Sb"], cx["SbK"], cx["ST"]
        b0, b1, tb, yb = SSM_BANKS[ch]
        Xr, Xi = "Xr%d" % ch, "Xi%d" % ch
        cur, prv = k % 2, 1 - (k % 2)
        K_, M_, N_ = slot["K"], slot["M"], slot["N"]
        b.mm(psb[b0][:], tri, X[:, 0:512], True, False, reads=["cbf", Xr], writes=[("ps", b0)])
        b.mm(psb[b0][:], e127, Sb[prv][:, 0:512], False, True, reads=["cbf", SbK[prv]], writes=[("ps", b0)])
        b.mm(psb[b1][:], tri, X[:, 512:1024], True, False, reads=["cbf", Xi], writes=[("ps", b1)])
        b.mm(psb[b1][:], e127n, Sb[prv][:, 512:1024], False, True, reads=["cbf", SbK[prv]], writes=[("ps", b1)])
        b.v("dve", "tensor_tensor", K_, psb[b0][:], Tre, ALU.mult, reads=[("ps", b0)] + TK + [Xi], writes=[sk("K")])
        b.v("dve", "tensor_tensor", M_, psb[b1][:], Tim, ALU.mult, reads=[("ps", b1)] + TK + [Xr], writes=[sk("M")])
        b.v("pool", "tensor_tensor", Sb[cur][:, 0:512], K_, M_, ALU.subtract, reads=[sk("K"), sk("M")], writes=[SbK[cur]])
        b.v("dve", "scalar_tensor_tensor", N_, psb[b1][:], -1.0, Tre, ALU.mult, ALU.mult,
            reads=[("ps", b1)] + TK + [Xi], writes=[sk("N")])
        b.v("dve", "scalar_tensor_tensor", K_, psb[b0][:], -1.0, Tim, ALU.mult, ALU.mult,
            reads=[("ps", b0)] + TK + [SbK[cur]], writes=[sk("K")])
        b.v("pool", "tensor_tensor", Sb[cur][:, 512:1024], N_, K_, ALU.add,
            reads=[sk("K"), sk("N")], writes=[SbK[cur]])

    def ssm_tile3(l, cx, k):
        slot, sk, ch, c = cx["slot"], cx["sk"], cx["ch"], cx["c"]
        Cm, Sb, SbK, ST = cx["Cm"], cx["Sb"], cx["SbK"], cx["ST"]
        b0, b1, tb, yb = SSM_BANKS[ch]
        g4 = k // 4
        cur = k % 2
        for blk in range(8):
            b.tr(psb_bf[tb][:, blk * 128:(blk + 1) * 128], Sb[cur][:, blk * 128:(blk + 1) * 128], ident,
                 reads=[SbK[cur], "cbf"], writes=[("ps", tb)])
        b.act(ST, psb_bf[tb][:, :].rearrange("p (k n) -> p k n", k=8), AF.Copy, reads=[("ps", tb)], writes=[sk("J")])
        for blk in range(8):
            b.mm(psb[yb][:, (k % 4) * 128:(k % 4 + 1) * 128], Cm[:, blk, :], ST[:, blk, :], blk == 0, blk == 7,
                 reads=[sk("B"), sk("J")], writes=[("ps", yb)])
        if k % 4 == 3:
            t4 = slice(g4 * 512, (g4 + 1) * 512)
            b.v("dve", "scalar_tensor_tensor", uT[:, c, t4], uT[:, c, t4], vec[:, l, 35 + c:36 + c], psb[yb][:],
                ALU.mult, ALU.add, reads=[("uT", c, g4), "vec", ("ps", yb)], writes=[("uT", c, g4)])

    def ssm(l, half=1):
        for pair in range(2):
            cxs = [ssm_prep(l, 2 * pair + ch, ch, half) for ch in range(2)]
            for k in range(16):
                for stage in (1, 2, 3):
                    for cx in cxs:
                        ssm_tile(l, cx, k, stage)
            if half == 0:
                for cx in cxs:
                    c = cx["c"]
                    b.dma(hfin_d[l, c:c + 1, :], cx["Sb"][1][127:128, :], reads=[cx["SbK"][1]], writes=[("sh", l, c)], key="hfin")
        dbg("yssm", uT, [128, 4, T], [("uT", c, g) for c in range(4) for g in range(4)], BF16)


    def group_norm(buf, keyf, gcol, l, g4):
        tok = slice(g4 * 512, (g4 + 1) * 512)
        ks = [keyf(c, g4) for c in range(4)]
        b.act(sq[:, 0:4, :], buf[:, :, tok], AF.Square, reads=ks, writes=SQK)
        pi = next_ps()
        rms_scale(SQK, 4, 512, lambda c: sq[:, c, :], pi, 1.0 / 512)
        for c in range(4):
            b.v("dve", "scalar_tensor_tensor", buf[:, c, tok], buf[:, c, tok], vec[:, l, gcol + c:gcol + c + 1], rstd[:],
                ALU.mult, ALU.mult, reads=[keyf(c, g4), "vec", "rstd"], writes=[keyf(c, g4)])

    def ssm_post(l):
        wglu = wb[1][:, 4096:6144].rearrange("p (k n) -> p k n", k=4)
        load_w(wglu, w_glu_d[l], ("wb", 1, "glu"))
        tK = A32(48, [128, 512])
        tM = A32(50, [128, 512])
        gates = A32(52, [128, 4, 512])
        for g4 in range(4):
            tok = slice(g4 * 512, (g4 + 1) * 512)
            for c in range(4):
                y = uT[:, c, tok]
                yk = ("uT", c, g4)
                b.v("dve", "tensor_tensor", tK, y, y, ALU.mult, reads=[yk], writes=["gK"])
                b.v("dve", "tensor_scalar", tK, tK, GB, 1.0, ALU.mult, ALU.add, reads=["gK"], writes=["gK"])
                b.v("dve", "tensor_tensor", tK, tK, y, ALU.mult, reads=["gK", yk], writes=["gK"])
                b.act(tM, tK, AF.Sigmoid, reads=["gK"], writes=["gM"], scale=GA)
                b.v("dve", "tensor_tensor", y, y, tM, ALU.mult, reads=["gM", yk], writes=[yk])
            for oc in range(4):
                pi = 4 + oc
                for k in range(4):
                    b.mm(psb[pi][:], wglu[:, k, oc * 128:(oc + 1) * 128], uT[:, k, tok], k == 0, k == 3,
                         reads=[("wb", 1, "glu"), ("uT", k, g4)], writes=[("ps", pi)])
                b.act(gates[:, oc, :], psb[pi][:], AF.Sigmoid, reads=[("ps", pi), "vec"], writes=[("gate", oc)],
                      bias=vec[:, l, 39 + oc:40 + oc], scale=1.0)
            for oc in range(4):
                b.v("dve", "tensor_tensor", uT[:, oc, tok], uT[:, oc, tok], gates[:, oc, :], ALU.mult,
                    reads=[("uT", oc, g4), ("gate", oc)], writes=[("uT", oc, g4)])
            group_norm(uT, lambda c, g: ("uT", c, g), 43, l, g4)
        dbg("yssmn", uT, [128, 4, T], [("uT", c, g) for c in range(4) for g in range(4)], BF16)

    def mix_out(l):
        wout = wb[0][:, :].rearrange("p (k n) -> p k n", k=8)
        load_w(wout, w_out_d[l], ("wb", 0))
        for g4 in range(4):
            tok = slice(g4 * 512, (g4 + 1) * 512)
            group_norm(ymla, lambda c, g: ("ymla", c, g), 47, l, g4)
            for oc in range(8):
                pi = next_ps()
                for k in range(8):
                    src = uT[:, k, tok] if k < 4 else ymla[:, k - 4, tok]
                    sk_ = ("uT", k, g4) if k < 4 else ("ymla", k - 4, g4)
                    b.mm(psb[pi][:], wout[:, k, oc * 128:(oc + 1) * 128], src, k == 0, k == 7,
                         reads=[("wb", 0), sk_], writes=[("ps", pi)])
                b.v("dve", "tensor_tensor", xT[:, oc, tok], xT[:, oc, tok], psb[pi][:], ALU.add,
                    reads=[("xT", oc, g4), ("ps", pi)], writes=[("xT", oc, g4)])
        dbg("x1", xT[:], [128, 8, T], [("xT", c, g) for c in range(8) for g in range(4)])

    CR_KEYS = ["mhT", "KmT", "Vm", "memx", "mtmp", "mss", "mkn", ("QmT", 0), ("QmT", 1), ("QmT", 2), ("QmT", 3)] + \
              [("omT", h, q) for h in range(4) for q in range(4)]

    def cross(l):
        mhT = A16(0, [128, 8, 256])
        KmT = A16(4, [128, 4, 256])
        Vm = A16(6, [128, 2, 4, 65])
        QmT = A16(8, [128, 4, T])
        omT = A16(24, [128, 4, T])
        memx = A32(40, [128, 8, 256])
        mtmp = A32(48, [128, 4, 64])
        mss = A32(49, [128, 4])
        mkn = A16(50, [128, 4, 64])
        mgq = rowb[:, 192:256]
        mgk = rowb[:, 256:320]
        mwkv = wb[1][:, 0:4096].rearrange("p (k n) -> p k n", k=8)
        mwq = wb[1][:, 4096:6144].rearrange("p (k n) -> p k n", k=8)
        load_w(mwkv, mwkv_d[l], ("wb", 1))
        load_w(mwq, mwq_d[l], ("wb", 1, "glu"))
        b.dma(memx, memT_d, writes=["memx"])
        sqm = sq[:, :, 0:256]
        b.act(sqm, memx, AF.Square, reads=["memx"], writes=SQK)
        pi = next_ps()
        rms_scale(SQK, 8, 256, lambda c: sq[:, c, 0:256], pi, 1.0 / D)
        for c in range(8):
            b.v("dve", "scalar_tensor_tensor", mhT[:, c, :], memx[:, c, :], vec[:, l, 24 + c:25 + c], rstd[:, 0:256],
                ALU.mult, ALU.mult, reads=["memx", "vec", "rstd"], writes=["mhT"])
        b.v("pool", "memset", Vm[:, :, :, 64:65], 1.0, writes=["Vm"])

        def head_norm(src, srck, gain, n_slots_bank, slot_fn):
            b.act(mtmp, src, AF.Square, reads=srck, writes=["mtmp"])
            b.v("dve", "tensor_reduce", mss, mtmp, AX.X, ALU.add, reads=["mtmp"], writes=["mss"])
            b.act(mss, mss, AF.Sqrt, reads=["mss", "cvec"], writes=["mss"], bias=epsv, scale=1.0 / 64)
            b.v("dve", "reciprocal", mss, mss, reads=["mss"], writes=["mss"])
            b.v("dve", "tensor_tensor", mtmp, src, mss.unsqueeze(2).to_broadcast([128, 4, 64]), ALU.mult,
                reads=srck + ["mss"], writes=["mtmp"])
            b.v("dve", "tensor_tensor", mkn, mtmp, gain.unsqueeze(1).to_broadcast([128, 4, 64]), ALU.mult,
                reads=["mtmp", "rowb"], writes=["mkn"])
            for h in range(4):
                sl = slot_fn(h)
                b.tr(psb_bf[n_slots_bank][0:64, sl * 128:(sl + 1) * 128], mkn[:, h, :], ident,
                     reads=["mkn", "cbf"], writes=[("ps", n_slots_bank)])

        for mt in range(2):
            pi = next_ps()
            for k in range(8):
                b.mm(psb[pi][:], mhT[:, k, mt * 128:(mt + 1) * 128], mwkv[:, k, :], k == 0, k == 7,
                     reads=["mhT", ("wb", 1)], writes=[("ps", pi)])
            kvv = psb[pi][:].rearrange("p (h d) -> p h d", h=4)
            b.act(Vm[:, mt, :, 0:64], kvv[:, :, 64:128], AF.Copy, reads=[("ps", pi)], writes=["Vm"])
            head_norm(kvv[:, :, 0:64], [("ps", pi)], mgk, 6, lambda h: h * 2 + mt)
        b.act(KmT[0:64, :, :], psb_bf[6][0:64, :].rearrange("p (h n) -> p h n", h=4), AF.Copy,
              reads=[("ps", 6)], writes=["KmT"])

        for hf in range(2):
            norm_x(l, 8, hf * 1024, 2, None)
            for t8 in range(8):
                t16 = hf * 8 + t8
                pi = next_ps()
                for k in range(8):
                    b.mm(psb[pi][:, 0:256], hT[:, k, t8 * 128:(t8 + 1) * 128], mwq[:, k, :], k == 0, k == 7,
                         reads=[("hT", t8 // 4), ("wb", 1, "glu")], writes=[("ps", pi)])
                head_norm(psb[pi][:, 0:256].rearrange("p (h d) -> p h d", h=4), [("ps", pi)], mgq, 7,
                          lambda h: h * 2 + (t16 % 2))
                if t16 % 2 == 1:
                    b.act(QmT[0:64, :, (t16 // 2) * 256:(t16 // 2 + 1) * 256],
                          psb_bf[7][0:64, :].rearrange("p (h n) -> p h n", h=4), AF.Copy,
                          reads=[("ps", 7)], writes=[("QmT", t16 // 4)])
        for h in range(4):
            for qt in range(4):
                po = 2 + (qt % 2)
                for kb in range(2):
                    b.mm(psb[kb][:], KmT[0:64, h, kb * 128:(kb + 1) * 128], QmT[0:64, h, qt * 512:(qt + 1) * 512], True, True,
                         reads=["KmT", ("QmT", qt)], writes=[("ps", kb)])
                for kb in range(2):
                    pi = kb
                    pt = PT[kb + 2 * (qt % 2)]
                    b.act(pt, psb[pi][:], AF.Exp, reads=[("ps", pi), "cvec"], writes=[("sq", kb + 2 * (qt % 2))], bias=nmshift, scale=1.0)
                    b.mm(psb[po][0:65, :], Vm[:, kb, h, :], pt, kb == 0, kb == 1, reads=["Vm", ("sq", kb + 2 * (qt % 2))], writes=[("ps", po)])
                b.v("dve", "reciprocal", rc[64:65, :], psb[po][64:65, :], reads=[("ps", po)], writes=["rc"])
                b.mm(psb[4][0:64, :], ones_f[64:65, 0:64], rc[64:65, :], True, True, reads=["cst", "rc"], writes=[("ps", 4)])
                b.act(rstd[0:64, :], psb[4][0:64, :], AF.Copy, reads=[("ps", 4)], writes=["rstd"])
                b.v("dve", "tensor_tensor", omT[0:64, h, qt * 512:(qt + 1) * 512], psb[po][0:64, :], rstd[0:64, :], ALU.mult,
                    reads=[("ps", po), "rstd"], writes=[("omT", h, qt)])
        mwo = wb[0][0:64, 0:4096].rearrange("p (h n) -> p h n", h=4)
        load_w(mwo, mwo_d[l], ("wb", 0))
        for g4 in range(4):
            tok = slice(g4 * 512, (g4 + 1) * 512)
            for oc in range(8):
                pi = next_ps()
                for h in range(4):
                    b.mm(psb[pi][:], mwo[:, h, oc * 128:(oc + 1) * 128], omT[0:64, h, tok], h == 0, h == 3,
                         reads=[("wb", 0), ("omT", h, g4)], writes=[("ps", pi)])
                b.v("dve", "tensor_tensor", xT[:, oc, tok], xT[:, oc, tok], psb[pi][:], ALU.add,
                    reads=[("xT", oc, g4), ("ps", pi)], writes=[("xT", oc, g4)])
        dbg("x2", xT[:], [128, 8, T], [("xT", c, g) for c in range(8) for g in range(4)])

    MLP_KEYS = [("hid", j, t) for j in range(32) for t in range(2)] + [("rt", 0), ("rt", 1)]

    def mlp(l):
        hid = A16(0, [128, 32, 1024])
        rt = [A16(64, [128, 512]), A16(65, [128, 512])]
        n = 0
        for hf in range(2):
            norm_x(l, 16, hf * 1024, 2, None)
            for jg in range(8):
                w1p = wb[jg % 2][:, 0:4096].rearrange("p (k n) -> p k n", k=8)
                load_w(w1p, w1_d[l][:, :, jg * 512:(jg + 1) * 512], ("wb", jg % 2))
                for jj in range(4):
                    j = jg * 4 + jj
                    for t in range(2):
                        pi = next_ps()
                        for k in range(8):
                            b.mm(psb[pi][:], w1p[:, k, jj * 128:(jj + 1) * 128], hT[:, k, t * 512:(t + 1) * 512], k == 0, k == 7,
                                 reads=[("wb", jg % 2), ("hT", t)], writes=[("ps", pi)])
                        r = rt[n % 2]
                        b.act(r, psb[pi][:], AF.Relu, reads=[("ps", pi)], writes=[("rt", n % 2)])
                        b.v("pool", "tensor_tensor", hid[:, j, t * 512:(t + 1) * 512], r, r, ALU.mult,
                            reads=[("rt", n % 2)], writes=[("hid", j, t)])
                        n += 1
            for oc in range(8):
                w2p = wb[oc % 2][:, 0:4096].rearrange("p (j n) -> p j n", j=32)
                load_w(w2p, w2_d[l][:, :, oc * 128:(oc + 1) * 128], ("wb", oc % 2))
                for t in range(2):
                    g4 = hf * 2 + t
                    tok = slice(g4 * 512, (g4 + 1) * 512)
                    pi = next_ps()
                    for j in range(32):
                        b.mm(psb[pi][:], w2p[:, j, :], hid[:, j, t * 512:(t + 1) * 512], j == 0, j == 31,
                             reads=[("wb", oc % 2), ("hid", j, t)], writes=[("ps", pi)])
                    b.v("dve", "tensor_tensor", xT[:, oc, tok], xT[:, oc, tok], psb[pi][:], ALU.add,
                        reads=[("xT", oc, g4), ("ps", pi)], writes=[("xT", oc, g4)])

    def run_layer(l, half=1, stop_after=None):
        rope_tables()
        layer(l)
        YMK = [("ymla", h, q) for h in range(4) for q in range(4)]
        barrier(PROJ_KEYS + SSM_KEYS + HK + YMK)
        ssm(l, half)
        if stop_after == "ssm":
            return
        barrier(SSM_KEYS + HK + YMK + ["gK", "gM"] + [("gate", i) for i in range(4)])
        ssm_post(l)
        if stop_after == "ssm_post":
            return
        barrier(SSM_KEYS + HK + YMK + ["gK", "gM"] + [("gate", i) for i in range(4)] + ATT_KEYS + SQK)
        mla(l, half)
        mix_out(l)
        if stop_after == "mix":
            return
        barrier(MIXER_KEYS + ATT_KEYS + CR_KEYS)
        cross(l)
        if stop_after == "cross":
            return
        barrier(CR_KEYS + MLP_KEYS)
        mlp(l)
        barrier(MLP_KEYS + MIXER_KEYS + PROJ_KEYS)

    def store_x(half):
        for c in range(8):
            b.dma(outT_d[half, :, c, :], xT[:, c, :], reads=[("xT", c, g) for g in range(4)], writes=[("dramout", "out")], key="out")

    def finish():
        b.p.add("sp", lambda e: e.nop(), reads=[("dramout", "out")] + [("dbgdone", k) for k in b.out_keys])

    b.ctx = dict(locals())
    return b


_CACHE = {}
N_LAYERS = 4


def _program():
    if "b" not in _CACHE:
        b = build_program(N_LAYERS)
        c = b.ctx
        for half in range(2):
            c["load_x"](half)
            for l in range(N_LAYERS):
                c["run_layer"](l, half)
            c["store_x"](half)
        c["finish"]()
        b.p.emit(b.nc, b.es)
        _CACHE["b"] = b
    return _CACHE["b"]


def kernel(**inputs):
    inp = {k: np.asarray(v) for k, v in inputs.items()}
    B = inp["x"].shape[0]
    b = _program()
    cst = host_constants()
    W = prep_weights(inp, range(N_LAYERS))
    in_maps = []
    for core in range(8):
        bb = core % B
        xs = np.asarray(inp["x"][bb], np.float32)
        xT = np.stack([chunkT(np.ascontiguousarray(xs[hf * T:(hf + 1) * T].T), 8) for hf in range(2)])
        pos = np.asarray(inp["positions"][bb]).astype(np.float32)
        m = {"xT": xT,
             "memT": chunkT(np.ascontiguousarray(np.asarray(inp["mem"][bb], np.float32).T), 8),
             "pos": np.stack([np.ascontiguousarray(pos[hf * T:(hf + 1) * T].reshape(16, 128).T) for hf in range(2)]),
             "cst": cst}
        m.update(W)
        in_maps.append(m)
    res = run_bass_kernel_spmd(b.nc, in_maps, core_ids=list(range(8)))
    out = np.zeros((B, 2 * T, D), np.float32)
    for bb in range(B):
        o = np.asarray(res.results[bb]["outT"], np.float32)
        for hf in range(2):
            out[bb, hf * T:(hf + 1) * T] = o[hf].transpose(2, 1, 0).reshape(T, D)
    return out
```

```python
import numpy as np
from contextlib import ExitStack
import concourse.bass as bass
import concourse.mybir as mybir
from concourse.bass_utils import run_bass_kernel_spmd

F32 = mybir.dt.float32
BF16 = mybir.dt.bfloat16
I32 = mybir.dt.int32
AF = mybir.ActivationFunctionType
ALU = mybir.AluOpType
AX = mybir.AxisListType

ENGS = ("pe", "act", "dve", "pool", "sp")


class Op:
    __slots__ = ("eng", "fn", "reads", "writes", "dma_key", "idx", "deps", "signal",
                 "tick", "waits")

    def __init__(self, eng, fn, reads, writes, dma_key):
        self.eng = eng
        self.fn = fn
        self.reads = reads
        self.writes = writes
        self.dma_key = dma_key
        self.deps = []
        self.signal = False
        self.tick = None
        self.waits = []


class Prog:
    def __init__(self):
        self.ops = []

    def add(self, eng, fn, reads=(), writes=(), dma_key=None):
        op = Op(eng, fn, tuple(reads), tuple(writes), dma_key)
        op.idx = len(self.ops)
        self.ops.append(op)
        return op

    def analyse(self):
        last_w = {}
        readers = {}
        for op in self.ops:
            deps = set()
            for k in op.reads:
                w = last_w.get(k)
                if w is not None:
                    deps.add(w.idx)
            for k in op.writes:
                w = last_w.get(k)
                if w is not None:
                    deps.add(w.idx)
                for r in readers.get(k, {}).values():
                    deps.add(r.idx)
            deps.discard(op.idx)
            keep = []
            for d in deps:
                a = self.ops[d]
                if a.dma_key is None and op.dma_key is None and a.eng == op.eng:
                    if a.eng == "pe":
                        continue
                    raw = any(k in a.writes for k in op.reads)
                    if not raw:
                        continue
                keep.append(d)
            op.deps = keep
            for d in keep:
                self.ops[d].signal = True
            for k in op.writes:
                last_w[k] = op
                readers[k] = {}
            for k in op.reads:
                rk = op.dma_key if op.dma_key is not None else op.eng
                readers.setdefault(k, {})[rk] = op
        cnt = {e: 0 for e in ENGS}
        dcnt = {}
        for op in self.ops:
            if op.dma_key is not None:
                dcnt[op.dma_key] = dcnt.get(op.dma_key, 0) + 1
                op.tick = ("dma", op.dma_key, dcnt[op.dma_key] * 16)
            elif op.signal:
                cnt[op.eng] += 1
                op.tick = ("eng", op.eng, cnt[op.eng])
        waited = {e: {} for e in ENGS}
        dma_issued = {}
        for op in self.ops:
            need = {}
            for d in op.deps:
                a = self.ops[d]
                t = a.tick
                if t[0] == "dma":
                    v = dma_issued.get(t[1], 0) * 16
                    v = max(v, t[2])
                else:
                    v = t[2]
                sk = (t[0], t[1])
                if v > need.get(sk, 0):
                    need[sk] = v
            for sk, v in need.items():
                if waited[op.eng].get(sk, 0) >= v:
                    continue
                waited[op.eng][sk] = v
                op.waits.append((sk, v))
            if op.dma_key is not None:
                dma_issued[op.dma_key] = dma_issued.get(op.dma_key, 0) + 1
        self.n_eng_sems = cnt
        self.dma_keys = list(dcnt.keys())

    def emit(self, nc, es):
        self.analyse()
        sems = {}
        for e in ENGS:
            sems[("eng", e)] = es.enter_context(nc.semaphore("s_" + e))
        for i, k in enumerate(self.dma_keys):
            sems[("dma", k)] = es.enter_context(nc.semaphore("d%d" % i))
        self.sems = sems
        per = {e: [op for op in self.ops if op.eng == e] for e in ENGS}
        block = es.enter_context(nc.Block())

        def run(engine, lst):
            for op in lst:
                for sk, v in op.waits:
                    engine.wait_ge(sems[sk], v)
                ins = op.fn(engine)
                if op.dma_key is not None:
                    ins.then_inc(sems[("dma", op.dma_key)], 16)
                elif op.signal:
                    ins.then_inc(sems[("eng", op.eng)], 1)

        @block.tensor
        def _(e):
            run(e, per["pe"])

        @block.scalar
        def _(e):
            run(e, per["act"])

        @block.vector
        def _(e):
            run(e, per["dve"])

        @block.gpsimd
        def _(e):
            run(e, per["pool"])

        @block.sync
        def _(e):
            run(e, per["sp"])


D = 1024
T = 2048
NT4 = T // 512
NT1 = T // 128
KC = D // 128
IN_COLS = 928
EPS = 1e-6
PI = float(np.pi)
SHIFT = 10.0


class Builder:
    def __init__(self, nc, es, n_layers, debug=()):
        self.nc = nc
        self.es = es
        self.p = Prog()
        self.L = n_layers
        self.debug = set(debug)
        self.dram_in = {}
        self.dram_out = {}
        self._uid = 0
        self.out_keys = []

    def din(self, name, shape, dt=F32):
        t = self.nc.dram_tensor(name, list(shape), dt, kind="ExternalInput")
        self.dram_in[name] = t
        return t

    def dout(self, name, shape, dt=F32):
        t = self.nc.dram_tensor(name, list(shape), dt, kind="ExternalOutput")
        self.dram_out[name] = t
        return t

    def sb(self, name, shape, dt=F32):
        return self.es.enter_context(self.nc.sbuf_tensor(name, list(shape), dt))

    def ps(self, name, shape, dt=F32):
        return self.es.enter_context(self.nc.psum_tensor(name, list(shape), dt))

    def dma(self, out, in_, reads=(), writes=(), key=None, eng="sp", **kw):
        writes = list(writes)
        if key is None:
            self._uid += 1
            key = "a%d_%s" % (self._uid % 12, eng)
            writes.append(("dmakey", key))
        return self.p.add(eng, lambda e, o=out, i=in_, kw=kw: e.dma_start(out=o, in_=i, **kw),
                          reads=reads, writes=writes, dma_key=key)

    def mm(self, out, lhsT, rhs, start, stop, reads=(), writes=()):
        return self.p.add("pe", lambda e: e.matmul(out, lhsT, rhs, start=start, stop=stop),
                          reads=reads, writes=writes)

    def tr(self, out, in_, ident, reads=(), writes=()):
        return self.p.add("pe", lambda e: e.transpose(out, in_, ident), reads=reads, writes=writes)

    def act(self, out, in_, func, reads=(), writes=(), eng="act", **kw):
        return self.p.add("act", lambda e: e.activation(out, in_, func, **kw), reads=reads, writes=writes)

    def v(self, eng, name, *args, reads=(), writes=(), **kw):
        return self.p.add(eng, lambda e: getattr(e, name)(*args, **kw), reads=reads, writes=writes)


def host_constants():
    c = np.zeros((128, 1664), np.float32)
    c[:, 0:128] = np.eye(128)
    c[:, 128:256] = 1.0
    r = np.arange(128)
    c[:, 256:384] = (r[:, None] <= r[None, :])
    c[127, 384:512] = 1.0
    c[127, 512:640] = -1.0
    m = np.arange(896)
    c[:, 640:1536] = (m[None, :] - r[:, None] >= 384)
    c[:, 1536:1552] = (10000.0 ** (-np.arange(16, dtype=np.float32) / 16))[None, :]
    c[:, 1552] = r + 1
    c[:, 1553] = -(r + 1)
    return c


def chunkT(v, k):
    return np.ascontiguousarray(v.reshape((k, 128) + v.shape[1:]).swapaxes(0, 1))


def prep_weights(inp, layers):
    f = lambda a: np.asarray(a, np.float32)
    W = {}
    Ls = list(layers)
    n = len(Ls)
    vec = np.zeros((n, 128, 64), np.float32)
    row = np.zeros((n, 320), np.float32)
    bm = np.zeros((n, 4, 128, 1024), np.float32)
    cm = np.zeros((n, 4, 128, 8, 128), np.float32)
    lam = np.zeros((n, 4, 3, 512), np.float32)
    for i, l in enumerate(Ls):
        vec[i, :, 0:8] = f(inp["norm_mix"][l]).reshape(8, 128).T
        vec[i, :, 8:16] = f(inp["norm_mem_q"][l]).reshape(8, 128).T
        vec[i, :, 16:24] = f(inp["norm_mlp"][l]).reshape(8, 128).T
        vec[i, :, 24:32] = f(inp["norm_mem_kv"][l]).reshape(8, 128).T
        vec[i, :, 32:34] = f(inp["mla_q_norm"][l]).reshape(2, 128).T
        vec[i, :, 34] = f(inp["mla_kv_norm"][l])
        vec[i, :, 35:39] = f(inp["ssm_d"][l]).reshape(4, 128).T
        vec[i, :, 39:43] = f(inp["ssm_b_glu"][l]).reshape(4, 128).T
        vec[i, :, 43:47] = f(inp["out_norm_ssm"][l]).reshape(4, 128).T
        vec[i, :, 47:51] = f(inp["out_norm_mla"][l]).reshape(4, 128).T
        row[i, 0:96] = f(inp["mla_q_gain"][l])
        row[i, 96:192] = f(inp["mla_k_gain"][l])
        row[i, 192:256] = f(inp["mem_q_gain"][l])
        row[i, 256:320] = f(inp["mem_k_gain"][l])
        bre = f(inp["ssm_b_re"][l])
        bim = f(inp["ssm_b_im"][l])
        cre = f(inp["ssm_c_re"][l])
        cim = f(inp["ssm_c_im"][l])
        lre = f(inp["ssm_lambda_re"][l])
        lim = f(inp["ssm_lambda_im"][l])
        lst = f(inp["ssm_log_step"][l])
        for c in range(4):
            for j in range(8):
                g = 8 * c + j
                bm[i, c, 16 * j:16 * j + 16, j * 64:(j + 1) * 64] = bre[g].T
                bm[i, c, 16 * j:16 * j + 16, 512 + j * 64:512 + (j + 1) * 64] = bim[g].T
                pr, e = j // 2, j % 2
                cm[i, c, e * 64:(e + 1) * 64, pr, 16 * j:16 * j + 16] = cre[g].T
                cm[i, c, e * 64:(e + 1) * 64, 4 + pr, 16 * j:16 * j + 16] = cim[g].T
                lam[i, c, 0, j * 64:(j + 1) * 64] = lre[g]
                lam[i, c, 1, j * 64:(j + 1) * 64] = lim[g]
                lam[i, c, 2, j * 64:(j + 1) * 64] = lst[g]
    W["vec"] = vec
    W["row"] = row
    W["bm"] = bm
    W["cm"] = cm
    W["lam"] = lam
    st = lambda name, k: np.stack([chunkT(f(inp[name][l]), k) for l in Ls])
    W["w_in"] = st("w_in", 8)
    W["w_uq"] = st("mla_w_uq", 2)
    W["w_ukv"] = np.stack([f(inp["mla_w_ukv"][l]) for l in Ls])
    W["w_glu"] = st("ssm_w_glu", 4)
    W["w_out"] = st("w_out", 8)
    W["mwq"] = st("mem_w_q", 8)
    W["mwkv"] = st("mem_w_kv", 8)
    W["mwo"] = np.stack([np.ascontiguousarray(f(inp["mem_w_o"][l]).reshape(4, 64, 1024).swapaxes(0, 1)) for l in Ls])
    W["w1"] = st("mlp_w1", 8)
    W["w2"] = st("mlp_w2", 32)
    return W


TWO_PI = 2.0 * PI
GA = 1.5957691216057308
GB = 0.044715
QSCALE = 1.0 / float(np.sqrt(96.0))
MSCALE = 1.0 / 8.0
MSHIFT = 8.0
NEG_BIG = -30000.0


def build_program(n_layers=1, debug=()):
    nc = bass.Bass("TRN2", target_bir_lowering=False)
    es = ExitStack()
    b = Builder(nc, es, n_layers, debug)
    P = b.p
    L = n_layers

    xT_d = b.din("xT", [2, 128, 8, T]).ap()
    memT_d = b.din("memT", [128, 8, 256]).ap()
    pos_d = b.din("pos", [2, 128, 16]).ap()
    hinit_d = nc.dram_tensor("sc_h", [L, 4, 1024], BF16, kind="Internal").ap()
    pkt_d = nc.dram_tensor("sc_kt", [L, 4, 96, 2, T], BF16, kind="Internal").ap()
    pve_d = nc.dram_tensor("sc_ve", [L, 4, 128, 16, 65], BF16, kind="Internal").ap()
    pvo_d = nc.dram_tensor("sc_vo", [L, 4, 128, 16, 128], BF16, kind="Internal").ap()
    cst_d = b.din("cst", [128, 1664]).ap()
    vec_d = b.din("vec", [L, 128, 64]).ap()
    row_d = b.din("row", [L, 192 + 64 + 64]).ap()
    w_in_d = b.din("w_in", [L, 128, 8, IN_COLS]).ap()
    w_uq_d = b.din("w_uq", [L, 128, 2, 768]).ap()
    w_ukv_d = b.din("w_ukv", [L, 128, 1024]).ap()
    w_glu_d = b.din("w_glu", [L, 128, 4, 512]).ap()
    w_out_d = b.din("w_out", [L, 128, 8, 1024]).ap()
    mwq_d = b.din("mwq", [L, 128, 8, 256]).ap()
    mwkv_d = b.din("mwkv", [L, 128, 8, 512]).ap()
    mwo_d = b.din("mwo", [L, 64, 4, 1024]).ap()
    w1_d = b.din("w1", [L, 128, 8, 4096]).ap()
    w2_d = b.din("w2", [L, 128, 32, 1024]).ap()
    bm_d = b.din("bm", [L, 4, 128, 1024]).ap()
    cm_d = b.din("cm", [L, 4, 128, 8, 128]).ap()
    lam_d = b.din("lam", [L, 4, 3, 512]).ap()
    outT_d = b.dout("outT", [2, 128, 8, T]).ap()
    okt_d, ove_d, ovo_d, hfin_d = pkt_d, pve_d, pvo_d, hinit_d
    dbg_d = {}

    xT = b.sb("xT_s", [128, 8, T])
    hT = b.sb("hT_s", [128, 8, 1024], BF16)
    wb = [b.sb("wb%d" % i, [128, 8192], BF16) for i in range(2)]
    cst = b.sb("cst_s", [128, 160])
    cbf = b.sb("cbf_s", [128, 1536], BF16)
    vec = b.sb("vec_s", [128, L, 64])
    rowb = b.sb("rowb_s", [128, 320])
    sq = b.sb("sq_s", [128, 8, 512], BF16)
    rstd = b.sb("rstd_s", [128, 512])
    posf = b.sb("pos_s", [128, 16])
    cvec = b.sb("cvec_s", [128, 8])
    AR = 37888
    arena = b.sb("arena", [128, AR], BF16)
    arena_f = arena.bitcast(F32)

    psp = [b.ps("psp%d" % i, [128, 1024]) for i in range(4)]
    psb = [psp[i // 2][:, (i % 2) * 512:(i % 2 + 1) * 512] for i in range(8)]
    psb_bf = [a_.bitcast(BF16) for a_ in psb]

    ident = cbf[:, 0:128]
    ones = cbf[:, 128:256]
    tri = cbf[:, 256:384]
    e127 = cbf[:, 384:512]
    e127n = cbf[:, 512:640]
    cmask = cbf[:, 640:1536]
    ones_f = cst[:, 0:128]
    invf = cst[:, 128:144]
    sp1 = cst[:, 144:145]
    nsp1 = cst[:, 145:146]

    def A16(off_kib, shape):
        n = int(np.prod(shape[1:]))
        o = int(round(off_kib * 512))
        assert o + n <= AR, (off_kib, shape)
        v = arena[:, o:o + n]
        if len(shape) == 3:
            v = v.rearrange("p (a b) -> p a b", a=shape[1])
        elif len(shape) == 4:
            v = v.rearrange("p (a b c) -> p a b c", a=shape[1], b=shape[2])
        return v

    def A32(off_kib, shape):
        n = int(np.prod(shape[1:]))
        o = int(round(off_kib * 256))
        assert (o + n) * 2 <= AR, (off_kib, shape)
        v = arena_f[:, o:o + n]
        if len(shape) == 3:
            v = v.rearrange("p (a b) -> p a b", a=shape[1])
        elif len(shape) == 4:
            v = v.rearrange("p (a b c) -> p a b c", a=shape[1], b=shape[2])
        return v

    uT = A16(0, [128, 4, T])
    cqn = A16(16, [128, 2, T])
    ckvn = A16(24, [128, T])
    krope = A32(28, [128, 16, 32])
    cossin = A32(30, [128, 2, 16, 16])
    ymla = A16(32, [128, 4, T])
    WK = 48

    b.dma(cst[:, 0:128], cst_d[:, 128:256], writes=["cst"])
    b.dma(cst[:, 128:160], cst_d[:, 1536:1568], writes=["cst"])
    b.dma(cbf[:], cst_d[:, 0:1536], writes=["cbf"], eng="pool")
    b.dma(vec[:], vec_d.rearrange("l p k -> p l k"), writes=["vec"])
    def load_x(half):
        b.dma(posf[:], pos_d[half], writes=["pos"])
        for c in range(8):
            b.dma(xT[:, c, :], xT_d[half, :, c, :], writes=[("xT", c, t4) for t4 in range(4)])
    b.v("pool", "memset", cvec[:, 0:1], -SHIFT, writes=["cvec"])
    b.v("pool", "memset", cvec[:, 1:2], -MSHIFT, writes=["cvec"])
    b.v("pool", "memset", cvec[:, 2:3], -PI, writes=["cvec"])
    b.v("pool", "memset", cvec[:, 3:4], EPS, writes=["cvec"])
    nshift = cvec[:, 0:1]
    nmshift = cvec[:, 1:2]
    npi = cvec[:, 2:3]
    epsv = cvec[:, 3:4]

    cos_t = cossin[:, 0]
    sin_t = cossin[:, 1]

    def wrap_pi(dst, src, ti, tf, tm, rk, wk):
        b.v("dve", "tensor_scalar_mul", ti, src, 1.0 / TWO_PI, reads=rk, writes=[wk + "_ti"])
        b.v("dve", "tensor_copy", tf, ti, reads=[wk + "_ti"], writes=[wk + "_tf"])
        b.v("dve", "scalar_tensor_tensor", dst, tf, -TWO_PI, src, ALU.mult, ALU.add, reads=[wk + "_tf"] + rk, writes=[wk])
        b.v("dve", "tensor_scalar", tm, dst, PI, -TWO_PI, ALU.is_gt, ALU.mult, reads=[wk], writes=[wk + "_tm"])
        b.v("dve", "tensor_tensor", dst, dst, tm, ALU.add, reads=[wk, wk + "_tm"], writes=[wk])
        b.v("dve", "tensor_scalar", tm, dst, -PI, TWO_PI, ALU.is_lt, ALU.mult, reads=[wk], writes=[wk + "_tm"])
        b.v("dve", "tensor_tensor", dst, dst, tm, ALU.add, reads=[wk, wk + "_tm"], writes=[wk])

    def rope_tables():
        ang = A32(WK, [128, 16, 16])
        ang2 = A32(WK + 1, [128, 16, 16])
        ti = A32(WK + 2, [128, 16, 16]).bitcast(I32)
        tf = A32(WK + 3, [128, 16, 16])
        tm = A32(WK + 4, [128, 16, 16])
        wr = A32(WK + 5, [128, 16, 16])
        for t in range(16):
            b.v("dve", "tensor_scalar_mul", ang[:, t, :], invf, posf[:, t:t + 1],
                reads=["cst", "pos"], writes=["ropeang"])
        wrap_pi(wr, ang, ti, tf, tm, ["ropeang"], "ropew")
        b.act(sin_t, wr, AF.Sin, reads=["ropew"], writes=["sin"])
        b.v("dve", "tensor_scalar_add", ang2, ang, PI / 2, reads=["ropeang"], writes=["ropeang2"])
        wrap_pi(wr, ang2, ti, tf, tm, ["ropeang2", "sin"], "ropew")
        b.act(cos_t, wr, AF.Sin, reads=["ropew"], writes=["cos"])

    state = {"ps": 0, "wb": 0}
    SQK = [("sq", i) for i in range(8)]

    def next_ps():
        i = state["ps"]
        state["ps"] = (i + 1) % 4
        return i

    def load_w(view, src, key, eng="pool"):
        return b.dma(view, src, writes=[key], key=("w", key), eng=eng)

    def rms_scale(src_keys, nchunk, n, src_fn, ps_i, inv_dim):
        for c in range(nchunk):
            b.mm(psb[ps_i][:, 0:n], ones, src_fn(c), c == 0, c == nchunk - 1,
                 reads=["cbf"] + src_keys, writes=[("ps", ps_i)])
        b.act(rstd[:, 0:n], psb[ps_i][:, 0:n], AF.Sqrt, reads=[("ps", ps_i), "cvec"], writes=["rstd"],
              bias=epsv, scale=inv_dim)
        b.v("dve", "reciprocal", rstd[:, 0:n], rstd[:, 0:n], reads=["rstd"], writes=["rstd"])

    def norm_x(l, gain_col, tok0, nt, hkeys):
        for t in range(nt):
            g4 = (tok0 // 512) + t
            xs = xT[:, :, g4 * 512:(g4 + 1) * 512]
            xk = [("xT", c, g4) for c in range(8)]
            b.act(sq[:], xs, AF.Square, reads=xk, writes=SQK)
            pi = next_ps()
            rms_scale(SQK, 8, 512, lambda c: sq[:, c, :], pi, 1.0 / D)
            for c in range(8):
                b.v("dve", "scalar_tensor_tensor", hT[:, c, t * 512:(t + 1) * 512], xT[:, c, g4 * 512:(g4 + 1) * 512],
                    vec[:, l, gain_col + c:gain_col + c + 1], rstd[:], ALU.mult, ALU.mult,
                    reads=[("xT", c, g4), "vec", "rstd"], writes=[("hT", t)])

    def dbg(name, view, shape, reads, dt=F32):
        if name in b.debug:
            d = b.dout("dbg_" + name, shape, dt).ap()
            b.dma(d, view, reads=reads, writes=[("dbgdone", "dbg_" + name)], key="dbg_" + name)
            b.out_keys.append("dbg_" + name)

    def layer(l):
        V0 = lambda c: vec[:, l, c:c + 1]
        b.dma(rowb[:], row_d[l].partition_broadcast(128), writes=["rowb"])
        gq = rowb[:, 0:96]
        gk = rowb[:, 96:192]
        mgq = rowb[:, 192:256]
        mgk = rowb[:, 256:320]
        b.v("dve", "tensor_scalar_mul", gq, gq, QSCALE, reads=["rowb"], writes=["rowb"])
        b.v("dve", "tensor_scalar_mul", mgq, mgq, MSCALE, reads=["rowb"], writes=["rowb"])

        w_in = wb[0][:, 0:8 * IN_COLS].rearrange("p (k n) -> p k n", k=8)
        load_w(w_in, w_in_d[l], ("wb", 0))
        cqraw = A32(WK, [128, 3, 512])
        for hf in range(2):
            norm_x(l, 0, hf * 1024, 2, None)
            for t in range(2):
                g4 = hf * 2 + t
                tok = slice(g4 * 512, (g4 + 1) * 512)
                hk = [("hT", t)]
                for oc in range(7):
                    pi = next_ps()
                    for k in range(8):
                        b.mm(psb[pi][:], w_in[:, k, oc * 128:(oc + 1) * 128], hT[:, k, t * 512:(t + 1) * 512],
                             k == 0, k == 7, reads=[("wb", 0)] + hk, writes=[("ps", pi)])
                    if oc < 4:
                        b.act(uT[:, oc, tok], psb[pi][:], AF.Copy, reads=[("ps", pi)], writes=[("uT", oc, g4)])
                    else:
                        b.act(cqraw[:, oc - 4, :], psb[pi][:], AF.Copy, reads=[("ps", pi)], writes=[("cqraw", oc - 4)])
                b.act(sq[:, 0:2, :], cqraw[:, 0:2, :], AF.Square, reads=[("cqraw", 0), ("cqraw", 1)], writes=SQK)
                pi = next_ps()
                rms_scale(SQK, 2, 512, lambda c: sq[:, c, :], pi, 1.0 / 256)
                for c in range(2):
                    b.v("dve", "scalar_tensor_tensor", cqn[:, c, tok], cqraw[:, c, :], V0(32 + c), rstd[:],
                        ALU.mult, ALU.mult, reads=[("cqraw", c), "vec", "rstd"], writes=[("cqn", g4)])
                b.act(sq[:, 2, :], cqraw[:, 2, :], AF.Square, reads=[("cqraw", 2)], writes=SQK)
                pi = next_ps()
                rms_scale(SQK, 1, 512, lambda c: sq[:, 2, :], pi, 1.0 / 128)
                b.v("dve", "scalar_tensor_tensor", ckvn[:, tok], cqraw[:, 2, :], V0(34), rstd[:],
                    ALU.mult, ALU.mult, reads=[("cqraw", 2), "vec", "rstd"], writes=[("ckvn", g4)])
                for s4 in range(4):
                    t16 = g4 * 4 + s4
                    pi = next_ps()
                    for k in range(8):
                        b.mm(psb[pi][:, 0:32], hT[:, k, t * 512 + s4 * 128:t * 512 + (s4 + 1) * 128], w_in[:, k, 896:928],
                             k == 0, k == 7, reads=[("wb", 0)] + hk, writes=[("ps", pi)])
                    b.act(krope[:, t16, :], psb[pi][:, 0:32], AF.Copy, reads=[("ps", pi)], writes=[("krope", t16)])
        dbg("uT", uT, [128, 4, T], [("uT", c, g) for c in range(4) for g in range(4)], BF16)
        dbg("cqn", cqn, [128, 2, T], [("cqn", g) for g in range(4)], BF16)
        dbg("ckvn", ckvn, [128, T], [("ckvn", g) for g in range(4)], BF16)
        dbg("krope", krope, [128, 16, 32], [("krope", g) for g in range(16)])
        dbg("cos", cos_t, [128, 16, 16], ["cos"])
        dbg("sin", sin_t, [128, 16, 16], ["sin"])


    hT_flat = hT[:].rearrange("p a b -> p (a b)")
    pkt_s = hT_flat[:, 0:4096].rearrange("p (e t) -> p e t", e=2)
    pve_s = hT_flat[:, 4096:4096 + 1040].rearrange("p (t c) -> p t c", t=16)
    pvo_s = hT_flat[:, 5248:5248 + 2048].rearrange("p (t c) -> p t c", t=16)
    HK = [("hT", 0), ("hT", 1)]
    QT = A16(48, [128, 2, T])
    KT = A16(56, [128, 2, T])
    Ve = A16(64, [128, 16, 65])
    Vo = A16(66.5, [128, 16, 128])
    sq_f = sq[:].rearrange("p a b -> p (a b)").bitcast(F32)
    sqq = sq_f[:, 0:768].rearrange("p (g e d) -> p g e d", g=4, e=2)
    qn = sq_f[:, 768:1536].rearrange("p (g e d) -> p g e d", g=4, e=2)
    SQ6 = [("sq", i) for i in range(6)]
    kraw = wb[1].bitcast(F32)[:, 2048:2816].rearrange("p (g e d) -> p g e d", g=4, e=2)
    KRK = ("wb", 1, "glu")
    qf = A16(70.5, [128, 4, 192]).rearrange("p g (e d) -> p g e d", e=2)
    rta = A32(72, [128, 4, 32]).rearrange("p g (e d) -> p g e d", e=2)
    rtb = A32(72.5, [128, 4, 32]).rearrange("p g (e d) -> p g e d", e=2)
    ssq = A32(73, [128, 4, 2])
    wb1f = wb[1].bitcast(F32)
    rc = wb1f[:, 3584:4096]
    PT = [sq[:, i, :] for i in range(8)]

    def qk_post(src, gain, dstT, ps_bank, g4, srck, tag):
        b.act(sqq, src, AF.Square, reads=srck, writes=SQ6)
        b.v("dve", "tensor_reduce", ssq, sqq, AX.X, ALU.add, reads=SQ6, writes=["ssq"])
        b.act(ssq, ssq, AF.Sqrt, reads=["ssq", "cvec"], writes=["ssq"], bias=epsv, scale=1.0 / 96)
        b.v("dve", "reciprocal", ssq, ssq, reads=["ssq"], writes=["ssq"])
        b.v("dve", "tensor_tensor", qn, src, ssq.unsqueeze(3).to_broadcast([128, 4, 2, 96]), ALU.mult,
            reads=srck + ["ssq"], writes=SQ6)
        b.v("dve", "tensor_tensor", qn, qn, gain.unsqueeze(1).unsqueeze(1).to_broadcast([128, 4, 2, 96]), ALU.mult,
            reads=SQ6 + ["rowb"], writes=SQ6)
        cs = cos_t[:, g4 * 4:(g4 + 1) * 4, :].unsqueeze(2).to_broadcast([128, 4, 2, 16])
        sn = sin_t[:, g4 * 4:(g4 + 1) * 4, :].unsqueeze(2).to_broadcast([128, 4, 2, 16])
        b.v("pool", "tensor_copy", qf[:, :, :, 0:64], qn[:, :, :, 0:64], reads=SQ6, writes=["qf"])
        b.v("pool", "tensor_tensor", rta, qn[:, :, :, 64:80], cs, ALU.mult, reads=SQ6 + ["cos"], writes=["rta"])
        b.v("pool", "tensor_tensor", rtb, qn[:, :, :, 80:96], sn, ALU.mult, reads=SQ6 + ["sin"], writes=["rtb"])
        b.v("pool", "tensor_tensor", qf[:, :, :, 64:80], rta, rtb, ALU.subtract, reads=["rta", "rtb"], writes=["qf"])
        b.v("pool", "tensor_tensor", rta, qn[:, :, :, 80:96], cs, ALU.mult, reads=SQ6 + ["cos", "qf"], writes=["rta"])
        b.v("pool", "tensor_tensor", rtb, qn[:, :, :, 64:80], sn, ALU.mult, reads=SQ6 + ["sin", "qf"], writes=["rtb"])
        b.v("pool", "tensor_tensor", qf[:, :, :, 80:96], rta, rtb, ALU.add, reads=["rta", "rtb"], writes=["qf"])
        for i in range(4):
            for e in range(2):
                slot = e * 4 + i
                b.tr(psb_bf[ps_bank][0:96, slot * 128:(slot + 1) * 128], qf[:, i, e, :], ident,
                     reads=["qf", "cbf"], writes=[("ps", ps_bank)])
        b.act(dstT[0:96, :, g4 * 512:(g4 + 1) * 512],
              psb_bf[ps_bank][0:96, :].rearrange("p (e n) -> p e n", e=2), AF.Copy,
              reads=[("ps", ps_bank)], writes=[(tag, g4)])

    def mla(l, half=1):
        wuq = wb[1][:, 0:1536].rearrange("p (k n) -> p k n", k=2)
        wukv = wb[1][:, 1536:2560]
        load_w(wuq, w_uq_d[l], ("wb", 1))
        load_w(wukv, w_ukv_d[l], ("wb", 1))
        gq = rowb[:, 0:96]
        gk = rowb[:, 96:192]
        b.v("pool", "memset", Ve[:, :, 64:65], 1.0, writes=["Ve"])
        b.v("pool", "memset", Vo[:, :, 0:64], 0.0, writes=["Vo"])
        b.v("pool", "memset", Vo[:, :, 0:1], 1.0, writes=["Vo"])
        for hp in range(4):
            if half == 1:
                b.dma(pkt_s[0:96], pkt_d[l, hp], reads=[("skv", l, hp)], writes=HK, key="pk")
                b.dma(pve_s, pve_d[l, hp], reads=[("skv", l, hp)], writes=HK, key="pk")
                b.dma(pvo_s, pvo_d[l, hp], reads=[("skv", l, hp)], writes=HK, key="pk")
            qps4 = psp[0][:, :].rearrange("p (g n) -> p g n", g=4)
            kvps4 = psp[1][:, :].rearrange("p (g n) -> p g n", g=4)
            PQ = [("ps", 0), ("ps", 1)]
            PKV = [("ps", 2), ("ps", 3)]
            for g4 in range(4):
                for i in range(4):
                    t16 = g4 * 4 + i
                    tok = slice(t16 * 128, (t16 + 1) * 128)
                    for k in range(2):
                        b.mm(qps4[:, i, 0:192], cqn[:, k, tok], wuq[:, k, hp * 192:(hp + 1) * 192], k == 0, k == 1,
                             reads=[("cqn", g4), ("wb", 1)], writes=[("ps", i // 2)])
                    b.mm(kvps4[:, i, :], ckvn[:, tok], wukv[:, hp * 256:(hp + 1) * 256], True, True,
                         reads=[("ckvn", g4), ("wb", 1)], writes=[("ps", 2 + i // 2)])
                qk_post(qps4[:, :, 0:192].rearrange("p g (e d) -> p g e d", e=2), gq, QT, 6, g4, PQ, "QT")
                kv4 = kvps4.rearrange("p g (e d) -> p g e d", e=2)
                b.act(kraw[:, :, :, 0:64], kv4[:, :, :, 0:64], AF.Copy, reads=PKV, writes=[KRK])
                b.v("pool", "tensor_copy", kraw[:, :, :, 64:96],
                    krope[:, g4 * 4:(g4 + 1) * 4, :].unsqueeze(2).to_broadcast([128, 4, 2, 32]),
                    reads=[("krope", g4 * 4 + i) for i in range(4)], writes=[KRK])
                b.act(Ve[:, g4 * 4:(g4 + 1) * 4, 0:64], kvps4[:, :, 64:128], AF.Copy, reads=PKV, writes=["Ve"])
                b.act(Vo[:, g4 * 4:(g4 + 1) * 4, 64:128], kvps4[:, :, 192:256], AF.Copy, reads=PKV, writes=["Vo"])
                qk_post(kraw, gk, KT, 7, g4, [KRK], "KT")
            QK = [("QT", g) for g in range(4)]
            KK = [("KT", g) for g in range(4)]
            if half == 0:
                b.dma(okt_d[l, hp], KT[0:96], reads=KK, writes=[("skv", l, hp)], key="okv")
                b.dma(ove_d[l, hp], Ve, reads=["Ve"], writes=[("skv", l, hp)], key="okv")
                b.dma(ovo_d[l, hp], Vo, reads=["Vo"], writes=[("skv", l, hp)], key="okv")
            for e in range(2):
                Vt = Ve if e == 0 else Vo
                vk = "Ve" if e == 0 else "Vo"
                pV = pve_s if e == 0 else pvo_s
                M = 65 if e == 0 else 128
                for qt in range(4):
                    blocks = ([("p", kb) for kb in range(16)] if half == 1 else []) + [("o", kb) for kb in range(4 * qt + 4)]
                    po = 2 + (qt % 2)
                    SB = [0, 1, 5]

                    def emit_qk(bi):
                        kind, kb = blocks[bi]
                        pi = SB[bi % 3]
                        kts = (pkt_s if kind == "p" else KT)[0:96, e, kb * 128:(kb + 1) * 128]
                        kk = HK if kind == "p" else [("KT", kb // 4)]
                        b.mm(psb[pi][:], kts, QT[0:96, e, qt * 512:(qt + 1) * 512], True, True,
                             reads=kk + [("QT", qt)], writes=[("ps", pi)])

                    emit_qk(0)
                    if len(blocks) > 1:
                        emit_qk(1)
                    for bi, (kind, kb) in enumerate(blocks):
                        pi = SB[bi % 3]
                        pt = PT[bi % 4]
                        ptk = ("sq", bi % 4)
                        b.act(pt, psb[pi][:], AF.Exp, reads=[("ps", pi), "cvec"], writes=[ptk],
                              bias=nshift, scale=1.0)
                        if kind == "o" and kb >= 4 * qt:
                            o = kb - 4 * qt
                            b.v("dve", "tensor_tensor", pt, pt, cmask[:, (3 - o) * 128:(3 - o) * 128 + 512], ALU.mult,
                                reads=[ptk, "cbf"], writes=[ptk])
                        if bi + 2 < len(blocks):
                            emit_qk(bi + 2)
                        vs = (pV if kind == "p" else Vt)[:, kb, 0:M]
                        b.mm(psb[po][0:M, :], vs, pt, bi == 0, bi == len(blocks) - 1,
                             reads=(HK if kind == "p" else [vk]) + [ptk], writes=[("ps", po)])
                    dr = 64 if e == 0 else 0
                    b.v("dve", "reciprocal", rc[dr:dr + 1, :], psb[po][dr:dr + 1, :], reads=[("ps", po)], writes=["rc"])
                    if e == 0:
                        b.mm(psb[4][0:64, :], ones_f[64:65, 0:64], rc[64:65, :], True, True, reads=["cst", "rc"], writes=[("ps", 4)])
                        rows = slice(0, 64)
                    else:
                        b.mm(psb[4][:, :], ones_f[0:1, :], rc[0:1, :], True, True, reads=["cst", "rc"], writes=[("ps", 4)])
                        rows = slice(64, 128)
                    b.act(rstd[rows, :], psb[4][rows, :], AF.Copy, reads=[("ps", 4)], writes=["rstd"])
                    b.v("dve", "tensor_tensor", ymla[rows, hp, qt * 512:(qt + 1) * 512], psb[po][rows, :], rstd[rows, :], ALU.mult,
                        reads=[("ps", po), "rstd"], writes=[("ymla", hp, qt)])
        dbg("ymla", ymla, [128, 4, T], [("ymla", h, q) for h in range(4) for q in range(4)], BF16)
        dbg("QT", QT, [128, 2, T], [("QT", g) for g in range(4)], BF16)
        dbg("KT", KT, [128, 2, T], [("KT", g) for g in range(4)], BF16)
        dbg("Vo", Vo, [128, 16, 128], ["Vo"], BF16)


    dummy = b.sb("dummy_s", [128, 2])

    def barrier(keys):
        b.v("pool", "memset", dummy[:, 0:1], 0.0, writes=list(keys))

    MIXER_KEYS = ([("uT", c, g) for c in range(4) for g in range(4)] + [("cqn", g) for g in range(4)]
                  + [("ckvn", g) for g in range(4)] + [("krope", g) for g in range(16)] + ["cos", "sin"]
                  + [("ymla", h, q) for h in range(4) for q in range(4)])
    ATT_KEYS = ([("QT", g) for g in range(4)] + [("KT", g) for g in range(4)]
                + ["Ve", "Vo", "sqq", "qn", "qf", "rta", "rtb", "ssq", "kraw"])
    PROJ_KEYS = [("cqraw", i) for i in range(3)] + ["ropeang", "ropeang2", "ropew", "ropew_ti", "ropew_tf", "ropew_tm"]

    hT_f = hT[:].rearrange("p a b -> p (a b)").bitcast(F32)
    SLOTS = [{n: A32(48 + 2 * i, [128, 512]) for i, n in enumerate("ABCDEFGHIJKMN")},
             dict([(n, A32(32 + 2 * i, [128, 512])) for i, n in enumerate("ABCDEFGH")]
                  + [(n, hT_f[:, i * 512:(i + 1) * 512]) for i, n in enumerate("IJKMN")])]
    SSM_KEYS = ["s%d%s" % (ch, n) for ch in range(2) for n in "ABCDEFGHIJKMN"] + \
               ["Xr0", "Xi0", "Xr1", "Xi1"]
    SSM_BANKS = [(0, 1, 2, 3), (4, 5, 6, 7)]

    def ssm_prep(l, c, ch, half):
        slot = SLOTS[ch]
        sk = lambda n: "s%d%s" % (ch, n)
        def tt(eng, out, a, bb_, op):
            b.v(eng, "tensor_tensor", slot[out], slot[a], slot[bb_], op, reads=[sk(a), sk(bb_)], writes=[sk(out)])
        def wrap(dst, src, ti, tf, tm):
            d_, s_, ti_, tf_, tm_ = slot[dst], slot[src], slot[ti].bitcast(I32), slot[tf], slot[tm]
            kd, ks, kti, ktf, ktm = sk(dst), sk(src), sk(ti), sk(tf), sk(tm)
            b.v("dve", "tensor_scalar_mul", ti_, s_, 1.0 / TWO_PI, reads=[ks], writes=[kti])
            b.v("dve", "tensor_copy", tf_, ti_, reads=[kti], writes=[ktf])
            b.v("dve", "scalar_tensor_tensor", d_, tf_, -TWO_PI, s_, ALU.mult, ALU.add, reads=[ktf, ks], writes=[kd])
            b.v("dve", "tensor_scalar", tm_, d_, PI, -TWO_PI, ALU.is_gt, ALU.mult, reads=[kd], writes=[ktm])
            b.v("dve", "tensor_tensor", d_, d_, tm_, ALU.add, reads=[kd, ktm], writes=[kd])
            b.v("dve", "tensor_scalar", tm_, d_, -PI, TWO_PI, ALU.is_lt, ALU.mult, reads=[kd], writes=[ktm])
            b.v("dve", "tensor_tensor", d_, d_, tm_, ALU.add, reads=[kd, ktm], writes=[kd])
        def actf(out, a, func, **kw):
            b.act(slot[out], slot[a], func, reads=[sk(a), "cst"], writes=[sk(out)], **kw)
        for i, n in enumerate("KMN"):
            b.dma(slot[n], lam_d[l, c, i].partition_broadcast(128), writes=[sk(n)])
        actf("N", "N", AF.Exp)
        tt("dve", "G", "K", "N", ALU.mult)
        tt("dve", "H", "M", "N", ALU.mult)
        actf("A", "G", AF.Exp, scale=sp1)
        actf("C", "G", AF.Exp, scale=nsp1)
        b.v("dve", "tensor_scalar_mul", slot["I"], slot["H"], sp1, reads=[sk("H"), "cst"], writes=[sk("I")])
        wrap("J", "I", "N", "E", "F")
        actf("B", "J", AF.Sin)
        b.v("dve", "tensor_scalar_add", slot["I"], slot["I"], PI / 2, reads=[sk("I")], writes=[sk("I")])
        wrap("J", "I", "N", "E", "F")
        actf("D", "J", AF.Sin)
        tt("dve", "E", "A", "D", ALU.mult)
        tt("dve", "F", "A", "B", ALU.mult)
        tt("dve", "A", "C", "D", ALU.mult)
        b.v("dve", "scalar_tensor_tensor", slot["B"], slot["C"], -1.0, slot["B"], ALU.mult, ALU.mult,
            reads=[sk("C"), sk("B")], writes=[sk("B")])
        wrap("J", "H", "N", "C", "D")
        actf("I", "J", AF.Sin)
        b.v("dve", "tensor_scalar_add", slot["H"], slot["H"], PI / 2, reads=[sk("H")], writes=[sk("H")])
        wrap("J", "H", "N", "C", "D")
        actf("N", "J", AF.Sin)
        actf("C", "G", AF.Exp)
        tt("dve", "D", "C", "N", ALU.mult)
        b.v("dve", "tensor_scalar_add", slot["D"], slot["D"], -1.0, reads=[sk("D")], writes=[sk("D")])
        tt("dve", "I", "C", "I", ALU.mult)
        tt("dve", "J", "K", "K", ALU.mult)
        tt("dve", "N", "M", "M", ALU.mult)
        tt("dve", "J", "J", "N", ALU.add)
        b.v("dve", "reciprocal", slot["J"], slot["J"], reads=[sk("J")], writes=[sk("J")])
        tt("dve", "C", "D", "K", ALU.mult)
        tt("dve", "N", "I", "M", ALU.mult)
        tt("dve", "C", "C", "N", ALU.add)
        tt("dve", "C", "C", "J", ALU.mult)
        tt("dve", "N", "I", "K", ALU.mult)
        tt("dve", "G", "D", "M", ALU.mult)
        tt("dve", "N", "N", "G", ALU.subtract)
        tt("dve", "N", "N", "J", ALU.mult)
        tt("dve", "D", "A", "C", ALU.mult)
        tt("dve", "G", "B", "N", ALU.mult)
        tt("dve", "D", "D", "G", ALU.subtract)
        tt("dve", "I", "A", "N", ALU.mult)
        tt("dve", "G", "B", "C", ALU.mult)
        tt("dve", "I", "I", "G", ALU.add)
        cx = dict(slot=slot, sk=sk, ch=ch, c=c,
                  Tre=slot["E"], Tim=slot["F"], Tpr=slot["D"], Tpi=slot["I"],
                  TK=[sk("E"), sk("F"), sk("D"), sk("I")],
                  Bm=slot["A"].bitcast(BF16),
                  Cm=slot["B"].bitcast(BF16).rearrange("p (k n) -> p k n", k=8),
                  X=slot["C"].bitcast(BF16),
                  Sb=[slot["G"].bitcast(BF16), slot["H"].bitcast(BF16)], SbK=[sk("G"), sk("H")],
                  ST=slot["J"].bitcast(BF16).rearrange("p (k n) -> p k n", k=8))
        load_w(cx["Bm"], bm_d[l, c], sk("A"))
        load_w(cx["Cm"], cm_d[l, c], sk("B"))
        b.v("pool", "memset", cx["Sb"][1], 0.0, writes=[cx["SbK"][1]])
        if half == 1:
            b.dma(cx["Sb"][1][127:128, :], hinit_d[l, c:c + 1, :], reads=[("sh", l, c)], writes=[cx["SbK"][1]])
        return cx

    def ssm_tile(l, cx, k, stage):
        slot, sk, ch, c = cx["slot"], cx["sk"], cx["ch"], cx["c"]
        Tre, Tim, Tpr, Tpi, TK = cx["Tre"], cx["Tim"], cx["Tpr"], cx["Tpi"], cx["TK"]
        Bm, Cm, X, Sb, SbK, ST = cx["Bm"], cx["Cm"], cx["X"], cx["Sb"], cx["SbK"], cx["ST"]
        b0, b1, tb, yb = SSM_BANKS[ch]
        Xr, Xi = "Xr%d" % ch, "Xi%d" % ch
        tok = slice(k * 128, (k + 1) * 128)
        g4 = k // 4
        cur, prv = k % 2, 1 - (k % 2)
        K_, M_, N_ = slot["K"], slot["M"], slot["N"]
        if stage == 2:
            return ssm_tile2(l, cx, k)
        if stage == 3:
            return ssm_tile3(l, cx, k)
        b.mm(psb[b0][:], uT[:, c, tok], Bm[:, 0:512], True, True, reads=[("uT", c, g4), sk("A")], writes=[("ps", b0)])
        b.mm(psb[b1][:], uT[:, c, tok], Bm[:, 512:1024], True, True, reads=[("uT", c, g4), sk("A")], writes=[("ps", b1)])
        b.v("dve", "tensor_tensor", K_, psb[b0][:], Tpr, ALU.mult, reads=[("ps", b0)] + TK, writes=[sk("K")])
        b.v("dve", "tensor_tensor", M_, psb[b1][:], Tpi, ALU.mult, reads=[("ps", b1)] + TK, writes=[sk("M")])
        b.v("pool", "tensor_tensor", X[:, 0:512], K_, M_, ALU.subtract, reads=[sk("K"), sk("M")], writes=[Xr])
        b.v("dve", "tensor_tensor", N_, psb[b1][:], Tpr, ALU.mult, reads=[("ps", b1)] + TK, writes=[sk("N")])
        b.v("dve", "tensor_tensor", K_, psb[b0][:], Tpi, ALU.mult, reads=[("ps", b0)] + TK + [Xr], writes=[sk("K")])
        b.v("pool", "tensor_tensor", X[:, 512:1024], N_, K_, ALU.add, reads=[sk("K"), sk("N")], writes=[Xi])

    def ssm_tile2(l, cx, k):
        slot, sk, ch, c = cx["slot"], cx["sk"], cx["ch"], cx["c"]
        Tre, Tim, Tpr, Tpi, TK = cx["Tre"], cx["Tim"], cx["Tpr"], cx["Tpi"], cx["TK"]
        Bm, Cm, X, Sb, SbK, ST = cx["Bm"], cx["Cm"], cx["X"], cx["Sb"], cx["SbK"], cx["ST"]
        b0, b1, tb, yb = SSM_BANKS[ch]
        Xr, Xi = "Xr%d" % ch, "Xi%d" % ch
        cur, prv = k % 2, 1 - (k % 2)
        K_, M_, N_ = slot["K"], slot["M"], slot["N"]
        b.mm(psb[b0][:], tri, X[:, 0:512], True, False, reads=["cbf", Xr], writes=[("ps", b0)])
        b.mm(psb[b0][:], e127, Sb[prv][:, 0:512], False, True, reads=["cbf", SbK[prv]], writes=[("ps", b0)])
        b.mm(psb[b1][:], tri, X[:, 512:1024], True, False, reads=["cbf", Xi], writes=[("ps", b1)])
        b.mm(psb[b1][:], e127n, Sb[prv][:, 512:1024], False, True, reads=["cbf", SbK[prv]], writes=[("ps", b1)])
        b.v("dve", "tensor_tensor", K_, psb[b0][:], Tre, ALU.mult, reads=[("ps", b0)] + TK + [Xi], writes=[sk("K")])
        b.v("dve", "tensor_tensor", M_, psb[b1][:], Tim, ALU.mult, reads=[("ps", b1)] + TK + [Xr], writes=[sk("M")])
        b.v("pool", "tensor_tensor", Sb[cur][:, 0:512], K_, M_, ALU.subtract, reads=[sk("K"), sk("M")], writes=[SbK[cur]])
        b.v("dve", "scalar_tensor_tensor", N_, psb[b1][:], -1.0, Tre, ALU.mult, ALU.mult,
            reads=[("ps", b1)] + TK + [Xi], writes=[sk("N")])
        b.v("dve", "scalar_tensor_tensor", K_, psb[b0][:], -1.0, Tim, ALU.mult, ALU.mult,
            reads=[("ps", b0)] + TK + [SbK[cur]], writes=[sk("K")])
        b.v("pool", "tensor_tensor", Sb[cur][:, 512:1024], N_, K_, ALU.add,
            reads=[sk("K"), sk("N")], writes=[SbK[cur]])

    def ssm_tile3(l, cx, k):
        slot, sk, ch, c = cx["slot"], cx["sk"], cx["ch"], cx["c"]
        Cm, Sb, SbK, ST = cx["Cm"], cx["Sb"], cx["SbK"], cx["ST"]
        b0, b1, tb, yb = SSM_BANKS[ch]
        g4 = k // 4
        cur = k % 2
        for blk in range(8):
            b.tr(psb_bf[tb][:, blk * 128:(blk + 1) * 128], Sb[cur][:, blk * 128:(blk + 1) * 128], ident,
                 reads=[SbK[cur], "cbf"], writes=[("ps", tb)])
        b.act(ST, psb_bf[tb][:, :].rearrange("p (k n) -> p k n", k=8), AF.Copy, reads=[("ps", tb)], writes=[sk("J")])
        for blk in range(8):
            b.mm(psb[yb][:, (k % 4) * 128:(k % 4 + 1) * 128], Cm[:, blk, :], ST[:, blk, :], blk == 0, blk == 7,
                 reads=[sk("B"), sk("J")], writes=[("ps", yb)])
        if k % 4 == 3:
            t4 = slice(g4 * 512, (g4 + 1) * 512)
            b.v("dve", "scalar_tensor_tensor", uT[:, c, t4], uT[:, c, t4], vec[:, l, 35 + c:36 + c], psb[yb][:],
                ALU.mult, ALU.add, reads=[("uT", c, g4), "vec", ("ps", yb)], writes=[("uT", c, g4)])

    def ssm(l, half=1):
        for pair in range(2):
            cxs = [ssm_prep(l, 2 * pair + ch, ch, half) for ch in range(2)]
            for k in range(16):
                for stage in (1, 2, 3):
                    for cx in cxs:
                        ssm_tile(l, cx, k, stage)
            if half == 0:
                for cx in cxs:
                    c = cx["c"]
                    b.dma(hfin_d[l, c:c + 1, :], cx["Sb"][1][127:128, :], reads=[cx["SbK"][1]], writes=[("sh", l, c)], key="hfin")
        dbg("yssm", uT, [128, 4, T], [("uT", c, g) for c in range(4) for g in range(4)], BF16)


    def group_norm(buf, keyf, gcol, l, g4):
        tok = slice(g4 * 512, (g4 + 1) * 512)
        ks = [keyf(c, g4) for c in range(4)]
        b.act(sq[:, 0:4, :], buf[:, :, tok], AF.Square, reads=ks, writes=SQK)
        pi = next_ps()
        rms_scale(SQK, 4, 512, lambda c: sq[:, c, :], pi, 1.0 / 512)
        for c in range(4):
            b.v("dve", "scalar_tensor_tensor", buf[:, c, tok], buf[:, c, tok], vec[:, l, gcol + c:gcol + c + 1], rstd[:],
                ALU.mult, ALU.mult, reads=[keyf(c, g4), "vec", "rstd"], writes=[keyf(c, g4)])

    def ssm_post(l):
        wglu = wb[1][:, 4096:6144].rearrange("p (k n) -> p k n", k=4)
        load_w(wglu, w_glu_d[l], ("wb", 1, "glu"))
        tK = A32(48, [128, 512])
        tM = A32(50, [128, 512])
        gates = A32(52, [128, 4, 512])
        for g4 in range(4):
            tok = slice(g4 * 512, (g4 + 1) * 512)
            for c in range(4):
                y = uT[:, c, tok]
                yk = ("uT", c, g4)
                b.v("dve", "tensor_tensor", tK, y, y, ALU.mult, reads=[yk], writes=["gK"])
                b.v("dve", "tensor_scalar", tK, tK, GB, 1.0, ALU.mult, ALU.add, reads=["gK"], writes=["gK"])
                b.v("dve", "tensor_tensor", tK, tK, y, ALU.mult, reads=["gK", yk], writes=["gK"])
                b.act(tM, tK, AF.Sigmoid, reads=["gK"], writes=["gM"], scale=GA)
                b.v("dve", "tensor_tensor", y, y, tM, ALU.mult, reads=["gM", yk], writes=[yk])
            for oc in range(4):
                pi = 4 + oc
                for k in range(4):
                    b.mm(psb[pi][:], wglu[:, k, oc * 128:(oc + 1) * 128], uT[:, k, tok], k == 0, k == 3,
                         reads=[("wb", 1, "glu"), ("uT", k, g4)], writes=[("ps", pi)])
                b.act(gates[:, oc, :], psb[pi][:], AF.Sigmoid, reads=[("ps", pi), "vec"], writes=[("gate", oc)],
                      bias=vec[:, l, 39 + oc:40 + oc], scale=1.0)
            for oc in range(4):
                b.v("dve", "tensor_tensor", uT[:, oc, tok], uT[:, oc, tok], gates[:, oc, :], ALU.mult,
                    reads=[("uT", oc, g4), ("gate", oc)], writes=[("uT", oc, g4)])
            group_norm(uT, lambda c, g: ("uT", c, g), 43, l, g4)
        dbg("yssmn", uT, [128, 4, T], [("uT", c, g) for c in range(4) for g in range(4)], BF16)

    def mix_out(l):
        wout = wb[0][:, :].rearrange("p (k n) -> p k n", k=8)
        load_w(wout, w_out_d[l], ("wb", 0))
        for g4 in range(4):
            tok = slice(g4 * 512, (g4 + 1) * 512)
            group_norm(ymla, lambda c, g: ("ymla", c, g), 47, l, g4)
            for oc in range(8):
                pi = next_ps()
                for k in range(8):
                    src = uT[:, k, tok] if k < 4 else ymla[:, k - 4, tok]
                    sk_ = ("uT", k, g4) if k < 4 else ("ymla", k - 4, g4)
                    b.mm(psb[pi][:], wout[:, k, oc * 128:(oc + 1) * 128], src, k == 0, k == 7,
                         reads=[("wb", 0), sk_], writes=[("ps", pi)])
                b.v("dve", "tensor_tensor", xT[:, oc, tok], xT[:, oc, tok], psb[pi][:], ALU.add,
                    reads=[("xT", oc, g4), ("ps", pi)], writes=[("xT", oc, g4)])
        dbg("x1", xT[:], [128, 8, T], [("xT", c, g) for c in range(8) for g in range(4)])

    CR_KEYS = ["mhT", "KmT", "Vm", "memx", "mtmp", "mss", "mkn", ("QmT", 0), ("QmT", 1), ("QmT", 2), ("QmT", 3)] + \
              [("omT", h, q) for h in range(4) for q in range(4)]

    def cross(l):
        mhT = A16(0, [128, 8, 256])
        KmT = A16(4, [128, 4, 256])
        Vm = A16(6, [128, 2, 4, 65])
        QmT = A16(8, [128, 4, T])
        omT = A16(24, [128, 4, T])
        memx = A32(40, [128, 8, 256])
        mtmp = A32(48, [128, 4, 64])
        mss = A32(49, [128, 4])
        mkn = A16(50, [128, 4, 64])
        mgq = rowb[:, 192:256]
        mgk = rowb[:, 256:320]
        mwkv = wb[1][:, 0:4096].rearrange("p (k n) -> p k n", k=8)
        mwq = wb[1][:, 4096:6144].rearrange("p (k n) -> p k n", k=8)
        load_w(mwkv, mwkv_d[l], ("wb", 1))
        load_w(mwq, mwq_d[l], ("wb", 1, "glu"))
        b.dma(memx, memT_d, writes=["memx"])
        sqm = sq[:, :, 0:256]
        b.act(sqm, memx, AF.Square, reads=["memx"], writes=SQK)
        pi = next_ps()
        rms_scale(SQK, 8, 256, lambda c: sq[:, c, 0:256], pi, 1.0 / D)
        for c in range(8):
            b.v("dve", "scalar_tensor_tensor", mhT[:, c, :], memx[:, c, :], vec[:, l, 24 + c:25 + c], rstd[:, 0:256],
                ALU.mult, ALU.mult, reads=["memx", "vec", "rstd"], writes=["mhT"])
        b.v("pool", "memset", Vm[:, :, :, 64:65], 1.0, writes=["Vm"])

        def head_norm(src, srck, gain, n_slots_bank, slot_fn):
            b.act(mtmp, src, AF.Square, reads=srck, writes=["mtmp"])
            b.v("dve", "tensor_reduce", mss, mtmp, AX.X, ALU.add, reads=["mtmp"], writes=["mss"])
            b.act(mss, mss, AF.Sqrt, reads=["mss", "cvec"], writes=["mss"], bias=epsv, scale=1.0 / 64)
            b.v("dve", "reciprocal", mss, mss, reads=["mss"], writes=["mss"])
            b.v("dve", "tensor_tensor", mtmp, src, mss.unsqueeze(2).to_broadcast([128, 4, 64]), ALU.mult,
                reads=srck + ["mss"], writes=["mtmp"])
            b.v("dve", "tensor_tensor", mkn, mtmp, gain.unsqueeze(1).to_broadcast([128, 4, 64]), ALU.mult,
                reads=["mtmp", "rowb"], writes=["mkn"])
            for h in range(4):
                sl = slot_fn(h)
                b.tr(psb_bf[n_slots_bank][0:64, sl * 128:(sl + 1) * 128], mkn[:, h, :], ident,
                     reads=["mkn", "cbf"], writes=[("ps", n_slots_bank)])

        for mt in range(2):
            pi = next_ps()
            for k in range(8):
                b.mm(psb[pi][:], mhT[:, k, mt * 128:(mt + 1) * 128], mwkv[:, k, :], k == 0, k == 7,
                     reads=["mhT", ("wb", 1)], writes=[("ps", pi)])
            kvv = psb[pi][:].rearrange("p (h d) -> p h d", h=4)
            b.act(Vm[:, mt, :, 0:64], kvv[:, :, 64:128], AF.Copy, reads=[("ps", pi)], writes=["Vm"])
            head_norm(kvv[:, :, 0:64], [("ps", pi)], mgk, 6, lambda h: h * 2 + mt)
        b.act(KmT[0:64, :, :], psb_bf[6][0:64, :].rearrange("p (h n) -> p h n", h=4), AF.Copy,
              reads=[("ps", 6)], writes=["KmT"])

        for hf in range(2):
            norm_x(l, 8, hf * 1024, 2, None)
            for t8 in range(8):
                t16 = hf * 8 + t8
                pi = next_ps()
                for k in range(8):
                    b.mm(psb[pi][:, 0:256], hT[:, k, t8 * 128:(t8 + 1) * 128], mwq[:, k, :], k == 0, k == 7,
                         reads=[("hT", t8 // 4), ("wb", 1, "glu")], writes=[("ps", pi)])
                head_norm(psb[pi][:, 0:256].rearrange("p (h d) -> p h d", h=4), [("ps", pi)], mgq, 7,
                          lambda h: h * 2 + (t16 % 2))
                if t16 % 2 == 1:
                    b.act(QmT[0:64, :, (t16 // 2) * 256:(t16 // 2 + 1) * 256],
                          psb_bf[7][0:64, :].rearrange("p (h n) -> p h n", h=4), AF.Copy,
                          reads=[("ps", 7)], writes=[("QmT", t16 // 4)])
        for h in range(4):
            for qt in range(4):
                po = 2 + (qt % 2)
                for kb in range(2):
                    b.mm(psb[kb][:], KmT[0:64, h, kb * 128:(kb + 1) * 128], QmT[0:64, h, qt * 512:(qt + 1) * 512], True, True,
                         reads=["KmT", ("QmT", qt)], writes=[("ps", kb)])
                for kb in range(2):
                    pi = kb
                    pt = PT[kb + 2 * (qt % 2)]
                    b.act(pt, psb[pi][:], AF.Exp, reads=[("ps", pi), "cvec"], writes=[("sq", kb + 2 * (qt % 2))], bias=nmshift, scale=1.0)
                    b.mm(psb[po][0:65, :], Vm[:, kb, h, :], pt, kb == 0, kb == 1, reads=["Vm", ("sq", kb + 2 * (qt % 2))], writes=[("ps", po)])
                b.v("dve", "reciprocal", rc[64:65, :], psb[po][64:65, :], reads=[("ps", po)], writes=["rc"])
                b.mm(psb[4][0:64, :], ones_f[64:65, 0:64], rc[64:65, :], True, True, reads=["cst", "rc"], writes=[("ps", 4)])
                b.act(rstd[0:64, :], psb[4][0:64, :], AF.Copy, reads=[("ps", 4)], writes=["rstd"])
                b.v("dve", "tensor_tensor", omT[0:64, h, qt * 512:(qt + 1) * 512], psb[po][0:64, :], rstd[0:64, :], ALU.mult,
                    reads=[("ps", po), "rstd"], writes=[("omT", h, qt)])
        mwo = wb[0][0:64, 0:4096].rearrange("p (h n) -> p h n", h=4)
        load_w(mwo, mwo_d[l], ("wb", 0))
        for g4 in range(4):
            tok = slice(g4 * 512, (g4 + 1) * 512)
            for oc in range(8):
                pi = next_ps()
                for h in range(4):
                    b.mm(psb[pi][:], mwo[:, h, oc * 128:(oc + 1) * 128], omT[0:64, h, tok], h == 0, h == 3,
                         reads=[("wb", 0), ("omT", h, g4)], writes=[("ps", pi)])
                b.v("dve", "tensor_tensor", xT[:, oc, tok], xT[:, oc, tok], psb[pi][:], ALU.add,
                    reads=[("xT", oc, g4), ("ps", pi)], writes=[("xT", oc, g4)])
        dbg("x2", xT[:], [128, 8, T], [("xT", c, g) for c in range(8) for g in range(4)])

    MLP_KEYS = [("hid", j, t) for j in range(32) for t in range(2)] + [("rt", i) for i in range(4)]

    WBH = [wb[0][:, 0:4096], wb[0][:, 4096:8192], wb[1][:, 0:4096], wb[1][:, 4096:8192]]
    WBH_KEYS = [("wbh", i) for i in range(4)] + [("wb", 0), ("wb", 1), ("wb", 1, "glu"), "rc"]

    def mlp(l):
        hid = A16(0, [128, 32, 1024])
        rt = [A16(64, [128, 512]), A16(65, [128, 512]), A16(66, [128, 512]), A16(67, [128, 512])]
        n = 0
        wi = 0
        for hf in range(2):
            norm_x(l, 16, hf * 1024, 2, None)
            for jg in range(8):
                ws = wi % 4
                wi += 1
                w1p = WBH[ws].rearrange("p (k n) -> p k n", k=8)
                b.dma(w1p, w1_d[l][:, :, jg * 512:(jg + 1) * 512], writes=[("wbh", ws)], key=("wh", ws), eng="pool")
                for jj in range(4):
                    j = jg * 4 + jj
                    for t in range(2):
                        pi = next_ps()
                        for k in range(8):
                            b.mm(psb[pi][:], w1p[:, k, jj * 128:(jj + 1) * 128], hT[:, k, t * 512:(t + 1) * 512], k == 0, k == 7,
                                 reads=[("wbh", ws), ("hT", t)], writes=[("ps", pi)])
                        r = rt[n % 4]
                        b.act(r, psb[pi][:], AF.Relu, reads=[("ps", pi)], writes=[("rt", n % 4)])
                        b.v("dve", "tensor_tensor", hid[:, j, t * 512:(t + 1) * 512], r, r, ALU.mult,
                            reads=[("rt", n % 4)], writes=[("hid", j, t)])
                        n += 1
            for oc in range(8):
                ws = wi % 4
                wi += 1
                w2p = WBH[ws].rearrange("p (j n) -> p j n", j=32)
                b.dma(w2p, w2_d[l][:, :, oc * 128:(oc + 1) * 128], writes=[("wbh", ws)], key=("wh", ws), eng="pool")
                for t in range(2):
                    g4 = hf * 2 + t
                    tok = slice(g4 * 512, (g4 + 1) * 512)
                    pi = next_ps()
                    for j in range(32):
                        b.mm(psb[pi][:], w2p[:, j, :], hid[:, j, t * 512:(t + 1) * 512], j == 0, j == 31,
                             reads=[("wbh", ws), ("hid", j, t)], writes=[("ps", pi)])
                    b.v("dve", "tensor_tensor", xT[:, oc, tok], xT[:, oc, tok], psb[pi][:], ALU.add,
                        reads=[("xT", oc, g4), ("ps", pi)], writes=[("xT", oc, g4)])

    def run_layer(l, half=1, stop_after=None):
        rope_tables()
        layer(l)
        YMK = [("ymla", h, q) for h in range(4) for q in range(4)]
        barrier(PROJ_KEYS + SSM_KEYS + HK + YMK)
        ssm(l, half)
        if stop_after == "ssm":
            return
        barrier(SSM_KEYS + HK + YMK + ["gK", "gM"] + [("gate", i) for i in range(4)])
        ssm_post(l)
        if stop_after == "ssm_post":
            return
        barrier(SSM_KEYS + HK + YMK + ["gK", "gM"] + [("gate", i) for i in range(4)] + ATT_KEYS + SQK)
        mla(l, half)
        mix_out(l)
        if stop_after == "mix":
            return
        barrier(MIXER_KEYS + ATT_KEYS + CR_KEYS)
        cross(l)
        if stop_after == "cross":
            return
        barrier(CR_KEYS + MLP_KEYS + WBH_KEYS)
        mlp(l)
        barrier(MLP_KEYS + MIXER_KEYS + PROJ_KEYS + WBH_KEYS)

    def store_x(half):
        for c in range(8):
            b.dma(outT_d[half, :, c, :], xT[:, c, :], reads=[("xT", c, g) for g in range(4)], writes=[("dramout", "out")], key="out")

    def finish():
        b.p.add("sp", lambda e: e.nop(), reads=[("dramout", "out")] + [("dbgdone", k) for k in b.out_keys])

    b.ctx = dict(locals())
    return b


_CACHE = {}
N_LAYERS = 4


def _program():
    if "b" not in _CACHE:
        b = build_program(N_LAYERS)
        c = b.ctx
        for half in range(2):
            c["load_x"](half)
            for l in range(N_LAYERS):
                c["run_layer"](l, half)
            c["store_x"](half)
        c["finish"]()
        b.p.emit(b.nc, b.es)
        _CACHE["b"] = b
    return _CACHE["b"]


def kernel(**inputs):
    inp = {k: np.asarray(v) for k, v in inputs.items()}
    B = inp["x"].shape[0]
    b = _program()
    cst = host_constants()
    W = prep_weights(inp, range(N_LAYERS))
    in_maps = []
    for core in range(8):
        bb = core % B
        xs = np.asarray(inp["x"][bb], np.float32)
        xT = np.stack([chunkT(np.ascontiguousarray(xs[hf * T:(hf + 1) * T].T), 8) for hf in range(2)])
        pos = np.asarray(inp["positions"][bb]).astype(np.float32)
        m = {"xT": xT,
             "memT": chunkT(np.ascontiguousarray(np.asarray(inp["mem"][bb], np.float32).T), 8),
             "pos": np.stack([np.ascontiguousarray(pos[hf * T:(hf + 1) * T].reshape(16, 128).T) for hf in range(2)]),
             "cst": cst}
        m.update(W)
        in_maps.append(m)
    res = run_bass_kernel_spmd(b.nc, in_maps, core_ids=list(range(8)))
    out = np.zeros((B, 2 * T, D), np.float32)
    for bb in range(B):
        o = np.asarray(res.results[bb]["outT"], np.float32)
        for hf in range(2):
            out[bb, hf * T:(hf + 1) * T] = o[hf].transpose(2, 1, 0).reshape(T, D)
    return out
```

```python
import numpy as np
from contextlib import ExitStack
import concourse.bass as bass
import concourse.mybir as mybir
from concourse.bass_utils import run_bass_kernel_spmd

F32 = mybir.dt.float32
BF16 = mybir.dt.bfloat16
I32 = mybir.dt.int32
AF = mybir.ActivationFunctionType
ALU = mybir.AluOpType
AX = mybir.AxisListType

ENGS = ("pe", "act", "dve", "pool", "sp")


class Op:
    __slots__ = ("eng", "fn", "reads", "writes", "dma_key", "idx", "deps", "signal",
                 "tick", "waits")

    def __init__(self, eng, fn, reads, writes, dma_key):
        self.eng = eng
        self.fn = fn
        self.reads = reads
        self.writes = writes
        self.dma_key = dma_key
        self.deps = []
        self.signal = False
        self.tick = None
        self.waits = []


class Prog:
    def __init__(self):
        self.ops = []

    def add(self, eng, fn, reads=(), writes=(), dma_key=None):
        op = Op(eng, fn, tuple(reads), tuple(writes), dma_key)
        op.idx = len(self.ops)
        self.ops.append(op)
        return op

    def analyse(self):
        last_w = {}
        readers = {}
        for op in self.ops:
            deps = set()
            for k in op.reads:
                w = last_w.get(k)
                if w is not None:
                    deps.add(w.idx)
            for k in op.writes:
                w = last_w.get(k)
                if w is not None:
                    deps.add(w.idx)
                for r in readers.get(k, {}).values():
                    deps.add(r.idx)
            deps.discard(op.idx)
            keep = []
            for d in deps:
                a = self.ops[d]
                if a.dma_key is None and op.dma_key is None and a.eng == op.eng:
                    if a.eng == "pe":
                        continue
                    raw = any(k in a.writes for k in op.reads)
                    if not raw:
                        continue
                keep.append(d)
            op.deps = keep
            for d in keep:
                self.ops[d].signal = True
            for k in op.writes:
                last_w[k] = op
                readers[k] = {}
            for k in op.reads:
                rk = op.dma_key if op.dma_key is not None else op.eng
                readers.setdefault(k, {})[rk] = op
        cnt = {e: 0 for e in ENGS}
        dcnt = {}
        for op in self.ops:
            if op.dma_key is not None:
                dcnt[op.dma_key] = dcnt.get(op.dma_key, 0) + 1
                op.tick = ("dma", op.dma_key, dcnt[op.dma_key] * 16)
            elif op.signal:
                cnt[op.eng] += 1
                op.tick = ("eng", op.eng, cnt[op.eng])
        waited = {e: {} for e in ENGS}
        dma_issued = {}
        for op in self.ops:
            need = {}
            for d in op.deps:
                a = self.ops[d]
                t = a.tick
                if t[0] == "dma":
                    v = dma_issued.get(t[1], 0) * 16
                    v = max(v, t[2])
                else:
                    v = t[2]
                sk = (t[0], t[1])
                if v > need.get(sk, 0):
                    need[sk] = v
            for sk, v in need.items():
                if waited[op.eng].get(sk, 0) >= v:
                    continue
                waited[op.eng][sk] = v
                op.waits.append((sk, v))
            if op.dma_key is not None:
                dma_issued[op.dma_key] = dma_issued.get(op.dma_key, 0) + 1
        self.n_eng_sems = cnt
        self.dma_keys = list(dcnt.keys())

    def emit(self, nc, es):
        self.analyse()
        sems = {}
        for e in ENGS:
            sems[("eng", e)] = es.enter_context(nc.semaphore("s_" + e))
        for i, k in enumerate(self.dma_keys):
            sems[("dma", k)] = es.enter_context(nc.semaphore("d%d" % i))
        self.sems = sems
        per = {e: [op for op in self.ops if op.eng == e] for e in ENGS}
        block = es.enter_context(nc.Block())

        def run(engine, lst):
            for op in lst:
                for sk, v in op.waits:
                    engine.wait_ge(sems[sk], v)
                ins = op.fn(engine)
                if op.dma_key is not None:
                    ins.then_inc(sems[("dma", op.dma_key)], 16)
                elif op.signal:
                    ins.then_inc(sems[("eng", op.eng)], 1)

        @block.tensor
        def _(e):
            run(e, per["pe"])

        @block.scalar
        def _(e):
            run(e, per["act"])

        @block.vector
        def _(e):
            run(e, per["dve"])

        @block.gpsimd
        def _(e):
            run(e, per["pool"])

        @block.sync
        def _(e):
            run(e, per["sp"])


D = 1024
T = 2048
NT4 = T // 512
NT1 = T // 128
KC = D // 128
IN_COLS = 928
EPS = 1e-6
PI = float(np.pi)
SHIFT = 10.0


class Builder:
    def __init__(self, nc, es, n_layers, debug=()):
        self.nc = nc
        self.es = es
        self.p = Prog()
        self.L = n_layers
        self.debug = set(debug)
        self.dram_in = {}
        self.dram_out = {}
        self._uid = 0
        self.out_keys = []

    def din(self, name, shape, dt=F32):
        t = self.nc.dram_tensor(name, list(shape), dt, kind="ExternalInput")
        self.dram_in[name] = t
        return t

    def dout(self, name, shape, dt=F32):
        t = self.nc.dram_tensor(name, list(shape), dt, kind="ExternalOutput")
        self.dram_out[name] = t
        return t

    def sb(self, name, shape, dt=F32):
        return self.es.enter_context(self.nc.sbuf_tensor(name, list(shape), dt))

    def ps(self, name, shape, dt=F32):
        return self.es.enter_context(self.nc.psum_tensor(name, list(shape), dt))

    def dma(self, out, in_, reads=(), writes=(), key=None, eng="sp", **kw):
        writes = list(writes)
        if key is None:
            self._uid += 1
            key = "a%d_%s" % (self._uid % 12, eng)
            writes.append(("dmakey", key))
        return self.p.add(eng, lambda e, o=out, i=in_, kw=kw: e.dma_start(out=o, in_=i, **kw),
                          reads=reads, writes=writes, dma_key=key)

    def mm(self, out, lhsT, rhs, start, stop, reads=(), writes=()):
        return self.p.add("pe", lambda e: e.matmul(out, lhsT, rhs, start=start, stop=stop),
                          reads=reads, writes=writes)

    def tr(self, out, in_, ident, reads=(), writes=()):
        return self.p.add("pe", lambda e: e.transpose(out, in_, ident), reads=reads, writes=writes)

    def act(self, out, in_, func, reads=(), writes=(), eng="act", **kw):
        return self.p.add("act", lambda e: e.activation(out, in_, func, **kw), reads=reads, writes=writes)

    def v(self, eng, name, *args, reads=(), writes=(), **kw):
        return self.p.add(eng, lambda e: getattr(e, name)(*args, **kw), reads=reads, writes=writes)


def host_constants():
    c = np.zeros((128, 1664), np.float32)
    c[:, 0:128] = np.eye(128)
    c[:, 128:256] = 1.0
    r = np.arange(128)
    c[:, 256:384] = (r[:, None] <= r[None, :])
    c[127, 384:512] = 1.0
    c[127, 512:640] = -1.0
    m = np.arange(896)
    c[:, 640:1536] = (m[None, :] - r[:, None] >= 384)
    c[:, 1536:1552] = (10000.0 ** (-np.arange(16, dtype=np.float32) / 16))[None, :]
    c[:, 1552] = r + 1
    c[:, 1553] = -(r + 1)
    return c


def chunkT(v, k):
    return np.ascontiguousarray(v.reshape((k, 128) + v.shape[1:]).swapaxes(0, 1))


def prep_weights(inp, layers):
    f = lambda a: np.asarray(a, np.float32)
    W = {}
    Ls = list(layers)
    n = len(Ls)
    vec = np.zeros((n, 128, 64), np.float32)
    row = np.zeros((n, 320), np.float32)
    bm = np.zeros((n, 4, 128, 1024), np.float32)
    cm = np.zeros((n, 4, 128, 8, 128), np.float32)
    lam = np.zeros((n, 4, 3, 512), np.float32)
    for i, l in enumerate(Ls):
        vec[i, :, 0:8] = f(inp["norm_mix"][l]).reshape(8, 128).T
        vec[i, :, 8:16] = f(inp["norm_mem_q"][l]).reshape(8, 128).T
        vec[i, :, 16:24] = f(inp["norm_mlp"][l]).reshape(8, 128).T
        vec[i, :, 24:32] = f(inp["norm_mem_kv"][l]).reshape(8, 128).T
        vec[i, :, 32:34] = f(inp["mla_q_norm"][l]).reshape(2, 128).T
        vec[i, :, 34] = f(inp["mla_kv_norm"][l])
        vec[i, :, 35:39] = f(inp["ssm_d"][l]).reshape(4, 128).T
        vec[i, :, 39:43] = f(inp["ssm_b_glu"][l]).reshape(4, 128).T
        vec[i, :, 43:47] = f(inp["out_norm_ssm"][l]).reshape(4, 128).T
        vec[i, :, 47:51] = f(inp["out_norm_mla"][l]).reshape(4, 128).T
        row[i, 0:96] = f(inp["mla_q_gain"][l])
        row[i, 96:192] = f(inp["mla_k_gain"][l])
        row[i, 192:256] = f(inp["mem_q_gain"][l])
        row[i, 256:320] = f(inp["mem_k_gain"][l])
        bre = f(inp["ssm_b_re"][l])
        bim = f(inp["ssm_b_im"][l])
        cre = f(inp["ssm_c_re"][l])
        cim = f(inp["ssm_c_im"][l])
        lre = f(inp["ssm_lambda_re"][l])
        lim = f(inp["ssm_lambda_im"][l])
        lst = f(inp["ssm_log_step"][l])
        for c in range(4):
            for j in range(8):
                g = 8 * c + j
                bm[i, c, 16 * j:16 * j + 16, j * 64:(j + 1) * 64] = bre[g].T
                bm[i, c, 16 * j:16 * j + 16, 512 + j * 64:512 + (j + 1) * 64] = bim[g].T
                pr, e = j // 2, j % 2
                cm[i, c, e * 64:(e + 1) * 64, pr, 16 * j:16 * j + 16] = cre[g].T
                cm[i, c, e * 64:(e + 1) * 64, 4 + pr, 16 * j:16 * j + 16] = cim[g].T
                lam[i, c, 0, j * 64:(j + 1) * 64] = lre[g]
                lam[i, c, 1, j * 64:(j + 1) * 64] = lim[g]
                lam[i, c, 2, j * 64:(j + 1) * 64] = lst[g]
    W["vec"] = vec
    W["row"] = row
    W["bm"] = bm
    W["cm"] = cm
    W["lam"] = lam
    st = lambda name, k: np.stack([chunkT(f(inp[name][l]), k) for l in Ls])
    W["w_in"] = st("w_in", 8)
    W["w_uq"] = st("mla_w_uq", 2)
    W["w_ukv"] = np.stack([f(inp["mla_w_ukv"][l]) for l in Ls])
    W["w_glu"] = st("ssm_w_glu", 4)
    W["w_out"] = st("w_out", 8)
    W["mwq"] = st("mem_w_q", 8)
    W["mwkv"] = st("mem_w_kv", 8)
    W["mwo"] = np.stack([np.ascontiguousarray(f(inp["mem_w_o"][l]).reshape(4, 64, 1024).swapaxes(0, 1)) for l in Ls])
    W["w1"] = st("mlp_w1", 8)
    W["w2"] = st("mlp_w2", 32)
    return W


TWO_PI = 2.0 * PI
GA = 1.5957691216057308
GB = 0.044715
QSCALE = 1.0 / float(np.sqrt(96.0))
MSCALE = 1.0 / 8.0
MSHIFT = 8.0
NEG_BIG = -30000.0


def build_program(n_layers=1, debug=()):
    nc = bass.Bass("TRN2", target_bir_lowering=False)
    es = ExitStack()
    b = Builder(nc, es, n_layers, debug)
    P = b.p
    L = n_layers

    xT_d = b.din("xT", [2, 128, 8, T]).ap()
    memT_d = b.din("memT", [128, 8, 256]).ap()
    pos_d = b.din("pos", [2, 128, 16]).ap()
    hinit_d = nc.dram_tensor("sc_h", [L, 4, 1024], BF16, kind="Internal").ap()
    pkt_d = nc.dram_tensor("sc_kt", [L, 4, 96, 2, T], BF16, kind="Internal").ap()
    pve_d = nc.dram_tensor("sc_ve", [L, 4, 128, 16, 65], BF16, kind="Internal").ap()
    pvo_d = nc.dram_tensor("sc_vo", [L, 4, 128, 16, 128], BF16, kind="Internal").ap()
    cst_d = b.din("cst", [128, 1664]).ap()
    vec_d = b.din("vec", [L, 128, 64]).ap()
    row_d = b.din("row", [L, 192 + 64 + 64]).ap()
    w_in_d = b.din("w_in", [L, 128, 8, IN_COLS]).ap()
    w_uq_d = b.din("w_uq", [L, 128, 2, 768]).ap()
    w_ukv_d = b.din("w_ukv", [L, 128, 1024]).ap()
    w_glu_d = b.din("w_glu", [L, 128, 4, 512]).ap()
    w_out_d = b.din("w_out", [L, 128, 8, 1024]).ap()
    mwq_d = b.din("mwq", [L, 128, 8, 256]).ap()
    mwkv_d = b.din("mwkv", [L, 128, 8, 512]).ap()
    mwo_d = b.din("mwo", [L, 64, 4, 1024]).ap()
    w1_d = b.din("w1", [L, 128, 8, 4096]).ap()
    w2_d = b.din("w2", [L, 128, 32, 1024]).ap()
    bm_d = b.din("bm", [L, 4, 128, 1024]).ap()
    cm_d = b.din("cm", [L, 4, 128, 8, 128]).ap()
    lam_d = b.din("lam", [L, 4, 3, 512]).ap()
    outT_d = b.dout("outT", [2, 128, 8, T]).ap()
    okt_d, ove_d, ovo_d, hfin_d = pkt_d, pve_d, pvo_d, hinit_d
    dbg_d = {}

    xT = b.sb("xT_s", [128, 8, T])
    hT = b.sb("hT_s", [128, 8, 1024], BF16)
    wb = [b.sb("wb%d" % i, [128, 8192], BF16) for i in range(2)]
    cst = b.sb("cst_s", [128, 160])
    cbf = b.sb("cbf_s", [128, 1536], BF16)
    vec = b.sb("vec_s", [128, L, 64])
    rowb = b.sb("rowb_s", [128, 320])
    sq = b.sb("sq_s", [128, 8, 512], BF16)
    rstd = b.sb("rstd_s", [128, 512])
    posf = b.sb("pos_s", [128, 16])
    cvec = b.sb("cvec_s", [128, 8])
    AR = 37888
    arena = b.sb("arena", [128, AR], BF16)
    arena_f = arena.bitcast(F32)

    psp = [b.ps("psp%d" % i, [128, 1024]) for i in range(4)]
    psb = [psp[i // 2][:, (i % 2) * 512:(i % 2 + 1) * 512] for i in range(8)]
    psb_bf = [a_.bitcast(BF16) for a_ in psb]

    ident = cbf[:, 0:128]
    ones = cbf[:, 128:256]
    tri = cbf[:, 256:384]
    e127 = cbf[:, 384:512]
    e127n = cbf[:, 512:640]
    cmask = cbf[:, 640:1536]
    ones_f = cst[:, 0:128]
    invf = cst[:, 128:144]
    sp1 = cst[:, 144:145]
    nsp1 = cst[:, 145:146]

    def A16(off_kib, shape):
        n = int(np.prod(shape[1:]))
        o = int(round(off_kib * 512))
        assert o + n <= AR, (off_kib, shape)
        v = arena[:, o:o + n]
        if len(shape) == 3:
            v = v.rearrange("p (a b) -> p a b", a=shape[1])
        elif len(shape) == 4:
            v = v.rearrange("p (a b c) -> p a b c", a=shape[1], b=shape[2])
        return v

    def A32(off_kib, shape):
        n = int(np.prod(shape[1:]))
        o = int(round(off_kib * 256))
        assert (o + n) * 2 <= AR, (off_kib, shape)
        v = arena_f[:, o:o + n]
        if len(shape) == 3:
            v = v.rearrange("p (a b) -> p a b", a=shape[1])
        elif len(shape) == 4:
            v = v.rearrange("p (a b c) -> p a b c", a=shape[1], b=shape[2])
        return v

    uT = A16(0, [128, 4, T])
    cqn = A16(16, [128, 2, T])
    ckvn = A16(24, [128, T])
    krope = A32(28, [128, 16, 32])
    cossin = A32(30, [128, 2, 16, 16])
    ymla = A16(32, [128, 4, T])
    WK = 48

    b.dma(cst[:, 0:128], cst_d[:, 128:256], writes=["cst"])
    b.dma(cst[:, 128:160], cst_d[:, 1536:1568], writes=["cst"])
    b.dma(cbf[:], cst_d[:, 0:1536], writes=["cbf"], eng="pool")
    b.dma(vec[:], vec_d.rearrange("l p k -> p l k"), writes=["vec"])
    def load_x(half):
        b.dma(posf[:], pos_d[half], writes=["pos"])
        for c in range(8):
            b.dma(xT[:, c, :], xT_d[half, :, c, :], writes=[("xT", c, t4) for t4 in range(4)])
    b.v("pool", "memset", cvec[:, 0:1], -SHIFT, writes=["cvec"])
    b.v("pool", "memset", cvec[:, 1:2], -MSHIFT, writes=["cvec"])
    b.v("pool", "memset", cvec[:, 2:3], -PI, writes=["cvec"])
    b.v("pool", "memset", cvec[:, 3:4], EPS, writes=["cvec"])
    nshift = cvec[:, 0:1]
    nmshift = cvec[:, 1:2]
    npi = cvec[:, 2:3]
    epsv = cvec[:, 3:4]

    cos_t = cossin[:, 0]
    sin_t = cossin[:, 1]

    def wrap_pi(dst, src, ti, tf, tm, rk, wk):
        b.v("dve", "tensor_scalar_mul", ti, src, 1.0 / TWO_PI, reads=rk, writes=[wk + "_ti"])
        b.v("dve", "tensor_copy", tf, ti, reads=[wk + "_ti"], writes=[wk + "_tf"])
        b.v("dve", "scalar_tensor_tensor", dst, tf, -TWO_PI, src, ALU.mult, ALU.add, reads=[wk + "_tf"] + rk, writes=[wk])
        b.v("dve", "tensor_scalar", tm, dst, PI, -TWO_PI, ALU.is_gt, ALU.mult, reads=[wk], writes=[wk + "_tm"])
        b.v("dve", "tensor_tensor", dst, dst, tm, ALU.add, reads=[wk, wk + "_tm"], writes=[wk])
        b.v("dve", "tensor_scalar", tm, dst, -PI, TWO_PI, ALU.is_lt, ALU.mult, reads=[wk], writes=[wk + "_tm"])
        b.v("dve", "tensor_tensor", dst, dst, tm, ALU.add, reads=[wk, wk + "_tm"], writes=[wk])

    def rope_tables():
        ang = A32(WK, [128, 16, 16])
        ang2 = A32(WK + 1, [128, 16, 16])
        ti = A32(WK + 2, [128, 16, 16]).bitcast(I32)
        tf = A32(WK + 3, [128, 16, 16])
        tm = A32(WK + 4, [128, 16, 16])
        wr = A32(WK + 5, [128, 16, 16])
        for t in range(16):
            b.v("dve", "tensor_scalar_mul", ang[:, t, :], invf, posf[:, t:t + 1],
                reads=["cst", "pos"], writes=["ropeang"])
        wrap_pi(wr, ang, ti, tf, tm, ["ropeang"], "ropew")
        b.act(sin_t, wr, AF.Sin, reads=["ropew"], writes=["sin"])
        b.v("dve", "tensor_scalar_add", ang2, ang, PI / 2, reads=["ropeang"], writes=["ropeang2"])
        wrap_pi(wr, ang2, ti, tf, tm, ["ropeang2", "sin"], "ropew")
        b.act(cos_t, wr, AF.Sin, reads=["ropew"], writes=["cos"])

    state = {"ps": 0, "wb": 0}
    SQK = [("sq", i) for i in range(8)]

    def next_ps():
        i = state["ps"]
        state["ps"] = (i + 1) % 4
        return i

    def load_w(view, src, key, eng="pool"):
        return b.dma(view, src, writes=[key], key=("w", key), eng=eng)

    def rms_scale(src_keys, nchunk, n, src_fn, ps_i, inv_dim):
        for c in range(nchunk):
            b.mm(psb[ps_i][:, 0:n], ones, src_fn(c), c == 0, c == nchunk - 1,
                 reads=["cbf"] + src_keys, writes=[("ps", ps_i)])
        b.act(rstd[:, 0:n], psb[ps_i][:, 0:n], AF.Sqrt, reads=[("ps", ps_i), "cvec"], writes=["rstd"],
              bias=epsv, scale=inv_dim)
        b.v("dve", "reciprocal", rstd[:, 0:n], rstd[:, 0:n], reads=["rstd"], writes=["rstd"])

    def norm_x(l, gain_col, tok0, nt, hkeys):
        for t in range(nt):
            g4 = (tok0 // 512) + t
            xs = xT[:, :, g4 * 512:(g4 + 1) * 512]
            xk = [("xT", c, g4) for c in range(8)]
            b.act(sq[:], xs, AF.Square, reads=xk, writes=SQK)
            pi = next_ps()
            rms_scale(SQK, 8, 512, lambda c: sq[:, c, :], pi, 1.0 / D)
            for c in range(8):
                b.v("dve", "scalar_tensor_tensor", hT[:, c, t * 512:(t + 1) * 512], xT[:, c, g4 * 512:(g4 + 1) * 512],
                    vec[:, l, gain_col + c:gain_col + c + 1], rstd[:], ALU.mult, ALU.mult,
                    reads=[("xT", c, g4), "vec", "rstd"], writes=[("hT", t)])

    def dbg(name, view, shape, reads, dt=F32):
        if name in b.debug:
            d = b.dout("dbg_" + name, shape, dt).ap()
            b.dma(d, view, reads=reads, writes=[("dbgdone", "dbg_" + name)], key="dbg_" + name)
            b.out_keys.append("dbg_" + name)

    def layer(l):
        V0 = lambda c: vec[:, l, c:c + 1]
        b.dma(rowb[:], row_d[l].partition_broadcast(128), writes=["rowb"])
        gq = rowb[:, 0:96]
        gk = rowb[:, 96:192]
        mgq = rowb[:, 192:256]
        mgk = rowb[:, 256:320]
        b.v("dve", "tensor_scalar_mul", gq, gq, QSCALE, reads=["rowb"], writes=["rowb"])
        b.v("dve", "tensor_scalar_mul", mgq, mgq, MSCALE, reads=["rowb"], writes=["rowb"])

        w_in = wb[0][:, 0:8 * IN_COLS].rearrange("p (k n) -> p k n", k=8)
        load_w(w_in, w_in_d[l], ("wb", 0))
        cqraw = A32(WK, [128, 3, 512])
        for hf in range(2):
            norm_x(l, 0, hf * 1024, 2, None)
            for t in range(2):
                g4 = hf * 2 + t
                tok = slice(g4 * 512, (g4 + 1) * 512)
                hk = [("hT", t)]
                for oc in range(7):
                    pi = next_ps()
                    for k in range(8):
                        b.mm(psb[pi][:], w_in[:, k, oc * 128:(oc + 1) * 128], hT[:, k, t * 512:(t + 1) * 512],
                             k == 0, k == 7, reads=[("wb", 0)] + hk, writes=[("ps", pi)])
                    if oc < 4:
                        b.act(uT[:, oc, tok], psb[pi][:], AF.Copy, reads=[("ps", pi)], writes=[("uT", oc, g4)])
                    else:
                        b.act(cqraw[:, oc - 4, :], psb[pi][:], AF.Copy, reads=[("ps", pi)], writes=[("cqraw", oc - 4)])
                b.act(sq[:, 0:2, :], cqraw[:, 0:2, :], AF.Square, reads=[("cqraw", 0), ("cqraw", 1)], writes=SQK)
                pi = next_ps()
                rms_scale(SQK, 2, 512, lambda c: sq[:, c, :], pi, 1.0 / 256)
                for c in range(2):
                    b.v("dve", "scalar_tensor_tensor", cqn[:, c, tok], cqraw[:, c, :], V0(32 + c), rstd[:],
                        ALU.mult, ALU.mult, reads=[("cqraw", c), "vec", "rstd"], writes=[("cqn", g4)])
                b.act(sq[:, 2, :], cqraw[:, 2, :], AF.Square, reads=[("cqraw", 2)], writes=SQK)
                pi = next_ps()
                rms_scale(SQK, 1, 512, lambda c: sq[:, 2, :], pi, 1.0 / 128)
                b.v("dve", "scalar_tensor_tensor", ckvn[:, tok], cqraw[:, 2, :], V0(34), rstd[:],
                    ALU.mult, ALU.mult, reads=[("cqraw", 2), "vec", "rstd"], writes=[("ckvn", g4)])
                for s4 in range(4):
                    t16 = g4 * 4 + s4
                    pi = next_ps()
                    for k in range(8):
                        b.mm(psb[pi][:, 0:32], hT[:, k, t * 512 + s4 * 128:t * 512 + (s4 + 1) * 128], w_in[:, k, 896:928],
                             k == 0, k == 7, reads=[("wb", 0)] + hk, writes=[("ps", pi)])
                    b.act(krope[:, t16, :], psb[pi][:, 0:32], AF.Copy, reads=[("ps", pi)], writes=[("krope", t16)])
        dbg("uT", uT, [128, 4, T], [("uT", c, g) for c in range(4) for g in range(4)], BF16)
        dbg("cqn", cqn, [128, 2, T], [("cqn", g) for g in range(4)], BF16)
        dbg("ckvn", ckvn, [128, T], [("ckvn", g) for g in range(4)], BF16)
        dbg("krope", krope, [128, 16, 32], [("krope", g) for g in range(16)])
        dbg("cos", cos_t, [128, 16, 16], ["cos"])
        dbg("sin", sin_t, [128, 16, 16], ["sin"])


    hT_flat = hT[:].rearrange("p a b -> p (a b)")
    pkt_s = hT_flat[:, 0:4096].rearrange("p (e t) -> p e t", e=2)
    pve_s = hT_flat[:, 4096:4096 + 1040].rearrange("p (t c) -> p t c", t=16)
    pvo_s = hT_flat[:, 5248:5248 + 2048].rearrange("p (t c) -> p t c", t=16)
    HK = [("hT", 0), ("hT", 1)]
    QT = A16(48, [128, 2, T])
    KT = A16(56, [128, 2, T])
    Ve = A16(64, [128, 16, 65])
    Vo = A16(66.5, [128, 16, 128])
    sq_f = sq[:].rearrange("p a b -> p (a b)").bitcast(F32)
    sqq = sq_f[:, 0:768].rearrange("p (g e d) -> p g e d", g=4, e=2)
    qn = sq_f[:, 768:1536].rearrange("p (g e d) -> p g e d", g=4, e=2)
    SQ6 = [("sq", i) for i in range(6)]
    kraw = wb[1].bitcast(F32)[:, 2048:2816].rearrange("p (g e d) -> p g e d", g=4, e=2)
    KRK = ("wb", 1, "glu")
    qf = A16(70.5, [128, 4, 192]).rearrange("p g (e d) -> p g e d", e=2)
    rta = A32(72, [128, 4, 32]).rearrange("p g (e d) -> p g e d", e=2)
    rtb = A32(72.5, [128, 4, 32]).rearrange("p g (e d) -> p g e d", e=2)
    ssq = A32(73, [128, 4, 2])
    wb1f = wb[1].bitcast(F32)
    rc = wb1f[:, 3584:4096]
    PT = [sq[:, i, :] for i in range(8)]

    rstd_bf = rstd[:].bitcast(BF16)
    sqq_k = wb[1].bitcast(F32)[:, 2816:3584].rearrange("p (g e d) -> p g e d", g=4, e=2)
    qf_k = rstd_bf[:, 0:768].rearrange("p (g e d) -> p g e d", g=4, e=2)
    ssq_k = rstd[:, 384:392].rearrange("p (g e) -> p g e", g=4)

    def qk_path(which, src, gain, dstT, ps_bank, g4, srck, tag):
        if which == "q":
            sq_, qn_, qf_, ssq_ = sqq, qn, qf, ssq
            ksq, kqn, kqf, kss = [("sq", i) for i in range(3)], [("sq", i) for i in range(3, 6)], "qf", "ssq"
        else:
            sq_, qn_, qf_, ssq_ = sqq_k, src, qf_k, ssq_k
            ksq, kqn, kqf, kss = [KRK], srck, "rstd", "rstd"
        cs = cos_t[:, g4 * 4:(g4 + 1) * 4, :].unsqueeze(2).to_broadcast([128, 4, 2, 16])
        sn = sin_t[:, g4 * 4:(g4 + 1) * 4, :].unsqueeze(2).to_broadcast([128, 4, 2, 16])
        g_b = gain.unsqueeze(1).unsqueeze(1).to_broadcast([128, 4, 2, 96])
        st = []
        st.append(lambda: b.act(sq_, src, AF.Square, reads=srck, writes=ksq))
        st.append(lambda: b.v("dve", "tensor_reduce", ssq_, sq_, AX.X, ALU.add, reads=ksq, writes=[kss]))
        st.append(lambda: b.act(ssq_, ssq_, AF.Sqrt, reads=[kss, "cvec"], writes=[kss], bias=epsv, scale=1.0 / 96))
        st.append(lambda: b.v("dve", "reciprocal", ssq_, ssq_, reads=[kss], writes=[kss]))
        st.append(lambda: b.v("dve", "tensor_tensor", qn_, src, ssq_.unsqueeze(3).to_broadcast([128, 4, 2, 96]), ALU.mult,
                              reads=srck + [kss], writes=kqn))
        st.append(lambda: b.v("dve", "tensor_tensor", qn_, qn_, g_b, ALU.mult, reads=kqn + ["rowb"], writes=kqn))

        def rope():
            b.v("pool", "tensor_copy", qf_[:, :, :, 0:64], qn_[:, :, :, 0:64], reads=kqn, writes=[kqf])
            b.v("pool", "tensor_tensor", rta, qn_[:, :, :, 64:80], cs, ALU.mult, reads=kqn + ["cos"], writes=["rta"])
            b.v("pool", "tensor_tensor", rtb, qn_[:, :, :, 80:96], sn, ALU.mult, reads=kqn + ["sin"], writes=["rtb"])
            b.v("pool", "tensor_tensor", qf_[:, :, :, 64:80], rta, rtb, ALU.subtract, reads=["rta", "rtb"], writes=[kqf])
            b.v("pool", "tensor_tensor", rta, qn_[:, :, :, 80:96], cs, ALU.mult, reads=kqn + ["cos", kqf], writes=["rta"])
            b.v("pool", "tensor_tensor", rtb, qn_[:, :, :, 64:80], sn, ALU.mult, reads=kqn + ["sin", kqf], writes=["rtb"])
            b.v("pool", "tensor_tensor", qf_[:, :, :, 80:96], rta, rtb, ALU.add, reads=["rta", "rtb"], writes=[kqf])
        st.append(rope)

        def trans():
            for i in range(4):
                for e in range(2):
                    slot = e * 4 + i
                    b.tr(psb_bf[ps_bank][0:96, slot * 128:(slot + 1) * 128], qf_[:, i, e, :], ident,
                         reads=[kqf, "cbf"], writes=[("ps", ps_bank)])
        st.append(trans)
        st.append(lambda: b.act(dstT[0:96, :, g4 * 512:(g4 + 1) * 512],
                                psb_bf[ps_bank][0:96, :].rearrange("p (e n) -> p e n", e=2), AF.Copy,
                                reads=[("ps", ps_bank)], writes=[(tag, g4)]))
        return st

    def mla(l, half=1):
        wuq = wb[1][:, 0:1536].rearrange("p (k n) -> p k n", k=2)
        wukv = wb[1][:, 1536:2560]
        gq = rowb[:, 0:96]
        gk = rowb[:, 96:192]
        b.v("pool", "memset", Ve[:, :, 64:65], 1.0, writes=["Ve"])
        b.v("pool", "memset", Vo[:, :, 0:64], 0.0, writes=["Vo"])
        b.v("pool", "memset", Vo[:, :, 0:1], 1.0, writes=["Vo"])
        for hp in range(4):
            if half == 1:
                b.dma(pkt_s[0:96], pkt_d[l, hp], reads=[("skv", l, hp)], writes=HK, key="pk")
                b.dma(pve_s, pve_d[l, hp], reads=[("skv", l, hp)], writes=HK, key="pk")
                b.dma(pvo_s, pvo_d[l, hp], reads=[("skv", l, hp)], writes=HK, key="pk")
            qps4 = psp[0][:, :].rearrange("p (g n) -> p g n", g=4)
            kvps4 = psp[1][:, :].rearrange("p (g n) -> p g n", g=4)
            PQ = [("ps", 0), ("ps", 1)]
            PKV = [("ps", 2), ("ps", 3)]
            for g4 in range(4):
                for i in range(4):
                    t16 = g4 * 4 + i
                    tok = slice(t16 * 128, (t16 + 1) * 128)
                    for k in range(2):
                        b.mm(qps4[:, i, 0:192], cqn[:, k, tok], wuq[:, k, hp * 192:(hp + 1) * 192], k == 0, k == 1,
                             reads=[("cqn", g4), ("wb", 1)], writes=[("ps", i // 2)])
                    b.mm(kvps4[:, i, :], ckvn[:, tok], wukv[:, hp * 256:(hp + 1) * 256], True, True,
                         reads=[("ckvn", g4), ("wb", 1)], writes=[("ps", 2 + i // 2)])
                kv4 = kvps4.rearrange("p g (e d) -> p g e d", e=2)
                b.act(kraw[:, :, :, 0:64], kv4[:, :, :, 0:64], AF.Copy, reads=PKV, writes=[KRK])
                b.v("pool", "tensor_copy", kraw[:, :, :, 64:96],
                    krope[:, g4 * 4:(g4 + 1) * 4, :].unsqueeze(2).to_broadcast([128, 4, 2, 32]),
                    reads=[("krope", g4 * 4 + i) for i in range(4)], writes=[KRK])
                sq_st = qk_path("q", qps4[:, :, 0:192].rearrange("p g (e d) -> p g e d", e=2), gq, QT, 6, g4, PQ, "QT")
                sk_st = qk_path("k", kraw, gk, KT, 7, g4, [KRK], "KT")
                sq_st[0]()
                b.act(Ve[:, g4 * 4:(g4 + 1) * 4, 0:64], kvps4[:, :, 64:128], AF.Copy, reads=PKV, writes=["Ve"])
                b.act(Vo[:, g4 * 4:(g4 + 1) * 4, 64:128], kvps4[:, :, 192:256], AF.Copy, reads=PKV, writes=["Vo"])
                sk_st[0]()
                for a_, b_ in zip(sq_st[1:], sk_st[1:]):
                    a_()
                    b_()
            QK = [("QT", g) for g in range(4)]
            KK = [("KT", g) for g in range(4)]
            if half == 0:
                b.dma(okt_d[l, hp], KT[0:96], reads=KK, writes=[("skv", l, hp)], key="okv")
                b.dma(ove_d[l, hp], Ve, reads=["Ve"], writes=[("skv", l, hp)], key="okv")
                b.dma(ovo_d[l, hp], Vo, reads=["Vo"], writes=[("skv", l, hp)], key="okv")
            for e in range(2):
                Vt = Ve if e == 0 else Vo
                vk = "Ve" if e == 0 else "Vo"
                pV = pve_s if e == 0 else pvo_s
                M = 65 if e == 0 else 128
                for qt in range(4):
                    blocks = ([("p", kb) for kb in range(16)] if half == 1 else []) + [("o", kb) for kb in range(4 * qt + 4)]
                    po = 2 + (qt % 2)
                    SB = [0, 1, 5]

                    def emit_qk(bi):
                        kind, kb = blocks[bi]
                        pi = SB[bi % 3]
                        kts = (pkt_s if kind == "p" else KT)[0:96, e, kb * 128:(kb + 1) * 128]
                        kk = HK if kind == "p" else [("KT", kb // 4)]
                        b.mm(psb[pi][:], kts, QT[0:96, e, qt * 512:(qt + 1) * 512], True, True,
                             reads=kk + [("QT", qt)], writes=[("ps", pi)])

                    emit_qk(0)
                    if len(blocks) > 1:
                        emit_qk(1)
                    for bi, (kind, kb) in enumerate(blocks):
                        pi = SB[bi % 3]
                        pt = PT[bi % 4]
                        ptk = ("sq", bi % 4)
                        b.act(pt, psb[pi][:], AF.Exp, reads=[("ps", pi), "cvec"], writes=[ptk],
                              bias=nshift, scale=1.0)
                        if kind == "o" and kb >= 4 * qt:
                            o = kb - 4 * qt
                            b.v("dve", "tensor_tensor", pt, pt, cmask[:, (3 - o) * 128:(3 - o) * 128 + 512], ALU.mult,
                                reads=[ptk, "cbf"], writes=[ptk])
                        if bi + 2 < len(blocks):
                            emit_qk(bi + 2)
                        vs = (pV if kind == "p" else Vt)[:, kb, 0:M]
                        b.mm(psb[po][0:M, :], vs, pt, bi == 0, bi == len(blocks) - 1,
                             reads=(HK if kind == "p" else [vk]) + [ptk], writes=[("ps", po)])
                    dr = 64 if e == 0 else 0
                    b.v("dve", "reciprocal", rc[dr:dr + 1, :], psb[po][dr:dr + 1, :], reads=[("ps", po)], writes=["rc"])
                    if e == 0:
                        b.mm(psb[4][0:64, :], ones_f[64:65, 0:64], rc[64:65, :], True, True, reads=["cst", "rc"], writes=[("ps", 4)])
                        rows = slice(0, 64)
                    else:
                        b.mm(psb[4][:, :], ones_f[0:1, :], rc[0:1, :], True, True, reads=["cst", "rc"], writes=[("ps", 4)])
                        rows = slice(64, 128)
                    b.act(rstd[rows, :], psb[4][rows, :], AF.Copy, reads=[("ps", 4)], writes=["rstd"])
                    b.v("dve", "tensor_tensor", ymla[rows, hp, qt * 512:(qt + 1) * 512], psb[po][rows, :], rstd[rows, :], ALU.mult,
                        reads=[("ps", po), "rstd"], writes=[("ymla", hp, qt)])
        dbg("ymla", ymla, [128, 4, T], [("ymla", h, q) for h in range(4) for q in range(4)], BF16)
        dbg("QT", QT, [128, 2, T], [("QT", g) for g in range(4)], BF16)
        dbg("KT", KT, [128, 2, T], [("KT", g) for g in range(4)], BF16)
        dbg("Vo", Vo, [128, 16, 128], ["Vo"], BF16)


    dummy = b.sb("dummy_s", [128, 2])

    def barrier(keys):
        b.v("pool", "memset", dummy[:, 0:1], 0.0, writes=list(keys))

    MIXER_KEYS = ([("uT", c, g) for c in range(4) for g in range(4)] + [("cqn", g) for g in range(4)]
                  + [("ckvn", g) for g in range(4)] + [("krope", g) for g in range(16)] + ["cos", "sin"]
                  + [("ymla", h, q) for h in range(4) for q in range(4)])
    ATT_KEYS = ([("QT", g) for g in range(4)] + [("KT", g) for g in range(4)]
                + ["Ve", "Vo", "sqq", "qn", "qf", "rta", "rtb", "ssq", "kraw"])
    PROJ_KEYS = [("cqraw", i) for i in range(3)] + ["ropeang", "ropeang2", "ropew", "ropew_ti", "ropew_tf", "ropew_tm"]

    hT_f = hT[:].rearrange("p a b -> p (a b)").bitcast(F32)
    SLOTS = [{n: A32(48 + 2 * i, [128, 512]) for i, n in enumerate("ABCDEFGHIJKMN")},
             dict([(n, A32(32 + 2 * i, [128, 512])) for i, n in enumerate("ABCDEFGH")]
                  + [(n, hT_f[:, i * 512:(i + 1) * 512]) for i, n in enumerate("IJKMN")])]
    SSM_KEYS = ["s%d%s" % (ch, n) for ch in range(2) for n in "ABCDEFGHIJKMN"] + \
               ["Xr0", "Xi0", "Xr1", "Xi1"]
    SSM_BANKS = [(0, 1, 2, 3), (4, 5, 6, 7)]

    def ssm_prep(l, c, ch, half):
        slot = SLOTS[ch]
        sk = lambda n: "s%d%s" % (ch, n)
        def tt(eng, out, a, bb_, op):
            b.v(eng, "tensor_tensor", slot[out], slot[a], slot[bb_], op, reads=[sk(a), sk(bb_)], writes=[sk(out)])
        def wrap(dst, src, ti, tf, tm):
            d_, s_, ti_, tf_, tm_ = slot[dst], slot[src], slot[ti].bitcast(I32), slot[tf], slot[tm]
            kd, ks, kti, ktf, ktm = sk(dst), sk(src), sk(ti), sk(tf), sk(tm)
            b.v("dve", "tensor_scalar_mul", ti_, s_, 1.0 / TWO_PI, reads=[ks], writes=[kti])
            b.v("dve", "tensor_copy", tf_, ti_, reads=[kti], writes=[ktf])
            b.v("dve", "scalar_tensor_tensor", d_, tf_, -TWO_PI, s_, ALU.mult, ALU.add, reads=[ktf, ks], writes=[kd])
            b.v("dve", "tensor_scalar", tm_, d_, PI, -TWO_PI, ALU.is_gt, ALU.mult, reads=[kd], writes=[ktm])
            b.v("dve", "tensor_tensor", d_, d_, tm_, ALU.add, reads=[kd, ktm], writes=[kd])
            b.v("dve", "tensor_scalar", tm_, d_, -PI, TWO_PI, ALU.is_lt, ALU.mult, reads=[kd], writes=[ktm])
            b.v("dve", "tensor_tensor", d_, d_, tm_, ALU.add, reads=[kd, ktm], writes=[kd])
        def actf(out, a, func, **kw):
            b.act(slot[out], slot[a], func, reads=[sk(a), "cst"], writes=[sk(out)], **kw)
        for i, n in enumerate("KMN"):
            b.dma(slot[n], lam_d[l, c, i].partition_broadcast(128), writes=[sk(n)])
        actf("N", "N", AF.Exp)
        tt("dve", "G", "K", "N", ALU.mult)
        tt("dve", "H", "M", "N", ALU.mult)
        actf("A", "G", AF.Exp, scale=sp1)
        actf("C", "G", AF.Exp, scale=nsp1)
        b.v("dve", "tensor_scalar_mul", slot["I"], slot["H"], sp1, reads=[sk("H"), "cst"], writes=[sk("I")])
        wrap("J", "I", "N", "E", "F")
        actf("B", "J", AF.Sin)
        b.v("dve", "tensor_scalar_add", slot["I"], slot["I"], PI / 2, reads=[sk("I")], writes=[sk("I")])
        wrap("J", "I", "N", "E", "F")
        actf("D", "J", AF.Sin)
        tt("dve", "E", "A", "D", ALU.mult)
        tt("dve", "F", "A", "B", ALU.mult)
        tt("dve", "A", "C", "D", ALU.mult)
        b.v("dve", "scalar_tensor_tensor", slot["B"], slot["C"], -1.0, slot["B"], ALU.mult, ALU.mult,
            reads=[sk("C"), sk("B")], writes=[sk("B")])
        wrap("J", "H", "N", "C", "D")
        actf("I", "J", AF.Sin)
        b.v("dve", "tensor_scalar_add", slot["H"], slot["H"], PI / 2, reads=[sk("H")], writes=[sk("H")])
        wrap("J", "H", "N", "C", "D")
        actf("N", "J", AF.Sin)
        actf("C", "G", AF.Exp)
        tt("dve", "D", "C", "N", ALU.mult)
        b.v("dve", "tensor_scalar_add", slot["D"], slot["D"], -1.0, reads=[sk("D")], writes=[sk("D")])
        tt("dve", "I", "C", "I", ALU.mult)
        tt("dve", "J", "K", "K", ALU.mult)
        tt("dve", "N", "M", "M", ALU.mult)
        tt("dve", "J", "J", "N", ALU.add)
        b.v("dve", "reciprocal", slot["J"], slot["J"], reads=[sk("J")], writes=[sk("J")])
        tt("dve", "C", "D", "K", ALU.mult)
        tt("dve", "N", "I", "M", ALU.mult)
        tt("dve", "C", "C", "N", ALU.add)
        tt("dve", "C", "C", "J", ALU.mult)
        tt("dve", "N", "I", "K", ALU.mult)
        tt("dve", "G", "D", "M", ALU.mult)
        tt("dve", "N", "N", "G", ALU.subtract)
        tt("dve", "N", "N", "J", ALU.mult)
        tt("dve", "D", "A", "C", ALU.mult)
        tt("dve", "G", "B", "N", ALU.mult)
        tt("dve", "D", "D", "G", ALU.subtract)
        tt("dve", "I", "A", "N", ALU.mult)
        tt("dve", "G", "B", "C", ALU.mult)
        tt("dve", "I", "I", "G", ALU.add)
        cx = dict(slot=slot, sk=sk, ch=ch, c=c,
                  Tre=slot["E"], Tim=slot["F"], Tpr=slot["D"], Tpi=slot["I"],
                  TK=[sk("E"), sk("F"), sk("D"), sk("I")],
                  Bm=slot["A"].bitcast(BF16),
                  Cm=slot["B"].bitcast(BF16).rearrange("p (k n) -> p k n", k=8),
                  X=slot["C"].bitcast(BF16),
                  Sb=[slot["G"].bitcast(BF16), slot["H"].bitcast(BF16)], SbK=[sk("G"), sk("H")],
                  ST=slot["J"].bitcast(BF16).rearrange("p (k n) -> p k n", k=8))
        load_w(cx["Bm"], bm_d[l, c], sk("A"))
        load_w(cx["Cm"], cm_d[l, c], sk("B"))
        b.v("pool", "memset", cx["Sb"][1], 0.0, writes=[cx["SbK"][1]])
        if half == 1:
            b.dma(cx["Sb"][1][127:128, :], hinit_d[l, c:c + 1, :], reads=[("sh", l, c)], writes=[cx["SbK"][1]])
        return cx

    def ssm_tile(l, cx, k, stage):
        slot, sk, ch, c = cx["slot"], cx["sk"], cx["ch"], cx["c"]
        Tre, Tim, Tpr, Tpi, TK = cx["Tre"], cx["Tim"], cx["Tpr"], cx["Tpi"], cx["TK"]
        Bm, Cm, X, Sb, SbK, ST = cx["Bm"], cx["Cm"], cx["X"], cx["Sb"], cx["SbK"], cx["ST"]
        b0, b1, tb, yb = SSM_BANKS[ch]
        Xr, Xi = "Xr%d" % ch, "Xi%d" % ch
        tok = slice(k * 128, (k + 1) * 128)
        g4 = k // 4
        cur, prv = k % 2, 1 - (k % 2)
        K_, M_, N_ = slot["K"], slot["M"], slot["N"]
        if stage == 2:
            return ssm_tile2(l, cx, k)
        if stage == 3:
            return ssm_tile3(l, cx, k)
        b.mm(psb[b0][:], uT[:, c, tok], Bm[:, 0:512], True, True, reads=[("uT", c, g4), sk("A")], writes=[("ps", b0)])
        b.mm(psb[b1][:], uT[:, c, tok], Bm[:, 512:1024], True, True, reads=[("uT", c, g4), sk("A")], writes=[("ps", b1)])
        b.v("dve", "tensor_tensor", K_, psb[b0][:], Tpr, ALU.mult, reads=[("ps", b0)] + TK, writes=[sk("K")])
        b.v("dve", "tensor_tensor", M_, psb[b1][:], Tpi, ALU.mult, reads=[("ps", b1)] + TK, writes=[sk("M")])
        b.v("pool", "tensor_tensor", X[:, 0:512], K_, M_, ALU.subtract, reads=[sk("K"), sk("M")], writes=[Xr])
        b.v("dve", "tensor_tensor", N_, psb[b1][:], Tpr, ALU.mult, reads=[("ps", b1)] + TK, writes=[sk("N")])
        b.v("dve", "tensor_tensor", K_, psb[b0][:], Tpi, ALU.mult, reads=[("ps", b0)] + TK + [Xr], writes=[sk("K")])
        b.v("pool", "tensor_tensor", X[:, 512:1024], N_, K_, ALU.add, reads=[sk("K"), sk("N")], writes=[Xi])

    def ssm_tile2(l, cx, k):
        slot, sk, ch, c = cx["slot"], cx["sk"], cx["ch"], cx["c"]
        Tre, Tim, Tpr, Tpi, TK = cx["Tre"], cx["Tim"], cx["Tpr"], cx["Tpi"], cx["TK"]
        Bm, Cm, X, Sb, SbK, ST = cx["Bm"], cx["Cm"], cx["X"], cx["Sb"], cx["SbK"], cx["ST"]
        b0, b1, tb, yb = SSM_BANKS[ch]
        Xr, Xi = "Xr%d" % ch, "Xi%d" % ch
        cur, prv = k % 2, 1 - (k % 2)
        K_, M_, N_ = slot["K"], slot["M"], slot["N"]
        b.mm(psb[b0][:], tri, X[:, 0:512], True, False, reads=["cbf", Xr], writes=[("ps", b0)])
        b.mm(psb[b0][:], e127, Sb[prv][:, 0:512], False, True, reads=["cbf", SbK[prv]], writes=[("ps", b0)])
        b.mm(psb[b1][:], tri, X[:, 512:1024], True, False, reads=["cbf", Xi], writes=[("ps", b1)])
        b.mm(psb[b1][:], e127n, Sb[prv][:, 512:1024], False, True, reads=["cbf", SbK[prv]], writes=[("ps", b1)])
        b.v("dve", "tensor_tensor", K_, psb[b0][:], Tre, ALU.mult, reads=[("ps", b0)] + TK + [Xi], writes=[sk("K")])
        b.v("dve", "tensor_tensor", M_, psb[b1][:], Tim, ALU.mult, reads=[("ps", b1)] + TK + [Xr], writes=[sk("M")])
        b.v("pool", "tensor_tensor", Sb[cur][:, 0:512], K_, M_, ALU.subtract, reads=[sk("K"), sk("M")], writes=[SbK[cur]])
        b.v("dve", "scalar_tensor_tensor", N_, psb[b1][:], -1.0, Tre, ALU.mult, ALU.mult,
            reads=[("ps", b1)] + TK + [Xi], writes=[sk("N")])
        b.v("dve", "scalar_tensor_tensor", K_, psb[b0][:], -1.0, Tim, ALU.mult, ALU.mult,
            reads=[("ps", b0)] + TK + [SbK[cur]], writes=[sk("K")])
        b.v("pool", "tensor_tensor", Sb[cur][:, 512:1024], N_, K_, ALU.add,
            reads=[sk("K"), sk("N")], writes=[SbK[cur]])

    def ssm_tile3(l, cx, k):
        slot, sk, ch, c = cx["slot"], cx["sk"], cx["ch"], cx["c"]
        Cm, Sb, SbK, ST = cx["Cm"], cx["Sb"], cx["SbK"], cx["ST"]
        b0, b1, tb, yb = SSM_BANKS[ch]
        g4 = k // 4
        cur = k % 2
        for blk in range(8):
            b.tr(psb_bf[tb][:, blk * 128:(blk + 1) * 128], Sb[cur][:, blk * 128:(blk + 1) * 128], ident,
                 reads=[SbK[cur], "cbf"], writes=[("ps", tb)])
        b.act(ST, psb_bf[tb][:, :].rearrange("p (k n) -> p k n", k=8), AF.Copy, reads=[("ps", tb)], writes=[sk("J")])
        for blk in range(8):
            b.mm(psb[yb][:, (k % 4) * 128:(k % 4 + 1) * 128], Cm[:, blk, :], ST[:, blk, :], blk == 0, blk == 7,
                 reads=[sk("B"), sk("J")], writes=[("ps", yb)])
        if k % 4 == 3:
            t4 = slice(g4 * 512, (g4 + 1) * 512)
            b.v("dve", "scalar_tensor_tensor", uT[:, c, t4], uT[:, c, t4], vec[:, l, 35 + c:36 + c], psb[yb][:],
                ALU.mult, ALU.add, reads=[("uT", c, g4), "vec", ("ps", yb)], writes=[("uT", c, g4)])

    def ssm(l, half=1):
        for pair in range(2):
            cxs = [ssm_prep(l, 2 * pair + ch, ch, half) for ch in range(2)]
            for k in range(16):
                for stage in (1, 2, 3):
                    for cx in cxs:
                        ssm_tile(l, cx, k, stage)
            if half == 0:
                for cx in cxs:
                    c = cx["c"]
                    b.dma(hfin_d[l, c:c + 1, :], cx["Sb"][1][127:128, :], reads=[cx["SbK"][1]], writes=[("sh", l, c)], key="hfin")
        dbg("yssm", uT, [128, 4, T], [("uT", c, g) for c in range(4) for g in range(4)], BF16)


    def group_norm(buf, keyf, gcol, l, g4):
        tok = slice(g4 * 512, (g4 + 1) * 512)
        ks = [keyf(c, g4) for c in range(4)]
        b.act(sq[:, 0:4, :], buf[:, :, tok], AF.Square, reads=ks, writes=SQK)
        pi = next_ps()
        rms_scale(SQK, 4, 512, lambda c: sq[:, c, :], pi, 1.0 / 512)
        for c in range(4):
            b.v("dve", "scalar_tensor_tensor", buf[:, c, tok], buf[:, c, tok], vec[:, l, gcol + c:gcol + c + 1], rstd[:],
                ALU.mult, ALU.mult, reads=[keyf(c, g4), "vec", "rstd"], writes=[keyf(c, g4)])

    def ssm_post(l):
        wglu = wb[1][:, 4096:6144].rearrange("p (k n) -> p k n", k=4)
        tK = A32(48, [128, 512])
        tM = A32(50, [128, 512])
        gates = A32(52, [128, 4, 512])
        for g4 in range(4):
            tok = slice(g4 * 512, (g4 + 1) * 512)
            for c in range(4):
                y = uT[:, c, tok]
                yk = ("uT", c, g4)
                b.v("dve", "tensor_tensor", tK, y, y, ALU.mult, reads=[yk], writes=["gK"])
                b.v("dve", "tensor_scalar", tK, tK, GB, 1.0, ALU.mult, ALU.add, reads=["gK"], writes=["gK"])
                b.v("dve", "tensor_tensor", tK, tK, y, ALU.mult, reads=["gK", yk], writes=["gK"])
                b.act(tM, tK, AF.Sigmoid, reads=["gK"], writes=["gM"], scale=GA)
                b.v("dve", "tensor_tensor", y, y, tM, ALU.mult, reads=["gM", yk], writes=[yk])
            for oc in range(4):
                pi = 4 + oc
                for k in range(4):
                    b.mm(psb[pi][:], wglu[:, k, oc * 128:(oc + 1) * 128], uT[:, k, tok], k == 0, k == 3,
                         reads=[("wb", 1, "glu"), ("uT", k, g4)], writes=[("ps", pi)])
                b.act(gates[:, oc, :], psb[pi][:], AF.Sigmoid, reads=[("ps", pi), "vec"], writes=[("gate", oc)],
                      bias=vec[:, l, 39 + oc:40 + oc], scale=1.0)
            for oc in range(4):
                b.v("dve", "tensor_tensor", uT[:, oc, tok], uT[:, oc, tok], gates[:, oc, :], ALU.mult,
                    reads=[("uT", oc, g4), ("gate", oc)], writes=[("uT", oc, g4)])
            group_norm(uT, lambda c, g: ("uT", c, g), 43, l, g4)
        dbg("yssmn", uT, [128, 4, T], [("uT", c, g) for c in range(4) for g in range(4)], BF16)

    def mix_out(l):
        wout = wb[0][:, :].rearrange("p (k n) -> p k n", k=8)
        for g4 in range(4):
            tok = slice(g4 * 512, (g4 + 1) * 512)
            group_norm(ymla, lambda c, g: ("ymla", c, g), 47, l, g4)
            for oc in range(8):
                pi = next_ps()
                for k in range(8):
                    src = uT[:, k, tok] if k < 4 else ymla[:, k - 4, tok]
                    sk_ = ("uT", k, g4) if k < 4 else ("ymla", k - 4, g4)
                    b.mm(psb[pi][:], wout[:, k, oc * 128:(oc + 1) * 128], src, k == 0, k == 7,
                         reads=[("wb", 0), sk_], writes=[("ps", pi)])
                b.v("dve", "tensor_tensor", xT[:, oc, tok], xT[:, oc, tok], psb[pi][:], ALU.add,
                    reads=[("xT", oc, g4), ("ps", pi)], writes=[("xT", oc, g4)])
        dbg("x1", xT[:], [128, 8, T], [("xT", c, g) for c in range(8) for g in range(4)])

    CR_KEYS = ["mhT", "KmT", "Vm", "memx", "mtmp", "mss", "mkn", ("QmT", 0), ("QmT", 1), ("QmT", 2), ("QmT", 3)] + \
              [("omT", h, q) for h in range(4) for q in range(4)]

    def cross(l):
        mhT = A16(0, [128, 8, 256])
        KmT = A16(4, [128, 4, 256])
        Vm = A16(6, [128, 2, 4, 65])
        QmT = A16(8, [128, 4, T])
        omT = A16(24, [128, 4, T])
        memx = A32(40, [128, 8, 256])
        mtmp = A32(48, [128, 4, 64])
        mss = A32(49, [128, 4])
        mkn = A16(50, [128, 4, 64])
        mgq = rowb[:, 192:256]
        mgk = rowb[:, 256:320]
        mwkv = wb[1][:, 0:4096].rearrange("p (k n) -> p k n", k=8)
        mwq = wb[1][:, 4096:6144].rearrange("p (k n) -> p k n", k=8)
        load_w(mwkv, mwkv_d[l], ("wb", 1))
        load_w(mwq, mwq_d[l], ("wb", 1, "glu"))
        b.dma(memx, memT_d, writes=["memx"])
        sqm = sq[:, :, 0:256]
        b.act(sqm, memx, AF.Square, reads=["memx"], writes=SQK)
        pi = next_ps()
        rms_scale(SQK, 8, 256, lambda c: sq[:, c, 0:256], pi, 1.0 / D)
        for c in range(8):
            b.v("dve", "scalar_tensor_tensor", mhT[:, c, :], memx[:, c, :], vec[:, l, 24 + c:25 + c], rstd[:, 0:256],
                ALU.mult, ALU.mult, reads=["memx", "vec", "rstd"], writes=["mhT"])
        b.v("pool", "memset", Vm[:, :, :, 64:65], 1.0, writes=["Vm"])

        def head_norm(src, srck, gain, n_slots_bank, slot_fn):
            b.act(mtmp, src, AF.Square, reads=srck, writes=["mtmp"])
            b.v("dve", "tensor_reduce", mss, mtmp, AX.X, ALU.add, reads=["mtmp"], writes=["mss"])
            b.act(mss, mss, AF.Sqrt, reads=["mss", "cvec"], writes=["mss"], bias=epsv, scale=1.0 / 64)
            b.v("dve", "reciprocal", mss, mss, reads=["mss"], writes=["mss"])
            b.v("dve", "tensor_tensor", mtmp, src, mss.unsqueeze(2).to_broadcast([128, 4, 64]), ALU.mult,
                reads=srck + ["mss"], writes=["mtmp"])
            b.v("dve", "tensor_tensor", mkn, mtmp, gain.unsqueeze(1).to_broadcast([128, 4, 64]), ALU.mult,
                reads=["mtmp", "rowb"], writes=["mkn"])
            for h in range(4):
                sl = slot_fn(h)
                b.tr(psb_bf[n_slots_bank][0:64, sl * 128:(sl + 1) * 128], mkn[:, h, :], ident,
                     reads=["mkn", "cbf"], writes=[("ps", n_slots_bank)])

        for mt in range(2):
            pi = next_ps()
            for k in range(8):
                b.mm(psb[pi][:], mhT[:, k, mt * 128:(mt + 1) * 128], mwkv[:, k, :], k == 0, k == 7,
                     reads=["mhT", ("wb", 1)], writes=[("ps", pi)])
            kvv = psb[pi][:].rearrange("p (h d) -> p h d", h=4)
            b.act(Vm[:, mt, :, 0:64], kvv[:, :, 64:128], AF.Copy, reads=[("ps", pi)], writes=["Vm"])
            head_norm(kvv[:, :, 0:64], [("ps", pi)], mgk, 6, lambda h: h * 2 + mt)
        b.act(KmT[0:64, :, :], psb_bf[6][0:64, :].rearrange("p (h n) -> p h n", h=4), AF.Copy,
              reads=[("ps", 6)], writes=["KmT"])

        for hf in range(2):
            norm_x(l, 8, hf * 1024, 2, None)
            for t8 in range(8):
                t16 = hf * 8 + t8
                pi = next_ps()
                for k in range(8):
                    b.mm(psb[pi][:, 0:256], hT[:, k, t8 * 128:(t8 + 1) * 128], mwq[:, k, :], k == 0, k == 7,
                         reads=[("hT", t8 // 4), ("wb", 1, "glu")], writes=[("ps", pi)])
                head_norm(psb[pi][:, 0:256].rearrange("p (h d) -> p h d", h=4), [("ps", pi)], mgq, 7,
                          lambda h: h * 2 + (t16 % 2))
                if t16 % 2 == 1:
                    b.act(QmT[0:64, :, (t16 // 2) * 256:(t16 // 2 + 1) * 256],
                          psb_bf[7][0:64, :].rearrange("p (h n) -> p h n", h=4), AF.Copy,
                          reads=[("ps", 7)], writes=[("QmT", t16 // 4)])
        for h in range(4):
            for qt in range(4):
                po = 2 + (qt % 2)
                for kb in range(2):
                    b.mm(psb[kb][:], KmT[0:64, h, kb * 128:(kb + 1) * 128], QmT[0:64, h, qt * 512:(qt + 1) * 512], True, True,
                         reads=["KmT", ("QmT", qt)], writes=[("ps", kb)])
                for kb in range(2):
                    pi = kb
                    pt = PT[kb + 2 * (qt % 2)]
                    b.act(pt, psb[pi][:], AF.Exp, reads=[("ps", pi), "cvec"], writes=[("sq", kb + 2 * (qt % 2))], bias=nmshift, scale=1.0)
                    b.mm(psb[po][0:65, :], Vm[:, kb, h, :], pt, kb == 0, kb == 1, reads=["Vm", ("sq", kb + 2 * (qt % 2))], writes=[("ps", po)])
                b.v("dve", "reciprocal", rc[64:65, :], psb[po][64:65, :], reads=[("ps", po)], writes=["rc"])
                b.mm(psb[4][0:64, :], ones_f[64:65, 0:64], rc[64:65, :], True, True, reads=["cst", "rc"], writes=[("ps", 4)])
                b.act(rstd[0:64, :], psb[4][0:64, :], AF.Copy, reads=[("ps", 4)], writes=["rstd"])
                b.v("dve", "tensor_tensor", omT[0:64, h, qt * 512:(qt + 1) * 512], psb[po][0:64, :], rstd[0:64, :], ALU.mult,
                    reads=[("ps", po), "rstd"], writes=[("omT", h, qt)])
        mwo = wb[0][0:64, 0:4096].rearrange("p (h n) -> p h n", h=4)
        load_w(mwo, mwo_d[l], ("wb", 0))
        for g4 in range(4):
            tok = slice(g4 * 512, (g4 + 1) * 512)
            for oc in range(8):
                pi = next_ps()
                for h in range(4):
                    b.mm(psb[pi][:], mwo[:, h, oc * 128:(oc + 1) * 128], omT[0:64, h, tok], h == 0, h == 3,
                         reads=[("wb", 0), ("omT", h, g4)], writes=[("ps", pi)])
                b.v("dve", "tensor_tensor", xT[:, oc, tok], xT[:, oc, tok], psb[pi][:], ALU.add,
                    reads=[("xT", oc, g4), ("ps", pi)], writes=[("xT", oc, g4)])
        dbg("x2", xT[:], [128, 8, T], [("xT", c, g) for c in range(8) for g in range(4)])

    MLP_KEYS = [("hid", j, t) for j in range(32) for t in range(2)] + [("rt", i) for i in range(4)]

    WBH = [wb[0][:, 0:4096], wb[0][:, 4096:8192], wb[1][:, 0:4096], wb[1][:, 4096:8192]]
    WBH_KEYS = [("wbh", i) for i in range(4)] + [("wb", 0), ("wb", 1), ("wb", 1, "glu"), "rc"]

    def mlp(l):
        hid = A16(0, [128, 32, 1024])
        rt = [A16(64, [128, 512]), A16(65, [128, 512]), A16(66, [128, 512]), A16(67, [128, 512])]
        n = 0
        wi = 0
        for hf in range(2):
            norm_x(l, 16, hf * 1024, 2, None)
            for jg in range(8):
                ws = wi % 4
                wi += 1
                w1p = WBH[ws].rearrange("p (k n) -> p k n", k=8)
                b.dma(w1p, w1_d[l][:, :, jg * 512:(jg + 1) * 512], writes=[("wbh", ws)], key=("wh", ws), eng="pool")
                for jj in range(4):
                    j = jg * 4 + jj
                    for t in range(2):
                        pi = next_ps()
                        for k in range(8):
                            b.mm(psb[pi][:], w1p[:, k, jj * 128:(jj + 1) * 128], hT[:, k, t * 512:(t + 1) * 512], k == 0, k == 7,
                                 reads=[("wbh", ws), ("hT", t)], writes=[("ps", pi)])
                        r = rt[n % 4]
                        b.act(r, psb[pi][:], AF.Relu, reads=[("ps", pi)], writes=[("rt", n % 4)])
                        b.v("dve", "tensor_tensor", hid[:, j, t * 512:(t + 1) * 512], r, r, ALU.mult,
                            reads=[("rt", n % 4)], writes=[("hid", j, t)])
                        n += 1
            for oc in range(8):
                ws = wi % 4
                wi += 1
                w2p = WBH[ws].rearrange("p (j n) -> p j n", j=32)
                b.dma(w2p, w2_d[l][:, :, oc * 128:(oc + 1) * 128], writes=[("wbh", ws)], key=("wh", ws), eng="pool")
                for t in range(2):
                    g4 = hf * 2 + t
                    tok = slice(g4 * 512, (g4 + 1) * 512)
                    pi = next_ps()
                    for j in range(32):
                        b.mm(psb[pi][:], w2p[:, j, :], hid[:, j, t * 512:(t + 1) * 512], j == 0, j == 31,
                             reads=[("wbh", ws), ("hid", j, t)], writes=[("ps", pi)])
                    b.v("dve", "tensor_tensor", xT[:, oc, tok], xT[:, oc, tok], psb[pi][:], ALU.add,
                        reads=[("xT", oc, g4), ("ps", pi)], writes=[("xT", oc, g4)])

    def run_layer(l, half=1, stop_after=None):
        rope_tables()
        load_w(wb[1][:, 0:1536].rearrange("p (k n) -> p k n", k=2), w_uq_d[l], ("wb", 1))
        load_w(wb[1][:, 1536:2560], w_ukv_d[l], ("wb", 1))
        load_w(wb[1][:, 4096:6144].rearrange("p (k n) -> p k n", k=4), w_glu_d[l], ("wb", 1, "glu"))
        layer(l)
        load_w(wb[0][:, :].rearrange("p (k n) -> p k n", k=8), w_out_d[l], ("wb", 0))
        YMK = [("ymla", h, q) for h in range(4) for q in range(4)]
        barrier(PROJ_KEYS + SSM_KEYS + HK + YMK)
        ssm(l, half)
        if stop_after == "ssm":
            return
        barrier(SSM_KEYS + HK + YMK + ["gK", "gM"] + [("gate", i) for i in range(4)])
        ssm_post(l)
        if stop_after == "ssm_post":
            return
        barrier(SSM_KEYS + HK + YMK + ["gK", "gM"] + [("gate", i) for i in range(4)] + ATT_KEYS + SQK)
        mla(l, half)
        mix_out(l)
        if stop_after == "mix":
            return
        barrier(MIXER_KEYS + ATT_KEYS + CR_KEYS)
        cross(l)
        if stop_after == "cross":
            return
        barrier(CR_KEYS + MLP_KEYS + WBH_KEYS)
        mlp(l)
        barrier(MLP_KEYS + MIXER_KEYS + PROJ_KEYS + WBH_KEYS)

    def store_x(half):
        for c in range(8):
            b.dma(outT_d[half, :, c, :], xT[:, c, :], reads=[("xT", c, g) for g in range(4)], writes=[("dramout", "out")], key="out")

    def finish():
        b.p.add("sp", lambda e: e.nop(), reads=[("dramout", "out")] + [("dbgdone", k) for k in b.out_keys])

    b.ctx = dict(locals())
    return b


_CACHE = {}
N_LAYERS = 4


def _program():
    if "b" not in _CACHE:
        b = build_program(N_LAYERS)
        c = b.ctx
        for half in range(2):
            c["load_x"](half)
            for l in range(N_LAYERS):
                c["run_layer"](l, half)
            c["store_x"](half)
        c["finish"]()
        b.p.emit(b.nc, b.es)
        _CACHE["b"] = b
    return _CACHE["b"]


def kernel(**inputs):
    inp = {k: np.asarray(v) for k, v in inputs.items()}
    B = inp["x"].shape[0]
    b = _program()
    cst = host_constants()
    W = prep_weights(inp, range(N_LAYERS))
    in_maps = []
    for core in range(8):
        bb = core % B
        xs = np.asarray(inp["x"][bb], np.float32)
        xT = np.stack([chunkT(np.ascontiguousarray(xs[hf * T:(hf + 1) * T].T), 8) for hf in range(2)])
        pos = np.asarray(inp["positions"][bb]).astype(np.float32)
        m = {"xT": xT,
             "memT": chunkT(np.ascontiguousarray(np.asarray(inp["mem"][bb], np.float32).T), 8),
             "pos": np.stack([np.ascontiguousarray(pos[hf * T:(hf + 1) * T].reshape(16, 128).T) for hf in range(2)]),
             "cst": cst}
        m.update(W)
        in_maps.append(m)
    res = run_bass_kernel_spmd(b.nc, in_maps, core_ids=list(range(8)))
    out = np.zeros((B, 2 * T, D), np.float32)
    for bb in range(B):
        o = np.asarray(res.results[bb]["outT"], np.float32)
        for hf in range(2):
            out[bb, hf * T:(hf + 1) * T] = o[hf].transpose(2, 1, 0).reshape(T, D)
    return out
```

```python
import numpy as np
from contextlib import ExitStack
import concourse.bass as bass
import concourse.mybir as mybir
from concourse.bass_utils import run_bass_kernel_spmd

F32 = mybir.dt.float32
BF16 = mybir.dt.bfloat16
I32 = mybir.dt.int32
AF = mybir.ActivationFunctionType
ALU = mybir.AluOpType
AX = mybir.AxisListType

ENGS = ("pe", "act", "dve", "pool", "sp")


class Op:
    __slots__ = ("eng", "fn", "reads", "writes", "dma_key", "idx", "deps", "signal",
                 "tick", "waits")

    def __init__(self, eng, fn, reads, writes, dma_key):
        self.eng = eng
        self.fn = fn
        self.reads = reads
        self.writes = writes
        self.dma_key = dma_key
        self.deps = []
        self.signal = False
        self.tick = None
        self.waits = []


class Prog:
    def __init__(self):
        self.ops = []

    def add(self, eng, fn, reads=(), writes=(), dma_key=None):
        op = Op(eng, fn, tuple(reads), tuple(writes), dma_key)
        op.idx = len(self.ops)
        self.ops.append(op)
        return op

    def analyse(self):
        last_w = {}
        readers = {}
        for op in self.ops:
            deps = set()
            for k in op.reads:
                w = last_w.get(k)
                if w is not None:
                    deps.add(w.idx)
            for k in op.writes:
                w = last_w.get(k)
                if w is not None:
                    deps.add(w.idx)
                for r in readers.get(k, {}).values():
                    deps.add(r.idx)
            deps.discard(op.idx)
            keep = []
            for d in deps:
                a = self.ops[d]
                if a.dma_key is None and op.dma_key is None and a.eng == op.eng:
                    if a.eng == "pe":
                        continue
                    raw = any(k in a.writes for k in op.reads)
                    if not raw:
                        continue
                keep.append(d)
            op.deps = keep
            for d in keep:
                self.ops[d].signal = True
            for k in op.writes:
                last_w[k] = op
                readers[k] = {}
            for k in op.reads:
                rk = op.dma_key if op.dma_key is not None else op.eng
                readers.setdefault(k, {})[rk] = op
        cnt = {e: 0 for e in ENGS}
        dcnt = {}
        for op in self.ops:
            if op.dma_key is not None:
                dcnt[op.dma_key] = dcnt.get(op.dma_key, 0) + 1
                op.tick = ("dma", op.dma_key, dcnt[op.dma_key] * 16)
            elif op.signal:
                cnt[op.eng] += 1
                op.tick = ("eng", op.eng, cnt[op.eng])
        waited = {e: {} for e in ENGS}
        dma_issued = {}
        for op in self.ops:
            need = {}
            for d in op.deps:
                a = self.ops[d]
                t = a.tick
                if t[0] == "dma":
                    v = dma_issued.get(t[1], 0) * 16
                    v = max(v, t[2])
                else:
                    v = t[2]
                sk = (t[0], t[1])
                if v > need.get(sk, 0):
                    need[sk] = v
            for sk, v in need.items():
                if waited[op.eng].get(sk, 0) >= v:
                    continue
                waited[op.eng][sk] = v
                op.waits.append((sk, v))
            if op.dma_key is not None:
                dma_issued[op.dma_key] = dma_issued.get(op.dma_key, 0) + 1
        self.n_eng_sems = cnt
        self.dma_keys = list(dcnt.keys())

    def emit(self, nc, es):
        self.analyse()
        sems = {}
        for e in ENGS:
            sems[("eng", e)] = es.enter_context(nc.semaphore("s_" + e))
        for i, k in enumerate(self.dma_keys):
            sems[("dma", k)] = es.enter_context(nc.semaphore("d%d" % i))
        self.sems = sems
        per = {e: [op for op in self.ops if op.eng == e] for e in ENGS}
        block = es.enter_context(nc.Block())

        def run(engine, lst):
            for op in lst:
                for sk, v in op.waits:
                    engine.wait_ge(sems[sk], v)
                ins = op.fn(engine)
                if op.dma_key is not None:
                    ins.then_inc(sems[("dma", op.dma_key)], 16)
                elif op.signal:
                    ins.then_inc(sems[("eng", op.eng)], 1)

        @block.tensor
        def _(e):
            run(e, per["pe"])

        @block.scalar
        def _(e):
            run(e, per["act"])

        @block.vector
        def _(e):
            run(e, per["dve"])

        @block.gpsimd
        def _(e):
            run(e, per["pool"])

        @block.sync
        def _(e):
            run(e, per["sp"])


D = 1024
T = 2048
NT4 = T // 512
NT1 = T // 128
KC = D // 128
IN_COLS = 928
EPS = 1e-6
PI = float(np.pi)
SHIFT = 10.0


class Builder:
    def __init__(self, nc, es, n_layers, debug=()):
        self.nc = nc
        self.es = es
        self.p = Prog()
        self.L = n_layers
        self.debug = set(debug)
        self.dram_in = {}
        self.dram_out = {}
        self._uid = 0
        self.out_keys = []

    def din(self, name, shape, dt=F32):
        t = self.nc.dram_tensor(name, list(shape), dt, kind="ExternalInput")
        self.dram_in[name] = t
        return t

    def dout(self, name, shape, dt=F32):
        t = self.nc.dram_tensor(name, list(shape), dt, kind="ExternalOutput")
        self.dram_out[name] = t
        return t

    def sb(self, name, shape, dt=F32):
        return self.es.enter_context(self.nc.sbuf_tensor(name, list(shape), dt))

    def ps(self, name, shape, dt=F32):
        return self.es.enter_context(self.nc.psum_tensor(name, list(shape), dt))

    def dma(self, out, in_, reads=(), writes=(), key=None, eng="sp", **kw):
        writes = list(writes)
        if key is None:
            self._uid += 1
            key = "a%d_%s" % (self._uid % 12, eng)
            writes.append(("dmakey", key))
        return self.p.add(eng, lambda e, o=out, i=in_, kw=kw: e.dma_start(out=o, in_=i, **kw),
                          reads=reads, writes=writes, dma_key=key)

    def mm(self, out, lhsT, rhs, start, stop, reads=(), writes=()):
        return self.p.add("pe", lambda e: e.matmul(out, lhsT, rhs, start=start, stop=stop),
                          reads=reads, writes=writes)

    def tr(self, out, in_, ident, reads=(), writes=()):
        return self.p.add("pe", lambda e: e.transpose(out, in_, ident), reads=reads, writes=writes)

    def act(self, out, in_, func, reads=(), writes=(), eng="act", **kw):
        return self.p.add("act", lambda e: e.activation(out, in_, func, **kw), reads=reads, writes=writes)

    def v(self, eng, name, *args, reads=(), writes=(), **kw):
        return self.p.add(eng, lambda e: getattr(e, name)(*args, **kw), reads=reads, writes=writes)


def host_constants():
    c = np.zeros((128, 1664), np.float32)
    c[:, 0:128] = np.eye(128)
    c[:, 128:256] = 1.0
    r = np.arange(128)
    c[:, 256:384] = (r[:, None] <= r[None, :])
    c[127, 384:512] = 1.0
    c[127, 512:640] = -1.0
    m = np.arange(896)
    c[:, 640:1536] = (m[None, :] - r[:, None] >= 384)
    c[:, 1536:1552] = (10000.0 ** (-np.arange(16, dtype=np.float32) / 16))[None, :]
    c[:, 1552] = r + 1
    c[:, 1553] = -(r + 1)
    return c


def chunkT(v, k):
    return np.ascontiguousarray(v.reshape((k, 128) + v.shape[1:]).swapaxes(0, 1))


def prep_weights(inp, layers):
    f = lambda a: np.asarray(a, np.float32)
    W = {}
    Ls = list(layers)
    n = len(Ls)
    vec = np.zeros((n, 128, 64), np.float32)
    row = np.zeros((n, 320), np.float32)
    bm = np.zeros((n, 4, 128, 1024), np.float32)
    cm = np.zeros((n, 4, 128, 8, 128), np.float32)
    lam = np.zeros((n, 4, 3, 512), np.float32)
    for i, l in enumerate(Ls):
        vec[i, :, 0:8] = f(inp["norm_mix"][l]).reshape(8, 128).T
        vec[i, :, 8:16] = f(inp["norm_mem_q"][l]).reshape(8, 128).T
        vec[i, :, 16:24] = f(inp["norm_mlp"][l]).reshape(8, 128).T
        vec[i, :, 24:32] = f(inp["norm_mem_kv"][l]).reshape(8, 128).T
        vec[i, :, 32:34] = f(inp["mla_q_norm"][l]).reshape(2, 128).T
        vec[i, :, 34] = f(inp["mla_kv_norm"][l])
        vec[i, :, 35:39] = f(inp["ssm_d"][l]).reshape(4, 128).T
        vec[i, :, 39:43] = f(inp["ssm_b_glu"][l]).reshape(4, 128).T
        vec[i, :, 43:47] = f(inp["out_norm_ssm"][l]).reshape(4, 128).T
        vec[i, :, 47:51] = f(inp["out_norm_mla"][l]).reshape(4, 128).T
        row[i, 0:96] = f(inp["mla_q_gain"][l])
        row[i, 96:192] = f(inp["mla_k_gain"][l])
        row[i, 192:256] = f(inp["mem_q_gain"][l])
        row[i, 256:320] = f(inp["mem_k_gain"][l])
        bre = f(inp["ssm_b_re"][l])
        bim = f(inp["ssm_b_im"][l])
        cre = f(inp["ssm_c_re"][l])
        cim = f(inp["ssm_c_im"][l])
        lre = f(inp["ssm_lambda_re"][l])
        lim = f(inp["ssm_lambda_im"][l])
        lst = f(inp["ssm_log_step"][l])
        for c in range(4):
            for j in range(8):
                g = 8 * c + j
                bm[i, c, 16 * j:16 * j + 16, j * 64:(j + 1) * 64] = bre[g].T
                bm[i, c, 16 * j:16 * j + 16, 512 + j * 64:512 + (j + 1) * 64] = bim[g].T
                pr, e = j // 2, j % 2
                cm[i, c, e * 64:(e + 1) * 64, pr, 16 * j:16 * j + 16] = cre[g].T
                cm[i, c, e * 64:(e + 1) * 64, 4 + pr, 16 * j:16 * j + 16] = cim[g].T
                lam[i, c, 0, j * 64:(j + 1) * 64] = lre[g]
                lam[i, c, 1, j * 64:(j + 1) * 64] = lim[g]
                lam[i, c, 2, j * 64:(j + 1) * 64] = lst[g]
    W["vec"] = vec
    W["row"] = row
    W["bm"] = bm
    W["cm"] = cm
    W["lam"] = lam
    st = lambda name, k: np.stack([chunkT(f(inp[name][l]), k) for l in Ls])
    W["w_in"] = st("w_in", 8)
    W["w_uq"] = st("mla_w_uq", 2)
    W["w_ukv"] = np.stack([f(inp["mla_w_ukv"][l]) for l in Ls])
    W["w_glu"] = st("ssm_w_glu", 4)
    W["w_out"] = st("w_out", 8)
    W["mwq"] = st("mem_w_q", 8)
    W["mwkv"] = st("mem_w_kv", 8)
    W["mwo"] = np.stack([np.ascontiguousarray(f(inp["mem_w_o"][l]).reshape(4, 64, 1024).swapaxes(0, 1)) for l in Ls])
    W["w1"] = st("mlp_w1", 8)
    W["w2"] = st("mlp_w2", 32)
    return W


TWO_PI = 2.0 * PI
GA = 1.5957691216057308
GB = 0.044715
QSCALE = 1.0 / float(np.sqrt(96.0))
MSCALE = 1.0 / 8.0
MSHIFT = 8.0
NEG_BIG = -30000.0


def build_program(n_layers=1, debug=()):
    nc = bass.Bass("TRN2", target_bir_lowering=False)
    es = ExitStack()
    b = Builder(nc, es, n_layers, debug)
    P = b.p
    L = n_layers

    xT_d = b.din("xT", [2, 128, 8, T]).ap()
    memT_d = b.din("memT", [128, 8, 256]).ap()
    pos_d = b.din("pos", [2, 128, 16]).ap()
    hinit_d = nc.dram_tensor("sc_h", [L, 4, 1024], BF16, kind="Internal").ap()
    pkt_d = nc.dram_tensor("sc_kt", [L, 4, 96, 2, T], BF16, kind="Internal").ap()
    pve_d = nc.dram_tensor("sc_ve", [L, 4, 128, 16, 65], BF16, kind="Internal").ap()
    pvo_d = nc.dram_tensor("sc_vo", [L, 4, 128, 16, 128], BF16, kind="Internal").ap()
    cst_d = b.din("cst", [128, 1664]).ap()
    vec_d = b.din("vec", [L, 128, 64]).ap()
    row_d = b.din("row", [L, 192 + 64 + 64]).ap()
    w_in_d = b.din("w_in", [L, 128, 8, IN_COLS]).ap()
    w_uq_d = b.din("w_uq", [L, 128, 2, 768]).ap()
    w_ukv_d = b.din("w_ukv", [L, 128, 1024]).ap()
    w_glu_d = b.din("w_glu", [L, 128, 4, 512]).ap()
    w_out_d = b.din("w_out", [L, 128, 8, 1024]).ap()
    mwq_d = b.din("mwq", [L, 128, 8, 256]).ap()
    mwkv_d = b.din("mwkv", [L, 128, 8, 512]).ap()
    mwo_d = b.din("mwo", [L, 64, 4, 1024]).ap()
    w1_d = b.din("w1", [L, 128, 8, 4096]).ap()
    w2_d = b.din("w2", [L, 128, 32, 1024]).ap()
    bm_d = b.din("bm", [L, 4, 128, 1024]).ap()
    cm_d = b.din("cm", [L, 4, 128, 8, 128]).ap()
    lam_d = b.din("lam", [L, 4, 3, 512]).ap()
    outT_d = b.dout("outT", [2, 128, 8, T]).ap()
    okt_d, ove_d, ovo_d, hfin_d = pkt_d, pve_d, pvo_d, hinit_d
    dbg_d = {}

    xT = b.sb("xT_s", [128, 8, T])
    hT = b.sb("hT_s", [128, 8, 1024], BF16)
    wb = [b.sb("wb%d" % i, [128, 8192], BF16) for i in range(2)]
    cst = b.sb("cst_s", [128, 160])
    cbf = b.sb("cbf_s", [128, 1536], BF16)
    vec = b.sb("vec_s", [128, L, 64])
    rowb = b.sb("rowb_s", [128, 320])
    sq = b.sb("sq_s", [128, 8, 512], BF16)
    rstd = b.sb("rstd_s", [128, 512])
    posf = b.sb("pos_s", [128, 16])
    cvec = b.sb("cvec_s", [128, 8])
    AR = 37888
    arena = b.sb("arena", [128, AR], BF16)
    arena_f = arena.bitcast(F32)

    psp = [b.ps("psp%d" % i, [128, 1024]) for i in range(4)]
    psb = [psp[i // 2][:, (i % 2) * 512:(i % 2 + 1) * 512] for i in range(8)]
    psb_bf = [a_.bitcast(BF16) for a_ in psb]

    ident = cbf[:, 0:128]
    ones = cbf[:, 128:256]
    tri = cbf[:, 256:384]
    e127 = cbf[:, 384:512]
    e127n = cbf[:, 512:640]
    cmask = cbf[:, 640:1536]
    ones_f = cst[:, 0:128]
    invf = cst[:, 128:144]
    sp1 = cst[:, 144:145]
    nsp1 = cst[:, 145:146]

    def A16(off_kib, shape):
        n = int(np.prod(shape[1:]))
        o = int(round(off_kib * 512))
        assert o + n <= AR, (off_kib, shape)
        v = arena[:, o:o + n]
        if len(shape) == 3:
            v = v.rearrange("p (a b) -> p a b", a=shape[1])
        elif len(shape) == 4:
            v = v.rearrange("p (a b c) -> p a b c", a=shape[1], b=shape[2])
        return v

    def A32(off_kib, shape):
        n = int(np.prod(shape[1:]))
        o = int(round(off_kib * 256))
        assert (o + n) * 2 <= AR, (off_kib, shape)
        v = arena_f[:, o:o + n]
        if len(shape) == 3:
            v = v.rearrange("p (a b) -> p a b", a=shape[1])
        elif len(shape) == 4:
            v = v.rearrange("p (a b c) -> p a b c", a=shape[1], b=shape[2])
        return v

    uT = A16(0, [128, 4, T])
    cqn = A16(16, [128, 2, T])
    ckvn = A16(24, [128, T])
    krope = A32(28, [128, 16, 32])
    cossin = A32(30, [128, 2, 16, 16])
    ymla = A16(32, [128, 4, T])
    WK = 48

    b.dma(cst[:, 0:128], cst_d[:, 128:256], writes=["cst"])
    b.dma(cst[:, 128:160], cst_d[:, 1536:1568], writes=["cst"])
    b.dma(cbf[:], cst_d[:, 0:1536], writes=["cbf"], eng="pool")
    b.dma(vec[:], vec_d.rearrange("l p k -> p l k"), writes=["vec"])
    def load_x(half):
        b.dma(posf[:], pos_d[half], writes=["pos"])
        for c in range(8):
            b.dma(xT[:, c, :], xT_d[half, :, c, :], writes=[("xT", c, t4) for t4 in range(4)])
    b.v("pool", "memset", cvec[:, 0:1], -SHIFT, writes=["cvec"])
    b.v("pool", "memset", cvec[:, 1:2], -MSHIFT, writes=["cvec"])
    b.v("pool", "memset", cvec[:, 2:3], -PI, writes=["cvec"])
    b.v("pool", "memset", cvec[:, 3:4], EPS, writes=["cvec"])
    nshift = cvec[:, 0:1]
    nmshift = cvec[:, 1:2]
    npi = cvec[:, 2:3]
    epsv = cvec[:, 3:4]

    cos_t = cossin[:, 0]
    sin_t = cossin[:, 1]

    def wrap_pi(dst, src, ti, tf, tm, rk, wk):
        b.v("dve", "tensor_scalar_mul", ti, src, 1.0 / TWO_PI, reads=rk, writes=[wk + "_ti"])
        b.v("dve", "tensor_copy", tf, ti, reads=[wk + "_ti"], writes=[wk + "_tf"])
        b.v("dve", "scalar_tensor_tensor", dst, tf, -TWO_PI, src, ALU.mult, ALU.add, reads=[wk + "_tf"] + rk, writes=[wk])
        b.v("dve", "tensor_scalar", tm, dst, PI, -TWO_PI, ALU.is_gt, ALU.mult, reads=[wk], writes=[wk + "_tm"])
        b.v("dve", "tensor_tensor", dst, dst, tm, ALU.add, reads=[wk, wk + "_tm"], writes=[wk])
        b.v("dve", "tensor_scalar", tm, dst, -PI, TWO_PI, ALU.is_lt, ALU.mult, reads=[wk], writes=[wk + "_tm"])
        b.v("dve", "tensor_tensor", dst, dst, tm, ALU.add, reads=[wk, wk + "_tm"], writes=[wk])

    def rope_tables():
        ang = A32(WK, [128, 16, 16])
        ang2 = A32(WK + 1, [128, 16, 16])
        ti = A32(WK + 2, [128, 16, 16]).bitcast(I32)
        tf = A32(WK + 3, [128, 16, 16])
        tm = A32(WK + 4, [128, 16, 16])
        wr = A32(WK + 5, [128, 16, 16])
        for t in range(16):
            b.v("dve", "tensor_scalar_mul", ang[:, t, :], invf, posf[:, t:t + 1],
                reads=["cst", "pos"], writes=["ropeang"])
        wrap_pi(wr, ang, ti, tf, tm, ["ropeang"], "ropew")
        b.act(sin_t, wr, AF.Sin, reads=["ropew"], writes=["sin"])
        b.v("dve", "tensor_scalar_add", ang2, ang, PI / 2, reads=["ropeang"], writes=["ropeang2"])
        wrap_pi(wr, ang2, ti, tf, tm, ["ropeang2", "sin"], "ropew")
        b.act(cos_t, wr, AF.Sin, reads=["ropew"], writes=["cos"])

    state = {"ps": 0, "wb": 0}
    SQK = [("sq", i) for i in range(8)]

    def next_ps():
        i = state["ps"]
        state["ps"] = (i + 1) % 4
        return i

    def load_w(view, src, key, eng="pool"):
        return b.dma(view, src, writes=[key], key=("w", key), eng=eng)

    def rms_scale(src_keys, nchunk, n, src_fn, ps_i, inv_dim):
        for c in range(nchunk):
            b.mm(psb[ps_i][:, 0:n], ones, src_fn(c), c == 0, c == nchunk - 1,
                 reads=["cbf"] + src_keys, writes=[("ps", ps_i)])
        b.act(rstd[:, 0:n], psb[ps_i][:, 0:n], AF.Sqrt, reads=[("ps", ps_i), "cvec"], writes=["rstd"],
              bias=epsv, scale=inv_dim)
        b.v("dve", "reciprocal", rstd[:, 0:n], rstd[:, 0:n], reads=["rstd"], writes=["rstd"])

    def norm_x(l, gain_col, tok0, nt, hkeys):
        for t in range(nt):
            g4 = (tok0 // 512) + t
            xs = xT[:, :, g4 * 512:(g4 + 1) * 512]
            xk = [("xT", c, g4) for c in range(8)]
            b.act(sq[:], xs, AF.Square, reads=xk, writes=SQK)
            pi = next_ps()
            rms_scale(SQK, 8, 512, lambda c: sq[:, c, :], pi, 1.0 / D)
            for c in range(8):
                b.v("dve", "scalar_tensor_tensor", hT[:, c, t * 512:(t + 1) * 512], xT[:, c, g4 * 512:(g4 + 1) * 512],
                    vec[:, l, gain_col + c:gain_col + c + 1], rstd[:], ALU.mult, ALU.mult,
                    reads=[("xT", c, g4), "vec", "rstd"], writes=[("hT", t)])

    def dbg(name, view, shape, reads, dt=F32):
        if name in b.debug:
            d = b.dout("dbg_" + name, shape, dt).ap()
            b.dma(d, view, reads=reads, writes=[("dbgdone", "dbg_" + name)], key="dbg_" + name)
            b.out_keys.append("dbg_" + name)

    def layer(l):
        V0 = lambda c: vec[:, l, c:c + 1]
        b.dma(rowb[:], row_d[l].partition_broadcast(128), writes=["rowb"])
        gq = rowb[:, 0:96]
        gk = rowb[:, 96:192]
        mgq = rowb[:, 192:256]
        mgk = rowb[:, 256:320]
        b.v("dve", "tensor_scalar_mul", gq, gq, QSCALE, reads=["rowb"], writes=["rowb"])
        b.v("dve", "tensor_scalar_mul", mgq, mgq, MSCALE, reads=["rowb"], writes=["rowb"])

        w_in = wb[0][:, 0:8 * IN_COLS].rearrange("p (k n) -> p k n", k=8)
        load_w(w_in, w_in_d[l], ("wb", 0))
        cqraw = A32(WK, [128, 3, 512])
        for hf in range(2):
            norm_x(l, 0, hf * 1024, 2, None)
            for t in range(2):
                g4 = hf * 2 + t
                tok = slice(g4 * 512, (g4 + 1) * 512)
                hk = [("hT", t)]
                for oc in range(7):
                    pi = next_ps()
                    for k in range(8):
                        b.mm(psb[pi][:], w_in[:, k, oc * 128:(oc + 1) * 128], hT[:, k, t * 512:(t + 1) * 512],
                             k == 0, k == 7, reads=[("wb", 0)] + hk, writes=[("ps", pi)])
                    if oc < 4:
                        b.act(uT[:, oc, tok], psb[pi][:], AF.Copy, reads=[("ps", pi)], writes=[("uT", oc, g4)])
                    else:
                        b.act(cqraw[:, oc - 4, :], psb[pi][:], AF.Copy, reads=[("ps", pi)], writes=[("cqraw", oc - 4)])
                b.act(sq[:, 0:2, :], cqraw[:, 0:2, :], AF.Square, reads=[("cqraw", 0), ("cqraw", 1)], writes=SQK)
                pi = next_ps()
                rms_scale(SQK, 2, 512, lambda c: sq[:, c, :], pi, 1.0 / 256)
                for c in range(2):
                    b.v("dve", "scalar_tensor_tensor", cqn[:, c, tok], cqraw[:, c, :], V0(32 + c), rstd[:],
                        ALU.mult, ALU.mult, reads=[("cqraw", c), "vec", "rstd"], writes=[("cqn", g4)])
                b.act(sq[:, 2, :], cqraw[:, 2, :], AF.Square, reads=[("cqraw", 2)], writes=SQK)
                pi = next_ps()
                rms_scale(SQK, 1, 512, lambda c: sq[:, 2, :], pi, 1.0 / 128)
                b.v("dve", "scalar_tensor_tensor", ckvn[:, tok], cqraw[:, 2, :], V0(34), rstd[:],
                    ALU.mult, ALU.mult, reads=[("cqraw", 2), "vec", "rstd"], writes=[("ckvn", g4)])
                for s4 in range(4):
                    t16 = g4 * 4 + s4
                    pi = next_ps()
                    for k in range(8):
                        b.mm(psb[pi][:, 0:32], hT[:, k, t * 512 + s4 * 128:t * 512 + (s4 + 1) * 128], w_in[:, k, 896:928],
                             k == 0, k == 7, reads=[("wb", 0)] + hk, writes=[("ps", pi)])
                    b.act(krope[:, t16, :], psb[pi][:, 0:32], AF.Copy, reads=[("ps", pi)], writes=[("krope", t16)])
        dbg("uT", uT, [128, 4, T], [("uT", c, g) for c in range(4) for g in range(4)], BF16)
        dbg("cqn", cqn, [128, 2, T], [("cqn", g) for g in range(4)], BF16)
        dbg("ckvn", ckvn, [128, T], [("ckvn", g) for g in range(4)], BF16)
        dbg("krope", krope, [128, 16, 32], [("krope", g) for g in range(16)])
        dbg("cos", cos_t, [128, 16, 16], ["cos"])
        dbg("sin", sin_t, [128, 16, 16], ["sin"])


    hT_flat = hT[:].rearrange("p a b -> p (a b)")
    pkt_s = hT_flat[:, 0:4096].rearrange("p (e t) -> p e t", e=2)
    pve_s = hT_flat[:, 4096:4096 + 1040].rearrange("p (t c) -> p t c", t=16)
    pvo_s = hT_flat[:, 5248:5248 + 2048].rearrange("p (t c) -> p t c", t=16)
    HK = [("hT", 0), ("hT", 1)]
    QT = A16(48, [128, 2, T])
    KT = A16(56, [128, 2, T])
    Ve = A16(64, [128, 16, 65])
    Vo = A16(66.5, [128, 16, 128])
    sq_f = sq[:].rearrange("p a b -> p (a b)").bitcast(F32)
    sqq = sq_f[:, 0:768].rearrange("p (g e d) -> p g e d", g=4, e=2)
    qn = sq_f[:, 768:1536].rearrange("p (g e d) -> p g e d", g=4, e=2)
    SQ6 = [("sq", i) for i in range(6)]
    kraw = wb[1].bitcast(F32)[:, 2048:2816].rearrange("p (g e d) -> p g e d", g=4, e=2)
    KRK = ("wb", 1, "glu")
    qf = A16(70.5, [128, 4, 192]).rearrange("p g (e d) -> p g e d", e=2)
    rta = A32(72, [128, 4, 32]).rearrange("p g (e d) -> p g e d", e=2)
    rtb = A32(72.5, [128, 4, 32]).rearrange("p g (e d) -> p g e d", e=2)
    ssq = A32(73, [128, 4, 2])
    wb1f = wb[1].bitcast(F32)
    rc = wb1f[:, 3584:4096]
    PT = [sq[:, i, :] for i in range(8)]

    rstd_bf = rstd[:].bitcast(BF16)
    sqq_k = wb[1].bitcast(F32)[:, 2816:3584].rearrange("p (g e d) -> p g e d", g=4, e=2)
    qf_k = rstd_bf[:, 0:768].rearrange("p (g e d) -> p g e d", g=4, e=2)
    ssq_k = rstd[:, 384:392].rearrange("p (g e) -> p g e", g=4)

    def qk_path(which, src, gain, dstT, ps_bank, g4, srck, tag):
        if which == "q":
            sq_, qn_, qf_, ssq_ = sqq, qn, qf, ssq
            ksq, kqn, kqf, kss = [("sq", i) for i in range(3)], [("sq", i) for i in range(3, 6)], "qf", "ssq"
        else:
            sq_, qn_, qf_, ssq_ = sqq_k, src, qf_k, ssq_k
            ksq, kqn, kqf, kss = [KRK], srck, "rstd", "rstd"
        cs = cos_t[:, g4 * 4:(g4 + 1) * 4, :].unsqueeze(2).to_broadcast([128, 4, 2, 16])
        sn = sin_t[:, g4 * 4:(g4 + 1) * 4, :].unsqueeze(2).to_broadcast([128, 4, 2, 16])
        g_b = gain.unsqueeze(1).unsqueeze(1).to_broadcast([128, 4, 2, 96])
        st = []
        st.append(lambda: b.act(sq_, src, AF.Square, reads=srck, writes=ksq))
        st.append(lambda: b.v("dve", "tensor_reduce", ssq_, sq_, AX.X, ALU.add, reads=ksq, writes=[kss]))
        st.append(lambda: b.act(ssq_, ssq_, AF.Sqrt, reads=[kss, "cvec"], writes=[kss], bias=epsv, scale=1.0 / 96))
        st.append(lambda: b.v("dve", "reciprocal", ssq_, ssq_, reads=[kss], writes=[kss]))
        st.append(lambda: b.v("dve", "tensor_tensor", qn_, src, ssq_.unsqueeze(3).to_broadcast([128, 4, 2, 96]), ALU.mult,
                              reads=srck + [kss], writes=kqn))
        st.append(lambda: b.v("dve", "tensor_tensor", qn_, qn_, g_b, ALU.mult, reads=kqn + ["rowb"], writes=kqn))

        def rope():
            b.v("pool", "tensor_copy", qf_[:, :, :, 0:64], qn_[:, :, :, 0:64], reads=kqn, writes=[kqf])
            b.v("pool", "tensor_tensor", rta, qn_[:, :, :, 64:80], cs, ALU.mult, reads=kqn + ["cos"], writes=["rta"])
            b.v("pool", "tensor_tensor", rtb, qn_[:, :, :, 80:96], sn, ALU.mult, reads=kqn + ["sin"], writes=["rtb"])
            b.v("pool", "tensor_tensor", qf_[:, :, :, 64:80], rta, rtb, ALU.subtract, reads=["rta", "rtb"], writes=[kqf])
            b.v("pool", "tensor_tensor", rta, qn_[:, :, :, 80:96], cs, ALU.mult, reads=kqn + ["cos", kqf], writes=["rta"])
            b.v("pool", "tensor_tensor", rtb, qn_[:, :, :, 64:80], sn, ALU.mult, reads=kqn + ["sin", kqf], writes=["rtb"])
            b.v("pool", "tensor_tensor", qf_[:, :, :, 80:96], rta, rtb, ALU.add, reads=["rta", "rtb"], writes=[kqf])
        st.append(rope)

        def trans():
            for i in range(4):
                for e in range(2):
                    slot = e * 4 + i
                    b.tr(psb_bf[ps_bank][0:96, slot * 128:(slot + 1) * 128], qf_[:, i, e, :], ident,
                         reads=[kqf, "cbf"], writes=[("ps", ps_bank)])
        st.append(trans)
        st.append(lambda: b.act(dstT[0:96, :, g4 * 512:(g4 + 1) * 512],
                                psb_bf[ps_bank][0:96, :].rearrange("p (e n) -> p e n", e=2), AF.Copy,
                                reads=[("ps", ps_bank)], writes=[(tag, g4)]))
        return st

    def mla(l, half=1):
        wuq = wb[1][:, 0:1536].rearrange("p (k n) -> p k n", k=2)
        wukv = wb[1][:, 1536:2560]
        gq = rowb[:, 0:96]
        gk = rowb[:, 96:192]
        b.v("pool", "memset", Ve[:, :, 64:65], 1.0, writes=["Ve"])
        b.v("pool", "memset", Vo[:, :, 0:64], 0.0, writes=["Vo"])
        b.v("pool", "memset", Vo[:, :, 0:1], 1.0, writes=["Vo"])
        for hp in range(4):
            if half == 1:
                b.dma(pkt_s[0:96], pkt_d[l, hp], reads=[("skv", l, hp)], writes=HK, key="pk")
                b.dma(pve_s, pve_d[l, hp], reads=[("skv", l, hp)], writes=HK, key="pk")
                b.dma(pvo_s, pvo_d[l, hp], reads=[("skv", l, hp)], writes=HK, key="pk")
            qps4 = psp[0][:, :].rearrange("p (g n) -> p g n", g=4)
            kvps4 = psp[1][:, :].rearrange("p (g n) -> p g n", g=4)
            PQ = [("ps", 0), ("ps", 1)]
            PKV = [("ps", 2), ("ps", 3)]
            for g4 in range(4):
                for i in range(4):
                    t16 = g4 * 4 + i
                    tok = slice(t16 * 128, (t16 + 1) * 128)
                    for k in range(2):
                        b.mm(qps4[:, i, 0:192], cqn[:, k, tok], wuq[:, k, hp * 192:(hp + 1) * 192], k == 0, k == 1,
                             reads=[("cqn", g4), ("wb", 1)], writes=[("ps", i // 2)])
                    b.mm(kvps4[:, i, :], ckvn[:, tok], wukv[:, hp * 256:(hp + 1) * 256], True, True,
                         reads=[("ckvn", g4), ("wb", 1)], writes=[("ps", 2 + i // 2)])
                kv4 = kvps4.rearrange("p g (e d) -> p g e d", e=2)
                b.act(kraw[:, :, :, 0:64], kv4[:, :, :, 0:64], AF.Copy, reads=PKV, writes=[KRK])
                b.v("pool", "tensor_copy", kraw[:, :, :, 64:96],
                    krope[:, g4 * 4:(g4 + 1) * 4, :].unsqueeze(2).to_broadcast([128, 4, 2, 32]),
                    reads=[("krope", g4 * 4 + i) for i in range(4)], writes=[KRK])
                sq_st = qk_path("q", qps4[:, :, 0:192].rearrange("p g (e d) -> p g e d", e=2), gq, QT, 6, g4, PQ, "QT")
                sk_st = qk_path("k", kraw, gk, KT, 7, g4, [KRK], "KT")
                sq_st[0]()
                b.act(Ve[:, g4 * 4:(g4 + 1) * 4, 0:64], kvps4[:, :, 64:128], AF.Copy, reads=PKV, writes=["Ve"])
                b.act(Vo[:, g4 * 4:(g4 + 1) * 4, 64:128], kvps4[:, :, 192:256], AF.Copy, reads=PKV, writes=["Vo"])
                sk_st[0]()
                for a_, b_ in zip(sq_st[1:], sk_st[1:]):
                    a_()
                    b_()
            QK = [("QT", g) for g in range(4)]
            KK = [("KT", g) for g in range(4)]
            if half == 0:
                b.dma(okt_d[l, hp], KT[0:96], reads=KK, writes=[("skv", l, hp)], key="okv")
                b.dma(ove_d[l, hp], Ve, reads=["Ve"], writes=[("skv", l, hp)], key="okv")
                b.dma(ovo_d[l, hp], Vo, reads=["Vo"], writes=[("skv", l, hp)], key="okv")
            for e in range(2):
                Vt = Ve if e == 0 else Vo
                vk = "Ve" if e == 0 else "Vo"
                pV = pve_s if e == 0 else pvo_s
                M = 65 if e == 0 else 128
                for qt in range(4):
                    blocks = ([("p", kb) for kb in range(16)] if half == 1 else []) + [("o", kb) for kb in range(4 * qt + 4)]
                    po = 2 + (qt % 2)
                    SB = [0, 1, 5]

                    def emit_qk(bi):
                        kind, kb = blocks[bi]
                        pi = SB[bi % 3]
                        kts = (pkt_s if kind == "p" else KT)[0:96, e, kb * 128:(kb + 1) * 128]
                        kk = HK if kind == "p" else [("KT", kb // 4)]
                        b.mm(psb[pi][:], kts, QT[0:96, e, qt * 512:(qt + 1) * 512], True, True,
                             reads=kk + [("QT", qt)], writes=[("ps", pi)])

                    emit_qk(0)
                    if len(blocks) > 1:
                        emit_qk(1)
                    for bi, (kind, kb) in enumerate(blocks):
                        pi = SB[bi % 3]
                        pt = PT[bi % 4]
                        ptk = ("sq", bi % 4)
                        b.act(pt, psb[pi][:], AF.Exp, reads=[("ps", pi), "cvec"], writes=[ptk],
                              bias=nshift, scale=1.0)
                        if kind == "o" and kb >= 4 * qt:
                            o = kb - 4 * qt
                            b.v("dve", "tensor_tensor", pt, pt, cmask[:, (3 - o) * 128:(3 - o) * 128 + 512], ALU.mult,
                                reads=[ptk, "cbf"], writes=[ptk])
                        if bi + 2 < len(blocks):
                            emit_qk(bi + 2)
                        vs = (pV if kind == "p" else Vt)[:, kb, 0:M]
                        b.mm(psb[po][0:M, :], vs, pt, bi == 0, bi == len(blocks) - 1,
                             reads=(HK if kind == "p" else [vk]) + [ptk], writes=[("ps", po)])
                    dr = 64 if e == 0 else 0
                    b.v("dve", "reciprocal", rc[dr:dr + 1, :], psb[po][dr:dr + 1, :], reads=[("ps", po)], writes=["rc"])
                    if e == 0:
                        b.mm(psb[4][0:64, :], ones_f[64:65, 0:64], rc[64:65, :], True, True, reads=["cst", "rc"], writes=[("ps", 4)])
                        rows = slice(0, 64)
                    else:
                        b.mm(psb[4][:, :], ones_f[0:1, :], rc[0:1, :], True, True, reads=["cst", "rc"], writes=[("ps", 4)])
                        rows = slice(64, 128)
                    b.act(rstd[rows, :], psb[4][rows, :], AF.Copy, reads=[("ps", 4)], writes=["rstd"])
                    b.v("dve", "tensor_tensor", ymla[rows, hp, qt * 512:(qt + 1) * 512], psb[po][rows, :], rstd[rows, :], ALU.mult,
                        reads=[("ps", po), "rstd"], writes=[("ymla", hp, qt)])
        dbg("ymla", ymla, [128, 4, T], [("ymla", h, q) for h in range(4) for q in range(4)], BF16)
        dbg("QT", QT, [128, 2, T], [("QT", g) for g in range(4)], BF16)
        dbg("KT", KT, [128, 2, T], [("KT", g) for g in range(4)], BF16)
        dbg("Vo", Vo, [128, 16, 128], ["Vo"], BF16)


    dummy = b.sb("dummy_s", [128, 2])

    def barrier(keys):
        b.v("pool", "memset", dummy[:, 0:1], 0.0, writes=list(keys))

    MIXER_KEYS = ([("uT", c, g) for c in range(4) for g in range(4)] + [("cqn", g) for g in range(4)]
                  + [("ckvn", g) for g in range(4)] + [("krope", g) for g in range(16)] + ["cos", "sin"]
                  + [("ymla", h, q) for h in range(4) for q in range(4)])
    ATT_KEYS = ([("QT", g) for g in range(4)] + [("KT", g) for g in range(4)]
                + ["Ve", "Vo", "sqq", "qn", "qf", "rta", "rtb", "ssq", "kraw"])
    PROJ_KEYS = [("cqraw", i) for i in range(3)] + ["ropeang", "ropeang2", "ropew", "ropew_ti", "ropew_tf", "ropew_tm"]

    hT_f = hT[:].rearrange("p a b -> p (a b)").bitcast(F32)
    SLOTS = [{n: A32(48 + 2 * i, [128, 512]) for i, n in enumerate("ABCDEFGHIJKMN")},
             dict([(n, A32(32 + 2 * i, [128, 512])) for i, n in enumerate("ABCDEFGH")]
                  + [(n, hT_f[:, i * 512:(i + 1) * 512]) for i, n in enumerate("IJKMN")])]
    SSM_KEYS = ["s%d%s" % (ch, n) for ch in range(2) for n in "ABCDEFGHIJKMN"] + \
               ["Xr0", "Xi0", "Xr1", "Xi1"]
    SSM_BANKS = [(0, 1, 2, 3), (4, 5, 6, 7)]

    def ssm_prep(l, c, ch, half):
        slot = SLOTS[ch]
        sk = lambda n: "s%d%s" % (ch, n)
        def tt(eng, out, a, bb_, op):
            b.v(eng, "tensor_tensor", slot[out], slot[a], slot[bb_], op, reads=[sk(a), sk(bb_)], writes=[sk(out)])
        def wrap(dst, src, ti, tf, tm):
            d_, s_, ti_, tf_, tm_ = slot[dst], slot[src], slot[ti].bitcast(I32), slot[tf], slot[tm]
            kd, ks, kti, ktf, ktm = sk(dst), sk(src), sk(ti), sk(tf), sk(tm)
            b.v("dve", "tensor_scalar_mul", ti_, s_, 1.0 / TWO_PI, reads=[ks], writes=[kti])
            b.v("dve", "tensor_copy", tf_, ti_, reads=[kti], writes=[ktf])
            b.v("dve", "scalar_tensor_tensor", d_, tf_, -TWO_PI, s_, ALU.mult, ALU.add, reads=[ktf, ks], writes=[kd])
            b.v("dve", "tensor_scalar", tm_, d_, PI, -TWO_PI, ALU.is_gt, ALU.mult, reads=[kd], writes=[ktm])
            b.v("dve", "tensor_tensor", d_, d_, tm_, ALU.add, reads=[kd, ktm], writes=[kd])
            b.v("dve", "tensor_scalar", tm_, d_, -PI, TWO_PI, ALU.is_lt, ALU.mult, reads=[kd], writes=[ktm])
            b.v("dve", "tensor_tensor", d_, d_, tm_, ALU.add, reads=[kd, ktm], writes=[kd])
        def actf(out, a, func, **kw):
            b.act(slot[out], slot[a], func, reads=[sk(a), "cst"], writes=[sk(out)], **kw)
        for i, n in enumerate("KMN"):
            b.dma(slot[n], lam_d[l, c, i].partition_broadcast(128), writes=[sk(n)])
        actf("N", "N", AF.Exp)
        tt("dve", "G", "K", "N", ALU.mult)
        tt("dve", "H", "M", "N", ALU.mult)
        actf("A", "G", AF.Exp, scale=sp1)
        actf("C", "G", AF.Exp, scale=nsp1)
        b.v("dve", "tensor_scalar_mul", slot["I"], slot["H"], sp1, reads=[sk("H"), "cst"], writes=[sk("I")])
        wrap("J", "I", "N", "E", "F")
        actf("B", "J", AF.Sin)
        b.v("dve", "tensor_scalar_add", slot["I"], slot["I"], PI / 2, reads=[sk("I")], writes=[sk("I")])
        wrap("J", "I", "N", "E", "F")
        actf("D", "J", AF.Sin)
        tt("dve", "E", "A", "D", ALU.mult)
        tt("dve", "F", "A", "B", ALU.mult)
        tt("dve", "A", "C", "D", ALU.mult)
        b.v("dve", "scalar_tensor_tensor", slot["B"], slot["C"], -1.0, slot["B"], ALU.mult, ALU.mult,
            reads=[sk("C"), sk("B")], writes=[sk("B")])
        wrap("J", "H", "N", "C", "D")
        actf("I", "J", AF.Sin)
        b.v("dve", "tensor_scalar_add", slot["H"], slot["H"], PI / 2, reads=[sk("H")], writes=[sk("H")])
        wrap("J", "H", "N", "C", "D")
        actf("N", "J", AF.Sin)
        actf("C", "G", AF.Exp)
        tt("dve", "D", "C", "N", ALU.mult)
        b.v("dve", "tensor_scalar_add", slot["D"], slot["D"], -1.0, reads=[sk("D")], writes=[sk("D")])
        tt("dve", "I", "C", "I", ALU.mult)
        tt("dve", "J", "K", "K", ALU.mult)
        tt("dve", "N", "M", "M", ALU.mult)
        tt("dve", "J", "J", "N", ALU.add)
        b.v("dve", "reciprocal", slot["J"], slot["J"], reads=[sk("J")], writes=[sk("J")])
        tt("dve", "C", "D", "K", ALU.mult)
        tt("dve", "N", "I", "M", ALU.mult)
        tt("dve", "C", "C", "N", ALU.add)
        tt("dve", "C", "C", "J", ALU.mult)
        tt("dve", "N", "I", "K", ALU.mult)
        tt("dve", "G", "D", "M", ALU.mult)
        tt("dve", "N", "N", "G", ALU.subtract)
        tt("dve", "N", "N", "J", ALU.mult)
        tt("dve", "D", "A", "C", ALU.mult)
        tt("dve", "G", "B", "N", ALU.mult)
        tt("dve", "D", "D", "G", ALU.subtract)
        tt("dve", "I", "A", "N", ALU.mult)
        tt("dve", "G", "B", "C", ALU.mult)
        tt("dve", "I", "I", "G", ALU.add)
        cx = dict(slot=slot, sk=sk, ch=ch, c=c,
                  Tre=slot["E"], Tim=slot["F"], Tpr=slot["D"], Tpi=slot["I"],
                  TK=[sk("E"), sk("F"), sk("D"), sk("I")],
                  Bm=slot["A"].bitcast(BF16),
                  Cm=slot["B"].bitcast(BF16).rearrange("p (k n) -> p k n", k=8),
                  X=slot["C"].bitcast(BF16),
                  Sb=[slot["G"].bitcast(BF16), slot["H"].bitcast(BF16)], SbK=[sk("G"), sk("H")],
                  ST=slot["J"].bitcast(BF16).rearrange("p (k n) -> p k n", k=8))
        load_w(cx["Bm"], bm_d[l, c], sk("A"))
        load_w(cx["Cm"], cm_d[l, c], sk("B"))
        b.v("pool", "memset", cx["Sb"][1], 0.0, writes=[cx["SbK"][1]])
        if half == 1:
            b.dma(cx["Sb"][1][127:128, :], hinit_d[l, c:c + 1, :], reads=[("sh", l, c)], writes=[cx["SbK"][1]])
        return cx

    def ssm_tile(l, cx, k, stage):
        slot, sk, ch, c = cx["slot"], cx["sk"], cx["ch"], cx["c"]
        Tre, Tim, Tpr, Tpi, TK = cx["Tre"], cx["Tim"], cx["Tpr"], cx["Tpi"], cx["TK"]
        Bm, Cm, X, Sb, SbK, ST = cx["Bm"], cx["Cm"], cx["X"], cx["Sb"], cx["SbK"], cx["ST"]
        b0, b1, tb, yb = SSM_BANKS[ch]
        Xr, Xi = "Xr%d" % ch, "Xi%d" % ch
        tok = slice(k * 128, (k + 1) * 128)
        g4 = k // 4
        cur, prv = k % 2, 1 - (k % 2)
        K_, M_, N_ = slot["K"], slot["M"], slot["N"]
        if stage == 2:
            return ssm_tile2(l, cx, k)
        if stage == 3:
            return ssm_tile3(l, cx, k)
        b.mm(psb[b0][:], uT[:, c, tok], Bm[:, 0:512], True, True, reads=[("uT", c, g4), sk("A")], writes=[("ps", b0)])
        b.mm(psb[b1][:], uT[:, c, tok], Bm[:, 512:1024], True, True, reads=[("uT", c, g4), sk("A")], writes=[("ps", b1)])
        b.v("dve", "tensor_tensor", K_, psb[b0][:], Tpr, ALU.mult, reads=[("ps", b0)] + TK, writes=[sk("K")])
        b.v("dve", "tensor_tensor", M_, psb[b1][:], Tpi, ALU.mult, reads=[("ps", b1)] + TK, writes=[sk("M")])
        b.v("pool", "tensor_tensor", X[:, 0:512], K_, M_, ALU.subtract, reads=[sk("K"), sk("M")], writes=[Xr])
        b.v("dve", "tensor_tensor", N_, psb[b1][:], Tpr, ALU.mult, reads=[("ps", b1)] + TK, writes=[sk("N")])
        b.v("dve", "tensor_tensor", K_, psb[b0][:], Tpi, ALU.mult, reads=[("ps", b0)] + TK + [Xr], writes=[sk("K")])
        b.v("pool", "tensor_tensor", X[:, 512:1024], N_, K_, ALU.add, reads=[sk("K"), sk("N")], writes=[Xi])

    def ssm_tile2(l, cx, k):
        slot, sk, ch, c = cx["slot"], cx["sk"], cx["ch"], cx["c"]
        Tre, Tim, Tpr, Tpi, TK = cx["Tre"], cx["Tim"], cx["Tpr"], cx["Tpi"], cx["TK"]
        Bm, Cm, X, Sb, SbK, ST = cx["Bm"], cx["Cm"], cx["X"], cx["Sb"], cx["SbK"], cx["ST"]
        b0, b1, tb, yb = SSM_BANKS[ch]
        Xr, Xi = "Xr%d" % ch, "Xi%d" % ch
        cur, prv = k % 2, 1 - (k % 2)
        K_, M_, N_ = slot["K"], slot["M"], slot["N"]
        b.mm(psb[b0][:], tri, X[:, 0:512], True, False, reads=["cbf", Xr], writes=[("ps", b0)])
        b.mm(psb[b0][:], e127, Sb[prv][:, 0:512], False, True, reads=["cbf", SbK[prv]], writes=[("ps", b0)])
        b.mm(psb[b1][:], tri, X[:, 512:1024], True, False, reads=["cbf", Xi], writes=[("ps", b1)])
        b.mm(psb[b1][:], e127n, Sb[prv][:, 512:1024], False, True, reads=["cbf", SbK[prv]], writes=[("ps", b1)])
        b.v("dve", "tensor_tensor", K_, psb[b0][:], Tre, ALU.mult, reads=[("ps", b0)] + TK + [Xi], writes=[sk("K")])
        b.v("dve", "tensor_tensor", M_, psb[b1][:], Tim, ALU.mult, reads=[("ps", b1)] + TK + [Xr], writes=[sk("M")])
        b.v("pool", "tensor_tensor", Sb[cur][:, 0:512], K_, M_, ALU.subtract, reads=[sk("K"), sk("M")], writes=[SbK[cur]])
        b.v("dve", "scalar_tensor_tensor", N_, psb[b1][:], -1.0, Tre, ALU.mult, ALU.mult,
            reads=[("ps", b1)] + TK + [Xi], writes=[sk("N")])
        b.v("dve", "scalar_tensor_tensor", K_, psb[b0][:], -1.0, Tim, ALU.mult, ALU.mult,
            reads=[("ps", b0)] + TK + [SbK[cur]], writes=[sk("K")])
        b.v("pool", "tensor_tensor", Sb[cur][:, 512:1024], N_, K_, ALU.add,
            reads=[sk("K"), sk("N")], writes=[SbK[cur]])

    def ssm_tile3(l, cx, k):
        slot, sk, ch, c = cx["slot"], cx["sk"], cx["ch"], cx["c"]
        Cm, Sb, SbK, ST = cx["Cm"], cx["Sb"], cx["SbK"], cx["ST"]
        b0, b1, tb, yb = SSM_BANKS[ch]
        g4 = k // 4
        cur = k % 2
        for blk in range(8):
            b.tr(psb_bf[tb][:, blk * 128:(blk + 1) * 128], Sb[cur][:, blk * 128:(blk + 1) * 128], ident,
                 reads=[SbK[cur], "cbf"], writes=[("ps", tb)])
        b.act(ST, psb_bf[tb][:, :].rearrange("p (k n) -> p k n", k=8), AF.Copy, reads=[("ps", tb)], writes=[sk("J")])
        for blk in range(8):
            b.mm(psb[yb][:, (k % 4) * 128:(k % 4 + 1) * 128], Cm[:, blk, :], ST[:, blk, :], blk == 0, blk == 7,
                 reads=[sk("B"), sk("J")], writes=[("ps", yb)])
        if k % 4 == 3:
            t4 = slice(g4 * 512, (g4 + 1) * 512)
            b.v("dve", "scalar_tensor_tensor", uT[:, c, t4], uT[:, c, t4], vec[:, l, 35 + c:36 + c], psb[yb][:],
                ALU.mult, ALU.add, reads=[("uT", c, g4), "vec", ("ps", yb)], writes=[("uT", c, g4)])

    def ssm(l, half=1):
        for pair in range(2):
            cxs = [ssm_prep(l, 2 * pair + ch, ch, half) for ch in range(2)]
            for k in range(16):
                for stage in (1, 2, 3):
                    for cx in cxs:
                        ssm_tile(l, cx, k, stage)
            if half == 0:
                for cx in cxs:
                    c = cx["c"]
                    b.dma(hfin_d[l, c:c + 1, :], cx["Sb"][1][127:128, :], reads=[cx["SbK"][1]], writes=[("sh", l, c)], key="hfin")
        dbg("yssm", uT, [128, 4, T], [("uT", c, g) for c in range(4) for g in range(4)], BF16)


    def group_norm(buf, keyf, gcol, l, g4):
        tok = slice(g4 * 512, (g4 + 1) * 512)
        ks = [keyf(c, g4) for c in range(4)]
        b.act(sq[:, 0:4, :], buf[:, :, tok], AF.Square, reads=ks, writes=SQK)
        pi = next_ps()
        rms_scale(SQK, 4, 512, lambda c: sq[:, c, :], pi, 1.0 / 512)
        for c in range(4):
            b.v("dve", "scalar_tensor_tensor", buf[:, c, tok], buf[:, c, tok], vec[:, l, gcol + c:gcol + c + 1], rstd[:],
                ALU.mult, ALU.mult, reads=[keyf(c, g4), "vec", "rstd"], writes=[keyf(c, g4)])

    def ssm_post(l):
        wglu = wb[1][:, 4096:6144].rearrange("p (k n) -> p k n", k=4)
        tK = A32(48, [128, 512])
        tM = A32(50, [128, 512])
        gates = A32(52, [128, 4, 512])
        for g4 in range(4):
            tok = slice(g4 * 512, (g4 + 1) * 512)
            for c in range(4):
                y = uT[:, c, tok]
                yk = ("uT", c, g4)
                b.v("dve", "tensor_tensor", tK, y, y, ALU.mult, reads=[yk], writes=["gK"])
                b.v("dve", "tensor_scalar", tK, tK, GB, 1.0, ALU.mult, ALU.add, reads=["gK"], writes=["gK"])
                b.v("dve", "tensor_tensor", tK, tK, y, ALU.mult, reads=["gK", yk], writes=["gK"])
                b.act(tM, tK, AF.Sigmoid, reads=["gK"], writes=["gM"], scale=GA)
                b.v("dve", "tensor_tensor", y, y, tM, ALU.mult, reads=["gM", yk], writes=[yk])
            for oc in range(4):
                pi = 4 + oc
                for k in range(4):
                    b.mm(psb[pi][:], wglu[:, k, oc * 128:(oc + 1) * 128], uT[:, k, tok], k == 0, k == 3,
                         reads=[("wb", 1, "glu"), ("uT", k, g4)], writes=[("ps", pi)])
                b.act(gates[:, oc, :], psb[pi][:], AF.Sigmoid, reads=[("ps", pi), "vec"], writes=[("gate", oc)],
                      bias=vec[:, l, 39 + oc:40 + oc], scale=1.0)
            for oc in range(4):
                b.v("dve", "tensor_tensor", uT[:, oc, tok], uT[:, oc, tok], gates[:, oc, :], ALU.mult,
                    reads=[("uT", oc, g4), ("gate", oc)], writes=[("uT", oc, g4)])
            group_norm(uT, lambda c, g: ("uT", c, g), 43, l, g4)
        dbg("yssmn", uT, [128, 4, T], [("uT", c, g) for c in range(4) for g in range(4)], BF16)

    def mix_out(l):
        wout = wb[0][:, :].rearrange("p (k n) -> p k n", k=8)
        for g4 in range(4):
            tok = slice(g4 * 512, (g4 + 1) * 512)
            group_norm(ymla, lambda c, g: ("ymla", c, g), 47, l, g4)
            for oc in range(8):
                pi = next_ps()
                for k in range(8):
                    src = uT[:, k, tok] if k < 4 else ymla[:, k - 4, tok]
                    sk_ = ("uT", k, g4) if k < 4 else ("ymla", k - 4, g4)
                    b.mm(psb[pi][:], wout[:, k, oc * 128:(oc + 1) * 128], src, k == 0, k == 7,
                         reads=[("wb", 0), sk_], writes=[("ps", pi)])
                b.v("dve", "tensor_tensor", xT[:, oc, tok], xT[:, oc, tok], psb[pi][:], ALU.add,
                    reads=[("xT", oc, g4), ("ps", pi)], writes=[("xT", oc, g4)])
        dbg("x1", xT[:], [128, 8, T], [("xT", c, g) for c in range(8) for g in range(4)])

    CR_KEYS = ["mhT", "KmT", "Vm", "memx", "mtmp", "mss", "mkn", "mtmp4", "mss4", "mkn4", ("QmT", 0), ("QmT", 1), ("QmT", 2), ("QmT", 3)] + \
              [("omT", h, q) for h in range(4) for q in range(4)]

    def cross(l):
        mhT = A16(0, [128, 8, 256])
        KmT = A16(4, [128, 4, 256])
        Vm = A16(6, [128, 2, 4, 65])
        QmT = A16(8, [128, 4, T])
        omT = A16(24, [128, 4, T])
        memx = A32(40, [128, 8, 256])
        mtmp = A32(48, [128, 4, 64])
        mss = A32(49, [128, 4])
        mkn = A16(50, [128, 4, 64])
        mtmp4 = A32(52, [128, 4, 256]).rearrange("p g (h d) -> p g h d", h=4)
        mss4 = A32(56, [128, 4, 4])
        mkn4 = A16(57, [128, 4, 256]).rearrange("p g (h d) -> p g h d", h=4)
        mgq = rowb[:, 192:256]
        mgk = rowb[:, 256:320]
        mwkv = wb[1][:, 0:4096].rearrange("p (k n) -> p k n", k=8)
        mwq = wb[1][:, 4096:6144].rearrange("p (k n) -> p k n", k=8)
        load_w(mwkv, mwkv_d[l], ("wb", 1))
        load_w(mwq, mwq_d[l], ("wb", 1, "glu"))
        b.dma(memx, memT_d, writes=["memx"])
        sqm = sq[:, :, 0:256]
        b.act(sqm, memx, AF.Square, reads=["memx"], writes=SQK)
        pi = next_ps()
        rms_scale(SQK, 8, 256, lambda c: sq[:, c, 0:256], pi, 1.0 / D)
        for c in range(8):
            b.v("dve", "scalar_tensor_tensor", mhT[:, c, :], memx[:, c, :], vec[:, l, 24 + c:25 + c], rstd[:, 0:256],
                ALU.mult, ALU.mult, reads=["memx", "vec", "rstd"], writes=["mhT"])
        b.v("pool", "memset", Vm[:, :, :, 64:65], 1.0, writes=["Vm"])

        def head_norm(src, srck, gain, n_slots_bank, slot_fn):
            b.act(mtmp, src, AF.Square, reads=srck, writes=["mtmp"])
            b.v("dve", "tensor_reduce", mss, mtmp, AX.X, ALU.add, reads=["mtmp"], writes=["mss"])
            b.act(mss, mss, AF.Sqrt, reads=["mss", "cvec"], writes=["mss"], bias=epsv, scale=1.0 / 64)
            b.v("dve", "reciprocal", mss, mss, reads=["mss"], writes=["mss"])
            b.v("dve", "tensor_tensor", mtmp, src, mss.unsqueeze(2).to_broadcast([128, 4, 64]), ALU.mult,
                reads=srck + ["mss"], writes=["mtmp"])
            b.v("dve", "tensor_tensor", mkn, mtmp, gain.unsqueeze(1).to_broadcast([128, 4, 64]), ALU.mult,
                reads=["mtmp", "rowb"], writes=["mkn"])
            for h in range(4):
                sl = slot_fn(h)
                b.tr(psb_bf[n_slots_bank][0:64, sl * 128:(sl + 1) * 128], mkn[:, h, :], ident,
                     reads=["mkn", "cbf"], writes=[("ps", n_slots_bank)])

        for mt in range(2):
            pi = next_ps()
            for k in range(8):
                b.mm(psb[pi][:], mhT[:, k, mt * 128:(mt + 1) * 128], mwkv[:, k, :], k == 0, k == 7,
                     reads=["mhT", ("wb", 1)], writes=[("ps", pi)])
            kvv = psb[pi][:].rearrange("p (h d) -> p h d", h=4)
            b.act(Vm[:, mt, :, 0:64], kvv[:, :, 64:128], AF.Copy, reads=[("ps", pi)], writes=["Vm"])
            head_norm(kvv[:, :, 0:64], [("ps", pi)], mgk, 6, lambda h: h * 2 + mt)
        b.act(KmT[0:64, :, :], psb_bf[6][0:64, :].rearrange("p (h n) -> p h n", h=4), AF.Copy,
              reads=[("ps", 6)], writes=["KmT"])

        for hf in range(2):
            norm_x(l, 8, hf * 1024, 2, None)
            for gg in range(2):
                g4 = hf * 2 + gg
                qp = psp[gg][:, :].rearrange("p (g n) -> p g n", g=4)
                PK = [("ps", 2 * gg), ("ps", 2 * gg + 1)]
                for i in range(4):
                    t8 = gg * 4 + i
                    for k in range(8):
                        b.mm(qp[:, i, :], hT[:, k, t8 * 128:(t8 + 1) * 128], mwq[:, k, :], k == 0, k == 7,
                             reads=[("hT", t8 // 4), ("wb", 1, "glu")], writes=[("ps", 2 * gg + i // 2)])
                src = qp.rearrange("p g (h d) -> p g h d", h=4)
                b.act(mtmp4, src, AF.Square, reads=PK, writes=["mtmp4"])
                b.v("dve", "tensor_reduce", mss4, mtmp4, AX.X, ALU.add, reads=["mtmp4"], writes=["mss4"])
                b.act(mss4, mss4, AF.Sqrt, reads=["mss4", "cvec"], writes=["mss4"], bias=epsv, scale=1.0 / 64)
                b.v("dve", "reciprocal", mss4, mss4, reads=["mss4"], writes=["mss4"])
                b.v("dve", "tensor_tensor", mtmp4, src, mss4.unsqueeze(3).to_broadcast([128, 4, 4, 64]), ALU.mult,
                    reads=PK + ["mss4"], writes=["mtmp4"])
                b.v("dve", "tensor_tensor", mkn4, mtmp4, mgq.unsqueeze(1).unsqueeze(1).to_broadcast([128, 4, 4, 64]), ALU.mult,
                    reads=["mtmp4", "rowb"], writes=["mkn4"])
                for i in range(4):
                    bank = 6 + i // 2
                    for h in range(4):
                        sl = h * 2 + (i % 2)
                        b.tr(psb_bf[bank][0:64, sl * 128:(sl + 1) * 128], mkn4[:, i, h, :], ident,
                             reads=["mkn4", "cbf"], writes=[("ps", bank)])
                for half2 in range(2):
                    t16p = g4 * 2 + half2
                    b.act(QmT[0:64, :, t16p * 256:(t16p + 1) * 256],
                          psb_bf[6 + half2][0:64, :].rearrange("p (h n) -> p h n", h=4), AF.Copy,
                          reads=[("ps", 6 + half2)], writes=[("QmT", g4)])
        for h in range(4):
            for qt in range(4):
                po = 2 + (qt % 2)
                for kb in range(2):
                    b.mm(psb[kb][:], KmT[0:64, h, kb * 128:(kb + 1) * 128], QmT[0:64, h, qt * 512:(qt + 1) * 512], True, True,
                         reads=["KmT", ("QmT", qt)], writes=[("ps", kb)])
                for kb in range(2):
                    pi = kb
                    pt = PT[kb + 2 * (qt % 2)]
                    b.act(pt, psb[pi][:], AF.Exp, reads=[("ps", pi), "cvec"], writes=[("sq", kb + 2 * (qt % 2))], bias=nmshift, scale=1.0)
                    b.mm(psb[po][0:65, :], Vm[:, kb, h, :], pt, kb == 0, kb == 1, reads=["Vm", ("sq", kb + 2 * (qt % 2))], writes=[("ps", po)])
                b.v("dve", "reciprocal", rc[64:65, :], psb[po][64:65, :], reads=[("ps", po)], writes=["rc"])
                b.mm(psb[4][0:64, :], ones_f[64:65, 0:64], rc[64:65, :], True, True, reads=["cst", "rc"], writes=[("ps", 4)])
                b.act(rstd[0:64, :], psb[4][0:64, :], AF.Copy, reads=[("ps", 4)], writes=["rstd"])
                b.v("dve", "tensor_tensor", omT[0:64, h, qt * 512:(qt + 1) * 512], psb[po][0:64, :], rstd[0:64, :], ALU.mult,
                    reads=[("ps", po), "rstd"], writes=[("omT", h, qt)])
        mwo = wb[0][0:64, 0:4096].rearrange("p (h n) -> p h n", h=4)
        load_w(mwo, mwo_d[l], ("wb", 0))
        for g4 in range(4):
            tok = slice(g4 * 512, (g4 + 1) * 512)
            for oc in range(8):
                pi = next_ps()
                for h in range(4):
                    b.mm(psb[pi][:], mwo[:, h, oc * 128:(oc + 1) * 128], omT[0:64, h, tok], h == 0, h == 3,
                         reads=[("wb", 0), ("omT", h, g4)], writes=[("ps", pi)])
                b.v("dve", "tensor_tensor", xT[:, oc, tok], xT[:, oc, tok], psb[pi][:], ALU.add,
                    reads=[("xT", oc, g4), ("ps", pi)], writes=[("xT", oc, g4)])
        dbg("x2", xT[:], [128, 8, T], [("xT", c, g) for c in range(8) for g in range(4)])

    MLP_KEYS = [("hid", j, t) for j in range(32) for t in range(2)] + [("rt", i) for i in range(4)]

    WBH = [wb[0][:, 0:4096], wb[0][:, 4096:8192], wb[1][:, 0:4096], wb[1][:, 4096:8192]]
    WBH_KEYS = [("wbh", i) for i in range(4)] + [("wb", 0), ("wb", 1), ("wb", 1, "glu"), "rc"]

    def mlp(l):
        hid = A16(0, [128, 32, 1024])
        rt = [A16(64, [128, 512]), A16(65, [128, 512]), A16(66, [128, 512]), A16(67, [128, 512])]
        n = 0
        wi = 0
        for hf in range(2):
            norm_x(l, 16, hf * 1024, 2, None)
            for jg in range(8):
                ws = wi % 4
                wi += 1
                w1p = WBH[ws].rearrange("p (k n) -> p k n", k=8)
                b.dma(w1p, w1_d[l][:, :, jg * 512:(jg + 1) * 512], writes=[("wbh", ws)], key=("wh", ws), eng="pool")
                for jj in range(4):
                    j = jg * 4 + jj
                    for t in range(2):
                        pi = next_ps()
                        for k in range(8):
                            b.mm(psb[pi][:], w1p[:, k, jj * 128:(jj + 1) * 128], hT[:, k, t * 512:(t + 1) * 512], k == 0, k == 7,
                                 reads=[("wbh", ws), ("hT", t)], writes=[("ps", pi)])
                        r = rt[n % 4]
                        b.act(r, psb[pi][:], AF.Relu, reads=[("ps", pi)], writes=[("rt", n % 4)])
                        b.v("dve", "tensor_tensor", hid[:, j, t * 512:(t + 1) * 512], r, r, ALU.mult,
                            reads=[("rt", n % 4)], writes=[("hid", j, t)])
                        n += 1
            for oc in range(8):
                ws = wi % 4
                wi += 1
                w2p = WBH[ws].rearrange("p (j n) -> p j n", j=32)
                b.dma(w2p, w2_d[l][:, :, oc * 128:(oc + 1) * 128], writes=[("wbh", ws)], key=("wh", ws), eng="pool")
                for t in range(2):
                    g4 = hf * 2 + t
                    tok = slice(g4 * 512, (g4 + 1) * 512)
                    pi = next_ps()
                    for j in range(32):
                        b.mm(psb[pi][:], w2p[:, j, :], hid[:, j, t * 512:(t + 1) * 512], j == 0, j == 31,
                             reads=[("wbh", ws), ("hid", j, t)], writes=[("ps", pi)])
                    b.v("dve", "tensor_tensor", xT[:, oc, tok], xT[:, oc, tok], psb[pi][:], ALU.add,
                        reads=[("xT", oc, g4), ("ps", pi)], writes=[("xT", oc, g4)])

    def run_layer(l, half=1, stop_after=None):
        rope_tables()
        load_w(wb[1][:, 0:1536].rearrange("p (k n) -> p k n", k=2), w_uq_d[l], ("wb", 1))
        load_w(wb[1][:, 1536:2560], w_ukv_d[l], ("wb", 1))
        load_w(wb[1][:, 4096:6144].rearrange("p (k n) -> p k n", k=4), w_glu_d[l], ("wb", 1, "glu"))
        layer(l)
        load_w(wb[0][:, :].rearrange("p (k n) -> p k n", k=8), w_out_d[l], ("wb", 0))
        YMK = [("ymla", h, q) for h in range(4) for q in range(4)]
        barrier(PROJ_KEYS + SSM_KEYS + HK + YMK)
        ssm(l, half)
        if stop_after == "ssm":
            return
        barrier(SSM_KEYS + HK + YMK + ["gK", "gM"] + [("gate", i) for i in range(4)])
        ssm_post(l)
        if stop_after == "ssm_post":
            return
        barrier(SSM_KEYS + HK + YMK + ["gK", "gM"] + [("gate", i) for i in range(4)] + ATT_KEYS + SQK)
        mla(l, half)
        mix_out(l)
        if stop_after == "mix":
            return
        barrier(MIXER_KEYS + ATT_KEYS + CR_KEYS)
        cross(l)
        if stop_after == "cross":
            return
        barrier(CR_KEYS + MLP_KEYS + WBH_KEYS)
        mlp(l)
        barrier(MLP_KEYS + MIXER_KEYS + PROJ_KEYS + WBH_KEYS)

    def store_x(half):
        for c in range(8):
            b.dma(outT_d[half, :, c, :], xT[:, c, :], reads=[("xT", c, g) for g in range(4)], writes=[("dramout", "out")], key="out")

    def finish():
        b.p.add("sp", lambda e: e.nop(), reads=[("dramout", "out")] + [("dbgdone", k) for k in b.out_keys])

    b.ctx = dict(locals())
    return b


_CACHE = {}
N_LAYERS = 4


def _program():
    if "b" not in _CACHE:
        b = build_program(N_LAYERS)
        c = b.ctx
        for half in range(2):
            c["load_x"](half)
            for l in range(N_LAYERS):
                c["run_layer"](l, half)
            c["store_x"](half)
        c["finish"]()
        b.p.emit(b.nc, b.es)
        _CACHE["b"] = b
    return _CACHE["b"]


def kernel(**inputs):
    inp = {k: np.asarray(v) for k, v in inputs.items()}
    B = inp["x"].shape[0]
    b = _program()
    cst = host_constants()
    W = prep_weights(inp, range(N_LAYERS))
    CORE_OF = [0, 1, 4, 5]
    maps = {}
    for bb in range(B):
        xs = np.asarray(inp["x"][bb], np.float32)
        xT = np.stack([chunkT(np.ascontiguousarray(xs[hf * T:(hf + 1) * T].T), 8) for hf in range(2)])
        pos = np.asarray(inp["positions"][bb]).astype(np.float32)
        m = {"xT": xT,
             "memT": chunkT(np.ascontiguousarray(np.asarray(inp["mem"][bb], np.float32).T), 8),
             "pos": np.stack([np.ascontiguousarray(pos[hf * T:(hf + 1) * T].reshape(16, 128).T) for hf in range(2)]),
             "cst": cst}
        m.update(W)
        maps[CORE_OF[bb]] = m
    zero = {k: np.zeros_like(v) for k, v in maps[CORE_OF[0]].items()}
    in_maps = [maps.get(core, zero) for core in range(8)]
    res = run_bass_kernel_spmd(b.nc, in_maps, core_ids=list(range(8)))
    out = np.zeros((B, 2 * T, D), np.float32)
    for bb in range(B):
        o = np.asarray(res.results[CORE_OF[bb]]["outT"], np.float32)
        for hf in range(2):
            out[bb, hf * T:(hf + 1) * T] = o[hf].transpose(2, 1, 0).reshape(T, D)
    return out
```

```python
import numpy as np
from contextlib import ExitStack
import concourse.bass as bass
import concourse.mybir as mybir
from concourse.bass_utils import run_bass_kernel_spmd

F32 = mybir.dt.float32
BF16 = mybir.dt.bfloat16
I32 = mybir.dt.int32
AF = mybir.ActivationFunctionType
ALU = mybir.AluOpType
AX = mybir.AxisListType

ENGS = ("pe", "act", "dve", "pool", "sp")


class Op:
    __slots__ = ("eng", "fn", "reads", "writes", "dma_key", "idx", "deps", "signal",
                 "tick", "waits")

    def __init__(self, eng, fn, reads, writes, dma_key):
        self.eng = eng
        self.fn = fn
        self.reads = reads
        self.writes = writes
        self.dma_key = dma_key
        self.deps = []
        self.signal = False
        self.tick = None
        self.waits = []


class Prog:
    def __init__(self):
        self.ops = []

    def add(self, eng, fn, reads=(), writes=(), dma_key=None):
        op = Op(eng, fn, tuple(reads), tuple(writes), dma_key)
        op.idx = len(self.ops)
        self.ops.append(op)
        return op

    def analyse(self):
        last_w = {}
        readers = {}
        for op in self.ops:
            deps = set()
            for k in op.reads:
                w = last_w.get(k)
                if w is not None:
                    deps.add(w.idx)
            for k in op.writes:
                w = last_w.get(k)
                if w is not None:
                    deps.add(w.idx)
                for r in readers.get(k, {}).values():
                    deps.add(r.idx)
            deps.discard(op.idx)
            keep = []
            for d in deps:
                a = self.ops[d]
                if a.dma_key is None and op.dma_key is None and a.eng == op.eng:
                    if a.eng == "pe":
                        continue
                    raw = any(k in a.writes for k in op.reads)
                    if not raw:
                        continue
                keep.append(d)
            op.deps = keep
            for d in keep:
                self.ops[d].signal = True
            for k in op.writes:
                last_w[k] = op
                readers[k] = {}
            for k in op.reads:
                rk = op.dma_key if op.dma_key is not None else op.eng
                readers.setdefault(k, {})[rk] = op
        cnt = {e: 0 for e in ENGS}
        dcnt = {}
        for op in self.ops:
            if op.dma_key is not None:
                dcnt[op.dma_key] = dcnt.get(op.dma_key, 0) + 1
                op.tick = ("dma", op.dma_key, dcnt[op.dma_key] * 16)
            elif op.signal:
                cnt[op.eng] += 1
                op.tick = ("eng", op.eng, cnt[op.eng])
        waited = {e: {} for e in ENGS}
        dma_issued = {}
        for op in self.ops:
            need = {}
            for d in op.deps:
                a = self.ops[d]
                t = a.tick
                if t[0] == "dma":
                    v = dma_issued.get(t[1], 0) * 16
                    v = max(v, t[2])
                else:
                    v = t[2]
                sk = (t[0], t[1])
                if v > need.get(sk, 0):
                    need[sk] = v
            for sk, v in need.items():
                if waited[op.eng].get(sk, 0) >= v:
                    continue
                waited[op.eng][sk] = v
                op.waits.append((sk, v))
            if op.dma_key is not None:
                dma_issued[op.dma_key] = dma_issued.get(op.dma_key, 0) + 1
        self.n_eng_sems = cnt
        self.dma_keys = list(dcnt.keys())

    def emit(self, nc, es):
        self.analyse()
        sems = {}
        for e in ENGS:
            sems[("eng", e)] = es.enter_context(nc.semaphore("s_" + e))
        for i, k in enumerate(self.dma_keys):
            sems[("dma", k)] = es.enter_context(nc.semaphore("d%d" % i))
        self.sems = sems
        per = {e: [op for op in self.ops if op.eng == e] for e in ENGS}
        block = es.enter_context(nc.Block())

        def run(engine, lst):
            for op in lst:
                for sk, v in op.waits:
                    engine.wait_ge(sems[sk], v)
                ins = op.fn(engine)
                if op.dma_key is not None:
                    ins.then_inc(sems[("dma", op.dma_key)], 16)
                elif op.signal:
                    ins.then_inc(sems[("eng", op.eng)], 1)

        @block.tensor
        def _(e):
            run(e, per["pe"])

        @block.scalar
        def _(e):
            run(e, per["act"])

        @block.vector
        def _(e):
            run(e, per["dve"])

        @block.gpsimd
        def _(e):
            run(e, per["pool"])

        @block.sync
        def _(e):
            run(e, per["sp"])


D = 1024
T = 2048
NT4 = T // 512
NT1 = T // 128
KC = D // 128
IN_COLS = 928
EPS = 1e-6
PI = float(np.pi)
SHIFT = 10.0


class Builder:
    def __init__(self, nc, es, n_layers, debug=()):
        self.nc = nc
        self.es = es
        self.p = Prog()
        self.L = n_layers
        self.debug = set(debug)
        self.dram_in = {}
        self.dram_out = {}
        self._uid = 0
        self.out_keys = []

    def din(self, name, shape, dt=F32):
        t = self.nc.dram_tensor(name, list(shape), dt, kind="ExternalInput")
        self.dram_in[name] = t
        return t

    def dout(self, name, shape, dt=F32):
        t = self.nc.dram_tensor(name, list(shape), dt, kind="ExternalOutput")
        self.dram_out[name] = t
        return t

    def sb(self, name, shape, dt=F32):
        return self.es.enter_context(self.nc.sbuf_tensor(name, list(shape), dt))

    def ps(self, name, shape, dt=F32):
        return self.es.enter_context(self.nc.psum_tensor(name, list(shape), dt))

    def dma(self, out, in_, reads=(), writes=(), key=None, eng="sp", **kw):
        writes = list(writes)
        if key is None:
            self._uid += 1
            key = "a%d_%s" % (self._uid % 12, eng)
            writes.append(("dmakey", key))
        return self.p.add(eng, lambda e, o=out, i=in_, kw=kw: e.dma_start(out=o, in_=i, **kw),
                          reads=reads, writes=writes, dma_key=key)

    def mm(self, out, lhsT, rhs, start, stop, reads=(), writes=()):
        return self.p.add("pe", lambda e: e.matmul(out, lhsT, rhs, start=start, stop=stop),
                          reads=reads, writes=writes)

    def tr(self, out, in_, ident, reads=(), writes=()):
        return self.p.add("pe", lambda e: e.transpose(out, in_, ident), reads=reads, writes=writes)

    def act(self, out, in_, func, reads=(), writes=(), eng="act", **kw):
        return self.p.add("act", lambda e: e.activation(out, in_, func, **kw), reads=reads, writes=writes)

    def v(self, eng, name, *args, reads=(), writes=(), **kw):
        return self.p.add(eng, lambda e: getattr(e, name)(*args, **kw), reads=reads, writes=writes)


def host_constants():
    c = np.zeros((128, 1664), np.float32)
    c[:, 0:128] = np.eye(128)
    c[:, 128:256] = 1.0
    r = np.arange(128)
    c[:, 256:384] = (r[:, None] <= r[None, :])
    c[127, 384:512] = 1.0
    c[127, 512:640] = -1.0
    m = np.arange(896)
    c[:, 640:1536] = (m[None, :] - r[:, None] >= 384)
    c[:, 1536:1552] = (10000.0 ** (-np.arange(16, dtype=np.float32) / 16))[None, :]
    c[:, 1552] = r + 1
    c[:, 1553] = -(r + 1)
    return c


def chunkT(v, k):
    return np.ascontiguousarray(v.reshape((k, 128) + v.shape[1:]).swapaxes(0, 1))


def prep_weights(inp, layers):
    f = lambda a: np.asarray(a, np.float32)
    W = {}
    Ls = list(layers)
    n = len(Ls)
    vec = np.zeros((n, 128, 64), np.float32)
    row = np.zeros((n, 320), np.float32)
    bm = np.zeros((n, 4, 128, 1024), np.float32)
    cm = np.zeros((n, 4, 128, 8, 128), np.float32)
    lam = np.zeros((n, 4, 3, 512), np.float32)
    for i, l in enumerate(Ls):
        vec[i, :, 0:8] = f(inp["norm_mix"][l]).reshape(8, 128).T
        vec[i, :, 8:16] = f(inp["norm_mem_q"][l]).reshape(8, 128).T
        vec[i, :, 16:24] = f(inp["norm_mlp"][l]).reshape(8, 128).T
        vec[i, :, 24:32] = f(inp["norm_mem_kv"][l]).reshape(8, 128).T
        vec[i, :, 32:34] = f(inp["mla_q_norm"][l]).reshape(2, 128).T
        vec[i, :, 34] = f(inp["mla_kv_norm"][l])
        vec[i, :, 35:39] = f(inp["ssm_d"][l]).reshape(4, 128).T
        vec[i, :, 39:43] = f(inp["ssm_b_glu"][l]).reshape(4, 128).T
        vec[i, :, 43:47] = f(inp["out_norm_ssm"][l]).reshape(4, 128).T
        vec[i, :, 47:51] = f(inp["out_norm_mla"][l]).reshape(4, 128).T
        row[i, 0:96] = f(inp["mla_q_gain"][l])
        row[i, 96:192] = f(inp["mla_k_gain"][l])
        row[i, 192:256] = f(inp["mem_q_gain"][l])
        row[i, 256:320] = f(inp["mem_k_gain"][l])
        bre = f(inp["ssm_b_re"][l])
        bim = f(inp["ssm_b_im"][l])
        cre = f(inp["ssm_c_re"][l])
        cim = f(inp["ssm_c_im"][l])
        lre = f(inp["ssm_lambda_re"][l])
        lim = f(inp["ssm_lambda_im"][l])
        lst = f(inp["ssm_log_step"][l])
        for c in range(4):
            for j in range(8):
                g = 8 * c + j
                bm[i, c, 16 * j:16 * j + 16, j * 64:(j + 1) * 64] = bre[g].T
                bm[i, c, 16 * j:16 * j + 16, 512 + j * 64:512 + (j + 1) * 64] = bim[g].T
                pr, e = j // 2, j % 2
                cm[i, c, e * 64:(e + 1) * 64, pr, 16 * j:16 * j + 16] = cre[g].T
                cm[i, c, e * 64:(e + 1) * 64, 4 + pr, 16 * j:16 * j + 16] = cim[g].T
                lam[i, c, 0, j * 64:(j + 1) * 64] = lre[g]
                lam[i, c, 1, j * 64:(j + 1) * 64] = lim[g]
                lam[i, c, 2, j * 64:(j + 1) * 64] = lst[g]
    W["vec"] = vec
    W["row"] = row
    W["bm"] = bm
    W["cm"] = cm
    W["lam"] = lam
    st = lambda name, k: np.stack([chunkT(f(inp[name][l]), k) for l in Ls])
    W["w_in"] = st("w_in", 8)
    W["w_uq"] = st("mla_w_uq", 2)
    W["w_ukv"] = np.stack([f(inp["mla_w_ukv"][l]) for l in Ls])
    W["w_glu"] = st("ssm_w_glu", 4)
    W["w_out"] = st("w_out", 8)
    W["mwq"] = st("mem_w_q", 8)
    W["mwkv"] = st("mem_w_kv", 8)
    W["mwo"] = np.stack([np.ascontiguousarray(f(inp["mem_w_o"][l]).reshape(4, 64, 1024).swapaxes(0, 1)) for l in Ls])
    W["w1"] = st("mlp_w1", 8)
    W["w2"] = st("mlp_w2", 32)
    return W


TWO_PI = 2.0 * PI
GA = 1.5957691216057308
GB = 0.044715
QSCALE = 1.0 / float(np.sqrt(96.0))
MSCALE = 1.0 / 8.0
MSHIFT = 8.0
NEG_BIG = -30000.0


def build_program(n_layers=1, debug=()):
    nc = bass.Bass("TRN2", target_bir_lowering=False)
    es = ExitStack()
    b = Builder(nc, es, n_layers, debug)
    P = b.p
    L = n_layers

    xT_d = b.din("xT", [2, 128, 8, T]).ap()
    memT_d = b.din("memT", [128, 8, 256]).ap()
    pos_d = b.din("pos", [2, 128, 16]).ap()
    hinit_d = nc.dram_tensor("sc_h", [L, 4, 1024], BF16, kind="Internal").ap()
    pkt_d = nc.dram_tensor("sc_kt", [L, 4, 96, 2, T], BF16, kind="Internal").ap()
    pve_d = nc.dram_tensor("sc_ve", [L, 4, 128, 16, 65], BF16, kind="Internal").ap()
    pvo_d = nc.dram_tensor("sc_vo", [L, 4, 128, 16, 128], BF16, kind="Internal").ap()
    cst_d = b.din("cst", [128, 1664]).ap()
    vec_d = b.din("vec", [L, 128, 64]).ap()
    row_d = b.din("row", [L, 192 + 64 + 64]).ap()
    w_in_d = b.din("w_in", [L, 128, 8, IN_COLS]).ap()
    w_uq_d = b.din("w_uq", [L, 128, 2, 768]).ap()
    w_ukv_d = b.din("w_ukv", [L, 128, 1024]).ap()
    w_glu_d = b.din("w_glu", [L, 128, 4, 512]).ap()
    w_out_d = b.din("w_out", [L, 128, 8, 1024]).ap()
    mwq_d = b.din("mwq", [L, 128, 8, 256]).ap()
    mwkv_d = b.din("mwkv", [L, 128, 8, 512]).ap()
    mwo_d = b.din("mwo", [L, 64, 4, 1024]).ap()
    w1_d = b.din("w1", [L, 128, 8, 4096]).ap()
    w2_d = b.din("w2", [L, 128, 32, 1024]).ap()
    bm_d = b.din("bm", [L, 4, 128, 1024]).ap()
    cm_d = b.din("cm", [L, 4, 128, 8, 128]).ap()
    lam_d = b.din("lam", [L, 4, 3, 512]).ap()
    outT_d = b.dout("outT", [2, 128, 8, T]).ap()
    okt_d, ove_d, ovo_d, hfin_d = pkt_d, pve_d, pvo_d, hinit_d
    dbg_d = {}

    xT = b.sb("xT_s", [128, 8, T])
    hT = b.sb("hT_s", [128, 8, 1024], BF16)
    wb = [b.sb("wb%d" % i, [128, 8192], BF16) for i in range(2)]
    cst = b.sb("cst_s", [128, 160])
    cbf = b.sb("cbf_s", [128, 1536], BF16)
    vec = b.sb("vec_s", [128, L, 64])
    rowb = b.sb("rowb_s", [128, 320])
    sq = b.sb("sq_s", [128, 8, 512], BF16)
    rstd = b.sb("rstd_s", [128, 512])
    posf = b.sb("pos_s", [128, 16])
    cvec = b.sb("cvec_s", [128, 8])
    AR = 37888
    arena = b.sb("arena", [128, AR], BF16)
    arena_f = arena.bitcast(F32)

    psp = [b.ps("psp%d" % i, [128, 1024]) for i in range(4)]
    psb = [psp[i // 2][:, (i % 2) * 512:(i % 2 + 1) * 512] for i in range(8)]
    psb_bf = [a_.bitcast(BF16) for a_ in psb]

    ident = cbf[:, 0:128]
    ones = cbf[:, 128:256]
    tri = cbf[:, 256:384]
    e127 = cbf[:, 384:512]
    e127n = cbf[:, 512:640]
    cmask = cbf[:, 640:1536]
    ones_f = cst[:, 0:128]
    invf = cst[:, 128:144]
    sp1 = cst[:, 144:145]
    nsp1 = cst[:, 145:146]

    def A16(off_kib, shape):
        n = int(np.prod(shape[1:]))
        o = int(round(off_kib * 512))
        assert o + n <= AR, (off_kib, shape)
        v = arena[:, o:o + n]
        if len(shape) == 3:
            v = v.rearrange("p (a b) -> p a b", a=shape[1])
        elif len(shape) == 4:
            v = v.rearrange("p (a b c) -> p a b c", a=shape[1], b=shape[2])
        return v

    def A32(off_kib, shape):
        n = int(np.prod(shape[1:]))
        o = int(round(off_kib * 256))
        assert (o + n) * 2 <= AR, (off_kib, shape)
        v = arena_f[:, o:o + n]
        if len(shape) == 3:
            v = v.rearrange("p (a b) -> p a b", a=shape[1])
        elif len(shape) == 4:
            v = v.rearrange("p (a b c) -> p a b c", a=shape[1], b=shape[2])
        return v

    uT = A16(0, [128, 4, T])
    cqn = A16(16, [128, 2, T])
    ckvn = A16(24, [128, T])
    krope = A32(28, [128, 16, 32])
    cossin = A32(30, [128, 2, 16, 16])
    ymla = A16(32, [128, 4, T])
    WK = 48

    b.dma(cst[:, 0:128], cst_d[:, 128:256], writes=["cst"])
    b.dma(cst[:, 128:160], cst_d[:, 1536:1568], writes=["cst"])
    b.dma(cbf[:], cst_d[:, 0:1536], writes=["cbf"], eng="pool")
    b.dma(vec[:], vec_d.rearrange("l p k -> p l k"), writes=["vec"])
    def load_x(half):
        b.dma(posf[:], pos_d[half], writes=["pos"])
        for c in range(8):
            b.dma(xT[:, c, :], xT_d[half, :, c, :], writes=[("xT", c, t4) for t4 in range(4)])
    b.v("pool", "memset", cvec[:, 0:1], -SHIFT, writes=["cvec"])
    b.v("pool", "memset", cvec[:, 1:2], -MSHIFT, writes=["cvec"])
    b.v("pool", "memset", cvec[:, 2:3], -PI, writes=["cvec"])
    b.v("pool", "memset", cvec[:, 3:4], EPS, writes=["cvec"])
    nshift = cvec[:, 0:1]
    nmshift = cvec[:, 1:2]
    npi = cvec[:, 2:3]
    epsv = cvec[:, 3:4]

    cos_t = cossin[:, 0]
    sin_t = cossin[:, 1]

    def wrap_pi(dst, src, ti, tf, tm, rk, wk):
        b.v("dve", "tensor_scalar_mul", ti, src, 1.0 / TWO_PI, reads=rk, writes=[wk + "_ti"])
        b.v("dve", "tensor_copy", tf, ti, reads=[wk + "_ti"], writes=[wk + "_tf"])
        b.v("dve", "scalar_tensor_tensor", dst, tf, -TWO_PI, src, ALU.mult, ALU.add, reads=[wk + "_tf"] + rk, writes=[wk])
        b.v("dve", "tensor_scalar", tm, dst, PI, -TWO_PI, ALU.is_gt, ALU.mult, reads=[wk], writes=[wk + "_tm"])
        b.v("dve", "tensor_tensor", dst, dst, tm, ALU.add, reads=[wk, wk + "_tm"], writes=[wk])
        b.v("dve", "tensor_scalar", tm, dst, -PI, TWO_PI, ALU.is_lt, ALU.mult, reads=[wk], writes=[wk + "_tm"])
        b.v("dve", "tensor_tensor", dst, dst, tm, ALU.add, reads=[wk, wk + "_tm"], writes=[wk])

    def rope_tables():
        ang = A32(WK, [128, 16, 16])
        ang2 = A32(WK + 1, [128, 16, 16])
        ti = A32(WK + 2, [128, 16, 16]).bitcast(I32)
        tf = A32(WK + 3, [128, 16, 16])
        tm = A32(WK + 4, [128, 16, 16])
        wr = A32(WK + 5, [128, 16, 16])
        for t in range(16):
            b.v("dve", "tensor_scalar_mul", ang[:, t, :], invf, posf[:, t:t + 1],
                reads=["cst", "pos"], writes=["ropeang"])
        wrap_pi(wr, ang, ti, tf, tm, ["ropeang"], "ropew")
        b.act(sin_t, wr, AF.Sin, reads=["ropew"], writes=["sin"])
        b.v("dve", "tensor_scalar_add", ang2, ang, PI / 2, reads=["ropeang"], writes=["ropeang2"])
        wrap_pi(wr, ang2, ti, tf, tm, ["ropeang2", "sin"], "ropew")
        b.act(cos_t, wr, AF.Sin, reads=["ropew"], writes=["cos"])

    state = {"ps": 0, "wb": 0}
    SQK = [("sq", i) for i in range(8)]

    def next_ps():
        i = state["ps"]
        state["ps"] = (i + 1) % 4
        return i

    def load_w(view, src, key, eng="pool"):
        return b.dma(view, src, writes=[key], key=("w", key), eng=eng)

    def rms_scale(src_keys, nchunk, n, src_fn, ps_i, inv_dim):
        for c in range(nchunk):
            b.mm(psb[ps_i][:, 0:n], ones, src_fn(c), c == 0, c == nchunk - 1,
                 reads=["cbf"] + src_keys, writes=[("ps", ps_i)])
        b.act(rstd[:, 0:n], psb[ps_i][:, 0:n], AF.Sqrt, reads=[("ps", ps_i), "cvec"], writes=["rstd"],
              bias=epsv, scale=inv_dim)
        b.v("dve", "reciprocal", rstd[:, 0:n], rstd[:, 0:n], reads=["rstd"], writes=["rstd"])

    def norm_x(l, gain_col, tok0, nt, hkeys):
        for t in range(nt):
            g4 = (tok0 // 512) + t
            xs = xT[:, :, g4 * 512:(g4 + 1) * 512]
            xk = [("xT", c, g4) for c in range(8)]
            b.act(sq[:], xs, AF.Square, reads=xk, writes=SQK)
            pi = next_ps()
            rms_scale(SQK, 8, 512, lambda c: sq[:, c, :], pi, 1.0 / D)
            for c in range(8):
                b.v("dve", "scalar_tensor_tensor", hT[:, c, t * 512:(t + 1) * 512], xT[:, c, g4 * 512:(g4 + 1) * 512],
                    vec[:, l, gain_col + c:gain_col + c + 1], rstd[:], ALU.mult, ALU.mult,
                    reads=[("xT", c, g4), "vec", "rstd"], writes=[("hT", t)])

    def dbg(name, view, shape, reads, dt=F32):
        if name in b.debug:
            d = b.dout("dbg_" + name, shape, dt).ap()
            b.dma(d, view, reads=reads, writes=[("dbgdone", "dbg_" + name)], key="dbg_" + name)
            b.out_keys.append("dbg_" + name)

    def layer(l):
        V0 = lambda c: vec[:, l, c:c + 1]
        b.dma(rowb[:], row_d[l].partition_broadcast(128), writes=["rowb"])
        gq = rowb[:, 0:96]
        gk = rowb[:, 96:192]
        mgq = rowb[:, 192:256]
        mgk = rowb[:, 256:320]
        b.v("dve", "tensor_scalar_mul", gq, gq, QSCALE, reads=["rowb"], writes=["rowb"])
        b.v("dve", "tensor_scalar_mul", mgq, mgq, MSCALE, reads=["rowb"], writes=["rowb"])

        w_in = wb[0][:, 0:8 * IN_COLS].rearrange("p (k n) -> p k n", k=8)
        load_w(w_in, w_in_d[l], ("wb", 0))
        cqraw = A32(WK, [128, 3, 512])
        for hf in range(2):
            norm_x(l, 0, hf * 1024, 2, None)
            for t in range(2):
                g4 = hf * 2 + t
                tok = slice(g4 * 512, (g4 + 1) * 512)
                hk = [("hT", t)]
                for oc in range(7):
                    pi = next_ps()
                    for k in range(8):
                        b.mm(psb[pi][:], w_in[:, k, oc * 128:(oc + 1) * 128], hT[:, k, t * 512:(t + 1) * 512],
                             k == 0, k == 7, reads=[("wb", 0)] + hk, writes=[("ps", pi)])
                    if oc < 4:
                        b.act(uT[:, oc, tok], psb[pi][:], AF.Copy, reads=[("ps", pi)], writes=[("uT", oc, g4)])
                    else:
                        b.act(cqraw[:, oc - 4, :], psb[pi][:], AF.Copy, reads=[("ps", pi)], writes=[("cqraw", oc - 4)])
                for s4 in range(4):
                    t16 = g4 * 4 + s4
                    pi = next_ps()
                    for k in range(8):
                        b.mm(psb[pi][:, 0:32], hT[:, k, t * 512 + s4 * 128:t * 512 + (s4 + 1) * 128], w_in[:, k, 896:928],
                             k == 0, k == 7, reads=[("wb", 0)] + hk, writes=[("ps", pi)])
                    b.act(krope[:, t16, :], psb[pi][:, 0:32], AF.Copy, reads=[("ps", pi)], writes=[("krope", t16)])
                b.act(sq[:, 0:2, :], cqraw[:, 0:2, :], AF.Square, reads=[("cqraw", 0), ("cqraw", 1)], writes=SQK)
                pi = next_ps()
                rms_scale(SQK, 2, 512, lambda c: sq[:, c, :], pi, 1.0 / 256)
                for c in range(2):
                    b.v("dve", "scalar_tensor_tensor", cqn[:, c, tok], cqraw[:, c, :], V0(32 + c), rstd[:],
                        ALU.mult, ALU.mult, reads=[("cqraw", c), "vec", "rstd"], writes=[("cqn", g4)])
                b.act(sq[:, 2, :], cqraw[:, 2, :], AF.Square, reads=[("cqraw", 2)], writes=SQK)
                pi = next_ps()
                rms_scale(SQK, 1, 512, lambda c: sq[:, 2, :], pi, 1.0 / 128)
                b.v("dve", "scalar_tensor_tensor", ckvn[:, tok], cqraw[:, 2, :], V0(34), rstd[:],
                    ALU.mult, ALU.mult, reads=[("cqraw", 2), "vec", "rstd"], writes=[("ckvn", g4)])
        dbg("uT", uT, [128, 4, T], [("uT", c, g) for c in range(4) for g in range(4)], BF16)
        dbg("cqn", cqn, [128, 2, T], [("cqn", g) for g in range(4)], BF16)
        dbg("ckvn", ckvn, [128, T], [("ckvn", g) for g in range(4)], BF16)
        dbg("krope", krope, [128, 16, 32], [("krope", g) for g in range(16)])
        dbg("cos", cos_t, [128, 16, 16], ["cos"])
        dbg("sin", sin_t, [128, 16, 16], ["sin"])


    hT_flat = hT[:].rearrange("p a b -> p (a b)")
    pkt_s = hT_flat[:, 0:4096].rearrange("p (e t) -> p e t", e=2)
    pve_s = hT_flat[:, 4096:4096 + 1040].rearrange("p (t c) -> p t c", t=16)
    pvo_s = hT_flat[:, 5248:5248 + 2048].rearrange("p (t c) -> p t c", t=16)
    HK = [("hT", 0), ("hT", 1)]
    QT = A16(48, [128, 2, T])
    KT = A16(56, [128, 2, T])
    Ve = A16(64, [128, 16, 65])
    Vo = A16(66.5, [128, 16, 128])
    sq_f = sq[:].rearrange("p a b -> p (a b)").bitcast(F32)
    sqq = sq_f[:, 0:768].rearrange("p (g e d) -> p g e d", g=4, e=2)
    qn = sq_f[:, 768:1536].rearrange("p (g e d) -> p g e d", g=4, e=2)
    SQ6 = [("sq", i) for i in range(6)]
    kraw = wb[1].bitcast(F32)[:, 2048:2816].rearrange("p (g e d) -> p g e d", g=4, e=2)
    KRK = ("wb", 1, "glu")
    qf = A16(70.5, [128, 4, 192]).rearrange("p g (e d) -> p g e d", e=2)
    rta = A32(72, [128, 4, 32]).rearrange("p g (e d) -> p g e d", e=2)
    rtb = A32(72.5, [128, 4, 32]).rearrange("p g (e d) -> p g e d", e=2)
    ssq = A32(73, [128, 4, 2])
    wb1f = wb[1].bitcast(F32)
    rc = wb1f[:, 3584:4096]
    PT = [sq[:, i, :] for i in range(8)]

    rstd_bf = rstd[:].bitcast(BF16)
    sqq_k = wb[1].bitcast(F32)[:, 2816:3584].rearrange("p (g e d) -> p g e d", g=4, e=2)
    qf_k = rstd_bf[:, 0:768].rearrange("p (g e d) -> p g e d", g=4, e=2)
    ssq_k = rstd[:, 384:392].rearrange("p (g e) -> p g e", g=4)

    def qk_path(which, src, gain, dstT, ps_bank, g4, srck, tag):
        if which == "q":
            sq_, qn_, qf_, ssq_ = sqq, qn, qf, ssq
            ksq, kqn, kqf, kss = [("sq", i) for i in range(3)], [("sq", i) for i in range(3, 6)], "qf", "ssq"
        else:
            sq_, qn_, qf_, ssq_ = sqq_k, src, qf_k, ssq_k
            ksq, kqn, kqf, kss = [KRK], srck, "rstd", "rstd"
        cs = cos_t[:, g4 * 4:(g4 + 1) * 4, :].unsqueeze(2).to_broadcast([128, 4, 2, 16])
        sn = sin_t[:, g4 * 4:(g4 + 1) * 4, :].unsqueeze(2).to_broadcast([128, 4, 2, 16])
        g_b = gain.unsqueeze(1).unsqueeze(1).to_broadcast([128, 4, 2, 96])
        st = []
        st.append(lambda: b.act(sq_, src, AF.Square, reads=srck, writes=ksq))
        st.append(lambda: b.v("dve", "tensor_reduce", ssq_, sq_, AX.X, ALU.add, reads=ksq, writes=[kss]))
        st.append(lambda: b.act(ssq_, ssq_, AF.Sqrt, reads=[kss, "cvec"], writes=[kss], bias=epsv, scale=1.0 / 96))
        st.append(lambda: b.v("dve", "reciprocal", ssq_, ssq_, reads=[kss], writes=[kss]))
        st.append(lambda: b.v("dve", "tensor_tensor", qn_, src, ssq_.unsqueeze(3).to_broadcast([128, 4, 2, 96]), ALU.mult,
                              reads=srck + [kss], writes=kqn))
        st.append(lambda: b.v("dve", "tensor_tensor", qn_, qn_, g_b, ALU.mult, reads=kqn + ["rowb"], writes=kqn))

        def rope():
            b.v("pool", "tensor_copy", qf_[:, :, :, 0:64], qn_[:, :, :, 0:64], reads=kqn, writes=[kqf])
            b.v("pool", "tensor_tensor", rta, qn_[:, :, :, 64:80], cs, ALU.mult, reads=kqn + ["cos"], writes=["rta"])
            b.v("pool", "tensor_tensor", rtb, qn_[:, :, :, 80:96], sn, ALU.mult, reads=kqn + ["sin"], writes=["rtb"])
            b.v("pool", "tensor_tensor", qf_[:, :, :, 64:80], rta, rtb, ALU.subtract, reads=["rta", "rtb"], writes=[kqf])
            b.v("pool", "tensor_tensor", rta, qn_[:, :, :, 80:96], cs, ALU.mult, reads=kqn + ["cos", kqf], writes=["rta"])
            b.v("pool", "tensor_tensor", rtb, qn_[:, :, :, 64:80], sn, ALU.mult, reads=kqn + ["sin", kqf], writes=["rtb"])
            b.v("pool", "tensor_tensor", qf_[:, :, :, 80:96], rta, rtb, ALU.add, reads=["rta", "rtb"], writes=[kqf])
        st.append(rope)

        def trans():
            for i in range(4):
                for e in range(2):
                    slot = e * 4 + i
                    b.tr(psb_bf[ps_bank][0:96, slot * 128:(slot + 1) * 128], qf_[:, i, e, :], ident,
                         reads=[kqf, "cbf"], writes=[("ps", ps_bank)])
        st.append(trans)
        st.append(lambda: b.act(dstT[0:96, :, g4 * 512:(g4 + 1) * 512],
                                psb_bf[ps_bank][0:96, :].rearrange("p (e n) -> p e n", e=2), AF.Copy,
                                reads=[("ps", ps_bank)], writes=[(tag, g4)]))
        return st

    def mla(l, half=1):
        wuq = wb[1][:, 0:1536].rearrange("p (k n) -> p k n", k=2)
        wukv = wb[1][:, 1536:2560]
        gq = rowb[:, 0:96]
        gk = rowb[:, 96:192]
        b.v("pool", "memset", Ve[:, :, 64:65], 1.0, writes=["Ve"])
        b.v("pool", "memset", Vo[:, :, 0:64], 0.0, writes=["Vo"])
        b.v("pool", "memset", Vo[:, :, 0:1], 1.0, writes=["Vo"])
        for hp in range(4):
            if half == 1:
                b.dma(pkt_s[0:96], pkt_d[l, hp], reads=[("skv", l, hp)], writes=HK, key="pk")
                b.dma(pve_s, pve_d[l, hp], reads=[("skv", l, hp)], writes=HK, key="pk")
                b.dma(pvo_s, pvo_d[l, hp], reads=[("skv", l, hp)], writes=HK, key="pk")
            qps4 = psp[0][:, :].rearrange("p (g n) -> p g n", g=4)
            kvps4 = psp[1][:, :].rearrange("p (g n) -> p g n", g=4)
            PQ = [("ps", 0), ("ps", 1)]
            PKV = [("ps", 2), ("ps", 3)]
            for g4 in range(4):
                for i in range(4):
                    t16 = g4 * 4 + i
                    tok = slice(t16 * 128, (t16 + 1) * 128)
                    for k in range(2):
                        b.mm(qps4[:, i, 0:192], cqn[:, k, tok], wuq[:, k, hp * 192:(hp + 1) * 192], k == 0, k == 1,
                             reads=[("cqn", g4), ("wb", 1)], writes=[("ps", i // 2)])
                    b.mm(kvps4[:, i, :], ckvn[:, tok], wukv[:, hp * 256:(hp + 1) * 256], True, True,
                         reads=[("ckvn", g4), ("wb", 1)], writes=[("ps", 2 + i // 2)])
                kv4 = kvps4.rearrange("p g (e d) -> p g e d", e=2)
                b.act(kraw[:, :, :, 0:64], kv4[:, :, :, 0:64], AF.Copy, reads=PKV, writes=[KRK])
                b.v("pool", "tensor_copy", kraw[:, :, :, 64:96],
                    krope[:, g4 * 4:(g4 + 1) * 4, :].unsqueeze(2).to_broadcast([128, 4, 2, 32]),
                    reads=[("krope", g4 * 4 + i) for i in range(4)], writes=[KRK])
                sq_st = qk_path("q", qps4[:, :, 0:192].rearrange("p g (e d) -> p g e d", e=2), gq, QT, 6, g4, PQ, "QT")
                sk_st = qk_path("k", kraw, gk, KT, 7, g4, [KRK], "KT")
                sq_st[0]()
                b.act(Ve[:, g4 * 4:(g4 + 1) * 4, 0:64], kvps4[:, :, 64:128], AF.Copy, reads=PKV, writes=["Ve"])
                b.act(Vo[:, g4 * 4:(g4 + 1) * 4, 64:128], kvps4[:, :, 192:256], AF.Copy, reads=PKV, writes=["Vo"])
                sk_st[0]()
                for a_, b_ in zip(sq_st[1:], sk_st[1:]):
                    a_()
                    b_()
            QK = [("QT", g) for g in range(4)]
            KK = [("KT", g) for g in range(4)]
            if half == 0:
                b.dma(okt_d[l, hp], KT[0:96], reads=KK, writes=[("skv", l, hp)], key="okv")
                b.dma(ove_d[l, hp], Ve, reads=["Ve"], writes=[("skv", l, hp)], key="okv")
                b.dma(ovo_d[l, hp], Vo, reads=["Vo"], writes=[("skv", l, hp)], key="okv")
            for e in range(2):
                Vt = Ve if e == 0 else Vo
                vk = "Ve" if e == 0 else "Vo"
                pV = pve_s if e == 0 else pvo_s
                M = 65 if e == 0 else 128
                for qt in range(4):
                    blocks = ([("p", kb) for kb in range(16)] if half == 1 else []) + [("o", kb) for kb in range(4 * qt + 4)]
                    po = 2 + (qt % 2)
                    SB = [0, 1, 5]

                    def emit_qk(bi):
                        kind, kb = blocks[bi]
                        pi = SB[bi % 3]
                        kts = (pkt_s if kind == "p" else KT)[0:96, e, kb * 128:(kb + 1) * 128]
                        kk = HK if kind == "p" else [("KT", kb // 4)]
                        b.mm(psb[pi][:], kts, QT[0:96, e, qt * 512:(qt + 1) * 512], True, True,
                             reads=kk + [("QT", qt)], writes=[("ps", pi)])

                    emit_qk(0)
                    if len(blocks) > 1:
                        emit_qk(1)
                    for bi, (kind, kb) in enumerate(blocks):
                        pi = SB[bi % 3]
                        pt = PT[bi % 4]
                        ptk = ("sq", bi % 4)
                        b.act(pt, psb[pi][:], AF.Exp, reads=[("ps", pi), "cvec"], writes=[ptk],
                              bias=nshift, scale=1.0)
                        if kind == "o" and kb >= 4 * qt:
                            o = kb - 4 * qt
                            b.v("dve", "tensor_tensor", pt, pt, cmask[:, (3 - o) * 128:(3 - o) * 128 + 512], ALU.mult,
                                reads=[ptk, "cbf"], writes=[ptk])
                        if bi + 2 < len(blocks):
                            emit_qk(bi + 2)
                        vs = (pV if kind == "p" else Vt)[:, kb, 0:M]
                        b.mm(psb[po][0:M, :], vs, pt, bi == 0, bi == len(blocks) - 1,
                             reads=(HK if kind == "p" else [vk]) + [ptk], writes=[("ps", po)])
                    dr = 64 if e == 0 else 0
                    b.v("dve", "reciprocal", rc[dr:dr + 1, :], psb[po][dr:dr + 1, :], reads=[("ps", po)], writes=["rc"])
                    if e == 0:
                        b.mm(psb[4][0:64, :], ones_f[64:65, 0:64], rc[64:65, :], True, True, reads=["cst", "rc"], writes=[("ps", 4)])
                        rows = slice(0, 64)
                    else:
                        b.mm(psb[4][:, :], ones_f[0:1, :], rc[0:1, :], True, True, reads=["cst", "rc"], writes=[("ps", 4)])
                        rows = slice(64, 128)
                    b.act(rstd[rows, :], psb[4][rows, :], AF.Copy, reads=[("ps", 4)], writes=["rstd"])
                    b.v("dve", "tensor_tensor", ymla[rows, hp, qt * 512:(qt + 1) * 512], psb[po][rows, :], rstd[rows, :], ALU.mult,
                        reads=[("ps", po), "rstd"], writes=[("ymla", hp, qt)])
        dbg("ymla", ymla, [128, 4, T], [("ymla", h, q) for h in range(4) for q in range(4)], BF16)
        dbg("QT", QT, [128, 2, T], [("QT", g) for g in range(4)], BF16)
        dbg("KT", KT, [128, 2, T], [("KT", g) for g in range(4)], BF16)
        dbg("Vo", Vo, [128, 16, 128], ["Vo"], BF16)


    dummy = b.sb("dummy_s", [128, 2])

    def barrier(keys):
        b.v("pool", "memset", dummy[:, 0:1], 0.0, writes=list(keys))

    MIXER_KEYS = ([("uT", c, g) for c in range(4) for g in range(4)] + [("cqn", g) for g in range(4)]
                  + [("ckvn", g) for g in range(4)] + [("krope", g) for g in range(16)] + ["cos", "sin"]
                  + [("ymla", h, q) for h in range(4) for q in range(4)])
    ATT_KEYS = ([("QT", g) for g in range(4)] + [("KT", g) for g in range(4)]
                + ["Ve", "Vo", "sqq", "qn", "qf", "rta", "rtb", "ssq", "kraw"])
    PROJ_KEYS = [("cqraw", i) for i in range(3)] + ["ropeang", "ropeang2", "ropew", "ropew_ti", "ropew_tf", "ropew_tm"]

    hT_f = hT[:].rearrange("p a b -> p (a b)").bitcast(F32)
    SLOTS = [{n: A32(48 + 2 * i, [128, 512]) for i, n in enumerate("ABCDEFGHIJKMN")},
             dict([(n, A32(32 + 2 * i, [128, 512])) for i, n in enumerate("ABCDEFGH")]
                  + [(n, hT_f[:, i * 512:(i + 1) * 512]) for i, n in enumerate("IJKMN")])]
    SSM_KEYS = ["s%d%s" % (ch, n) for ch in range(2) for n in "ABCDEFGHIJKMN"] + \
               ["Xr0", "Xi0", "Xr1", "Xi1"]
    SSM_BANKS = [(0, 1, 2, 3), (4, 5, 6, 7)]

    def ssm_prep(l, c, ch, half):
        slot = SLOTS[ch]
        sk = lambda n: "s%d%s" % (ch, n)
        def tt(eng, out, a, bb_, op):
            b.v(eng, "tensor_tensor", slot[out], slot[a], slot[bb_], op, reads=[sk(a), sk(bb_)], writes=[sk(out)])
        def wrap(dst, src, ti, tf, tm):
            d_, s_, ti_, tf_, tm_ = slot[dst], slot[src], slot[ti].bitcast(I32), slot[tf], slot[tm]
            kd, ks, kti, ktf, ktm = sk(dst), sk(src), sk(ti), sk(tf), sk(tm)
            b.v("dve", "tensor_scalar_mul", ti_, s_, 1.0 / TWO_PI, reads=[ks], writes=[kti])
            b.v("dve", "tensor_copy", tf_, ti_, reads=[kti], writes=[ktf])
            b.v("dve", "scalar_tensor_tensor", d_, tf_, -TWO_PI, s_, ALU.mult, ALU.add, reads=[ktf, ks], writes=[kd])
            b.v("dve", "tensor_scalar", tm_, d_, PI, -TWO_PI, ALU.is_gt, ALU.mult, reads=[kd], writes=[ktm])
            b.v("dve", "tensor_tensor", d_, d_, tm_, ALU.add, reads=[kd, ktm], writes=[kd])
            b.v("dve", "tensor_scalar", tm_, d_, -PI, TWO_PI, ALU.is_lt, ALU.mult, reads=[kd], writes=[ktm])
            b.v("dve", "tensor_tensor", d_, d_, tm_, ALU.add, reads=[kd, ktm], writes=[kd])
        def actf(out, a, func, **kw):
            b.act(slot[out], slot[a], func, reads=[sk(a), "cst"], writes=[sk(out)], **kw)
        for i, n in enumerate("KMN"):
            b.dma(slot[n], lam_d[l, c, i].partition_broadcast(128), writes=[sk(n)])
        actf("N", "N", AF.Exp)
        tt("dve", "G", "K", "N", ALU.mult)
        tt("dve", "H", "M", "N", ALU.mult)
        actf("A", "G", AF.Exp, scale=sp1)
        actf("C", "G", AF.Exp, scale=nsp1)
        b.v("dve", "tensor_scalar_mul", slot["I"], slot["H"], sp1, reads=[sk("H"), "cst"], writes=[sk("I")])
        wrap("J", "I", "N", "E", "F")
        actf("B", "J", AF.Sin)
        b.v("dve", "tensor_scalar_add", slot["I"], slot["I"], PI / 2, reads=[sk("I")], writes=[sk("I")])
        wrap("J", "I", "N", "E", "F")
        actf("D", "J", AF.Sin)
        tt("dve", "E", "A", "D", ALU.mult)
        tt("dve", "F", "A", "B", ALU.mult)
        tt("dve", "A", "C", "D", ALU.mult)
        b.v("dve", "scalar_tensor_tensor", slot["B"], slot["C"], -1.0, slot["B"], ALU.mult, ALU.mult,
            reads=[sk("C"), sk("B")], writes=[sk("B")])
        wrap("J", "H", "N", "C", "D")
        actf("I", "J", AF.Sin)
        b.v("dve", "tensor_scalar_add", slot["H"], slot["H"], PI / 2, reads=[sk("H")], writes=[sk("H")])
        wrap("J", "H", "N", "C", "D")
        actf("N", "J", AF.Sin)
        actf("C", "G", AF.Exp)
        tt("dve", "D", "C", "N", ALU.mult)
        b.v("dve", "tensor_scalar_add", slot["D"], slot["D"], -1.0, reads=[sk("D")], writes=[sk("D")])
        tt("dve", "I", "C", "I", ALU.mult)
        tt("dve", "J", "K", "K", ALU.mult)
        tt("dve", "N", "M", "M", ALU.mult)
        tt("dve", "J", "J", "N", ALU.add)
        b.v("dve", "reciprocal", slot["J"], slot["J"], reads=[sk("J")], writes=[sk("J")])
        tt("dve", "C", "D", "K", ALU.mult)
        tt("dve", "N", "I", "M", ALU.mult)
        tt("dve", "C", "C", "N", ALU.add)
        tt("dve", "C", "C", "J", ALU.mult)
        tt("dve", "N", "I", "K", ALU.mult)
        tt("dve", "G", "D", "M", ALU.mult)
        tt("dve", "N", "N", "G", ALU.subtract)
        tt("dve", "N", "N", "J", ALU.mult)
        tt("dve", "D", "A", "C", ALU.mult)
        tt("dve", "G", "B", "N", ALU.mult)
        tt("dve", "D", "D", "G", ALU.subtract)
        tt("dve", "I", "A", "N", ALU.mult)
        tt("dve", "G", "B", "C", ALU.mult)
        tt("dve", "I", "I", "G", ALU.add)
        cx = dict(slot=slot, sk=sk, ch=ch, c=c,
                  Tre=slot["E"], Tim=slot["F"], Tpr=slot["D"], Tpi=slot["I"],
                  TK=[sk("E"), sk("F"), sk("D"), sk("I")],
                  Bm=slot["A"].bitcast(BF16),
                  Cm=slot["B"].bitcast(BF16).rearrange("p (k n) -> p k n", k=8),
                  X=slot["C"].bitcast(BF16),
                  Sb=[slot["G"].bitcast(BF16), slot["H"].bitcast(BF16)], SbK=[sk("G"), sk("H")],
                  ST=slot["J"].bitcast(BF16).rearrange("p (k n) -> p k n", k=8))
        load_w(cx["Bm"], bm_d[l, c], sk("A"))
        load_w(cx["Cm"], cm_d[l, c], sk("B"))
        b.v("pool", "memset", cx["Sb"][1], 0.0, writes=[cx["SbK"][1]])
        if half == 1:
            b.dma(cx["Sb"][1][127:128, :], hinit_d[l, c:c + 1, :], reads=[("sh", l, c)], writes=[cx["SbK"][1]])
        return cx

    def ssm_tile(l, cx, k, stage):
        slot, sk, ch, c = cx["slot"], cx["sk"], cx["ch"], cx["c"]
        Tre, Tim, Tpr, Tpi, TK = cx["Tre"], cx["Tim"], cx["Tpr"], cx["Tpi"], cx["TK"]
        Bm, Cm, X, Sb, SbK, ST = cx["Bm"], cx["Cm"], cx["X"], cx["Sb"], cx["SbK"], cx["ST"]
        b0, b1, tb, yb = SSM_BANKS[ch]
        Xr, Xi = "Xr%d" % ch, "Xi%d" % ch
        tok = slice(k * 128, (k + 1) * 128)
        g4 = k // 4
        cur, prv = k % 2, 1 - (k % 2)
        K_, M_, N_ = slot["K"], slot["M"], slot["N"]
        if stage == 2:
            return ssm_tile2(l, cx, k)
        if stage == 3:
            return ssm_tile3(l, cx, k)
        b.mm(psb[b0][:], uT[:, c, tok], Bm[:, 0:512], True, True, reads=[("uT", c, g4), sk("A")], writes=[("ps", b0)])
        b.mm(psb[b1][:], uT[:, c, tok], Bm[:, 512:1024], True, True, reads=[("uT", c, g4), sk("A")], writes=[("ps", b1)])
        b.v("dve", "tensor_tensor", K_, psb[b0][:], Tpr, ALU.mult, reads=[("ps", b0)] + TK, writes=[sk("K")])
        b.v("dve", "tensor_tensor", M_, psb[b1][:], Tpi, ALU.mult, reads=[("ps", b1)] + TK, writes=[sk("M")])
        b.v("pool", "tensor_tensor", X[:, 0:512], K_, M_, ALU.subtract, reads=[sk("K"), sk("M")], writes=[Xr])
        b.v("dve", "tensor_tensor", N_, psb[b1][:], Tpr, ALU.mult, reads=[("ps", b1)] + TK, writes=[sk("N")])
        b.v("dve", "tensor_tensor", K_, psb[b0][:], Tpi, ALU.mult, reads=[("ps", b0)] + TK + [Xr], writes=[sk("K")])
        b.v("pool", "tensor_tensor", X[:, 512:1024], N_, K_, ALU.add, reads=[sk("K"), sk("N")], writes=[Xi])

    def ssm_tile2(l, cx, k):
        slot, sk, ch, c = cx["slot"], cx["sk"], cx["ch"], cx["c"]
        Tre, Tim, Tpr, Tpi, TK = cx["Tre"], cx["Tim"], cx["Tpr"], cx["Tpi"], cx["TK"]
        Bm, Cm, X, Sb, SbK, ST = cx["Bm"], cx["Cm"], cx["X"], cx["Sb"], cx["SbK"], cx["ST"]
        b0, b1, tb, yb = SSM_BANKS[ch]
        Xr, Xi = "Xr%d" % ch, "Xi%d" % ch
        cur, prv = k % 2, 1 - (k % 2)
        K_, M_, N_ = slot["K"], slot["M"], slot["N"]
        b.mm(psb[b0][:], tri, X[:, 0:512], True, False, reads=["cbf", Xr], writes=[("ps", b0)])
        b.mm(psb[b0][:], e127, Sb[prv][:, 0:512], False, True, reads=["cbf", SbK[prv]], writes=[("ps", b0)])
        b.mm(psb[b1][:], tri, X[:, 512:1024], True, False, reads=["cbf", Xi], writes=[("ps", b1)])
        b.mm(psb[b1][:], e127n, Sb[prv][:, 512:1024], False, True, reads=["cbf", SbK[prv]], writes=[("ps", b1)])
        b.v("dve", "tensor_tensor", K_, psb[b0][:], Tre, ALU.mult, reads=[("ps", b0)] + TK + [Xi], writes=[sk("K")])
        b.v("dve", "tensor_tensor", M_, psb[b1][:], Tim, ALU.mult, reads=[("ps", b1)] + TK + [Xr], writes=[sk("M")])
        b.v("pool", "tensor_tensor", Sb[cur][:, 0:512], K_, M_, ALU.subtract, reads=[sk("K"), sk("M")], writes=[SbK[cur]])
        b.v("dve", "scalar_tensor_tensor", N_, psb[b1][:], -1.0, Tre, ALU.mult, ALU.mult,
            reads=[("ps", b1)] + TK + [Xi], writes=[sk("N")])
        b.v("dve", "scalar_tensor_tensor", K_, psb[b0][:], -1.0, Tim, ALU.mult, ALU.mult,
            reads=[("ps", b0)] + TK + [SbK[cur]], writes=[sk("K")])
        b.v("pool", "tensor_tensor", Sb[cur][:, 512:1024], N_, K_, ALU.add,
            reads=[sk("K"), sk("N")], writes=[SbK[cur]])

    def ssm_tile3(l, cx, k):
        slot, sk, ch, c = cx["slot"], cx["sk"], cx["ch"], cx["c"]
        Cm, Sb, SbK, ST = cx["Cm"], cx["Sb"], cx["SbK"], cx["ST"]
        b0, b1, tb, yb = SSM_BANKS[ch]
        g4 = k // 4
        cur = k % 2
        for blk in range(8):
            b.tr(psb_bf[tb][:, blk * 128:(blk + 1) * 128], Sb[cur][:, blk * 128:(blk + 1) * 128], ident,
                 reads=[SbK[cur], "cbf"], writes=[("ps", tb)])
        b.act(ST, psb_bf[tb][:, :].rearrange("p (k n) -> p k n", k=8), AF.Copy, reads=[("ps", tb)], writes=[sk("J")])
        for blk in range(8):
            b.mm(psb[yb][:, (k % 4) * 128:(k % 4 + 1) * 128], Cm[:, blk, :], ST[:, blk, :], blk == 0, blk == 7,
                 reads=[sk("B"), sk("J")], writes=[("ps", yb)])
        if k % 4 == 3:
            t4 = slice(g4 * 512, (g4 + 1) * 512)
            b.v("dve", "scalar_tensor_tensor", uT[:, c, t4], uT[:, c, t4], vec[:, l, 35 + c:36 + c], psb[yb][:],
                ALU.mult, ALU.add, reads=[("uT", c, g4), "vec", ("ps", yb)], writes=[("uT", c, g4)])

    def ssm(l, half=1):
        for pair in range(2):
            cxs = [ssm_prep(l, 2 * pair + ch, ch, half) for ch in range(2)]
            for k in range(16):
                for stage in (1, 2, 3):
                    for cx in cxs:
                        ssm_tile(l, cx, k, stage)
            if half == 0:
                for cx in cxs:
                    c = cx["c"]
                    b.dma(hfin_d[l, c:c + 1, :], cx["Sb"][1][127:128, :], reads=[cx["SbK"][1]], writes=[("sh", l, c)], key="hfin")
        dbg("yssm", uT, [128, 4, T], [("uT", c, g) for c in range(4) for g in range(4)], BF16)


    def group_norm(buf, keyf, gcol, l, g4):
        tok = slice(g4 * 512, (g4 + 1) * 512)
        ks = [keyf(c, g4) for c in range(4)]
        b.act(sq[:, 0:4, :], buf[:, :, tok], AF.Square, reads=ks, writes=SQK)
        pi = next_ps()
        rms_scale(SQK, 4, 512, lambda c: sq[:, c, :], pi, 1.0 / 512)
        for c in range(4):
            b.v("dve", "scalar_tensor_tensor", buf[:, c, tok], buf[:, c, tok], vec[:, l, gcol + c:gcol + c + 1], rstd[:],
                ALU.mult, ALU.mult, reads=[keyf(c, g4), "vec", "rstd"], writes=[keyf(c, g4)])

    def ssm_post(l):
        wglu = wb[1][:, 4096:6144].rearrange("p (k n) -> p k n", k=4)
        tK = A32(48, [128, 512])
        tM = A32(50, [128, 512])
        gates = A32(52, [128, 4, 512])
        for g4 in range(4):
            tok = slice(g4 * 512, (g4 + 1) * 512)
            for c in range(4):
                y = uT[:, c, tok]
                yk = ("uT", c, g4)
                b.v("dve", "tensor_tensor", tK, y, y, ALU.mult, reads=[yk], writes=["gK"])
                b.v("dve", "tensor_scalar", tK, tK, GB, 1.0, ALU.mult, ALU.add, reads=["gK"], writes=["gK"])
                b.v("dve", "tensor_tensor", tK, tK, y, ALU.mult, reads=["gK", yk], writes=["gK"])
                b.act(tM, tK, AF.Sigmoid, reads=["gK"], writes=["gM"], scale=GA)
                b.v("dve", "tensor_tensor", y, y, tM, ALU.mult, reads=["gM", yk], writes=[yk])
            for oc in range(4):
                pi = 4 + oc
                for k in range(4):
                    b.mm(psb[pi][:], wglu[:, k, oc * 128:(oc + 1) * 128], uT[:, k, tok], k == 0, k == 3,
                         reads=[("wb", 1, "glu"), ("uT", k, g4)], writes=[("ps", pi)])
                b.act(gates[:, oc, :], psb[pi][:], AF.Sigmoid, reads=[("ps", pi), "vec"], writes=[("gate", oc)],
                      bias=vec[:, l, 39 + oc:40 + oc], scale=1.0)
            for oc in range(4):
                b.v("dve", "tensor_tensor", uT[:, oc, tok], uT[:, oc, tok], gates[:, oc, :], ALU.mult,
                    reads=[("uT", oc, g4), ("gate", oc)], writes=[("uT", oc, g4)])
            group_norm(uT, lambda c, g: ("uT", c, g), 43, l, g4)
        dbg("yssmn", uT, [128, 4, T], [("uT", c, g) for c in range(4) for g in range(4)], BF16)

    def mix_out(l):
        wout = wb[0][:, :].rearrange("p (k n) -> p k n", k=8)
        group_norm(ymla, lambda c, g: ("ymla", c, g), 47, l, 0)
        for g4 in range(4):
            tok = slice(g4 * 512, (g4 + 1) * 512)
            if g4 + 1 < 4:
                group_norm(ymla, lambda c, g: ("ymla", c, g), 47, l, g4 + 1)
            for oc in range(8):
                pi = next_ps()
                for k in range(8):
                    src = uT[:, k, tok] if k < 4 else ymla[:, k - 4, tok]
                    sk_ = ("uT", k, g4) if k < 4 else ("ymla", k - 4, g4)
                    b.mm(psb[pi][:], wout[:, k, oc * 128:(oc + 1) * 128], src, k == 0, k == 7,
                         reads=[("wb", 0), sk_], writes=[("ps", pi)])
                b.v("dve", "tensor_tensor", xT[:, oc, tok], xT[:, oc, tok], psb[pi][:], ALU.add,
                    reads=[("xT", oc, g4), ("ps", pi)], writes=[("xT", oc, g4)])
        dbg("x1", xT[:], [128, 8, T], [("xT", c, g) for c in range(8) for g in range(4)])

    CR_KEYS = ["mhT", "KmT", "Vm", "memx", "mtmp", "mss", "mkn", "mtmp4", "mss4", "mkn4", ("QmT", 0), ("QmT", 1), ("QmT", 2), ("QmT", 3)] + \
              [("omT", h, q) for h in range(4) for q in range(4)]

    def cross(l):
        mhT = A16(0, [128, 8, 256])
        KmT = A16(4, [128, 4, 256])
        Vm = A16(6, [128, 2, 4, 65])
        QmT = A16(8, [128, 4, T])
        omT = A16(24, [128, 4, T])
        memx = A32(40, [128, 8, 256])
        mtmp = A32(48, [128, 4, 64])
        mss = A32(49, [128, 4])
        mkn = A16(50, [128, 4, 64])
        mtmp4 = A32(52, [128, 4, 256]).rearrange("p g (h d) -> p g h d", h=4)
        mss4 = A32(56, [128, 4, 4])
        mkn4 = A16(57, [128, 4, 256]).rearrange("p g (h d) -> p g h d", h=4)
        mgq = rowb[:, 192:256]
        mgk = rowb[:, 256:320]
        mwkv = wb[1][:, 0:4096].rearrange("p (k n) -> p k n", k=8)
        mwq = wb[1][:, 4096:6144].rearrange("p (k n) -> p k n", k=8)
        load_w(mwkv, mwkv_d[l], ("wb", 1))
        load_w(mwq, mwq_d[l], ("wb", 1, "glu"))
        b.dma(memx, memT_d, writes=["memx"])
        sqm = sq[:, :, 0:256]
        b.act(sqm, memx, AF.Square, reads=["memx"], writes=SQK)
        pi = next_ps()
        rms_scale(SQK, 8, 256, lambda c: sq[:, c, 0:256], pi, 1.0 / D)
        for c in range(8):
            b.v("dve", "scalar_tensor_tensor", mhT[:, c, :], memx[:, c, :], vec[:, l, 24 + c:25 + c], rstd[:, 0:256],
                ALU.mult, ALU.mult, reads=["memx", "vec", "rstd"], writes=["mhT"])
        b.v("pool", "memset", Vm[:, :, :, 64:65], 1.0, writes=["Vm"])

        def head_norm(src, srck, gain, n_slots_bank, slot_fn):
            b.act(mtmp, src, AF.Square, reads=srck, writes=["mtmp"])
            b.v("dve", "tensor_reduce", mss, mtmp, AX.X, ALU.add, reads=["mtmp"], writes=["mss"])
            b.act(mss, mss, AF.Sqrt, reads=["mss", "cvec"], writes=["mss"], bias=epsv, scale=1.0 / 64)
            b.v("dve", "reciprocal", mss, mss, reads=["mss"], writes=["mss"])
            b.v("dve", "tensor_tensor", mtmp, src, mss.unsqueeze(2).to_broadcast([128, 4, 64]), ALU.mult,
                reads=srck + ["mss"], writes=["mtmp"])
            b.v("dve", "tensor_tensor", mkn, mtmp, gain.unsqueeze(1).to_broadcast([128, 4, 64]), ALU.mult,
                reads=["mtmp", "rowb"], writes=["mkn"])
            for h in range(4):
                sl = slot_fn(h)
                b.tr(psb_bf[n_slots_bank][0:64, sl * 128:(sl + 1) * 128], mkn[:, h, :], ident,
                     reads=["mkn", "cbf"], writes=[("ps", n_slots_bank)])

        for mt in range(2):
            pi = next_ps()
            for k in range(8):
                b.mm(psb[pi][:], mhT[:, k, mt * 128:(mt + 1) * 128], mwkv[:, k, :], k == 0, k == 7,
                     reads=["mhT", ("wb", 1)], writes=[("ps", pi)])
            kvv = psb[pi][:].rearrange("p (h d) -> p h d", h=4)
            b.act(Vm[:, mt, :, 0:64], kvv[:, :, 64:128], AF.Copy, reads=[("ps", pi)], writes=["Vm"])
            head_norm(kvv[:, :, 0:64], [("ps", pi)], mgk, 6, lambda h: h * 2 + mt)
        b.act(KmT[0:64, :, :], psb_bf[6][0:64, :].rearrange("p (h n) -> p h n", h=4), AF.Copy,
              reads=[("ps", 6)], writes=["KmT"])

        for hf in range(2):
            norm_x(l, 8, hf * 1024, 2, None)
            for gg in range(2):
                g4 = hf * 2 + gg
                qp = psp[gg][:, :].rearrange("p (g n) -> p g n", g=4)
                PK = [("ps", 2 * gg), ("ps", 2 * gg + 1)]
                for i in range(4):
                    t8 = gg * 4 + i
                    for k in range(8):
                        b.mm(qp[:, i, :], hT[:, k, t8 * 128:(t8 + 1) * 128], mwq[:, k, :], k == 0, k == 7,
                             reads=[("hT", t8 // 4), ("wb", 1, "glu")], writes=[("ps", 2 * gg + i // 2)])
                src = qp.rearrange("p g (h d) -> p g h d", h=4)
                b.act(mtmp4, src, AF.Square, reads=PK, writes=["mtmp4"])
                b.v("dve", "tensor_reduce", mss4, mtmp4, AX.X, ALU.add, reads=["mtmp4"], writes=["mss4"])
                b.act(mss4, mss4, AF.Sqrt, reads=["mss4", "cvec"], writes=["mss4"], bias=epsv, scale=1.0 / 64)
                b.v("dve", "reciprocal", mss4, mss4, reads=["mss4"], writes=["mss4"])
                b.v("dve", "tensor_tensor", mtmp4, src, mss4.unsqueeze(3).to_broadcast([128, 4, 4, 64]), ALU.mult,
                    reads=PK + ["mss4"], writes=["mtmp4"])
                b.v("dve", "tensor_tensor", mkn4, mtmp4, mgq.unsqueeze(1).unsqueeze(1).to_broadcast([128, 4, 4, 64]), ALU.mult,
                    reads=["mtmp4", "rowb"], writes=["mkn4"])
                for i in range(4):
                    bank = 6 + i // 2
                    for h in range(4):
                        sl = h * 2 + (i % 2)
                        b.tr(psb_bf[bank][0:64, sl * 128:(sl + 1) * 128], mkn4[:, i, h, :], ident,
                             reads=["mkn4", "cbf"], writes=[("ps", bank)])
                for half2 in range(2):
                    t16p = g4 * 2 + half2
                    b.act(QmT[0:64, :, t16p * 256:(t16p + 1) * 256],
                          psb_bf[6 + half2][0:64, :].rearrange("p (h n) -> p h n", h=4), AF.Copy,
                          reads=[("ps", 6 + half2)], writes=[("QmT", g4)])
        for h in range(4):
            for qt in range(4):
                po = 2 + (qt % 2)
                for kb in range(2):
                    b.mm(psb[kb][:], KmT[0:64, h, kb * 128:(kb + 1) * 128], QmT[0:64, h, qt * 512:(qt + 1) * 512], True, True,
                         reads=["KmT", ("QmT", qt)], writes=[("ps", kb)])
                for kb in range(2):
                    pi = kb
                    pt = PT[kb + 2 * (qt % 2)]
                    b.act(pt, psb[pi][:], AF.Exp, reads=[("ps", pi), "cvec"], writes=[("sq", kb + 2 * (qt % 2))], bias=nmshift, scale=1.0)
                    b.mm(psb[po][0:65, :], Vm[:, kb, h, :], pt, kb == 0, kb == 1, reads=["Vm", ("sq", kb + 2 * (qt % 2))], writes=[("ps", po)])
                b.v("dve", "reciprocal", rc[64:65, :], psb[po][64:65, :], reads=[("ps", po)], writes=["rc"])
                b.mm(psb[4][0:64, :], ones_f[64:65, 0:64], rc[64:65, :], True, True, reads=["cst", "rc"], writes=[("ps", 4)])
                b.act(rstd[0:64, :], psb[4][0:64, :], AF.Copy, reads=[("ps", 4)], writes=["rstd"])
                b.v("dve", "tensor_tensor", omT[0:64, h, qt * 512:(qt + 1) * 512], psb[po][0:64, :], rstd[0:64, :], ALU.mult,
                    reads=[("ps", po), "rstd"], writes=[("omT", h, qt)])
        mwo = wb[0][0:64, 0:4096].rearrange("p (h n) -> p h n", h=4)
        load_w(mwo, mwo_d[l], ("wb", 0))
        for g4 in range(4):
            tok = slice(g4 * 512, (g4 + 1) * 512)
            for oc in range(8):
                pi = next_ps()
                for h in range(4):
                    b.mm(psb[pi][:], mwo[:, h, oc * 128:(oc + 1) * 128], omT[0:64, h, tok], h == 0, h == 3,
                         reads=[("wb", 0), ("omT", h, g4)], writes=[("ps", pi)])
                b.v("dve", "tensor_tensor", xT[:, oc, tok], xT[:, oc, tok], psb[pi][:], ALU.add,
                    reads=[("xT", oc, g4), ("ps", pi)], writes=[("xT", oc, g4)])
        dbg("x2", xT[:], [128, 8, T], [("xT", c, g) for c in range(8) for g in range(4)])

    MLP_KEYS = [("hid", j, t) for j in range(32) for t in range(2)] + [("rt", i) for i in range(4)]

    WBH = [wb[0][:, 0:4096], wb[0][:, 4096:8192], wb[1][:, 0:4096], wb[1][:, 4096:8192]]
    WBH_KEYS = [("wbh", i) for i in range(4)] + [("wb", 0), ("wb", 1), ("wb", 1, "glu"), "rc"]

    def mlp(l):
        hid = A16(0, [128, 32, 1024])
        rt = [A16(64, [128, 512]), A16(65, [128, 512]), A16(66, [128, 512]), A16(67, [128, 512])]
        n = 0
        wi = 0
        for hf in range(2):
            norm_x(l, 16, hf * 1024, 2, None)
            for jg in range(8):
                ws = wi % 4
                wi += 1
                w1p = WBH[ws].rearrange("p (k n) -> p k n", k=8)
                b.dma(w1p, w1_d[l][:, :, jg * 512:(jg + 1) * 512], writes=[("wbh", ws)], key=("wh", ws), eng="pool")
                for jj in range(4):
                    j = jg * 4 + jj
                    for t in range(2):
                        pi = next_ps()
                        for k in range(8):
                            b.mm(psb[pi][:], w1p[:, k, jj * 128:(jj + 1) * 128], hT[:, k, t * 512:(t + 1) * 512], k == 0, k == 7,
                                 reads=[("wbh", ws), ("hT", t)], writes=[("ps", pi)])
                        r = rt[n % 4]
                        b.act(r, psb[pi][:], AF.Relu, reads=[("ps", pi)], writes=[("rt", n % 4)])
                        b.v("dve", "tensor_tensor", hid[:, j, t * 512:(t + 1) * 512], r, r, ALU.mult,
                            reads=[("rt", n % 4)], writes=[("hid", j, t)])
                        n += 1
            for oc in range(8):
                ws = wi % 4
                wi += 1
                w2p = WBH[ws].rearrange("p (j n) -> p j n", j=32)
                b.dma(w2p, w2_d[l][:, :, oc * 128:(oc + 1) * 128], writes=[("wbh", ws)], key=("wh", ws), eng="pool")
                for t in range(2):
                    g4 = hf * 2 + t
                    tok = slice(g4 * 512, (g4 + 1) * 512)
                    pi = next_ps()
                    for j in range(32):
                        b.mm(psb[pi][:], w2p[:, j, :], hid[:, j, t * 512:(t + 1) * 512], j == 0, j == 31,
                             reads=[("wbh", ws), ("hid", j, t)], writes=[("ps", pi)])
                    b.v("dve", "tensor_tensor", xT[:, oc, tok], xT[:, oc, tok], psb[pi][:], ALU.add,
                        reads=[("xT", oc, g4), ("ps", pi)], writes=[("xT", oc, g4)])

    def run_layer(l, half=1, stop_after=None):
        rope_tables()
        load_w(wb[1][:, 0:1536].rearrange("p (k n) -> p k n", k=2), w_uq_d[l], ("wb", 1))
        load_w(wb[1][:, 1536:2560], w_ukv_d[l], ("wb", 1))
        load_w(wb[1][:, 4096:6144].rearrange("p (k n) -> p k n", k=4), w_glu_d[l], ("wb", 1, "glu"))
        layer(l)
        load_w(wb[0][:, :].rearrange("p (k n) -> p k n", k=8), w_out_d[l], ("wb", 0))
        YMK = [("ymla", h, q) for h in range(4) for q in range(4)]
        barrier(PROJ_KEYS + SSM_KEYS + HK + YMK)
        ssm(l, half)
        if stop_after == "ssm":
            return
        barrier(SSM_KEYS + HK + YMK + ["gK", "gM"] + [("gate", i) for i in range(4)])
        ssm_post(l)
        if stop_after == "ssm_post":
            return
        barrier(SSM_KEYS + HK + YMK + ["gK", "gM"] + [("gate", i) for i in range(4)] + ATT_KEYS + SQK)
        mla(l, half)
        mix_out(l)
        if stop_after == "mix":
            return
        barrier(MIXER_KEYS + ATT_KEYS + CR_KEYS)
        cross(l)
        if stop_after == "cross":
            return
        barrier(CR_KEYS + MLP_KEYS + WBH_KEYS)
        mlp(l)
        barrier(MLP_KEYS + MIXER_KEYS + PROJ_KEYS + WBH_KEYS)

    def store_x(half):
        for c in range(8):
            b.dma(outT_d[half, :, c, :], xT[:, c, :], reads=[("xT", c, g) for g in range(4)], writes=[("dramout", "out")], key="out")

    def finish():
        b.p.add("sp", lambda e: e.nop(), reads=[("dramout", "out")] + [("dbgdone", k) for k in b.out_keys])

    b.ctx = dict(locals())
    return b


_CACHE = {}
N_LAYERS = 4


def _program():
    if "b" not in _CACHE:
        b = build_program(N_LAYERS)
        c = b.ctx
        for half in range(2):
            c["load_x"](half)
            for l in range(N_LAYERS):
                c["run_layer"](l, half)
            c["store_x"](half)
        c["finish"]()
        b.p.emit(b.nc, b.es)
        _CACHE["b"] = b
    return _CACHE["b"]


def kernel(**inputs):
    inp = {k: np.asarray(v) for k, v in inputs.items()}
    B = inp["x"].shape[0]
    b = _program()
    cst = host_constants()
    W = prep_weights(inp, range(N_LAYERS))
    CORE_OF = [0, 1, 4, 5]
    maps = {}
    for bb in range(B):
        xs = np.asarray(inp["x"][bb], np.float32)
        xT = np.stack([chunkT(np.ascontiguousarray(xs[hf * T:(hf + 1) * T].T), 8) for hf in range(2)])
        pos = np.asarray(inp["positions"][bb]).astype(np.float32)
        m = {"xT": xT,
             "memT": chunkT(np.ascontiguousarray(np.asarray(inp["mem"][bb], np.float32).T), 8),
             "pos": np.stack([np.ascontiguousarray(pos[hf * T:(hf + 1) * T].reshape(16, 128).T) for hf in range(2)]),
             "cst": cst}
        m.update(W)
        maps[CORE_OF[bb]] = m
    zero = {k: np.zeros_like(v) for k, v in maps[CORE_OF[0]].items()}
    in_maps = [maps.get(core, zero) for core in range(8)]
    res = run_bass_kernel_spmd(b.nc, in_maps, core_ids=list(range(8)))
    out = np.zeros((B, 2 * T, D), np.float32)
    for bb in range(B):
        o = np.asarray(res.results[CORE_OF[bb]]["outT"], np.float32)
        for hf in range(2):
            out[bb, hf * T:(hf + 1) * T] = o[hf].transpose(2, 1, 0).reshape(T, D)
    return out
```

```python
import numpy as np
from contextlib import ExitStack
import concourse.bass as bass
import concourse.mybir as mybir
from concourse.bass_utils import run_bass_kernel_spmd

F32 = mybir.dt.float32
BF16 = mybir.dt.bfloat16
I32 = mybir.dt.int32
AF = mybir.ActivationFunctionType
ALU = mybir.AluOpType
AX = mybir.AxisListType

ENGS = ("pe", "act", "dve", "pool", "sp")


class Op:
    __slots__ = ("eng", "fn", "reads", "writes", "dma_key", "idx", "deps", "signal",
                 "tick", "waits")

    def __init__(self, eng, fn, reads, writes, dma_key):
        self.eng = eng
        self.fn = fn
        self.reads = reads
        self.writes = writes
        self.dma_key = dma_key
        self.deps = []
        self.signal = False
        self.tick = None
        self.waits = []


class Prog:
    def __init__(self):
        self.ops = []

    def add(self, eng, fn, reads=(), writes=(), dma_key=None):
        op = Op(eng, fn, tuple(reads), tuple(writes), dma_key)
        op.idx = len(self.ops)
        self.ops.append(op)
        return op

    def analyse(self):
        last_w = {}
        readers = {}
        for op in self.ops:
            deps = set()
            for k in op.reads:
                w = last_w.get(k)
                if w is not None:
                    deps.add(w.idx)
            for k in op.writes:
                w = last_w.get(k)
                if w is not None:
                    deps.add(w.idx)
                for r in readers.get(k, {}).values():
                    deps.add(r.idx)
            deps.discard(op.idx)
            keep = []
            for d in deps:
                a = self.ops[d]
                if a.dma_key is None and op.dma_key is None and a.eng == op.eng:
                    if a.eng == "pe":
                        continue
                    raw = any(k in a.writes for k in op.reads)
                    if not raw:
                        continue
                keep.append(d)
            op.deps = keep
            for d in keep:
                self.ops[d].signal = True
            for k in op.writes:
                last_w[k] = op
                readers[k] = {}
            for k in op.reads:
                rk = op.dma_key if op.dma_key is not None else op.eng
                readers.setdefault(k, {})[rk] = op
        cnt = {e: 0 for e in ENGS}
        dcnt = {}
        for op in self.ops:
            if op.dma_key is not None:
                dcnt[op.dma_key] = dcnt.get(op.dma_key, 0) + 1
                op.tick = ("dma", op.dma_key, dcnt[op.dma_key] * 16)
            elif op.signal:
                cnt[op.eng] += 1
                op.tick = ("eng", op.eng, cnt[op.eng])
        waited = {e: {} for e in ENGS}
        dma_issued = {}
        for op in self.ops:
            need = {}
            for d in op.deps:
                a = self.ops[d]
                t = a.tick
                if t[0] == "dma":
                    v = dma_issued.get(t[1], 0) * 16
                    v = max(v, t[2])
                else:
                    v = t[2]
                sk = (t[0], t[1])
                if v > need.get(sk, 0):
                    need[sk] = v
            for sk, v in need.items():
                if waited[op.eng].get(sk, 0) >= v:
                    continue
                waited[op.eng][sk] = v
                op.waits.append((sk, v))
            if op.dma_key is not None:
                dma_issued[op.dma_key] = dma_issued.get(op.dma_key, 0) + 1
        self.n_eng_sems = cnt
        self.dma_keys = list(dcnt.keys())

    def emit(self, nc, es):
        self.analyse()
        sems = {}
        for e in ENGS:
            sems[("eng", e)] = es.enter_context(nc.semaphore("s_" + e))
        for i, k in enumerate(self.dma_keys):
            sems[("dma", k)] = es.enter_context(nc.semaphore("d%d" % i))
        self.sems = sems
        per = {e: [op for op in self.ops if op.eng == e] for e in ENGS}
        block = es.enter_context(nc.Block())

        def run(engine, lst):
            for op in lst:
                for sk, v in op.waits:
                    engine.wait_ge(sems[sk], v)
                ins = op.fn(engine)
                if op.dma_key is not None:
                    ins.then_inc(sems[("dma", op.dma_key)], 16)
                elif op.signal:
                    ins.then_inc(sems[("eng", op.eng)], 1)

        @block.tensor
        def _(e):
            run(e, per["pe"])

        @block.scalar
        def _(e):
            run(e, per["act"])

        @block.vector
        def _(e):
            run(e, per["dve"])

        @block.gpsimd
        def _(e):
            run(e, per["pool"])

        @block.sync
        def _(e):
            run(e, per["sp"])


D = 1024
T = 2048
NT4 = T // 512
NT1 = T // 128
KC = D // 128
IN_COLS = 928
EPS = 1e-6
PI = float(np.pi)
SHIFT = 10.0


class Builder:
    def __init__(self, nc, es, n_layers, debug=()):
        self.nc = nc
        self.es = es
        self.p = Prog()
        self.L = n_layers
        self.debug = set(debug)
        self.dram_in = {}
        self.dram_out = {}
        self._uid = 0
        self.out_keys = []

    def din(self, name, shape, dt=F32):
        t = self.nc.dram_tensor(name, list(shape), dt, kind="ExternalInput")
        self.dram_in[name] = t
        return t

    def dout(self, name, shape, dt=F32):
        t = self.nc.dram_tensor(name, list(shape), dt, kind="ExternalOutput")
        self.dram_out[name] = t
        return t

    def sb(self, name, shape, dt=F32):
        return self.es.enter_context(self.nc.sbuf_tensor(name, list(shape), dt))

    def ps(self, name, shape, dt=F32):
        return self.es.enter_context(self.nc.psum_tensor(name, list(shape), dt))

    def dma(self, out, in_, reads=(), writes=(), key=None, eng="sp", **kw):
        writes = list(writes)
        if key is None:
            self._uid += 1
            key = "a%d_%s" % (self._uid % 12, eng)
            writes.append(("dmakey", key))
        return self.p.add(eng, lambda e, o=out, i=in_, kw=kw: e.dma_start(out=o, in_=i, **kw),
                          reads=reads, writes=writes, dma_key=key)

    def mm(self, out, lhsT, rhs, start, stop, reads=(), writes=()):
        return self.p.add("pe", lambda e: e.matmul(out, lhsT, rhs, start=start, stop=stop),
                          reads=reads, writes=writes)

    def tr(self, out, in_, ident, reads=(), writes=()):
        return self.p.add("pe", lambda e: e.transpose(out, in_, ident), reads=reads, writes=writes)

    def act(self, out, in_, func, reads=(), writes=(), eng="act", **kw):
        return self.p.add("act", lambda e: e.activation(out, in_, func, **kw), reads=reads, writes=writes)

    def v(self, eng, name, *args, reads=(), writes=(), **kw):
        return self.p.add(eng, lambda e: getattr(e, name)(*args, **kw), reads=reads, writes=writes)


def host_constants():
    c = np.zeros((128, 1664), np.float32)
    c[:, 0:128] = np.eye(128)
    c[:, 128:256] = 1.0
    r = np.arange(128)
    c[:, 256:384] = (r[:, None] <= r[None, :])
    c[127, 384:512] = 1.0
    c[127, 512:640] = -1.0
    m = np.arange(896)
    c[:, 640:1536] = (m[None, :] - r[:, None] >= 384)
    c[:, 1536:1552] = (10000.0 ** (-np.arange(16, dtype=np.float32) / 16))[None, :]
    c[:, 1552] = r + 1
    c[:, 1553] = -(r + 1)
    return c


def chunkT(v, k):
    return np.ascontiguousarray(v.reshape((k, 128) + v.shape[1:]).swapaxes(0, 1))


def prep_weights(inp, layers):
    f = lambda a: np.asarray(a, np.float32)
    W = {}
    Ls = list(layers)
    n = len(Ls)
    vec = np.zeros((n, 128, 64), np.float32)
    row = np.zeros((n, 320), np.float32)
    bm = np.zeros((n, 4, 128, 1024), np.float32)
    cm = np.zeros((n, 4, 128, 8, 128), np.float32)
    lam = np.zeros((n, 4, 3, 512), np.float32)
    for i, l in enumerate(Ls):
        vec[i, :, 0:8] = f(inp["norm_mix"][l]).reshape(8, 128).T
        vec[i, :, 8:16] = f(inp["norm_mem_q"][l]).reshape(8, 128).T
        vec[i, :, 16:24] = f(inp["norm_mlp"][l]).reshape(8, 128).T
        vec[i, :, 24:32] = f(inp["norm_mem_kv"][l]).reshape(8, 128).T
        vec[i, :, 32:34] = f(inp["mla_q_norm"][l]).reshape(2, 128).T
        vec[i, :, 34] = f(inp["mla_kv_norm"][l])
        vec[i, :, 35:39] = f(inp["ssm_d"][l]).reshape(4, 128).T
        vec[i, :, 39:43] = f(inp["ssm_b_glu"][l]).reshape(4, 128).T
        vec[i, :, 43:47] = f(inp["out_norm_ssm"][l]).reshape(4, 128).T
        vec[i, :, 47:51] = f(inp["out_norm_mla"][l]).reshape(4, 128).T
        row[i, 0:96] = f(inp["mla_q_gain"][l])
        row[i, 96:192] = f(inp["mla_k_gain"][l])
        row[i, 192:256] = f(inp["mem_q_gain"][l])
        row[i, 256:320] = f(inp["mem_k_gain"][l])
        bre = f(inp["ssm_b_re"][l])
        bim = f(inp["ssm_b_im"][l])
        cre = f(inp["ssm_c_re"][l])
        cim = f(inp["ssm_c_im"][l])
        lre = f(inp["ssm_lambda_re"][l])
        lim = f(inp["ssm_lambda_im"][l])
        lst = f(inp["ssm_log_step"][l])
        for c in range(4):
            for j in range(8):
                g = 8 * c + j
                bm[i, c, 16 * j:16 * j + 16, j * 64:(j + 1) * 64] = bre[g].T
                bm[i, c, 16 * j:16 * j + 16, 512 + j * 64:512 + (j + 1) * 64] = bim[g].T
                pr, e = j // 2, j % 2
                cm[i, c, e * 64:(e + 1) * 64, pr, 16 * j:16 * j + 16] = cre[g].T
                cm[i, c, e * 64:(e + 1) * 64, 4 + pr, 16 * j:16 * j + 16] = cim[g].T
                lam[i, c, 0, j * 64:(j + 1) * 64] = lre[g]
                lam[i, c, 1, j * 64:(j + 1) * 64] = lim[g]
                lam[i, c, 2, j * 64:(j + 1) * 64] = lst[g]
    W["vec"] = vec
    W["row"] = row
    W["bm"] = bm
    W["cm"] = cm
    W["lam"] = lam
    st = lambda name, k: np.stack([chunkT(f(inp[name][l]), k) for l in Ls])
    W["w_in"] = st("w_in", 8)
    W["w_uq"] = st("mla_w_uq", 2)
    W["w_ukv"] = np.stack([f(inp["mla_w_ukv"][l]) for l in Ls])
    W["w_glu"] = st("ssm_w_glu", 4)
    W["w_out"] = st("w_out", 8)
    W["mwq"] = st("mem_w_q", 8)
    W["mwkv"] = st("mem_w_kv", 8)
    W["mwo"] = np.stack([np.ascontiguousarray(f(inp["mem_w_o"][l]).reshape(4, 64, 1024).swapaxes(0, 1)) for l in Ls])
    W["w1"] = st("mlp_w1", 8)
    W["w2"] = st("mlp_w2", 32)
    return W


TWO_PI = 2.0 * PI
GA = 1.5957691216057308
GB = 0.044715
QSCALE = 1.0 / float(np.sqrt(96.0))
MSCALE = 1.0 / 8.0
MSHIFT = 8.0
NEG_BIG = -30000.0


def build_program(n_layers=1, debug=()):
    nc = bass.Bass("TRN2", target_bir_lowering=False)
    es = ExitStack()
    b = Builder(nc, es, n_layers, debug)
    P = b.p
    L = n_layers

    xT_d = b.din("xT", [2, 128, 8, T]).ap()
    memT_d = b.din("memT", [128, 8, 256]).ap()
    pos_d = b.din("pos", [2, 128, 16]).ap()
    hinit_d = nc.dram_tensor("sc_h", [L, 4, 1024], BF16, kind="Internal").ap()
    pkt_d = nc.dram_tensor("sc_kt", [L, 4, 96, 2, T], BF16, kind="Internal").ap()
    pve_d = nc.dram_tensor("sc_ve", [L, 4, 128, 16, 65], BF16, kind="Internal").ap()
    pvo_d = nc.dram_tensor("sc_vo", [L, 4, 128, 16, 128], BF16, kind="Internal").ap()
    cst_d = b.din("cst", [128, 1664]).ap()
    vec_d = b.din("vec", [L, 128, 64]).ap()
    row_d = b.din("row", [L, 192 + 64 + 64]).ap()
    w_in_d = b.din("w_in", [L, 128, 8, IN_COLS]).ap()
    w_uq_d = b.din("w_uq", [L, 128, 2, 768]).ap()
    w_ukv_d = b.din("w_ukv", [L, 128, 1024]).ap()
    w_glu_d = b.din("w_glu", [L, 128, 4, 512]).ap()
    w_out_d = b.din("w_out", [L, 128, 8, 1024]).ap()
    mwq_d = b.din("mwq", [L, 128, 8, 256]).ap()
    mwkv_d = b.din("mwkv", [L, 128, 8, 512]).ap()
    mwo_d = b.din("mwo", [L, 64, 4, 1024]).ap()
    w1_d = b.din("w1", [L, 128, 8, 4096]).ap()
    w2_d = b.din("w2", [L, 128, 32, 1024]).ap()
    bm_d = b.din("bm", [L, 4, 128, 1024]).ap()
    cm_d = b.din("cm", [L, 4, 128, 8, 128]).ap()
    lam_d = b.din("lam", [L, 4, 3, 512]).ap()
    outT_d = b.dout("outT", [2, 128, 8, T]).ap()
    okt_d, ove_d, ovo_d, hfin_d = pkt_d, pve_d, pvo_d, hinit_d
    dbg_d = {}

    xT = b.sb("xT_s", [128, 8, T])
    hT = b.sb("hT_s", [128, 8, 1024], BF16)
    wb = [b.sb("wb%d" % i, [128, 8192], BF16) for i in range(2)]
    cst = b.sb("cst_s", [128, 160])
    cbf = b.sb("cbf_s", [128, 1536], BF16)
    vec = b.sb("vec_s", [128, L, 64])
    rowb = b.sb("rowb_s", [128, 320])
    sq = b.sb("sq_s", [128, 8, 512], BF16)
    rstd = b.sb("rstd_s", [128, 512])
    posf = b.sb("pos_s", [128, 16])
    cvec = b.sb("cvec_s", [128, 8])
    AR = 37888
    arena = b.sb("arena", [128, AR], BF16)
    arena_f = arena.bitcast(F32)

    psp = [b.ps("psp%d" % i, [128, 1024]) for i in range(4)]
    psb = [psp[i // 2][:, (i % 2) * 512:(i % 2 + 1) * 512] for i in range(8)]
    psb_bf = [a_.bitcast(BF16) for a_ in psb]

    ident = cbf[:, 0:128]
    ones = cbf[:, 128:256]
    tri = cbf[:, 256:384]
    e127 = cbf[:, 384:512]
    e127n = cbf[:, 512:640]
    cmask = cbf[:, 640:1536]
    ones_f = cst[:, 0:128]
    invf = cst[:, 128:144]
    sp1 = cst[:, 144:145]
    nsp1 = cst[:, 145:146]

    def A16(off_kib, shape):
        n = int(np.prod(shape[1:]))
        o = int(round(off_kib * 512))
        assert o + n <= AR, (off_kib, shape)
        v = arena[:, o:o + n]
        if len(shape) == 3:
            v = v.rearrange("p (a b) -> p a b", a=shape[1])
        elif len(shape) == 4:
            v = v.rearrange("p (a b c) -> p a b c", a=shape[1], b=shape[2])
        return v

    def A32(off_kib, shape):
        n = int(np.prod(shape[1:]))
        o = int(round(off_kib * 256))
        assert (o + n) * 2 <= AR, (off_kib, shape)
        v = arena_f[:, o:o + n]
        if len(shape) == 3:
            v = v.rearrange("p (a b) -> p a b", a=shape[1])
        elif len(shape) == 4:
            v = v.rearrange("p (a b c) -> p a b c", a=shape[1], b=shape[2])
        return v

    uT = A16(0, [128, 4, T])
    cqn = A16(16, [128, 2, T])
    ckvn = A16(24, [128, T])
    krope = A32(28, [128, 16, 32])
    cossin = A32(30, [128, 2, 16, 16])
    ymla = A16(32, [128, 4, T])
    WK = 48

    b.dma(cst[:, 0:128], cst_d[:, 128:256], writes=["cst"])
    b.dma(cst[:, 128:160], cst_d[:, 1536:1568], writes=["cst"])
    b.dma(cbf[:], cst_d[:, 0:1536], writes=["cbf"], eng="pool")
    b.dma(vec[:], vec_d.rearrange("l p k -> p l k"), writes=["vec"])
    def load_x(half):
        b.dma(posf[:], pos_d[half], writes=["pos"])
        for c in range(8):
            b.dma(xT[:, c, :], xT_d[half, :, c, :], writes=[("xT", c, t4) for t4 in range(4)])
    b.v("pool", "memset", cvec[:, 0:1], -SHIFT, writes=["cvec"])
    b.v("pool", "memset", cvec[:, 1:2], -MSHIFT, writes=["cvec"])
    b.v("pool", "memset", cvec[:, 2:3], -PI, writes=["cvec"])
    b.v("pool", "memset", cvec[:, 3:4], EPS, writes=["cvec"])
    nshift = cvec[:, 0:1]
    nmshift = cvec[:, 1:2]
    npi = cvec[:, 2:3]
    epsv = cvec[:, 3:4]

    cos_t = cossin[:, 0]
    sin_t = cossin[:, 1]

    def wrap_pi(dst, src, ti, tf, tm, rk, wk):
        b.v("dve", "tensor_scalar_mul", ti, src, 1.0 / TWO_PI, reads=rk, writes=[wk + "_ti"])
        b.v("dve", "tensor_copy", tf, ti, reads=[wk + "_ti"], writes=[wk + "_tf"])
        b.v("dve", "scalar_tensor_tensor", dst, tf, -TWO_PI, src, ALU.mult, ALU.add, reads=[wk + "_tf"] + rk, writes=[wk])
        b.v("dve", "tensor_scalar", tm, dst, PI, -TWO_PI, ALU.is_gt, ALU.mult, reads=[wk], writes=[wk + "_tm"])
        b.v("dve", "tensor_tensor", dst, dst, tm, ALU.add, reads=[wk, wk + "_tm"], writes=[wk])
        b.v("dve", "tensor_scalar", tm, dst, -PI, TWO_PI, ALU.is_lt, ALU.mult, reads=[wk], writes=[wk + "_tm"])
        b.v("dve", "tensor_tensor", dst, dst, tm, ALU.add, reads=[wk, wk + "_tm"], writes=[wk])

    def rope_tables():
        ang = A32(WK, [128, 16, 16])
        ang2 = A32(WK + 1, [128, 16, 16])
        ti = A32(WK + 2, [128, 16, 16]).bitcast(I32)
        tf = A32(WK + 3, [128, 16, 16])
        tm = A32(WK + 4, [128, 16, 16])
        wr = A32(WK + 5, [128, 16, 16])
        for t in range(16):
            b.v("dve", "tensor_scalar_mul", ang[:, t, :], invf, posf[:, t:t + 1],
                reads=["cst", "pos"], writes=["ropeang"])
        wrap_pi(wr, ang, ti, tf, tm, ["ropeang"], "ropew")
        b.act(sin_t, wr, AF.Sin, reads=["ropew"], writes=["sin"])
        b.v("dve", "tensor_scalar_add", ang2, ang, PI / 2, reads=["ropeang"], writes=["ropeang2"])
        wrap_pi(wr, ang2, ti, tf, tm, ["ropeang2", "sin"], "ropew")
        b.act(cos_t, wr, AF.Sin, reads=["ropew"], writes=["cos"])

    state = {"ps": 0, "wb": 0}
    SQK = [("sq", i) for i in range(8)]

    def next_ps():
        i = state["ps"]
        state["ps"] = (i + 1) % 4
        return i

    def load_w(view, src, key, eng="pool"):
        return b.dma(view, src, writes=[key], key=("w", key), eng=eng)

    def rms_scale(src_keys, nchunk, n, src_fn, ps_i, inv_dim):
        for c in range(nchunk):
            b.mm(psb[ps_i][:, 0:n], ones, src_fn(c), c == 0, c == nchunk - 1,
                 reads=["cbf"] + src_keys, writes=[("ps", ps_i)])
        b.act(rstd[:, 0:n], psb[ps_i][:, 0:n], AF.Sqrt, reads=[("ps", ps_i), "cvec"], writes=["rstd"],
              bias=epsv, scale=inv_dim)
        b.v("dve", "reciprocal", rstd[:, 0:n], rstd[:, 0:n], reads=["rstd"], writes=["rstd"])

    def norm_x(l, gain_col, tok0, nt, hkeys, t0=0):
        for ti in range(nt):
            g4 = (tok0 // 512) + ti
            t = t0 + ti
            xs = xT[:, :, g4 * 512:(g4 + 1) * 512]
            xk = [("xT", c, g4) for c in range(8)]
            b.act(sq[:], xs, AF.Square, reads=xk, writes=SQK)
            pi = next_ps()
            rms_scale(SQK, 8, 512, lambda c: sq[:, c, :], pi, 1.0 / D)
            for c in range(8):
                b.v("dve", "scalar_tensor_tensor", hT[:, c, t * 512:(t + 1) * 512], xT[:, c, g4 * 512:(g4 + 1) * 512],
                    vec[:, l, gain_col + c:gain_col + c + 1], rstd[:], ALU.mult, ALU.mult,
                    reads=[("xT", c, g4), "vec", "rstd"], writes=[("hT", t)])

    def dbg(name, view, shape, reads, dt=F32):
        if name in b.debug:
            d = b.dout("dbg_" + name, shape, dt).ap()
            b.dma(d, view, reads=reads, writes=[("dbgdone", "dbg_" + name)], key="dbg_" + name)
            b.out_keys.append("dbg_" + name)

    def layer(l):
        V0 = lambda c: vec[:, l, c:c + 1]
        b.dma(rowb[:], row_d[l].partition_broadcast(128), writes=["rowb"])
        gq = rowb[:, 0:96]
        gk = rowb[:, 96:192]
        mgq = rowb[:, 192:256]
        mgk = rowb[:, 256:320]
        b.v("dve", "tensor_scalar_mul", gq, gq, QSCALE, reads=["rowb"], writes=["rowb"])
        b.v("dve", "tensor_scalar_mul", mgq, mgq, MSCALE, reads=["rowb"], writes=["rowb"])

        w_in = wb[0][:, 0:8 * IN_COLS].rearrange("p (k n) -> p k n", k=8)
        load_w(w_in, w_in_d[l], ("wb", 0))
        cqraw = A32(WK, [128, 3, 512])
        norm_x(l, 0, 0, 2, None)
        for hf in range(2):
            for t in range(2):
                if hf == 1 or t == 1:
                    pass
                g4 = hf * 2 + t
                tok = slice(g4 * 512, (g4 + 1) * 512)
                hk = [("hT", t)]
                for oc in range(7):
                    pi = next_ps()
                    for k in range(8):
                        b.mm(psb[pi][:], w_in[:, k, oc * 128:(oc + 1) * 128], hT[:, k, t * 512:(t + 1) * 512],
                             k == 0, k == 7, reads=[("wb", 0)] + hk, writes=[("ps", pi)])
                    if oc < 4:
                        b.act(uT[:, oc, tok], psb[pi][:], AF.Copy, reads=[("ps", pi)], writes=[("uT", oc, g4)])
                    else:
                        b.act(cqraw[:, oc - 4, :], psb[pi][:], AF.Copy, reads=[("ps", pi)], writes=[("cqraw", oc - 4)])
                for s4 in range(4):
                    t16 = g4 * 4 + s4
                    pi = next_ps()
                    for k in range(8):
                        b.mm(psb[pi][:, 0:32], hT[:, k, t * 512 + s4 * 128:t * 512 + (s4 + 1) * 128], w_in[:, k, 896:928],
                             k == 0, k == 7, reads=[("wb", 0)] + hk, writes=[("ps", pi)])
                    b.act(krope[:, t16, :], psb[pi][:, 0:32], AF.Copy, reads=[("ps", pi)], writes=[("krope", t16)])
                b.act(sq[:, 0:2, :], cqraw[:, 0:2, :], AF.Square, reads=[("cqraw", 0), ("cqraw", 1)], writes=SQK)
                pi = next_ps()
                rms_scale(SQK, 2, 512, lambda c: sq[:, c, :], pi, 1.0 / 256)
                for c in range(2):
                    b.v("dve", "scalar_tensor_tensor", cqn[:, c, tok], cqraw[:, c, :], V0(32 + c), rstd[:],
                        ALU.mult, ALU.mult, reads=[("cqraw", c), "vec", "rstd"], writes=[("cqn", g4)])
                b.act(sq[:, 2, :], cqraw[:, 2, :], AF.Square, reads=[("cqraw", 2)], writes=SQK)
                pi = next_ps()
                rms_scale(SQK, 1, 512, lambda c: sq[:, 2, :], pi, 1.0 / 128)
                b.v("dve", "scalar_tensor_tensor", ckvn[:, tok], cqraw[:, 2, :], V0(34), rstd[:],
                    ALU.mult, ALU.mult, reads=[("cqraw", 2), "vec", "rstd"], writes=[("ckvn", g4)])
                if hf == 0:
                    norm_x(l, 0, 1024 + t * 512, 1, None, t0=t)
        dbg("uT", uT, [128, 4, T], [("uT", c, g) for c in range(4) for g in range(4)], BF16)
        dbg("cqn", cqn, [128, 2, T], [("cqn", g) for g in range(4)], BF16)
        dbg("ckvn", ckvn, [128, T], [("ckvn", g) for g in range(4)], BF16)
        dbg("krope", krope, [128, 16, 32], [("krope", g) for g in range(16)])
        dbg("cos", cos_t, [128, 16, 16], ["cos"])
        dbg("sin", sin_t, [128, 16, 16], ["sin"])


    hT_flat = hT[:].rearrange("p a b -> p (a b)")
    pkt_s = hT_flat[:, 0:4096].rearrange("p (e t) -> p e t", e=2)
    pve_s = hT_flat[:, 4096:4096 + 1040].rearrange("p (t c) -> p t c", t=16)
    pvo_s = hT_flat[:, 5248:5248 + 2048].rearrange("p (t c) -> p t c", t=16)
    HK = [("hT", 0), ("hT", 1)]
    QT = A16(48, [128, 2, T])
    KT = A16(56, [128, 2, T])
    Ve = A16(64, [128, 16, 65])
    Vo = A16(66.5, [128, 16, 128])
    sq_f = sq[:].rearrange("p a b -> p (a b)").bitcast(F32)
    sqq = sq_f[:, 0:768].rearrange("p (g e d) -> p g e d", g=4, e=2)
    qn = sq_f[:, 768:1536].rearrange("p (g e d) -> p g e d", g=4, e=2)
    SQ6 = [("sq", i) for i in range(6)]
    kraw = wb[1].bitcast(F32)[:, 2048:2816].rearrange("p (g e d) -> p g e d", g=4, e=2)
    KRK = ("wb", 1, "glu")
    qf = A16(70.5, [128, 4, 192]).rearrange("p g (e d) -> p g e d", e=2)
    rta = A32(72, [128, 4, 32]).rearrange("p g (e d) -> p g e d", e=2)
    rtb = A32(72.5, [128, 4, 32]).rearrange("p g (e d) -> p g e d", e=2)
    ssq = A32(73, [128, 4, 2])
    wb1f = wb[1].bitcast(F32)
    rc = wb1f[:, 3584:4096]
    PT = [sq[:, i, :] for i in range(8)]

    rstd_bf = rstd[:].bitcast(BF16)
    sqq_k = wb[1].bitcast(F32)[:, 2816:3584].rearrange("p (g e d) -> p g e d", g=4, e=2)
    qf_k = rstd_bf[:, 0:768].rearrange("p (g e d) -> p g e d", g=4, e=2)
    ssq_k = rstd[:, 384:392].rearrange("p (g e) -> p g e", g=4)

    def qk_path(which, src, gain, dstT, ps_bank, g4, srck, tag):
        if which == "q":
            sq_, qn_, qf_, ssq_ = sqq, qn, qf, ssq
            ksq, kqn, kqf, kss = [("sq", i) for i in range(3)], [("sq", i) for i in range(3, 6)], "qf", "ssq"
        else:
            sq_, qn_, qf_, ssq_ = sqq_k, src, qf_k, ssq_k
            ksq, kqn, kqf, kss = [KRK], srck, "rstd", "rstd"
        cs = cos_t[:, g4 * 4:(g4 + 1) * 4, :].unsqueeze(2).to_broadcast([128, 4, 2, 16])
        sn = sin_t[:, g4 * 4:(g4 + 1) * 4, :].unsqueeze(2).to_broadcast([128, 4, 2, 16])
        g_b = gain.unsqueeze(1).unsqueeze(1).to_broadcast([128, 4, 2, 96])
        st = []
        st.append(lambda: b.act(sq_, src, AF.Square, reads=srck, writes=ksq))
        st.append(lambda: b.v("dve", "tensor_reduce", ssq_, sq_, AX.X, ALU.add, reads=ksq, writes=[kss]))
        st.append(lambda: b.act(ssq_, ssq_, AF.Sqrt, reads=[kss, "cvec"], writes=[kss], bias=epsv, scale=1.0 / 96))
        st.append(lambda: b.v("dve", "reciprocal", ssq_, ssq_, reads=[kss], writes=[kss]))
        st.append(lambda: b.v("dve", "tensor_tensor", qn_, src, ssq_.unsqueeze(3).to_broadcast([128, 4, 2, 96]), ALU.mult,
                              reads=srck + [kss], writes=kqn))
        st.append(lambda: b.v("dve", "tensor_tensor", qn_, qn_, g_b, ALU.mult, reads=kqn + ["rowb"], writes=kqn))

        def rope():
            b.v("pool", "tensor_copy", qf_[:, :, :, 0:64], qn_[:, :, :, 0:64], reads=kqn, writes=[kqf])
            b.v("pool", "tensor_tensor", rta, qn_[:, :, :, 64:80], cs, ALU.mult, reads=kqn + ["cos"], writes=["rta"])
            b.v("pool", "tensor_tensor", rtb, qn_[:, :, :, 80:96], sn, ALU.mult, reads=kqn + ["sin"], writes=["rtb"])
            b.v("pool", "tensor_tensor", qf_[:, :, :, 64:80], rta, rtb, ALU.subtract, reads=["rta", "rtb"], writes=[kqf])
            b.v("pool", "tensor_tensor", rta, qn_[:, :, :, 80:96], cs, ALU.mult, reads=kqn + ["cos", kqf], writes=["rta"])
            b.v("pool", "tensor_tensor", rtb, qn_[:, :, :, 64:80], sn, ALU.mult, reads=kqn + ["sin", kqf], writes=["rtb"])
            b.v("pool", "tensor_tensor", qf_[:, :, :, 80:96], rta, rtb, ALU.add, reads=["rta", "rtb"], writes=[kqf])
        st.append(rope)

        def trans():
            for i in range(4):
                for e in range(2):
                    slot = e * 4 + i
                    b.tr(psb_bf[ps_bank][0:96, slot * 128:(slot + 1) * 128], qf_[:, i, e, :], ident,
                         reads=[kqf, "cbf"], writes=[("ps", ps_bank)])
        st.append(trans)
        st.append(lambda: b.act(dstT[0:96, :, g4 * 512:(g4 + 1) * 512],
                                psb_bf[ps_bank][0:96, :].rearrange("p (e n) -> p e n", e=2), AF.Copy,
                                reads=[("ps", ps_bank)], writes=[(tag, g4)]))
        return st

    def mla(l, half=1):
        wuq = wb[1][:, 0:1536].rearrange("p (k n) -> p k n", k=2)
        wukv = wb[1][:, 1536:2560]
        gq = rowb[:, 0:96]
        gk = rowb[:, 96:192]
        b.v("pool", "memset", Ve[:, :, 64:65], 1.0, writes=["Ve"])
        b.v("pool", "memset", Vo[:, :, 0:64], 0.0, writes=["Vo"])
        b.v("pool", "memset", Vo[:, :, 0:1], 1.0, writes=["Vo"])
        for hp in range(4):
            if half == 1:
                b.dma(pkt_s[0:96], pkt_d[l, hp], reads=[("skv", l, hp)], writes=HK, key="pk")
                b.dma(pve_s, pve_d[l, hp], reads=[("skv", l, hp)], writes=HK, key="pk")
                b.dma(pvo_s, pvo_d[l, hp], reads=[("skv", l, hp)], writes=HK, key="pk")
            qps4 = psp[0][:, :].rearrange("p (g n) -> p g n", g=4)
            kvps4 = psp[1][:, :].rearrange("p (g n) -> p g n", g=4)
            PQ = [("ps", 0), ("ps", 1)]
            PKV = [("ps", 2), ("ps", 3)]
            for g4 in range(4):
                for i in range(4):
                    t16 = g4 * 4 + i
                    tok = slice(t16 * 128, (t16 + 1) * 128)
                    for k in range(2):
                        b.mm(qps4[:, i, 0:192], cqn[:, k, tok], wuq[:, k, hp * 192:(hp + 1) * 192], k == 0, k == 1,
                             reads=[("cqn", g4), ("wb", 1)], writes=[("ps", i // 2)])
                    b.mm(kvps4[:, i, :], ckvn[:, tok], wukv[:, hp * 256:(hp + 1) * 256], True, True,
                         reads=[("ckvn", g4), ("wb", 1)], writes=[("ps", 2 + i // 2)])
                kv4 = kvps4.rearrange("p g (e d) -> p g e d", e=2)
                b.act(kraw[:, :, :, 0:64], kv4[:, :, :, 0:64], AF.Copy, reads=PKV, writes=[KRK])
                b.v("pool", "tensor_copy", kraw[:, :, :, 64:96],
                    krope[:, g4 * 4:(g4 + 1) * 4, :].unsqueeze(2).to_broadcast([128, 4, 2, 32]),
                    reads=[("krope", g4 * 4 + i) for i in range(4)], writes=[KRK])
                sq_st = qk_path("q", qps4[:, :, 0:192].rearrange("p g (e d) -> p g e d", e=2), gq, QT, 6, g4, PQ, "QT")
                sk_st = qk_path("k", kraw, gk, KT, 7, g4, [KRK], "KT")
                sq_st[0]()
                b.act(Ve[:, g4 * 4:(g4 + 1) * 4, 0:64], kvps4[:, :, 64:128], AF.Copy, reads=PKV, writes=["Ve"])
                b.act(Vo[:, g4 * 4:(g4 + 1) * 4, 64:128], kvps4[:, :, 192:256], AF.Copy, reads=PKV, writes=["Vo"])
                sk_st[0]()
                for a_, b_ in zip(sq_st[1:], sk_st[1:]):
                    a_()
                    b_()
            QK = [("QT", g) for g in range(4)]
            KK = [("KT", g) for g in range(4)]
            if half == 0:
                b.dma(okt_d[l, hp], KT[0:96], reads=KK, writes=[("skv", l, hp)], key="okv")
                b.dma(ove_d[l, hp], Ve, reads=["Ve"], writes=[("skv", l, hp)], key="okv")
                b.dma(ovo_d[l, hp], Vo, reads=["Vo"], writes=[("skv", l, hp)], key="okv")
            for e in range(2):
                Vt = Ve if e == 0 else Vo
                vk = "Ve" if e == 0 else "Vo"
                pV = pve_s if e == 0 else pvo_s
                M = 65 if e == 0 else 128
                for qt in range(4):
                    blocks = ([("p", kb) for kb in range(16)] if half == 1 else []) + [("o", kb) for kb in range(4 * qt + 4)]
                    po = 2 + (qt % 2)
                    SB = [0, 1, 5]

                    def emit_qk(bi):
                        kind, kb = blocks[bi]
                        pi = SB[bi % 3]
                        kts = (pkt_s if kind == "p" else KT)[0:96, e, kb * 128:(kb + 1) * 128]
                        kk = HK if kind == "p" else [("KT", kb // 4)]
                        b.mm(psb[pi][:], kts, QT[0:96, e, qt * 512:(qt + 1) * 512], True, True,
                             reads=kk + [("QT", qt)], writes=[("ps", pi)])

                    emit_qk(0)
                    if len(blocks) > 1:
                        emit_qk(1)
                    for bi, (kind, kb) in enumerate(blocks):
                        pi = SB[bi % 3]
                        pt = PT[bi % 4]
                        ptk = ("sq", bi % 4)
                        b.act(pt, psb[pi][:], AF.Exp, reads=[("ps", pi), "cvec"], writes=[ptk],
                              bias=nshift, scale=1.0)
                        if kind == "o" and kb >= 4 * qt:
                            o = kb - 4 * qt
                            b.v("dve", "tensor_tensor", pt, pt, cmask[:, (3 - o) * 128:(3 - o) * 128 + 512], ALU.mult,
                                reads=[ptk, "cbf"], writes=[ptk])
                        if bi + 2 < len(blocks):
                            emit_qk(bi + 2)
                        vs = (pV if kind == "p" else Vt)[:, kb, 0:M]
                        b.mm(psb[po][0:M, :], vs, pt, bi == 0, bi == len(blocks) - 1,
                             reads=(HK if kind == "p" else [vk]) + [ptk], writes=[("ps", po)])
                    dr = 64 if e == 0 else 0
                    b.v("dve", "reciprocal", rc[dr:dr + 1, :], psb[po][dr:dr + 1, :], reads=[("ps", po)], writes=["rc"])
                    if e == 0:
                        b.mm(psb[4][0:64, :], ones_f[64:65, 0:64], rc[64:65, :], True, True, reads=["cst", "rc"], writes=[("ps", 4)])
                        rows = slice(0, 64)
                    else:
                        b.mm(psb[4][:, :], ones_f[0:1, :], rc[0:1, :], True, True, reads=["cst", "rc"], writes=[("ps", 4)])
                        rows = slice(64, 128)
                    b.act(rstd[rows, :], psb[4][rows, :], AF.Copy, reads=[("ps", 4)], writes=["rstd"])
                    b.v("dve", "tensor_tensor", ymla[rows, hp, qt * 512:(qt + 1) * 512], psb[po][rows, :], rstd[rows, :], ALU.mult,
                        reads=[("ps", po), "rstd"], writes=[("ymla", hp, qt)])
        dbg("ymla", ymla, [128, 4, T], [("ymla", h, q) for h in range(4) for q in range(4)], BF16)
        dbg("QT", QT, [128, 2, T], [("QT", g) for g in range(4)], BF16)
        dbg("KT", KT, [128, 2, T], [("KT", g) for g in range(4)], BF16)
        dbg("Vo", Vo, [128, 16, 128], ["Vo"], BF16)


    dummy = b.sb("dummy_s", [128, 2])

    def barrier(keys):
        b.v("pool", "memset", dummy[:, 0:1], 0.0, writes=list(keys))

    MIXER_KEYS = ([("uT", c, g) for c in range(4) for g in range(4)] + [("cqn", g) for g in range(4)]
                  + [("ckvn", g) for g in range(4)] + [("krope", g) for g in range(16)] + ["cos", "sin"]
                  + [("ymla", h, q) for h in range(4) for q in range(4)])
    ATT_KEYS = ([("QT", g) for g in range(4)] + [("KT", g) for g in range(4)]
                + ["Ve", "Vo", "sqq", "qn", "qf", "rta", "rtb", "ssq", "kraw"])
    PROJ_KEYS = [("cqraw", i) for i in range(3)] + ["ropeang", "ropeang2", "ropew", "ropew_ti", "ropew_tf", "ropew_tm"]

    hT_f = hT[:].rearrange("p a b -> p (a b)").bitcast(F32)
    SLOTS = [{n: A32(48 + 2 * i, [128, 512]) for i, n in enumerate("ABCDEFGHIJKMN")},
             dict([(n, A32(32 + 2 * i, [128, 512])) for i, n in enumerate("ABCDEFGH")]
                  + [(n, hT_f[:, i * 512:(i + 1) * 512]) for i, n in enumerate("IJKMN")])]
    SSM_KEYS = ["s%d%s" % (ch, n) for ch in range(2) for n in "ABCDEFGHIJKMN"] + \
               ["Xr0", "Xi0", "Xr1", "Xi1"]
    SSM_BANKS = [(0, 1, 2, 3), (4, 5, 6, 7)]

    def ssm_prep(l, c, ch, half):
        slot = SLOTS[ch]
        sk = lambda n: "s%d%s" % (ch, n)
        def tt(eng, out, a, bb_, op):
            b.v(eng, "tensor_tensor", slot[out], slot[a], slot[bb_], op, reads=[sk(a), sk(bb_)], writes=[sk(out)])
        def wrap(dst, src, ti, tf, tm):
            d_, s_, ti_, tf_, tm_ = slot[dst], slot[src], slot[ti].bitcast(I32), slot[tf], slot[tm]
            kd, ks, kti, ktf, ktm = sk(dst), sk(src), sk(ti), sk(tf), sk(tm)
            b.v("dve", "tensor_scalar_mul", ti_, s_, 1.0 / TWO_PI, reads=[ks], writes=[kti])
            b.v("dve", "tensor_copy", tf_, ti_, reads=[kti], writes=[ktf])
            b.v("dve", "scalar_tensor_tensor", d_, tf_, -TWO_PI, s_, ALU.mult, ALU.add, reads=[ktf, ks], writes=[kd])
            b.v("dve", "tensor_scalar", tm_, d_, PI, -TWO_PI, ALU.is_gt, ALU.mult, reads=[kd], writes=[ktm])
            b.v("dve", "tensor_tensor", d_, d_, tm_, ALU.add, reads=[kd, ktm], writes=[kd])
            b.v("dve", "tensor_scalar", tm_, d_, -PI, TWO_PI, ALU.is_lt, ALU.mult, reads=[kd], writes=[ktm])
            b.v("dve", "tensor_tensor", d_, d_, tm_, ALU.add, reads=[kd, ktm], writes=[kd])
        def actf(out, a, func, **kw):
            b.act(slot[out], slot[a], func, reads=[sk(a), "cst"], writes=[sk(out)], **kw)
        for i, n in enumerate("KMN"):
            b.dma(slot[n], lam_d[l, c, i].partition_broadcast(128), writes=[sk(n)])
        actf("N", "N", AF.Exp)
        tt("dve", "G", "K", "N", ALU.mult)
        tt("dve", "H", "M", "N", ALU.mult)
        actf("A", "G", AF.Exp, scale=sp1)
        actf("C", "G", AF.Exp, scale=nsp1)
        b.v("dve", "tensor_scalar_mul", slot["I"], slot["H"], sp1, reads=[sk("H"), "cst"], writes=[sk("I")])
        wrap("J", "I", "N", "E", "F")
        actf("B", "J", AF.Sin)
        b.v("dve", "tensor_scalar_add", slot["I"], slot["I"], PI / 2, reads=[sk("I")], writes=[sk("I")])
        wrap("J", "I", "N", "E", "F")
        actf("D", "J", AF.Sin)
        tt("dve", "E", "A", "D", ALU.mult)
        tt("dve", "F", "A", "B", ALU.mult)
        tt("dve", "A", "C", "D", ALU.mult)
        b.v("dve", "scalar_tensor_tensor", slot["B"], slot["C"], -1.0, slot["B"], ALU.mult, ALU.mult,
            reads=[sk("C"), sk("B")], writes=[sk("B")])
        wrap("J", "H", "N", "C", "D")
        actf("I", "J", AF.Sin)
        b.v("dve", "tensor_scalar_add", slot["H"], slot["H"], PI / 2, reads=[sk("H")], writes=[sk("H")])
        wrap("J", "H", "N", "C", "D")
        actf("N", "J", AF.Sin)
        actf("C", "G", AF.Exp)
        tt("dve", "D", "C", "N", ALU.mult)
        b.v("dve", "tensor_scalar_add", slot["D"], slot["D"], -1.0, reads=[sk("D")], writes=[sk("D")])
        tt("dve", "I", "C", "I", ALU.mult)
        tt("dve", "J", "K", "K", ALU.mult)
        tt("dve", "N", "M", "M", ALU.mult)
        tt("dve", "J", "J", "N", ALU.add)
        b.v("dve", "reciprocal", slot["J"], slot["J"], reads=[sk("J")], writes=[sk("J")])
        tt("dve", "C", "D", "K", ALU.mult)
        tt("dve", "N", "I", "M", ALU.mult)
        tt("dve", "C", "C", "N", ALU.add)
        tt("dve", "C", "C", "J", ALU.mult)
        tt("dve", "N", "I", "K", ALU.mult)
        tt("dve", "G", "D", "M", ALU.mult)
        tt("dve", "N", "N", "G", ALU.subtract)
        tt("dve", "N", "N", "J", ALU.mult)
        tt("dve", "D", "A", "C", ALU.mult)
        tt("dve", "G", "B", "N", ALU.mult)
        tt("dve", "D", "D", "G", ALU.subtract)
        tt("dve", "I", "A", "N", ALU.mult)
        tt("dve", "G", "B", "C", ALU.mult)
        tt("dve", "I", "I", "G", ALU.add)
        cx = dict(slot=slot, sk=sk, ch=ch, c=c,
                  Tre=slot["E"], Tim=slot["F"], Tpr=slot["D"], Tpi=slot["I"],
                  TK=[sk("E"), sk("F"), sk("D"), sk("I")],
                  Bm=slot["A"].bitcast(BF16),
                  Cm=slot["B"].bitcast(BF16).rearrange("p (k n) -> p k n", k=8),
                  X=slot["C"].bitcast(BF16),
                  Sb=[slot["G"].bitcast(BF16), slot["H"].bitcast(BF16)], SbK=[sk("G"), sk("H")],
                  ST=slot["J"].bitcast(BF16).rearrange("p (k n) -> p k n", k=8))
        load_w(cx["Bm"], bm_d[l, c], sk("A"))
        load_w(cx["Cm"], cm_d[l, c], sk("B"))
        b.v("pool", "memset", cx["Sb"][1], 0.0, writes=[cx["SbK"][1]])
        if half == 1:
            b.dma(cx["Sb"][1][127:128, :], hinit_d[l, c:c + 1, :], reads=[("sh", l, c)], writes=[cx["SbK"][1]])
        return cx

    def ssm_tile(l, cx, k, stage):
        slot, sk, ch, c = cx["slot"], cx["sk"], cx["ch"], cx["c"]
        Tre, Tim, Tpr, Tpi, TK = cx["Tre"], cx["Tim"], cx["Tpr"], cx["Tpi"], cx["TK"]
        Bm, Cm, X, Sb, SbK, ST = cx["Bm"], cx["Cm"], cx["X"], cx["Sb"], cx["SbK"], cx["ST"]
        b0, b1, tb, yb = SSM_BANKS[ch]
        Xr, Xi = "Xr%d" % ch, "Xi%d" % ch
        tok = slice(k * 128, (k + 1) * 128)
        g4 = k // 4
        cur, prv = k % 2, 1 - (k % 2)
        K_, M_, N_ = slot["K"], slot["M"], slot["N"]
        if stage == 2:
            return ssm_tile2(l, cx, k)
        if stage == 3:
            return ssm_tile3(l, cx, k)
        b.mm(psb[b0][:], uT[:, c, tok], Bm[:, 0:512], True, True, reads=[("uT", c, g4), sk("A")], writes=[("ps", b0)])
        b.mm(psb[b1][:], uT[:, c, tok], Bm[:, 512:1024], True, True, reads=[("uT", c, g4), sk("A")], writes=[("ps", b1)])
        b.v("dve", "tensor_tensor", K_, psb[b0][:], Tpr, ALU.mult, reads=[("ps", b0)] + TK, writes=[sk("K")])
        b.v("dve", "tensor_tensor", M_, psb[b1][:], Tpi, ALU.mult, reads=[("ps", b1)] + TK, writes=[sk("M")])
        b.v("pool", "tensor_tensor", X[:, 0:512], K_, M_, ALU.subtract, reads=[sk("K"), sk("M")], writes=[Xr])
        b.v("dve", "tensor_tensor", N_, psb[b1][:], Tpr, ALU.mult, reads=[("ps", b1)] + TK, writes=[sk("N")])
        b.v("dve", "tensor_tensor", K_, psb[b0][:], Tpi, ALU.mult, reads=[("ps", b0)] + TK + [Xr], writes=[sk("K")])
        b.v("pool", "tensor_tensor", X[:, 512:1024], N_, K_, ALU.add, reads=[sk("K"), sk("N")], writes=[Xi])

    def ssm_tile2(l, cx, k):
        slot, sk, ch, c = cx["slot"], cx["sk"], cx["ch"], cx["c"]
        Tre, Tim, Tpr, Tpi, TK = cx["Tre"], cx["Tim"], cx["Tpr"], cx["Tpi"], cx["TK"]
        Bm, Cm, X, Sb, SbK, ST = cx["Bm"], cx["Cm"], cx["X"], cx["Sb"], cx["SbK"], cx["ST"]
        b0, b1, tb, yb = SSM_BANKS[ch]
        Xr, Xi = "Xr%d" % ch, "Xi%d" % ch
        cur, prv = k % 2, 1 - (k % 2)
        K_, M_, N_ = slot["K"], slot["M"], slot["N"]
        b.mm(psb[b0][:], tri, X[:, 0:512], True, False, reads=["cbf", Xr], writes=[("ps", b0)])
        b.mm(psb[b0][:], e127, Sb[prv][:, 0:512], False, True, reads=["cbf", SbK[prv]], writes=[("ps", b0)])
        b.mm(psb[b1][:], tri, X[:, 512:1024], True, False, reads=["cbf", Xi], writes=[("ps", b1)])
        b.mm(psb[b1][:], e127n, Sb[prv][:, 512:1024], False, True, reads=["cbf", SbK[prv]], writes=[("ps", b1)])
        b.v("dve", "tensor_tensor", K_, psb[b0][:], Tre, ALU.mult, reads=[("ps", b0)] + TK + [Xi], writes=[sk("K")])
        b.v("dve", "tensor_tensor", M_, psb[b1][:], Tim, ALU.mult, reads=[("ps", b1)] + TK + [Xr], writes=[sk("M")])
        b.v("pool", "tensor_tensor", Sb[cur][:, 0:512], K_, M_, ALU.subtract, reads=[sk("K"), sk("M")], writes=[SbK[cur]])
        b.v("dve", "scalar_tensor_tensor", N_, psb[b1][:], -1.0, Tre, ALU.mult, ALU.mult,
            reads=[("ps", b1)] + TK + [Xi], writes=[sk("N")])
        b.v("dve", "scalar_tensor_tensor", K_, psb[b0][:], -1.0, Tim, ALU.mult, ALU.mult,
            reads=[("ps", b0)] + TK + [SbK[cur]], writes=[sk("K")])
        b.v("pool", "tensor_tensor", Sb[cur][:, 512:1024], N_, K_, ALU.add,
            reads=[sk("K"), sk("N")], writes=[SbK[cur]])

    def ssm_tile3(l, cx, k):
        slot, sk, ch, c = cx["slot"], cx["sk"], cx["ch"], cx["c"]
        Cm, Sb, SbK, ST = cx["Cm"], cx["Sb"], cx["SbK"], cx["ST"]
        b0, b1, tb, yb = SSM_BANKS[ch]
        g4 = k // 4
        cur = k % 2
        for blk in range(8):
            b.tr(psb_bf[tb][:, blk * 128:(blk + 1) * 128], Sb[cur][:, blk * 128:(blk + 1) * 128], ident,
                 reads=[SbK[cur], "cbf"], writes=[("ps", tb)])
        b.act(ST, psb_bf[tb][:, :].rearrange("p (k n) -> p k n", k=8), AF.Copy, reads=[("ps", tb)], writes=[sk("J")])
        for blk in range(8):
            b.mm(psb[yb][:, (k % 4) * 128:(k % 4 + 1) * 128], Cm[:, blk, :], ST[:, blk, :], blk == 0, blk == 7,
                 reads=[sk("B"), sk("J")], writes=[("ps", yb)])
        if k % 4 == 3:
            t4 = slice(g4 * 512, (g4 + 1) * 512)
            b.v("dve", "scalar_tensor_tensor", uT[:, c, t4], uT[:, c, t4], vec[:, l, 35 + c:36 + c], psb[yb][:],
                ALU.mult, ALU.add, reads=[("uT", c, g4), "vec", ("ps", yb)], writes=[("uT", c, g4)])

    def ssm(l, half=1):
        for pair in range(2):
            cxs = [ssm_prep(l, 2 * pair + ch, ch, half) for ch in range(2)]
            for k in range(16):
                for stage in (1, 2, 3):
                    for cx in cxs:
                        ssm_tile(l, cx, k, stage)
            if half == 0:
                for cx in cxs:
                    c = cx["c"]
                    b.dma(hfin_d[l, c:c + 1, :], cx["Sb"][1][127:128, :], reads=[cx["SbK"][1]], writes=[("sh", l, c)], key="hfin")
        dbg("yssm", uT, [128, 4, T], [("uT", c, g) for c in range(4) for g in range(4)], BF16)


    def group_norm(buf, keyf, gcol, l, g4):
        tok = slice(g4 * 512, (g4 + 1) * 512)
        ks = [keyf(c, g4) for c in range(4)]
        b.act(sq[:, 0:4, :], buf[:, :, tok], AF.Square, reads=ks, writes=SQK)
        pi = next_ps()
        rms_scale(SQK, 4, 512, lambda c: sq[:, c, :], pi, 1.0 / 512)
        for c in range(4):
            b.v("dve", "scalar_tensor_tensor", buf[:, c, tok], buf[:, c, tok], vec[:, l, gcol + c:gcol + c + 1], rstd[:],
                ALU.mult, ALU.mult, reads=[keyf(c, g4), "vec", "rstd"], writes=[keyf(c, g4)])

    def ssm_post(l):
        wglu = wb[1][:, 4096:6144].rearrange("p (k n) -> p k n", k=4)
        tK = A32(48, [128, 512])
        tM = A32(50, [128, 512])
        gates = A32(52, [128, 4, 512])
        for g4 in range(4):
            tok = slice(g4 * 512, (g4 + 1) * 512)
            for c in range(4):
                y = uT[:, c, tok]
                yk = ("uT", c, g4)
                b.v("dve", "tensor_tensor", tK, y, y, ALU.mult, reads=[yk], writes=["gK"])
                b.v("dve", "tensor_scalar", tK, tK, GB, 1.0, ALU.mult, ALU.add, reads=["gK"], writes=["gK"])
                b.v("dve", "tensor_tensor", tK, tK, y, ALU.mult, reads=["gK", yk], writes=["gK"])
                b.act(tM, tK, AF.Sigmoid, reads=["gK"], writes=["gM"], scale=GA)
                b.v("dve", "tensor_tensor", y, y, tM, ALU.mult, reads=["gM", yk], writes=[yk])
            for oc in range(4):
                pi = 4 + oc
                for k in range(4):
                    b.mm(psb[pi][:], wglu[:, k, oc * 128:(oc + 1) * 128], uT[:, k, tok], k == 0, k == 3,
                         reads=[("wb", 1, "glu"), ("uT", k, g4)], writes=[("ps", pi)])
                b.act(gates[:, oc, :], psb[pi][:], AF.Sigmoid, reads=[("ps", pi), "vec"], writes=[("gate", oc)],
                      bias=vec[:, l, 39 + oc:40 + oc], scale=1.0)
            for oc in range(4):
                b.v("dve", "tensor_tensor", uT[:, oc, tok], uT[:, oc, tok], gates[:, oc, :], ALU.mult,
                    reads=[("uT", oc, g4), ("gate", oc)], writes=[("uT", oc, g4)])
            group_norm(uT, lambda c, g: ("uT", c, g), 43, l, g4)
        dbg("yssmn", uT, [128, 4, T], [("uT", c, g) for c in range(4) for g in range(4)], BF16)

    def mix_out(l):
        wout = wb[0][:, :].rearrange("p (k n) -> p k n", k=8)
        group_norm(ymla, lambda c, g: ("ymla", c, g), 47, l, 0)
        for g4 in range(4):
            tok = slice(g4 * 512, (g4 + 1) * 512)
            if g4 + 1 < 4:
                group_norm(ymla, lambda c, g: ("ymla", c, g), 47, l, g4 + 1)
            for oc in range(8):
                pi = next_ps()
                for k in range(8):
                    src = uT[:, k, tok] if k < 4 else ymla[:, k - 4, tok]
                    sk_ = ("uT", k, g4) if k < 4 else ("ymla", k - 4, g4)
                    b.mm(psb[pi][:], wout[:, k, oc * 128:(oc + 1) * 128], src, k == 0, k == 7,
                         reads=[("wb", 0), sk_], writes=[("ps", pi)])
                b.v("dve", "tensor_tensor", xT[:, oc, tok], xT[:, oc, tok], psb[pi][:], ALU.add,
                    reads=[("xT", oc, g4), ("ps", pi)], writes=[("xT", oc, g4)])
        dbg("x1", xT[:], [128, 8, T], [("xT", c, g) for c in range(8) for g in range(4)])

    CR_KEYS = ["mhT", "KmT", "Vm", "memx", "mtmp", "mss", "mkn", "mtmp4", "mss4", "mkn4", ("QmT", 0), ("QmT", 1), ("QmT", 2), ("QmT", 3)] + \
              [("omT", h, q) for h in range(4) for q in range(4)]

    def cross(l):
        mhT = A16(0, [128, 8, 256])
        KmT = A16(4, [128, 4, 256])
        Vm = A16(6, [128, 2, 4, 65])
        QmT = A16(8, [128, 4, T])
        omT = A16(24, [128, 4, T])
        memx = A32(40, [128, 8, 256])
        mtmp = A32(48, [128, 4, 64])
        mss = A32(49, [128, 4])
        mkn = A16(50, [128, 4, 64])
        mtmp4 = A32(52, [128, 4, 256]).rearrange("p g (h d) -> p g h d", h=4)
        mss4 = A32(56, [128, 4, 4])
        mkn4 = A16(57, [128, 4, 256]).rearrange("p g (h d) -> p g h d", h=4)
        mgq = rowb[:, 192:256]
        mgk = rowb[:, 256:320]
        mwkv = wb[1][:, 0:4096].rearrange("p (k n) -> p k n", k=8)
        mwq = wb[1][:, 4096:6144].rearrange("p (k n) -> p k n", k=8)
        load_w(mwkv, mwkv_d[l], ("wb", 1))
        load_w(mwq, mwq_d[l], ("wb", 1, "glu"))
        b.dma(memx, memT_d, writes=["memx"])
        sqm = sq[:, :, 0:256]
        b.act(sqm, memx, AF.Square, reads=["memx"], writes=SQK)
        pi = next_ps()
        rms_scale(SQK, 8, 256, lambda c: sq[:, c, 0:256], pi, 1.0 / D)
        for c in range(8):
            b.v("dve", "scalar_tensor_tensor", mhT[:, c, :], memx[:, c, :], vec[:, l, 24 + c:25 + c], rstd[:, 0:256],
                ALU.mult, ALU.mult, reads=["memx", "vec", "rstd"], writes=["mhT"])
        b.v("pool", "memset", Vm[:, :, :, 64:65], 1.0, writes=["Vm"])

        def head_norm(src, srck, gain, n_slots_bank, slot_fn):
            b.act(mtmp, src, AF.Square, reads=srck, writes=["mtmp"])
            b.v("dve", "tensor_reduce", mss, mtmp, AX.X, ALU.add, reads=["mtmp"], writes=["mss"])
            b.act(mss, mss, AF.Sqrt, reads=["mss", "cvec"], writes=["mss"], bias=epsv, scale=1.0 / 64)
            b.v("dve", "reciprocal", mss, mss, reads=["mss"], writes=["mss"])
            b.v("dve", "tensor_tensor", mtmp, src, mss.unsqueeze(2).to_broadcast([128, 4, 64]), ALU.mult,
                reads=srck + ["mss"], writes=["mtmp"])
            b.v("dve", "tensor_tensor", mkn, mtmp, gain.unsqueeze(1).to_broadcast([128, 4, 64]), ALU.mult,
                reads=["mtmp", "rowb"], writes=["mkn"])
            for h in range(4):
                sl = slot_fn(h)
                b.tr(psb_bf[n_slots_bank][0:64, sl * 128:(sl + 1) * 128], mkn[:, h, :], ident,
                     reads=["mkn", "cbf"], writes=[("ps", n_slots_bank)])

        for mt in range(2):
            pi = next_ps()
            for k in range(8):
                b.mm(psb[pi][:], mhT[:, k, mt * 128:(mt + 1) * 128], mwkv[:, k, :], k == 0, k == 7,
                     reads=["mhT", ("wb", 1)], writes=[("ps", pi)])
            kvv = psb[pi][:].rearrange("p (h d) -> p h d", h=4)
            b.act(Vm[:, mt, :, 0:64], kvv[:, :, 64:128], AF.Copy, reads=[("ps", pi)], writes=["Vm"])
            head_norm(kvv[:, :, 0:64], [("ps", pi)], mgk, 6, lambda h: h * 2 + mt)
        b.act(KmT[0:64, :, :], psb_bf[6][0:64, :].rearrange("p (h n) -> p h n", h=4), AF.Copy,
              reads=[("ps", 6)], writes=["KmT"])

        norm_x(l, 8, 0, 2, None)
        for hf in range(2):
            for gg in range(2):
                g4 = hf * 2 + gg
                qp = psp[gg][:, :].rearrange("p (g n) -> p g n", g=4)
                PK = [("ps", 2 * gg), ("ps", 2 * gg + 1)]
                for i in range(4):
                    t8 = gg * 4 + i
                    for k in range(8):
                        b.mm(qp[:, i, :], hT[:, k, t8 * 128:(t8 + 1) * 128], mwq[:, k, :], k == 0, k == 7,
                             reads=[("hT", t8 // 4), ("wb", 1, "glu")], writes=[("ps", 2 * gg + i // 2)])
                src = qp.rearrange("p g (h d) -> p g h d", h=4)
                b.act(mtmp4, src, AF.Square, reads=PK, writes=["mtmp4"])
                b.v("dve", "tensor_reduce", mss4, mtmp4, AX.X, ALU.add, reads=["mtmp4"], writes=["mss4"])
                b.act(mss4, mss4, AF.Sqrt, reads=["mss4", "cvec"], writes=["mss4"], bias=epsv, scale=1.0 / 64)
                b.v("dve", "reciprocal", mss4, mss4, reads=["mss4"], writes=["mss4"])
                b.v("dve", "tensor_tensor", mtmp4, src, mss4.unsqueeze(3).to_broadcast([128, 4, 4, 64]), ALU.mult,
                    reads=PK + ["mss4"], writes=["mtmp4"])
                b.v("dve", "tensor_tensor", mkn4, mtmp4, mgq.unsqueeze(1).unsqueeze(1).to_broadcast([128, 4, 4, 64]), ALU.mult,
                    reads=["mtmp4", "rowb"], writes=["mkn4"])
                for i in range(4):
                    bank = 6 + i // 2
                    for h in range(4):
                        sl = h * 2 + (i % 2)
                        b.tr(psb_bf[bank][0:64, sl * 128:(sl + 1) * 128], mkn4[:, i, h, :], ident,
                             reads=["mkn4", "cbf"], writes=[("ps", bank)])
                for half2 in range(2):
                    t16p = g4 * 2 + half2
                    b.act(QmT[0:64, :, t16p * 256:(t16p + 1) * 256],
                          psb_bf[6 + half2][0:64, :].rearrange("p (h n) -> p h n", h=4), AF.Copy,
                          reads=[("ps", 6 + half2)], writes=[("QmT", g4)])
                if hf == 0:
                    norm_x(l, 8, 1024 + gg * 512, 1, None, t0=gg)
        for h in range(4):
            for qt in range(4):
                po = 2 + (qt % 2)
                for kb in range(2):
                    b.mm(psb[kb][:], KmT[0:64, h, kb * 128:(kb + 1) * 128], QmT[0:64, h, qt * 512:(qt + 1) * 512], True, True,
                         reads=["KmT", ("QmT", qt)], writes=[("ps", kb)])
                for kb in range(2):
                    pi = kb
                    pt = PT[kb + 2 * (qt % 2)]
                    b.act(pt, psb[pi][:], AF.Exp, reads=[("ps", pi), "cvec"], writes=[("sq", kb + 2 * (qt % 2))], bias=nmshift, scale=1.0)
                    b.mm(psb[po][0:65, :], Vm[:, kb, h, :], pt, kb == 0, kb == 1, reads=["Vm", ("sq", kb + 2 * (qt % 2))], writes=[("ps", po)])
                b.v("dve", "reciprocal", rc[64:65, :], psb[po][64:65, :], reads=[("ps", po)], writes=["rc"])
                b.mm(psb[4][0:64, :], ones_f[64:65, 0:64], rc[64:65, :], True, True, reads=["cst", "rc"], writes=[("ps", 4)])
                b.act(rstd[0:64, :], psb[4][0:64, :], AF.Copy, reads=[("ps", 4)], writes=["rstd"])
                b.v("dve", "tensor_tensor", omT[0:64, h, qt * 512:(qt + 1) * 512], psb[po][0:64, :], rstd[0:64, :], ALU.mult,
                    reads=[("ps", po), "rstd"], writes=[("omT", h, qt)])
        mwo = wb[0][0:64, 0:4096].rearrange("p (h n) -> p h n", h=4)
        load_w(mwo, mwo_d[l], ("wb", 0))
        for g4 in range(4):
            tok = slice(g4 * 512, (g4 + 1) * 512)
            for oc in range(8):
                pi = next_ps()
                for h in range(4):
                    b.mm(psb[pi][:], mwo[:, h, oc * 128:(oc + 1) * 128], omT[0:64, h, tok], h == 0, h == 3,
                         reads=[("wb", 0), ("omT", h, g4)], writes=[("ps", pi)])
                b.v("dve", "tensor_tensor", xT[:, oc, tok], xT[:, oc, tok], psb[pi][:], ALU.add,
                    reads=[("xT", oc, g4), ("ps", pi)], writes=[("xT", oc, g4)])
        dbg("x2", xT[:], [128, 8, T], [("xT", c, g) for c in range(8) for g in range(4)])

    MLP_KEYS = [("hid", j, t) for j in range(32) for t in range(2)] + [("rt", i) for i in range(4)]

    WBH = [wb[0][:, 0:4096], wb[0][:, 4096:8192], wb[1][:, 0:4096], wb[1][:, 4096:8192]]
    WBH_KEYS = [("wbh", i) for i in range(4)] + [("wb", 0), ("wb", 1), ("wb", 1, "glu"), "rc"]

    def mlp(l):
        hid = A16(0, [128, 32, 1024])
        rt = [A16(64, [128, 512]), A16(65, [128, 512]), A16(66, [128, 512]), A16(67, [128, 512])]
        n = 0
        wi = 0
        norm_x(l, 16, 0, 2, None)
        for hf in range(2):
            for jg in range(8):
                ws = wi % 4
                wi += 1
                w1p = WBH[ws].rearrange("p (k n) -> p k n", k=8)
                b.dma(w1p, w1_d[l][:, :, jg * 512:(jg + 1) * 512], writes=[("wbh", ws)], key=("wh", ws), eng="pool")
                for jj in range(4):
                    j = jg * 4 + jj
                    for t in range(2):
                        pi = next_ps()
                        for k in range(8):
                            b.mm(psb[pi][:], w1p[:, k, jj * 128:(jj + 1) * 128], hT[:, k, t * 512:(t + 1) * 512], k == 0, k == 7,
                                 reads=[("wbh", ws), ("hT", t)], writes=[("ps", pi)])
                        r = rt[n % 4]
                        b.act(r, psb[pi][:], AF.Relu, reads=[("ps", pi)], writes=[("rt", n % 4)])
                        b.v("dve", "tensor_tensor", hid[:, j, t * 512:(t + 1) * 512], r, r, ALU.mult,
                            reads=[("rt", n % 4)], writes=[("hid", j, t)])
                        n += 1
            if hf == 0:
                norm_x(l, 16, 1024, 2, None)
            for oc in range(8):
                ws = wi % 4
                wi += 1
                w2p = WBH[ws].rearrange("p (j n) -> p j n", j=32)
                b.dma(w2p, w2_d[l][:, :, oc * 128:(oc + 1) * 128], writes=[("wbh", ws)], key=("wh", ws), eng="pool")
                for t in range(2):
                    g4 = hf * 2 + t
                    tok = slice(g4 * 512, (g4 + 1) * 512)
                    pi = next_ps()
                    for j in range(32):
                        b.mm(psb[pi][:], w2p[:, j, :], hid[:, j, t * 512:(t + 1) * 512], j == 0, j == 31,
                             reads=[("wbh", ws), ("hid", j, t)], writes=[("ps", pi)])
                    b.v("dve", "tensor_tensor", xT[:, oc, tok], xT[:, oc, tok], psb[pi][:], ALU.add,
                        reads=[("xT", oc, g4), ("ps", pi)], writes=[("xT", oc, g4)])

    def run_layer(l, half=1, stop_after=None):
        rope_tables()
        load_w(wb[1][:, 0:1536].rearrange("p (k n) -> p k n", k=2), w_uq_d[l], ("wb", 1))
        load_w(wb[1][:, 1536:2560], w_ukv_d[l], ("wb", 1))
        load_w(wb[1][:, 4096:6144].rearrange("p (k n) -> p k n", k=4), w_glu_d[l], ("wb", 1, "glu"))
        layer(l)
        load_w(wb[0][:, :].rearrange("p (k n) -> p k n", k=8), w_out_d[l], ("wb", 0))
        YMK = [("ymla", h, q) for h in range(4) for q in range(4)]
        barrier(PROJ_KEYS + SSM_KEYS + HK + YMK)
        ssm(l, half)
        if stop_after == "ssm":
            return
        barrier(SSM_KEYS + HK + YMK + ["gK", "gM"] + [("gate", i) for i in range(4)])
        ssm_post(l)
        if stop_after == "ssm_post":
            return
        barrier(SSM_KEYS + HK + YMK + ["gK", "gM"] + [("gate", i) for i in range(4)] + ATT_KEYS + SQK)
        mla(l, half)
        mix_out(l)
        if stop_after == "mix":
            return
        barrier(MIXER_KEYS + ATT_KEYS + CR_KEYS)
        cross(l)
        if stop_after == "cross":
            return
        barrier(CR_KEYS + MLP_KEYS + WBH_KEYS)
        mlp(l)
        barrier(MLP_KEYS + MIXER_KEYS + PROJ_KEYS + WBH_KEYS)

    def store_x(half):
        for c in range(8):
            b.dma(outT_d[half, :, c, :], xT[:, c, :], reads=[("xT", c, g) for g in range(4)], writes=[("dramout", "out")], key="out")

    def finish():
        b.p.add("sp", lambda e: e.nop(), reads=[("dramout", "out")] + [("dbgdone", k) for k in b.out_keys])

    b.ctx = dict(locals())
    return b


_CACHE = {}
N_LAYERS = 4


def _program():
    if "b" not in _CACHE:
        b = build_program(N_LAYERS)
        c = b.ctx
        for half in range(2):
            c["load_x"](half)
            for l in range(N_LAYERS):
                c["run_layer"](l, half)
            c["store_x"](half)
        c["finish"]()
        b.p.emit(b.nc, b.es)
        _CACHE["b"] = b
    return _CACHE["b"]


def kernel(**inputs):
    inp = {k: np.asarray(v) for k, v in inputs.items()}
    B = inp["x"].shape[0]
    b = _program()
    cst = host_constants()
    W = prep_weights(inp, range(N_LAYERS))
    CORE_OF = [0, 1, 4, 5]
    maps = {}
    for bb in range(B):
        xs = np.asarray(inp["x"][bb], np.float32)
        xT = np.stack([chunkT(np.ascontiguousarray(xs[hf * T:(hf + 1) * T].T), 8) for hf in range(2)])
        pos = np.asarray(inp["positions"][bb]).astype(np.float32)
        m = {"xT": xT,
             "memT": chunkT(np.ascontiguousarray(np.asarray(inp["mem"][bb], np.float32).T), 8),
             "pos": np.stack([np.ascontiguousarray(pos[hf * T:(hf + 1) * T].reshape(16, 128).T) for hf in range(2)]),
             "cst": cst}
        m.update(W)
        maps[CORE_OF[bb]] = m
    zero = {k: np.zeros_like(v) for k, v in maps[CORE_OF[0]].items()}
    in_maps = [maps.get(core, zero) for core in range(8)]
    res = run_bass_kernel_spmd(b.nc, in_maps, core_ids=list(range(8)))
    out = np.zeros((B, 2 * T, D), np.float32)
    for bb in range(B):
        o = np.asarray(res.results[CORE_OF[bb]]["outT"], np.float32)
        for hf in range(2):
            out[bb, hf * T:(hf + 1) * T] = o[hf].transpose(2, 1, 0).reshape(T, D)
    return out
```
